# Optimizing a Trainium2 kernel written in Bass

```python
import math
import functools
import jax
import jax.numpy as jnp
from jax import lax
import numpy as np

D_MODEL = 1024
BATCH = 4
SEQ = 4096
DEPTH = 1
DEC_BATCH = 128
DEC_SEQ = 4
PAST_LEN = 8192
PAGE_SIZE = 128

MIX_WIDTH = D_MODEL
SSM_WIDTH = MIX_WIDTH // 2
SSM_GROUP = 16
SSM_GROUPS = SSM_WIDTH // SSM_GROUP
SSM_STATE = 64
ATT_WIDTH = MIX_WIDTH - SSM_WIDTH
D_NOPE = 64
D_ROPE = 32
D_V = 64
N_HEADS = ATT_WIDTH // D_V
Q_LORA = 3 * D_MODEL // 8
KV_LORA = D_MODEL // 4
IN_WIDTH = SSM_WIDTH + Q_LORA + KV_LORA + D_ROPE
ROPE_THETA = 10000.0
ATTN_SCALE = 1.0 / math.sqrt(D_NOPE + D_ROPE)
Q_BLOCK = 128
PEER_HEADS = 8
PEER_KEYS = 128
PEER_EXPERTS = PEER_KEYS * PEER_KEYS
PEER_QUERY = 256
PEER_HALF = PEER_QUERY // 2
PEER_TOPK = 16
PEER_BLOCK = 128
EPS = 1e-6

kernel_name = 'hymba_s5_mla_peer_step'


def _rmsnorm(x, g):
    x32 = x.astype(jnp.float32)
    y = x32 * lax.rsqrt(jnp.mean(x32 * x32, axis=-1, keepdims=True) + EPS)
    return (y * g.astype(jnp.float32)).astype(x.dtype)


def _rope_angles(pos):
    inv = ROPE_THETA ** (-jnp.arange(0, D_ROPE, 2, dtype=jnp.float32) / D_ROPE)
    ang = pos.astype(jnp.float32)[:, None] * inv[None, :]
    return jnp.cos(ang), jnp.sin(ang)


def _rope(x, cos, sin):
    half = x.shape[-1] // 2
    x32 = x.astype(jnp.float32)
    x1, x2 = x32[..., :half], x32[..., half:]
    return jnp.concatenate([x1 * cos - x2 * sin, x2 * cos + x1 * sin], axis=-1).astype(x.dtype)


def _linear_combine(e1, e2):
    a1, b1 = e1
    a2, b2 = e2
    return a1 * a2, a2 * b1 + b2


def _s5(u, h0, a_re, a_im, log_dt, b, c, d_skip):
    bsz, t, _ = u.shape
    f32 = jnp.float32
    u32 = u.astype(f32)
    lam = lax.complex(a_re.astype(f32), a_im.astype(f32))
    dt = jnp.exp(log_dt.astype(f32))[:, None]
    a_bar = jnp.exp(lam * dt)
    b_c = lax.complex(b[..., 0].astype(f32), b[..., 1].astype(f32))
    c_c = lax.complex(c[..., 0].astype(f32), c[..., 1].astype(f32))
    b_bar = ((a_bar - 1.0) / lam)[:, :, None] * b_c
    ug = u32.reshape(bsz, t, SSM_GROUPS, SSM_GROUP).astype(jnp.complex64)
    bu = jnp.einsum('btgh,gph->btgp', ug, b_bar)
    bu = bu.at[:, 0].add(a_bar[None] * h0)
    a_seq = jnp.broadcast_to(a_bar, bu.shape)
    _, h = lax.associative_scan(_linear_combine, (a_seq, bu), axis=1)
    y = jnp.einsum('btgp,ghp->btgh', h, c_c).real.reshape(bsz, t, SSM_WIDTH)
    y = y + d_skip.astype(f32) * u32
    return y.astype(u.dtype), h[:, -1]


def _latent_keys(c_kv, w_uk, g_kn):
    return _rmsnorm(jnp.einsum('btc,chd->bthd', c_kv, w_uk), g_kn)


def _attend(qn, qr, kn, kr, ckv, mask):
    s = jnp.einsum('bqhd,bkhd->bhqk', qn, kn) + jnp.einsum('bqhd,bkd->bhqk', qr, kr)
    s = jnp.where(mask, s.astype(jnp.float32) * ATTN_SCALE, -jnp.inf)
    p = jax.nn.softmax(s, axis=-1)
    return jnp.einsum('bhqk,bkc->bqhc', p.astype(ckv.dtype), ckv)


def _prompt_attention(q_nope, q_rope, c_kv, k_rope, w_uk, g_kn):
    bsz, s = q_nope.shape[:2]
    k_nope = _latent_keys(c_kv, w_uk, g_kn)
    kpos = jnp.arange(s)

    def block(i):
        start = i * Q_BLOCK
        qn = lax.dynamic_slice_in_dim(q_nope, start, Q_BLOCK, axis=1)
        qr = lax.dynamic_slice_in_dim(q_rope, start, Q_BLOCK, axis=1)
        mask = kpos[None, :] <= (start + jnp.arange(Q_BLOCK))[:, None]
        return _attend(qn, qr, k_nope, k_rope, c_kv, mask)

    o = lax.map(block, jnp.arange(s // Q_BLOCK))
    return jnp.moveaxis(o, 0, 1).reshape(bsz, s, N_HEADS, KV_LORA)


def _sample_attention(q_nope, q_rope, c_kv, k_rope, w_uk, g_kn, cache_lat, cache_kr, page_table, layer):
    past = page_table.shape[1] * PAGE_SIZE
    t = q_nope.shape[1]
    mask = jnp.arange(past + t)[None, :] <= (past + jnp.arange(t))[:, None]

    def one(args):
        pt, qn, qr, cn, krn = args
        lat = jnp.concatenate([cache_lat[layer, pt].reshape(past, KV_LORA).astype(cn.dtype), cn], axis=0)[None]
        krr = jnp.concatenate([cache_kr[layer, pt].reshape(past, D_ROPE).astype(krn.dtype), krn], axis=0)[None]
        kn = _latent_keys(lat, w_uk, g_kn)
        return _attend(qn[None], qr[None], kn, krr, lat, mask)[0]

    return lax.map(one, (page_table, q_nope, q_rope, c_kv, k_rope))


def _peer(x, wq, keys, u_tab, v_tab):
    shape = x.shape
    xf = x.reshape(-1, shape[-1])
    n = xf.shape[0]
    xf = jnp.pad(xf, ((0, (-n) % PEER_BLOCK), (0, 0)))
    kk = PEER_TOPK * PEER_TOPK

    def block(xb):
        q = (xb @ wq).reshape(PEER_BLOCK, PEER_HEADS, 2, PEER_HALF)
        s = jnp.einsum('qhcd,ckd->qhck', q, keys).astype(jnp.float32)
        s1, i1 = lax.top_k(s[:, :, 0], PEER_TOPK)
        s2, i2 = lax.top_k(s[:, :, 1], PEER_TOPK)
        cand = (s1[..., :, None] + s2[..., None, :]).reshape(PEER_BLOCK, PEER_HEADS, kk)
        cidx = (i1[..., :, None] * PEER_KEYS + i2[..., None, :]).reshape(PEER_BLOCK, PEER_HEADS, kk)
        top, sel = lax.top_k(cand, PEER_TOPK)
        idx = jnp.take_along_axis(cidx, sel, axis=-1)
        g = jax.nn.softmax(top, axis=-1)
        act = jax.nn.gelu(jnp.einsum('qhkd,qd->qhk', jnp.take(u_tab, idx, axis=0), xb).astype(jnp.float32))
        return jnp.einsum('qhk,qhkd->qd', (g * act).astype(xb.dtype), jnp.take(v_tab, idx, axis=0))

    out = lax.map(block, xf.reshape(-1, PEER_BLOCK, shape[-1]))
    return out.reshape(-1, shape[-1])[:n].reshape(shape)


def _layer(x, pos, h0, attend, norm_mix, w_in, norm_q_lora, w_uq, norm_kv_lora, w_uk, w_uv,
           g_qn, g_qr, g_kn, g_kr, a_re, a_im, log_dt, b, c, d_skip, w_glu, w_out,
           norm_ffn, peer_wq, peer_keys, peer_u, peer_v):
    xn = _rmsnorm(x, norm_mix)
    z = xn @ w_in
    u = z[..., :SSM_WIDTH]
    cq = z[..., SSM_WIDTH:SSM_WIDTH + Q_LORA]
    ckv = z[..., SSM_WIDTH + Q_LORA:SSM_WIDTH + Q_LORA + KV_LORA]
    kr = z[..., SSM_WIDTH + Q_LORA + KV_LORA:]
    cos, sin = _rope_angles(pos)
    q = (_rmsnorm(cq, norm_q_lora) @ w_uq).reshape(*cq.shape[:-1], N_HEADS, D_NOPE + D_ROPE)
    q_nope = _rmsnorm(q[..., :D_NOPE], g_qn)
    q_rope = _rope(_rmsnorm(q[..., D_NOPE:], g_qr), cos[:, None], sin[:, None])
    c_kv = _rmsnorm(ckv, norm_kv_lora)
    k_rope = _rope(_rmsnorm(kr, g_kr), cos, sin)
    o_lat = attend(q_nope, q_rope, c_kv, k_rope, w_uk, g_kn)
    o_att = jnp.einsum('bthc,chd->bthd', o_lat, w_uv).reshape(*x.shape[:-1], ATT_WIDTH)
    y_ssm, h_last = _s5(u, h0, a_re, a_im, log_dt, b, c, d_skip)
    gl = jax.nn.gelu(y_ssm) @ w_glu
    glu = gl[..., :SSM_WIDTH] * jax.nn.sigmoid(gl[..., SSM_WIDTH:])
    x = (x + jnp.concatenate([glu, o_att.astype(glu.dtype)], axis=-1) @ w_out).astype(x.dtype)
    x = (x + _peer(_rmsnorm(x, norm_ffn), peer_wq, peer_keys, peer_u, peer_v)).astype(x.dtype)
    return x, c_kv, k_rope, jnp.stack([h_last.real, h_last.imag], axis=-1)


def setup_inputs(seed: int = 0) -> dict:
    key = jax.random.key(seed)
    ks = jax.random.split(key, 32)
    f32 = jnp.float32
    n_pages = PAST_LEN // PAGE_SIZE
    n_used = DEC_BATCH * n_pages
    n_phys = n_used + (n_used + 3) // 4

    def nrm(k, shape, scale):
        return jax.random.normal(k, shape, f32) * scale

    def gain(k, shape):
        return 1.0 + 0.02 * jax.random.normal(k, shape, f32)

    page_table = jax.random.permutation(ks[5], n_phys)[:n_used].reshape(DEC_BATCH, n_pages).astype(jnp.int32)
    n_idx = jnp.arange(SSM_STATE, dtype=f32)
    return {
        'x_prompt': nrm(ks[0], (BATCH, SEQ, D_MODEL), 1.0),
        'x_sample': nrm(ks[1], (DEC_BATCH, DEC_SEQ, D_MODEL), 1.0),
        'cache_kv_latent': nrm(ks[2], (DEPTH, n_phys, PAGE_SIZE, KV_LORA), 1.0),
        'cache_k_rope': nrm(ks[3], (DEPTH, n_phys, PAGE_SIZE, D_ROPE), 1.0),
        'state_ssm': nrm(ks[4], (DEPTH, DEC_BATCH, SSM_GROUPS, SSM_STATE, 2), 0.1),
        'page_table': page_table,
        'norm_mix': gain(ks[6], (DEPTH, D_MODEL)),
        'w_in': nrm(ks[7], (DEPTH, D_MODEL, IN_WIDTH), D_MODEL ** -0.5),
        'norm_q_lora': gain(ks[8], (DEPTH, Q_LORA)),
        'w_uq': nrm(ks[9], (DEPTH, Q_LORA, N_HEADS * (D_NOPE + D_ROPE)), Q_LORA ** -0.5),
        'norm_kv_lora': gain(ks[10], (DEPTH, KV_LORA)),
        'w_uk': nrm(ks[11], (DEPTH, KV_LORA, N_HEADS, D_NOPE), KV_LORA ** -0.5),
        'w_uv': nrm(ks[12], (DEPTH, KV_LORA, N_HEADS, D_V), KV_LORA ** -0.5),
        'qk_gain_q_nope': gain(ks[13], (DEPTH, D_NOPE)),
        'qk_gain_q_rope': gain(ks[14], (DEPTH, D_ROPE)),
        'qk_gain_k_nope': gain(ks[15], (DEPTH, D_NOPE)),
        'qk_gain_k_rope': gain(ks[16], (DEPTH, D_ROPE)),
        'ssm_a_re': -0.5 + nrm(ks[17], (DEPTH, SSM_GROUPS, SSM_STATE), 0.01),
        'ssm_a_im': math.pi * n_idx + nrm(ks[18], (DEPTH, SSM_GROUPS, SSM_STATE), 0.01),
        'ssm_log_dt': jax.random.uniform(ks[19], (DEPTH, SSM_GROUPS), f32, math.log(1e-3), math.log(1e-1)),
        'ssm_b': nrm(ks[20], (DEPTH, SSM_GROUPS, SSM_STATE, SSM_GROUP, 2), (2.0 * SSM_GROUP) ** -0.5),
        'ssm_c': nrm(ks[21], (DEPTH, SSM_GROUPS, SSM_GROUP, SSM_STATE, 2), (2.0 * SSM_STATE) ** -0.5),
        'ssm_d': nrm(ks[22], (DEPTH, SSM_WIDTH), 1.0),
        'w_glu': nrm(ks[23], (DEPTH, SSM_WIDTH, 2 * SSM_WIDTH), SSM_WIDTH ** -0.5),
        'w_out': nrm(ks[24], (DEPTH, MIX_WIDTH, D_MODEL), MIX_WIDTH ** -0.5),
        'norm_ffn': gain(ks[25], (DEPTH, D_MODEL)),
        'peer_wq': nrm(ks[26], (DEPTH, D_MODEL, PEER_HEADS * PEER_QUERY), D_MODEL ** -0.5),
        'peer_keys': nrm(ks[27], (DEPTH, 2, PEER_KEYS, PEER_HALF), PEER_HALF ** -0.5),
        'peer_u': nrm(ks[28], (DEPTH, PEER_EXPERTS, D_MODEL), D_MODEL ** -0.5),
        'peer_v': nrm(ks[29], (DEPTH, PEER_EXPERTS, D_MODEL), (PEER_HEADS * PEER_TOPK) ** -0.5),
    }


def reference(x_prompt, x_sample, cache_kv_latent, cache_k_rope, state_ssm, page_table,
              norm_mix, w_in, norm_q_lora, w_uq, norm_kv_lora, w_uk, w_uv,
              qk_gain_q_nope, qk_gain_q_rope, qk_gain_k_nope, qk_gain_k_rope,
              ssm_a_re, ssm_a_im, ssm_log_dt, ssm_b, ssm_c, ssm_d, w_glu, w_out,
              norm_ffn, peer_wq, peer_keys, peer_u, peer_v):
    seq = x_prompt.shape[1]
    dec_seq = x_sample.shape[1]
    past = page_table.shape[1] * PAGE_SIZE
    pos_p = jnp.arange(seq)
    pos_s = past + jnp.arange(dec_seq)
    h0_p = jnp.zeros((x_prompt.shape[0], SSM_GROUPS, SSM_STATE), jnp.complex64)
    xp, xs = x_prompt, x_sample
    lat_p, kr_p, ssm_p, lat_s, kr_s, ssm_s = [], [], [], [], [], []
    for l in range(DEPTH):
        params = (norm_mix[l], w_in[l], norm_q_lora[l], w_uq[l], norm_kv_lora[l], w_uk[l], w_uv[l],
                  qk_gain_q_nope[l], qk_gain_q_rope[l], qk_gain_k_nope[l], qk_gain_k_rope[l],
                  ssm_a_re[l], ssm_a_im[l], ssm_log_dt[l], ssm_b[l], ssm_c[l], ssm_d[l],
                  w_glu[l], w_out[l], norm_ffn[l], peer_wq[l], peer_keys[l], peer_u[l], peer_v[l])
        st = state_ssm[l].astype(jnp.float32)
        h0_s = lax.complex(st[..., 0], st[..., 1])
        attend_s = functools.partial(_sample_attention, cache_lat=cache_kv_latent, cache_kr=cache_k_rope,
                                     page_table=page_table, layer=l)
        xp, c1, k1, s1 = _layer(xp, pos_p, h0_p, _prompt_attention, *params)
        xs, c2, k2, s2 = _layer(xs, pos_s, h0_s, attend_s, *params)
        lat_p.append(c1)
        kr_p.append(k1)
        ssm_p.append(s1)
        lat_s.append(c2)
        kr_s.append(k2)
        ssm_s.append(s2)
    return (xp, xs, jnp.stack(lat_p), jnp.stack(kr_p), jnp.stack(ssm_p),
            jnp.stack(lat_s), jnp.stack(kr_s), jnp.stack(ssm_s))
```

```python
from contextlib import ExitStack
import math
import numpy as np
import concourse.bass as bass
import concourse.mybir as mybir
from concourse.bass_utils import run_bass_kernel_spmd

F32 = mybir.dt.float32
BF16 = mybir.dt.bfloat16
I32 = mybir.dt.int32
U32 = mybir.dt.uint32
AF = mybir.ActivationFunctionType
ALU = mybir.AluOpType
AX = mybir.AxisListType

D_MODEL = 1024
SEQ = 4096
NCORES = 8
EPS = 1e-6
PAST = 8192
NPAGE = 64
SPC = 16
STOK = 64

STAGE = 1
SPC_RUN = 16
PAGE_LIST = list(range(65))


class Prog:
    ENGS = ("pe", "act", "dve", "pool", "sp")

    def __init__(self, nc, es):
        self.nc = nc
        self.es = es
        self.q = {e: [] for e in self.ENGS}
        self.count = {e: 0 for e in self.ENGS}
        self.sem = {e: nc.alloc_semaphore(name="c_" + e) for e in self.ENGS}
        self.seen = {e: {f: 0 for f in self.ENGS} for e in self.ENGS}
        self.last_w = {}
        self.readers = {}
        self.R = 12
        self.ring = {}
        self.ring_n = {}
        for qn in ("sp", "pool", "act"):
            self.ring[qn] = [nc.alloc_semaphore(name="d_%s%d" % (qn, i)) for i in range(self.R)]
            self.ring_n[qn] = 0
        self.dseen = {e: {} for e in self.ENGS}
        self.ntens = 0

    def sb(self, shape, dt, name=None):
        self.ntens += 1
        name = "s_" + (name or "t%d" % self.ntens)
        return self.es.enter_context(self.nc.sbuf_tensor(name, list(shape), dt))

    def ps(self, shape, dt, name=None):
        self.ntens += 1
        name = "ps_" + (name or "p%d" % self.ntens)
        return self.es.enter_context(self.nc.psum_tensor(name, list(shape), dt))

    def _deps(self, reads, writes):
        toks = []
        for k in list(reads) + list(writes):
            t = self.last_w.get(k)
            if t is not None:
                toks.append(t)
        for k in writes:
            toks.extend(self.readers.get(k, ()))
        return toks

    def _emit_waits(self, eng, toks):
        need = {}
        dneed = {}
        for t in toks:
            if t[0] == "e":
                _, f, c = t
                if f == eng and eng == "pe":
                    continue
                if f == eng and eng == "sp":
                    continue
                if self.seen[eng][f] < c:
                    need[f] = max(need.get(f, 0), c)
            else:
                _, qn, slot, val = t
                key = (qn, slot)
                if self.dseen[eng].get(key, 0) < val:
                    dneed[key] = max(dneed.get(key, 0), val)
        for f, c in need.items():
            self.seen[eng][f] = c
            sem = self.sem[f]
            self.q[eng].append(lambda E, sem=sem, c=c: E.wait_ge(sem, c))
        for (qn, slot), val in dneed.items():
            self.dseen[eng][(qn, slot)] = val
            sem = self.ring[qn][slot]
            self.q[eng].append(lambda E, sem=sem, val=val: E.wait_ge(sem, val))

    def _record(self, tok, reads, writes):
        for k in writes:
            self.last_w[k] = tok
            self.readers[k] = []
        for k in reads:
            if k in writes:
                continue
            self.readers.setdefault(k, []).append(tok)

    def op(self, eng, fn, reads=(), writes=()):
        self._emit_waits(eng, self._deps(reads, writes))
        self.count[eng] += 1
        c = self.count[eng]
        sem = self.sem[eng]
        self.q[eng].append(lambda E, fn=fn, sem=sem: fn(E).then_inc(sem, 1))
        self._record(("e", eng, c), reads, writes)

    def dma(self, qn, fn, reads=(), writes=()):
        toks = self._deps(reads, writes)
        n = self.ring_n[qn]
        self.ring_n[qn] = n + 1
        slot = n % self.R
        use = n // self.R
        if use > 0:
            toks.append(("d", qn, slot, 16 * use))
        self._emit_waits(qn, toks)
        sem = self.ring[qn][slot]
        self.q[qn].append(lambda E, fn=fn, sem=sem: fn(E).then_inc(sem, 16))
        self._record(("d", qn, slot, 16 * (use + 1)), reads, writes)

    def barrier(self):
        toks = [("e", f, self.count[f]) for f in self.ENGS if self.count[f] > 0]
        for qn in self.ring:
            n = self.ring_n[qn]
            for slot in range(min(n, self.R)):
                uses = (n - 1 - slot) // self.R + 1
                toks.append(("d", qn, slot, 16 * uses))
        for e in self.ENGS:
            self._emit_waits(e, [t for t in toks if not (t[0] == "e" and t[1] == e)])

    def finish(self, keys):
        toks = []
        for k in keys:
            t = self.last_w.get(k)
            if t is not None:
                toks.append(t)
        self._emit_waits("sp", toks)

    def run(self):
        nc = self.nc
        with nc.Block() as block:
            @block.tensor
            def _(E):
                for f in self.q["pe"]:
                    f(E)

            @block.scalar
            def _(E):
                for f in self.q["act"]:
                    f(E)

            @block.vector
            def _(E):
                for f in self.q["dve"]:
                    f(E)

            @block.gpsimd
            def _(E):
                for f in self.q["pool"]:
                    f(E)

            @block.sync
            def _(E):
                for f in self.q["sp"]:
                    f(E)


def build_program(n_phys=10240):
    nc = bass.Bass("TRN2", target_bir_lowering=False)
    es = ExitStack()
    with es:
        es.enter_context(nc.allow_low_precision("bf16 matmul operands, fp32 accumulation"))
        P = Prog(nc, es)

        def din(name, shape, dt=F32):
            return nc.dram_tensor(name, list(shape), dt, kind="ExternalInput").ap()

        def dout(name, shape, dt=F32):
            return nc.dram_tensor(name, list(shape), dt, kind="ExternalOutput").ap()

        xf = din("xf", [SEQ, D_MODEL])
        xs = din("xs", [128, D_MODEL])
        ident_d = din("ident", [128, 128])
        rope_f = din("rope_f", [128, 32, 32])
        rope_s = din("rope_s", [128, 32])
        norm_mix = din("norm_mix", [128, 8])
        w_in = din("w_in", [D_MODEL, 1184])
        norm_kv = din("norm_kv_lora", [256])
        g_kr = din("qk_gain_k_rope", [32])
        d_are = din("ssm_a_re", [128, 16])
        d_aim = din("ssm_a_im", [128, 16])
        d_ldt = din("ssm_log_dt", [128, 16])
        d_b = din("ssm_b", [128, 16, 32])
        d_c = din("ssm_c", [128, 16, 32])
        d_d = din("ssm_d", [128, 4])
        d_st = din("state_ssm", [128, 16, 16, 2])
        d_par = din("parity", [128, 2])
        xo = din("xo", [2048, D_MODEL])
        rope_o = din("rope_o", [128, 16, 32])
        d_masks = din("masks", [128, 2, 128])
        d_wuk = din("w_uk", [256, 512])
        d_wuv = din("w_uv", [256, 512])
        d_wuq = din("w_uq", [384, 768])
        d_wglu = din("w_glu", [512, 1024])
        d_wout = din("w_out", [1024, 1024])
        d_gql = din("norm_q_lora", [384])
        d_gqn = din("qk_gain_q_nope", [64])
        d_gqr = din("qk_gain_q_rope", [32])
        d_gkn = din("qk_gain_k_nope", [64])
        d_gffn = din("norm_ffn", [D_MODEL])
        d_wq = din("peer_wq", [D_MODEL, 2048])
        d_keysT = din("peer_keysT", [128, 2, 128])
        d_pu = din("peer_u", [16384, D_MODEL])
        d_pv = din("peer_v", [16384, D_MODEL])
        d_iota = din("iota16", [128, 16])
        d_clat = din("cache_lat", [n_phys * 128, 256])
        d_ckr = din("cache_kr", [n_phys * 128, 32])
        d_ptab = din("page_tab", [1, 1024], I32)
        d_piota = din("piota", [128, 1])
        d_wukT = din("w_ukT", [64, 8, 256])
        d_maskn = din("maskn", [4, 32])
        lat_s_scr = nc.dram_tensor("lat_s_scr", [128, 256], F32, kind="Internal").ap()
        kr_s_scr = nc.dram_tensor("kr_s_scr", [128, 32], F32, kind="Internal").ap()
        o_y_s = dout("o_y_s", [128, D_MODEL])
        KT_d = nc.dram_tensor("KT_scr", [96, 8, SEQ], BF16, kind="Internal").ap()
        V_d = nc.dram_tensor("V_scr", [SEQ, 520], BF16, kind="Internal").ap()

        o_lat_p = dout("o_lat_p", [SEQ, 256])
        o_kr_p = dout("o_kr_p", [SEQ, 32])
        o_lat_s = dout("o_lat_s", [128, 256])
        o_kr_s = dout("o_kr_s", [128, 32])
        o_y_p = dout("o_y_p", [2048, D_MODEL])
        o_ssm_p = dout("o_ssm_p", [16, 128, 2])
        o_ssm_s = dout("o_ssm_s", [16, 16, 128, 2])

        out_keys = []

        def tt(eng, out, a, b, op, r, w):
            P.op(eng, lambda E: E.tensor_tensor(out=out, in0=a, in1=b, op=op), reads=r, writes=w)

        def tsc(eng, out, a, s1, op0, r, w, s2=None, op1=None):
            if op1 is None:
                P.op(eng, lambda E: E.tensor_scalar(out=out, in0=a, scalar1=s1, scalar2=None, op0=op0),
                     reads=r, writes=w)
            else:
                P.op(eng, lambda E: E.tensor_scalar(out=out, in0=a, scalar1=s1, scalar2=s2, op0=op0, op1=op1),
                     reads=r, writes=w)

        def stt(out, a, sc, b, op0, op1, r, w):
            P.op("dve", lambda E: E.scalar_tensor_tensor(out=out, in0=a, scalar=sc, in1=b, op0=op0, op1=op1),
                 reads=r, writes=w)

        def act(out, a, func, r, w, **kw):
            P.op("act", lambda E: E.activation(out=out, in_=a, func=func, **kw), reads=r, writes=w)

        def load(dst, src, key):
            P.dma("sp", lambda E: E.dma_start(out=dst, in_=src), writes=[key])

        ident_f = P.sb([128, 128], F32, "ident_f")
        ident = P.sb([128, 128], BF16, "ident")
        load(ident_f[:], ident_d, "ident_f")
        P.op("dve", lambda E: E.tensor_copy(out=ident[:], in_=ident_f[:]), reads=["ident_f"], writes=["ident"])

        ropeF = P.sb([128, 32, 32], F32, "ropeF")
        ropeS = P.sb([128, 32], F32, "ropeS")
        load(ropeF[:], rope_f, "ropeF")
        load(ropeS[:], rope_s, "ropeS")
        gmix = P.sb([128, 8], F32, "gmix")
        load(gmix[:], norm_mix, "gmix")
        gkv = P.sb([128, 256], F32, "gkv")
        load(gkv[:], norm_kv.unsqueeze(0).to_broadcast([128, 256]), "gkv")
        gkr = P.sb([128, 32], F32, "gkr")
        load(gkr[:], g_kr.unsqueeze(0).to_broadcast([128, 32]), "gkr")
        par = P.sb([128, 2], F32, "par")
        load(par[:], d_par, "par")

        pb = [P.ps([128, 512], F32, "pb%d" % i) for i in range(8)]
        pT = pb[0][:].bitcast(BF16)
        pTk = "pb0"

        w_in_b = P.sb([128, 8, 1184], BF16, "w_in_b")
        xt = [P.sb([128, D_MODEL], F32, "xt%d" % i) for i in range(2)]
        for kt in range(8):
            load(xt[0][:], w_in[kt * 128:(kt + 1) * 128, 0:1024], "xt0")
            load(xt[1][:, 0:160], w_in[kt * 128:(kt + 1) * 128, 1024:1184], "xt1")
            tsc("dve", w_in_b[:, kt, 0:1024], xt[0][:], gmix[:, kt:kt + 1], ALU.mult, ["xt0", "gmix"], [("w_in_b", kt)])
            tsc("dve", w_in_b[:, kt, 1024:1184], xt[1][:, 0:160], gmix[:, kt:kt + 1], ALU.mult, ["xt1", "gmix", ("w_in_b", kt)],
                [("w_in_b", kt)])

        NRT = 16
        TB = 512
        sc = P.sb([128, 24, 16], F32, "s5sc")
        SCK = "s5sc"
        for j, src in enumerate((d_are, d_aim, d_ldt)):
            load(sc[:, j, :], src, SCK)
        ARE, AIM, DT, AR, TH, RR, T0, T1, T2, T3, C0, S0, FRE, FIM, ABR, ABI, C511, S511 = range(18)

        def S(j):
            return sc[:, j, :]
        act(S(DT), S(DT), AF.Exp, [SCK], [SCK])
        tt("dve", S(AR), S(ARE), S(DT), ALU.mult, [SCK], [SCK])
        tt("dve", S(TH), S(AIM), S(DT), ALU.mult, [SCK], [SCK])
        act(S(RR), S(AR), AF.Exp, [SCK], [SCK])
        sci = P.sb([128, 16], I32, "s5sci")
        tsc("dve", S(T0), S(TH), 1.0 / (2.0 * math.pi), ALU.mult, [SCK], [SCK])
        P.op("dve", lambda E: E.tensor_copy(out=sci[:], in_=S(T0)), reads=[SCK], writes=["s5sci"])
        P.op("dve", lambda E: E.tensor_copy(out=S(T1), in_=sci[:]), reads=["s5sci"], writes=[SCK])
        tt("dve", S(T0), S(T0), S(T1), ALU.subtract, [SCK], [SCK])
        tsc("dve", S(T0), S(T0), 2.0 * math.pi / 4.0, ALU.mult, [SCK], [SCK])
        hp = P.sb([128, 1], F32, "halfpi")
        P.op("dve", lambda E: E.memset(hp[:], math.pi / 2.0), writes=["halfpi"])
        act(S(T1), S(T0), AF.Sin, [SCK], [SCK])
        act(S(T2), S(T0), AF.Sin, [SCK, "halfpi"], [SCK], bias=hp[:, 0:1])

        def dbl(cd, sd, cs_, ss_):
            tt("dve", S(T3), cs_, cs_, ALU.mult, [SCK], [SCK])
            tt("dve", cd, ss_, ss_, ALU.mult, [SCK], [SCK])
            tt("dve", cd, S(T3), cd, ALU.subtract, [SCK], [SCK])
            tt("dve", sd, cs_, ss_, ALU.mult, [SCK], [SCK])
            tsc("dve", sd, sd, 2.0, ALU.mult, [SCK], [SCK])
        dbl(S(C511), S(S511), S(T2), S(T1))
        dbl(S(C0), S(S0), S(C511), S(S511))
        tt("dve", S(ABR), S(RR), S(C0), ALU.mult, [SCK], [SCK])
        tt("dve", S(ABI), S(RR), S(S0), ALU.mult, [SCK], [SCK])
        tsc("dve", S(T0), S(ABR), -1.0, ALU.add, [SCK], [SCK])
        tt("dve", S(T1), S(ARE), S(ARE), ALU.mult, [SCK], [SCK])
        tt("dve", S(T2), S(AIM), S(AIM), ALU.mult, [SCK], [SCK])
        tt("dve", S(T1), S(T1), S(T2), ALU.add, [SCK], [SCK])
        P.op("dve", lambda E: E.reciprocal(out=S(T1), in_=S(T1)), reads=[SCK], writes=[SCK])
        tt("dve", S(T2), S(T0), S(ARE), ALU.mult, [SCK], [SCK])
        tt("dve", S(T3), S(ABI), S(AIM), ALU.mult, [SCK], [SCK])
        tt("dve", S(T2), S(T2), S(T3), ALU.add, [SCK], [SCK])
        tt("dve", S(FRE), S(T2), S(T1), ALU.mult, [SCK], [SCK])
        tt("dve", S(T2), S(ABI), S(ARE), ALU.mult, [SCK], [SCK])
        tt("dve", S(T3), S(T0), S(AIM), ALU.mult, [SCK], [SCK])
        tt("dve", S(T2), S(T2), S(T3), ALU.subtract, [SCK], [SCK])
        tt("dve", S(FIM), S(T2), S(T1), ALU.mult, [SCK], [SCK])

        rotc = P.sb([128, 10, 16], F32, "rotc")
        rots = P.sb([128, 10, 16], F32, "rots")
        RK = "rot"
        P.op("dve", lambda E: E.tensor_copy(out=rotc[:, 0, :], in_=S(C0)), reads=[SCK], writes=[RK])
        P.op("dve", lambda E: E.tensor_copy(out=rots[:, 0, :], in_=S(S0)), reads=[SCK], writes=[RK])
        for k in range(9):
            tt("dve", S(T3), rotc[:, k, :], rotc[:, k, :], ALU.mult, [RK, SCK], [SCK])
            tt("dve", S(T2), rots[:, k, :], rots[:, k, :], ALU.mult, [RK, SCK], [SCK])
            tt("dve", rotc[:, k + 1, :], S(T3), S(T2), ALU.subtract, [SCK, RK], [RK])
            tt("dve", S(T3), rotc[:, k, :], rots[:, k, :], ALU.mult, [RK, SCK], [SCK])
            tsc("dve", rots[:, k + 1, :], S(T3), 2.0, ALU.mult, [SCK, RK], [RK])
        tt("dve", S(T2), rotc[:, 9, :], S(C0), ALU.mult, [RK, SCK], [SCK])
        tt("dve", S(T3), rots[:, 9, :], S(S0), ALU.mult, [RK, SCK], [SCK])
        tt("dve", S(C511), S(T2), S(T3), ALU.add, [SCK], [SCK])
        tt("dve", S(T2), rots[:, 9, :], S(C0), ALU.mult, [RK, SCK], [SCK])
        tt("dve", S(T3), rotc[:, 9, :], S(S0), ALU.mult, [RK, SCK], [SCK])
        tt("dve", S(S511), S(T2), S(T3), ALU.subtract, [SCK], [SCK])

        cosT = P.sb([128, NRT, TB], BF16, "cosT")
        sinT = P.sb([128, NRT, TB], BF16, "sinT")
        Gre = P.sb([128, TB], F32, "Gre")
        Gim = P.sb([128, TB], F32, "Gim")
        q1 = P.sb([128, TB], F32, "q1")
        q2 = P.sb([128, TB], F32, "q2")
        for r0 in range(NRT):
            ch = r0 // 4
            P.op("pool", lambda E: E.memset(Gre[:, 0:1], 1.0), writes=["Gre"])
            P.op("pool", lambda E: E.memset(Gim[:, 0:1], 0.0), writes=["Gim"])
            for k in range(9):
                n = 1 << k
                ck = rotc[:, k, r0:r0 + 1]
                sk_ = rots[:, k, r0:r0 + 1]
                tsc("dve", q1[:, 0:n], Gre[:, 0:n], ck, ALU.mult, ["Gre", RK], ["q1"])
                tsc("dve", q2[:, 0:n], Gim[:, 0:n], ck, ALU.mult, ["Gim", RK], ["q2"])
                stt(Gre[:, n:2 * n], Gim[:, 0:n], sk_, q1[:, 0:n], ALU.mult, ALU.subtract, ["Gim", "q1", RK], ["Gre"])
                stt(Gim[:, n:2 * n], Gre[:, 0:n], sk_, q2[:, 0:n], ALU.mult, ALU.add, ["Gre", "q2", RK], ["Gim"])
                tsc("dve", Gre[:, n:2 * n], Gre[:, n:2 * n], -1.0, ALU.mult, ["Gre"], ["Gre"])
            P.op("act", lambda E, r0=r0: E.copy(out=cosT[:, r0, :], in_=Gre[:]), reads=["Gre"],
                 writes=[("cosT", ch)])
            P.op("act", lambda E, r0=r0: E.copy(out=sinT[:, r0, :], in_=Gim[:]), reads=["Gim"],
                 writes=[("sinT", ch)])

        bsb = P.sb([128, 16, 16, 2], F32, "bsb")
        csb = P.sb([128, 16, 16, 2], F32, "csb")
        load(bsb[:], d_b.rearrange("p t (h c) -> p t h c", c=2), "bsb")
        load(csb[:], d_c.rearrange("p t (h c) -> p t h c", c=2), "csb")
        bbr = P.sb([128, 16, 16], F32, "bbr")
        bbi = P.sb([128, 16, 16], F32, "bbi")
        tb1 = P.sb([128, 16, 16], F32, "tb1")
        fre_bc = S(FRE).unsqueeze(2).to_broadcast([128, 16, 16])
        fim_bc = S(FIM).unsqueeze(2).to_broadcast([128, 16, 16])
        tt("dve", bbr[:], bsb[:, :, :, 0], fre_bc, ALU.mult, ["bsb", SCK], ["bbr"])
        tt("dve", tb1[:], bsb[:, :, :, 1], fim_bc, ALU.mult, ["bsb", SCK], ["tb1"])
        tt("dve", bbr[:], bbr[:], tb1[:], ALU.subtract, ["tb1"], ["bbr"])
        tt("dve", bbi[:], bsb[:, :, :, 1], fre_bc, ALU.mult, ["bsb", SCK], ["bbi"])
        tt("dve", tb1[:], bsb[:, :, :, 0], fim_bc, ALU.mult, ["bsb", SCK], ["tb1"])
        tt("dve", bbi[:], bbi[:], tb1[:], ALU.add, ["tb1"], ["bbi"])

        BTr = P.sb([128, NRT, 128], BF16, "BTr")
        BTi = P.sb([128, NRT, 128], BF16, "BTi")
        CZr = P.sb([128, NRT, 128], BF16, "CZr")
        CZi = P.sb([128, NRT, 128], BF16, "CZi")
        zr = P.sb([128, 128], BF16, "zr")
        zi = P.sb([128, 128], BF16, "zi")
        P.op("pool", lambda E: E.memset(CZr[:], 0.0), writes=["CZr"])
        P.op("pool", lambda E: E.memset(CZi[:], 0.0), writes=["CZi"])
        for rt_ in range(NRT):
            c0 = 32 * (rt_ % 4)
            P.op("pool", lambda E: E.memset(zr[:], 0.0), writes=["zr"])
            P.op("pool", lambda E: E.memset(zi[:], 0.0), writes=["zi"])
            for gl in range(2):
                rows = slice(64 * gl, 64 * gl + 64)
                cols = slice(c0 + 16 * gl, c0 + 16 * gl + 16)
                P.op("dve", lambda E, rows=rows, cols=cols, rt_=rt_: E.tensor_copy(out=zr[rows, cols], in_=bbr[rows, rt_, :]),
                     reads=["bbr"], writes=["zr"])
                P.op("dve", lambda E, rows=rows, cols=cols, rt_=rt_: E.tensor_copy(out=zi[rows, cols], in_=bbi[rows, rt_, :]),
                     reads=["bbi"], writes=["zi"])
                P.op("dve", lambda E, rows=rows, cols=cols, rt_=rt_: E.tensor_copy(out=CZr[rows, rt_, cols], in_=csb[rows, rt_, :, 0]),
                     reads=["csb"], writes=["CZr"])
                P.op("dve", lambda E, rows=rows, cols=cols, rt_=rt_: E.tensor_scalar(
                    out=CZi[rows, rt_, cols], in0=csb[rows, rt_, :, 1], scalar1=-1.0, scalar2=None, op0=ALU.mult),
                    reads=["csb"], writes=["CZi"])
            P.op("pe", lambda E: E.transpose(out=pT[:, 0:128], in_=zr[:], identity=ident[:]),
                 reads=["zr", "ident"], writes=[pTk])
            P.op("pe", lambda E: E.transpose(out=pT[:, 128:256], in_=zi[:], identity=ident[:]),
                 reads=["zi", "ident"], writes=[pTk])
            P.op("act", lambda E, rt_=rt_: E.copy(out=BTr[:, rt_, :], in_=pT[:, 0:128]), reads=[pTk], writes=[("BTr", rt_)])
            P.op("act", lambda E, rt_=rt_: E.copy(out=BTi[:, rt_, :], in_=pT[:, 128:256]), reads=[pTk], writes=[("BTi", rt_)])
        dsk = P.sb([128, 4], F32, "dsk")
        load(dsk[:], d_d, "dsk")

        w_uk_b = P.sb([128, 2, 512], BF16, "w_uk_b")
        w_uv_b = P.sb([128, 2, 512], BF16, "w_uv_b")
        for c_ in range(2):
            load(xt[0][:, 0:512], d_wuk[c_ * 128:(c_ + 1) * 128, :], "xt0")
            P.op("act", lambda E, c_=c_: E.copy(out=w_uk_b[:, c_, :], in_=xt[0][:, 0:512]), reads=["xt0"], writes=["w_uk_b"])
            load(xt[1][:, 0:512], d_wuv[c_ * 128:(c_ + 1) * 128, :], "xt1")
            P.op("act", lambda E, c_=c_: E.copy(out=w_uv_b[:, c_, :], in_=xt[1][:, 0:512]), reads=["xt1"], writes=["w_uv_b"])
        gkn = P.sb([128, 64], F32, "gkn")
        load(gkn[:], d_gkn.unsqueeze(0).to_broadcast([128, 64]), "gkn")
        lnb = P.sb([128, 256], BF16, "lnb")
        ckvT = P.sb([128, 2, 128], BF16, "ckvT")
        ssk = P.sb([128, 16], F32, "ssk")
        tmpk = P.sb([128, 8, 64], F32, "tmpk")
        Kcat = P.sb([128, 8, 96], BF16, "Kcat")
        KTt = [P.sb([96, 8, 128], BF16, "KTt%d" % i_) for i_ in range(2)]
        Vt = [P.sb([128, 8, 65], BF16, "Vt%d" % i_) for i_ in range(2)]
        for i_ in range(2):
            P.op("pool", lambda E, i_=i_: E.memset(Vt[i_][:], 1.0), writes=["Vt%d" % i_])

        junk = P.sb([128, D_MODEL], F32, "junk")
        xnb = P.sb([128, D_MODEL], BF16, "xnb")
        xnT4 = P.sb([128, 8, 512], BF16, "xnT4")
        pkv = pb[1]
        ss = P.sb([128, 8], F32, "ss")
        latn = [P.sb([128, 256], F32, "latn%d" % i) for i in range(2)]
        krn = P.sb([128, 32], F32, "krn")
        kro = [P.sb([128, 32], F32, "kro%d" % i) for i in range(2)]
        rtt = P.sb([128, 4, 16], F32, "rt")
        uT = P.sb([128, 4, 512], BF16, "uT")

        def rstd_from_ss(ss_ap, key, d):
            tsc("dve", ss_ap, ss_ap, 1.0 / d, ALU.mult, [key], [key], s2=EPS, op1=ALU.add)
            act(ss_ap, ss_ap, AF.Sqrt, [key], [key])
            P.op("dve", lambda E: E.reciprocal(out=ss_ap, in_=ss_ap), reads=[key], writes=[key])

        def rope(dst, src, cs, keys_r, keys_w):
            cos = cs[:, 0:16]
            sin = cs[:, 16:32]
            x1 = src[:, 0:16]
            x2 = src[:, 16:32]
            tt("dve", rtt[:, 0, :], x1, cos, ALU.mult, keys_r, ["rt0"])
            tt("dve", rtt[:, 1, :], x2, sin, ALU.mult, keys_r, ["rt1"])
            tt("dve", rtt[:, 2, :], x2, cos, ALU.mult, keys_r, ["rt2"])
            tt("dve", rtt[:, 3, :], x1, sin, ALU.mult, keys_r, ["rt3"])
            tt("dve", dst[:, 0:16], rtt[:, 0, :], rtt[:, 1, :], ALU.subtract, ["rt0", "rt1"], keys_w)
            tt("dve", dst[:, 16:32], rtt[:, 2, :], rtt[:, 3, :], ALU.add, ["rt2", "rt3"] + list(keys_w), keys_w)

        def kv_tile(i, j, src_ap, cs_ap, lat_out, kr_out):
            x = xt[i % 2]
            xk = "xt%d" % (i % 2)
            load(x[:], src_ap, xk)
            act(junk[:], x[:], AF.Square, [xk], ["junk", "ss0"], accum_out=ss[:, 0:1])
            rstd_from_ss(ss[:, 0:1], "ss0", D_MODEL)
            tsc("dve", xnb[:], x[:], ss[:, 0:1], ALU.mult, [xk, "ss0"], ["xnb"])
            for kt in range(8):
                P.op("pe", lambda E, kt=kt: E.transpose(out=pT[:, kt * 128:(kt + 1) * 128],
                                                         in_=xnb[:, kt * 128:(kt + 1) * 128], identity=ident[:]),
                     reads=["xnb", "ident"], writes=[pTk])
            P.op("act", lambda E: E.copy(out=xnT4[:, :, j * 128:(j + 1) * 128],
                                         in_=pT.rearrange("p (k t) -> p k t", k=8)),
                 reads=[pTk], writes=[("xnT4", j)])
            for kt in range(8):
                P.op("pe", lambda E, kt=kt: E.matmul(pkv[:, 0:288], lhsT=xnT4[:, kt, j * 128:(j + 1) * 128],
                                                      rhs=w_in_b[:, kt, 896:1184], start=(kt == 0), stop=(kt == 7)),
                     reads=[("xnT4", j), ("w_in_b", kt)], writes=["pb1"])
            act(junk[:, 0:256], pkv[:, 0:256], AF.Square, ["pb1"], ["junk", "ss1"], accum_out=ss[:, 1:2])
            rstd_from_ss(ss[:, 1:2], "ss1", 256)
            ln = latn[i % 2]
            lk = "latn%d" % (i % 2)
            stt(ln[:], pkv[:, 0:256], ss[:, 1:2], gkv[:], ALU.mult, ALU.mult, ["pb1", "ss1", "gkv"], [lk])
            P.dma("sp", lambda E: E.dma_start(out=lat_out, in_=ln[:]), reads=[lk], writes=[("out", id(lat_out))])
            out_keys.append(("out", id(lat_out)))
            act(junk[:, 0:32], pkv[:, 256:288], AF.Square, ["pb1"], ["junk", "ss2"], accum_out=ss[:, 2:3])
            rstd_from_ss(ss[:, 2:3], "ss2", 32)
            stt(krn[:], pkv[:, 256:288], ss[:, 2:3], gkr[:], ALU.mult, ALU.mult, ["pb1", "ss2", "gkr"], ["krn"])
            ko = kro[i % 2]
            kk = "kro%d" % (i % 2)
            rope(ko, krn, cs_ap, ["krn", "ropeF", "ropeS"], [kk])
            P.dma("sp", lambda E: E.dma_start(out=kr_out, in_=ko[:]), reads=[kk], writes=[("out", id(kr_out))])
            out_keys.append(("out", id(kr_out)))
            if i >= 32:
                P.dma("sp", lambda E: E.dma_start(out=lat_s_scr, in_=ln[:]), reads=[lk], writes=["lat_s_scr"])
                P.dma("sp", lambda E: E.dma_start(out=kr_s_scr, in_=ko[:]), reads=[kk], writes=["kr_s_scr"])
                return
            P.op("act", lambda E: E.copy(out=lnb[:], in_=ln[:]), reads=[lk], writes=["lnb"])
            for c_ in range(2):
                P.op("pe", lambda E, c_=c_: E.transpose(out=pT[:, c_ * 128:(c_ + 1) * 128], in_=lnb[:, c_ * 128:(c_ + 1) * 128],
                                                         identity=ident[:]), reads=["lnb", "ident"], writes=[pTk])
            P.op("act", lambda E: E.copy(out=ckvT[:], in_=pT[:, 0:256].rearrange("p (c t) -> p c t", c=2)),
                 reads=[pTk], writes=["ckvT"])
            for c_ in range(2):
                P.op("pe", lambda E, c_=c_: E.matmul(pb[6][:], lhsT=ckvT[:, c_, :], rhs=w_uk_b[:, c_, :],
                                                      start=(c_ == 0), stop=(c_ == 1)),
                     reads=["ckvT", "w_uk_b"], writes=["pb6"])
            for c_ in range(2):
                P.op("pe", lambda E, c_=c_: E.matmul(pb[7][:], lhsT=ckvT[:, c_, :], rhs=w_uv_b[:, c_, :],
                                                      start=(c_ == 0), stop=(c_ == 1)),
                     reads=["ckvT", "w_uv_b"], writes=["pb7"])
            act(junk[:, 0:512], pb[6][:], AF.Square, ["pb6"], ["junk"])
            P.op("dve", lambda E: E.tensor_reduce(out=ssk[:, 0:8], in_=junk[:, 0:512].rearrange("p (h d) -> p h d", h=8),
                                                  axis=AX.X, op=ALU.add), reads=["junk"], writes=["ssk"])
            rstd_from_ss(ssk[:, 0:8], "ssk", 64)
            tt("dve", tmpk[:], pb[6][:].rearrange("p (h d) -> p h d", h=8),
               ssk[:, 0:8].unsqueeze(2).to_broadcast([128, 8, 64]), ALU.mult, ["pb6", "ssk"], ["tmpk"])
            tt("dve", Kcat[:, :, 0:64], tmpk[:], gkn[:].unsqueeze(1).to_broadcast([128, 8, 64]), ALU.mult,
               ["tmpk", "gkn"], ["Kcat"])
            P.op("dve", lambda E: E.tensor_copy(out=Kcat[:, :, 64:96], in_=ko[:].unsqueeze(1).to_broadcast([128, 8, 32])),
                 reads=[kk, "Kcat"], writes=["Kcat"])
            for h in range(8):
                P.op("pe", lambda E, h=h: E.transpose(out=pT[0:96, h * 128:(h + 1) * 128], in_=Kcat[:, h, :], identity=ident[:]),
                     reads=["Kcat", "ident"], writes=[pTk])
            kt_ = KTt[i % 2]
            ktk = "KTt%d" % (i % 2)
            P.op("act", lambda E: E.copy(out=kt_[:], in_=pT[0:96, :].rearrange("p (h t) -> p h t", h=8)),
                 reads=[pTk], writes=[ktk])
            P.dma("sp", lambda E: E.dma_start(out=KT_d[:, :, i * 128:(i + 1) * 128], in_=kt_[:]), reads=[ktk],
                  writes=[("KTd", i)])
            vt_ = Vt[i % 2]
            vtk = "Vt%d" % (i % 2)
            P.op("act", lambda E: E.copy(out=vt_[:, :, 0:64], in_=pb[7][:].rearrange("p (h d) -> p h d", h=8)),
                 reads=["pb7"], writes=[vtk])
            P.dma("sp", lambda E: E.dma_start(out=V_d[i * 128:(i + 1) * 128, :], in_=vt_[:].rearrange("p h d -> p (h d)")),
                  reads=[vtk], writes=[("Vd", i)])

        def u_proj(ntok):
            for ft in range(4):
                for kt in range(8):
                    P.op("pe", lambda E, ft=ft, kt=kt: E.matmul(
                        pb[2][:, 0:ntok], lhsT=w_in_b[:, kt, ft * 128:(ft + 1) * 128], rhs=xnT4[:, kt, 0:ntok],
                        start=(kt == 0), stop=(kt == 7)),
                        reads=[("xnT4", jj) for jj in range(4)] + [("w_in_b", kt)], writes=["pb2"])
                P.op("act", lambda E, ft=ft: E.copy(out=uT[:, ft, 0:ntok], in_=pb[2][:, 0:ntok]),
                     reads=["pb2"], writes=[("uT", ft)])

        m1 = P.sb([128, TB], F32, "m1")
        m2 = P.sb([128, TB], F32, "m2")
        gre = P.sb([128, TB], F32, "gre")
        gim = P.sb([128, TB], F32, "gim")
        hre = P.sb([128, TB], BF16, "hre")
        him = P.sb([128, TB], BF16, "him")
        car = P.sb([128, NRT, 2], F32, "car")
        ctmp = P.sb([128, 2], F32, "ctmp")
        ytmp = P.sb([128, TB], F32, "ytmp")
        ytm2 = P.sb([128, 2, 128], F32, "ytm2")
        Yown = P.sb([128, 4, 2048], BF16, "Yown")
        ssmo = P.sb([128, NRT, 2], F32, "ssmo")
        P.op("pool", lambda E: E.memset(car[:], 0.0), writes=["car"])

        def s5_block(n, last):
            for ft in range(4):
                for r4 in range(4):
                    rt_ = ft * 4 + r4
                    ch = rt_ // 4
                    P.op("pe", lambda E, rt_=rt_, ft=ft: E.matmul(pb[3][:], lhsT=BTr[:, rt_, :], rhs=uT[:, ft, :],
                                                                 start=True, stop=True),
                         reads=[("BTr", rt_), ("uT", ft)], writes=["pb3"])
                    P.op("pe", lambda E, rt_=rt_, ft=ft: E.matmul(pb[4][:], lhsT=BTi[:, rt_, :], rhs=uT[:, ft, :],
                                                                 start=True, stop=True),
                         reads=[("BTi", rt_), ("uT", ft)], writes=["pb4"])
                    cT = cosT[:, rt_, :]
                    sT = sinT[:, rt_, :]
                    ck_, sk2 = ("cosT", ch), ("sinT", ch)
                    tt("dve", m1[:], pb[3][:], cT, ALU.mult, ["pb3", ck_], ["m1"])
                    tt("dve", m2[:], pb[4][:], sT, ALU.mult, ["pb4", sk2], ["m2"])
                    tt("pool", gre[:], m1[:], m2[:], ALU.add, ["m1", "m2"], ["gre"])
                    tt("dve", m1[:], pb[4][:], cT, ALU.mult, ["pb4", ck_], ["m1"])
                    tt("dve", m2[:], pb[3][:], sT, ALU.mult, ["pb3", sk2], ["m2"])
                    tt("pool", gim[:], m1[:], m2[:], ALU.subtract, ["m1", "m2"], ["gim"])
                    P.op("dve", lambda E, rt_=rt_: E.tensor_tensor_scan(
                        out=Gre[:], data0=sc[:, RR, rt_:rt_ + 1].to_broadcast([128, TB]), data1=gre[:], initial=car[:, rt_, 0:1],
                        op0=ALU.mult, op1=ALU.add), reads=[SCK, "gre", "car"], writes=["Gre"])
                    P.op("dve", lambda E, rt_=rt_: E.tensor_tensor_scan(
                        out=Gim[:], data0=sc[:, RR, rt_:rt_ + 1].to_broadcast([128, TB]), data1=gim[:], initial=car[:, rt_, 1:2],
                        op0=ALU.mult, op1=ALU.add), reads=[SCK, "gim", "car"], writes=["Gim"])
                    c9 = rotc[:, 9, rt_:rt_ + 1]
                    s9 = rots[:, 9, rt_:rt_ + 1]
                    if not last:
                        tsc("dve", ctmp[:, 0:1], Gim[:, TB - 1:TB], s9, ALU.mult, ["Gim", RK], ["ctmp"])
                        tsc("dve", ctmp[:, 1:2], Gre[:, TB - 1:TB], s9, ALU.mult, ["Gre", RK], ["ctmp"])
                        stt(car[:, rt_, 0:1], Gre[:, TB - 1:TB], c9, ctmp[:, 0:1], ALU.mult, ALU.subtract,
                            ["Gre", RK, "ctmp"], ["car"])
                        stt(car[:, rt_, 1:2], Gim[:, TB - 1:TB], c9, ctmp[:, 1:2], ALU.mult, ALU.add,
                            ["Gim", RK, "ctmp"], ["car"])
                    else:
                        c5 = sc[:, C511, rt_:rt_ + 1]
                        s5 = sc[:, S511, rt_:rt_ + 1]
                        tsc("dve", ctmp[:, 0:1], Gim[:, TB - 1:TB], s5, ALU.mult, ["Gim", SCK], ["ctmp"])
                        tsc("dve", ctmp[:, 1:2], Gre[:, TB - 1:TB], s5, ALU.mult, ["Gre", SCK], ["ctmp"])
                        stt(ssmo[:, rt_, 0:1], Gre[:, TB - 1:TB], c5, ctmp[:, 0:1], ALU.mult, ALU.subtract,
                            ["Gre", SCK, "ctmp"], ["ssmo"])
                        stt(ssmo[:, rt_, 1:2], Gim[:, TB - 1:TB], c5, ctmp[:, 1:2], ALU.mult, ALU.add,
                            ["Gim", SCK, "ctmp"], ["ssmo"])
                    tt("pool", q1[:], Gre[:], cT, ALU.mult, ["Gre", ck_], ["q1"])
                    tt("pool", q2[:], Gim[:], sT, ALU.mult, ["Gim", sk2], ["q2"])
                    tt("pool", hre[:], q1[:], q2[:], ALU.subtract, ["q1", "q2"], ["hre"])
                    tt("pool", q1[:], Gre[:], sT, ALU.mult, ["Gre", sk2], ["q1"])
                    tt("pool", q2[:], Gim[:], cT, ALU.mult, ["Gim", ck_], ["q2"])
                    tt("pool", him[:], q1[:], q2[:], ALU.add, ["q1", "q2"], ["him"])
                    P.op("pe", lambda E, rt_=rt_, r4=r4: E.matmul(pb[5][:], lhsT=CZr[:, rt_, :], rhs=hre[:],
                                                                 start=(r4 == 0), stop=False),
                         reads=["CZr", "hre"], writes=["pb5"])
                    P.op("pe", lambda E, rt_=rt_, r4=r4: E.matmul(pb[5][:], lhsT=CZi[:, rt_, :], rhs=him[:],
                                                                 start=False, stop=(r4 == 3)),
                         reads=["CZi", "him"], writes=["pb5"])
                stt(ytmp[:], uT[:, ft, :], dsk[:, ft:ft + 1], pb[5][:], ALU.mult, ALU.add,
                    [("uT", ft), "dsk", "pb5"], ["ytmp"])
                yv = ytmp[:].rearrange("p (a b t) -> p a b t", a=2, b=2)
                tsc("dve", ytm2[:], yv[:, :, 0, :], par[:, 0:1], ALU.mult, ["ytmp", "par"], ["ytm2"])
                stt(Yown[:, ft, n * 256:(n + 1) * 256].rearrange("p (a t) -> p a t", a=2), yv[:, :, 1, :],
                    par[:, 1:2], ytm2[:], ALU.mult, ALU.add, ["ytmp", "par", "ytm2"], [("Yown", ft, n)])

        for n in range(8):
            for j in range(4):
                i = n * 4 + j
                kv_tile(i, j, xf[i * 128:(i + 1) * 128, :], ropeF[:, i, :],
                        o_lat_p[i * 128:(i + 1) * 128, :], o_kr_p[i * 128:(i + 1) * 128, :])
            u_proj(512)
            s5_block(n, n == 7)
        P.dma("sp", lambda E: E.dma_start(out=o_ssm_p.rearrange("t p c -> p t c"), in_=ssmo[:]),
              reads=["ssmo"], writes=["o_ssm_p"])
        out_keys.append("o_ssm_p")

        kv_tile(32, 0, xs[:, :], ropeS[:, :], o_lat_s[:, :], o_kr_s[:, :])
        u_proj(128)
        bur = P.sb([128, NRT, 64], F32, "bur")
        bui = P.sb([128, NRT, 64], F32, "bui")
        for rt_ in range(NRT):
            ft = rt_ // 4
            P.op("pe", lambda E, rt_=rt_, ft=ft: E.matmul(pb[3][:, 0:128], lhsT=BTr[:, rt_, :], rhs=uT[:, ft, 0:128],
                                                         start=True, stop=True),
                 reads=[("BTr", rt_), ("uT", ft)], writes=["pb3"])
            P.op("pe", lambda E, rt_=rt_, ft=ft: E.matmul(pb[4][:, 0:128], lhsT=BTi[:, rt_, :], rhs=uT[:, ft, 0:128],
                                                         start=True, stop=True),
                 reads=[("BTi", rt_), ("uT", ft)], writes=["pb4"])
            P.op("act", lambda E, rt_=rt_: E.copy(out=bur[:, rt_, :], in_=pb[3][:, 0:64]), reads=["pb3"], writes=["bur"])
            P.op("act", lambda E, rt_=rt_: E.copy(out=bui[:, rt_, :], in_=pb[4][:, 0:64]), reads=["pb4"], writes=["bui"])
        st0 = P.sb([128, NRT, 16, 2], F32, "st0")
        load(st0[:], d_st, "st0")
        Hs = P.sb([128, NRT, 16, 4, 2], F32, "Hs")
        e1 = P.sb([128, NRT, 16], F32, "e1")
        e2 = P.sb([128, NRT, 16], F32, "e2")
        abr_bc = S(ABR).unsqueeze(2).to_broadcast([128, NRT, 16])
        abi_bc = S(ABI).unsqueeze(2).to_broadcast([128, NRT, 16])
        burv = bur[:].rearrange("p r (b t) -> p r b t", t=4)
        buiv = bui[:].rearrange("p r (b t) -> p r b t", t=4)
        for t in range(4):
            if t == 0:
                pr, pi_ = st0[:, :, :, 0], st0[:, :, :, 1]
                pk = ["st0"]
            else:
                pr, pi_ = Hs[:, :, :, t - 1, 0], Hs[:, :, :, t - 1, 1]
                pk = ["Hs"]
            tt("dve", e1[:], pr, abr_bc, ALU.mult, pk + [SCK], ["e1"])
            tt("dve", e2[:], pi_, abi_bc, ALU.mult, pk + [SCK], ["e2"])
            tt("dve", e1[:], e1[:], e2[:], ALU.subtract, ["e2"], ["e1"])
            tt("dve", Hs[:, :, :, t, 0], e1[:], burv[:, :, :, t], ALU.add, ["e1", "bur"], ["Hs"])
            tt("dve", e1[:], pi_, abr_bc, ALU.mult, pk + [SCK], ["e1"])
            tt("dve", e2[:], pr, abi_bc, ALU.mult, pk + [SCK], ["e2"])
            tt("dve", e1[:], e1[:], e2[:], ALU.add, ["e2"], ["e1"])
            tt("dve", Hs[:, :, :, t, 1], e1[:], buiv[:, :, :, t], ALU.add, ["e1", "bui"], ["Hs"])
        for rt_ in range(NRT):
            P.dma("sp", lambda E, rt_=rt_: E.dma_start(out=o_ssm_s[:, rt_, :, :].rearrange("b p c -> p b c"),
                                                     in_=Hs[:, rt_, :, 3, :]),
                  reads=["Hs"], writes=[("o_ssm_s", rt_)])
            out_keys.append(("o_ssm_s", rt_))

        hsr = Gre[:].bitcast(BF16)
        hsi = Gim[:].bitcast(BF16)
        P.op("act", lambda E: E.copy(out=hsr.rearrange("p (r b t) -> p r b t", r=16, b=16), in_=Hs[:, :, :, :, 0]),
             reads=["Hs"], writes=["Gre"])
        P.op("act", lambda E: E.copy(out=hsi.rearrange("p (r b t) -> p r b t", r=16, b=16), in_=Hs[:, :, :, :, 1]),
             reads=["Hs"], writes=["Gim"])
        Ys = ytmp[:, 0:256].rearrange("p (f t) -> p f t", f=4)
        for ft in range(4):
            for r4 in range(4):
                rt_ = ft * 4 + r4
                P.op("pe", lambda E, rt_=rt_, r4=r4: E.matmul(pb[5][:, 0:64], lhsT=CZr[:, rt_, :], rhs=hsr[:, rt_ * 64:(rt_ + 1) * 64],
                                                             start=(r4 == 0), stop=False), reads=["CZr", "Gre"], writes=["pb5"])
                P.op("pe", lambda E, rt_=rt_, r4=r4: E.matmul(pb[5][:, 0:64], lhsT=CZi[:, rt_, :], rhs=hsi[:, rt_ * 64:(rt_ + 1) * 64],
                                                             start=False, stop=(r4 == 3)), reads=["CZi", "Gim"], writes=["pb5"])
            stt(Ys[:, ft, :], uT[:, ft, 0:64], dsk[:, ft:ft + 1], pb[5][:, 0:64], ALU.mult, ALU.add,
                [("uT", ft), "dsk", "pb5"], ["ytmp"])

        P.barrier()
        w_out_b = cosT[:].rearrange("p (k a) t -> p k (a t)", k=8)
        w_glu_b = sinT[:, 0:8, :].rearrange("p (k a) t -> p k (a t)", k=4)
        w_uq_b = sinT[:, 8:13, :].rearrange("p a t -> p (a t)")[:, 0:2304].rearrange("p (c n) -> p c n", c=3)
        for k in range(8):
            load(xt[k % 2][:], d_wout[k * 128:(k + 1) * 128, :], "xt%d" % (k % 2))
            P.op("act", lambda E, k=k: E.copy(out=w_out_b[:, k, :], in_=xt[k % 2][:]), reads=["xt%d" % (k % 2)], writes=["w_out_b"])
        for k in range(4):
            load(xt[k % 2][:], d_wglu[k * 128:(k + 1) * 128, :], "xt%d" % (k % 2))
            P.op("act", lambda E, k=k: E.copy(out=w_glu_b[:, k, :], in_=xt[k % 2][:]), reads=["xt%d" % (k % 2)], writes=["w_glu_b"])
        for k in range(3):
            load(xt[k % 2][:, 0:768], d_wuq[k * 128:(k + 1) * 128, :], "xt%d" % (k % 2))
            P.op("act", lambda E, k=k: E.copy(out=w_uq_b[:, k, :], in_=xt[k % 2][:, 0:768]), reads=["xt%d" % (k % 2)], writes=["w_uq_b"])
        gql = P.sb([128, 384], F32, "gql")
        load(gql[:], d_gql.unsqueeze(0).to_broadcast([128, 384]), "gql")
        gqn = P.sb([128, 64], F32, "gqn")
        load(gqn[:], d_gqn.unsqueeze(0).to_broadcast([128, 64]), "gqn")
        gqr = P.sb([128, 32], F32, "gqr")
        load(gqr[:], d_gqr.unsqueeze(0).to_broadcast([128, 32]), "gqr")
        ropeO = P.sb([128, 16, 32], F32, "ropeO")
        load(ropeO[:], rope_o, "ropeO")
        mskf = P.sb([128, 2, 128], F32, "mskf")
        msk = P.sb([128, 2, 128], BF16, "msk")
        load(mskf[:], d_masks, "mskf")
        P.op("dve", lambda E: E.tensor_copy(out=msk[:], in_=mskf[:]), reads=["mskf"], writes=["msk"])
        zer = P.sb([128, 512], BF16, "zer")
        P.op("pool", lambda E: E.memset(zer[:], 0.0), writes=["zer"])
        cqn = P.sb([128, 384], BF16, "cqn")
        cqT = P.sb([128, 3, 128], BF16, "cqT")
        Qcat = P.sb([128, 8, 96], BF16, "Qcat")
        qrn = P.sb([128, 8, 32], F32, "qrn")
        qra = P.sb([128, 8, 16], F32, "qra")
        qrb = P.sb([128, 8, 16], F32, "qrb")
        QT = P.sb([96, 8, 128], BF16, "QT")
        KTb = [P.sb([96, 8, 128], BF16, "KTb%d" % i_) for i_ in range(2)]
        Vb = [P.sb([128, 520], BF16, "Vb%d" % i_) for i_ in range(2)]
        PT = [P.sb([128, 512], BF16, "PT%d" % i_) for i_ in range(2)]
        rec = P.sb([128, 8], F32, "rec")
        oatt = P.sb([128, 8, 64], BF16, "oatt")
        burb = bur[:].rearrange("p a b -> p (a b)").bitcast(BF16)
        mixT = burb[:, 0:1024].rearrange("p (k t) -> p k t", k=8)
        gy = burb[:, 1024:1536].rearrange("p (k t) -> p k t", k=4)
        g1 = m1
        g2 = m2
        sig = gre
        x2 = Hs[:].rearrange("p a b c d -> p (a b c d)")[:, 0:1024]
        ATT_SCALE = 1.0 / math.sqrt(96.0)

        gffn = bui[:].rearrange("p a b -> p (a b)")
        load(gffn, d_gffn.unsqueeze(0).to_broadcast([128, D_MODEL]), "gffn")
        xn2 = uT[:].rearrange("p a b -> p (a b)").bitcast(F32)
        xn2T = xnT4[:, :, 128:256]
        q2c = xnT4[:, 0, 256:384]
        wqs = BTr[:].rearrange("p a b -> p (a b)").bitcast(F32).rearrange("p (k n) -> p k n", k=8)
        wqb = CZr[:].rearrange("p a b -> p (a b)")[:, 0:1024].rearrange("p (k n) -> p k n", k=8)
        czf = CZi[:].rearrange("p a b -> p (a b)").bitcast(F32)
        czb = CZi[:].rearrange("p a b -> p (a b)")
        btu = BTi[:].rearrange("p a b -> p (a b)").bitcast(U32)
        keysf = czf[:, 0:256].rearrange("p (c k) -> p c k", c=2)
        keysb = czb[:, 1024:1280].rearrange("p (c k) -> p c k", c=2)
        load(keysf, d_keysT, "keysf")
        P.op("dve", lambda E: E.tensor_copy(out=keysb, in_=keysf), reads=["keysf"], writes=["keysb"])
        iot = czf[:, 300:316]
        load(iot, d_iota, "iot")
        gsum = czf[:, 320:328]
        s_top = Gre[:, 0:256].rearrange("p (a b) -> p a b", a=16)
        i_topf = Gre[:, 256:512].rearrange("p (a b) -> p a b", a=16)
        wrk = Gim[:, 0:256]
        cand = Gim[:, 256:512].rearrange("p (a b) -> p a b", a=16)
        top16 = q1[:, 0:128].rearrange("p (a b) -> p a b", a=8)
        pa_f = q1[:, 128:256].rearrange("p (a b) -> p a b", a=8)
        pb_f = q1[:, 256:384].rearrange("p (a b) -> p a b", a=8)
        gsm = q1[:, 384:512].rearrange("p (a b) -> p a b", a=8)
        eqt = q2[:, 0:256].rearrange("p (a b) -> p a b", a=16)
        isel = q2[:, 256:512].rearrange("p (c h j) -> p c h j", c=2, h=8)
        idxf = gim[:, 0:128]
        pre = gim[:, 128:256]
        pg1 = gim[:, 256:384]
        coef = gim[:, 384:512]
        i_top = btu[:, 0:256].rearrange("p (a b) -> p a b", a=16)
        pos = btu[:, 256:384].rearrange("p (a b) -> p a b", a=8)
        pa_u = btu[:, 384:512].rearrange("p (a b) -> p a b", a=8)
        pb_u = btu[:, 512:640].rearrange("p (a b) -> p a b", a=8)
        idxu = btu[:, 640:768]
        NEG = -1.0e30

        def top16_of(vals_ap, n, out_vals, out_idx, rkeys, wkeys):
            P.op("dve", lambda E: E.max(out=out_vals[:, 0:8], in_=vals_ap), reads=rkeys, writes=wkeys)
            P.op("dve", lambda E: E.max_index(out=out_idx[:, 0:8], in_max=out_vals[:, 0:8], in_values=vals_ap),
                 reads=rkeys + wkeys, writes=wkeys)
            P.op("dve", lambda E: E.match_replace(out=wrk[:, 0:n], in_to_replace=out_vals[:, 0:8], in_values=vals_ap,
                                                  imm_value=NEG), reads=rkeys + wkeys, writes=["wrk"])
            P.op("dve", lambda E: E.max(out=out_vals[:, 8:16], in_=wrk[:, 0:n]), reads=["wrk"] + wkeys, writes=wkeys)
            P.op("dve", lambda E: E.max_index(out=out_idx[:, 8:16], in_max=out_vals[:, 8:16], in_values=wrk[:, 0:n]),
                 reads=["wrk"] + wkeys, writes=wkeys)

        def peer(x2_ap, x2k, gbufs):
            act(xnb[:], x2_ap, AF.Square, [x2k], ["xnb", "ss0"], accum_out=ss[:, 0:1])
            rstd_from_ss(ss[:, 0:1], "ss0", D_MODEL)
            stt(xn2, x2_ap, ss[:, 0:1], gffn, ALU.mult, ALU.mult, [x2k, "ss0", "gffn"], ["xn2"])
            P.op("act", lambda E: E.copy(out=xnb[:], in_=xn2), reads=["xn2"], writes=["xnb"])
            for kt in range(8):
                P.op("pe", lambda E, kt=kt: E.transpose(out=pT[:, kt * 128:(kt + 1) * 128],
                                                         in_=xnb[:, kt * 128:(kt + 1) * 128], identity=ident[:]),
                     reads=["xnb", "ident"], writes=[pTk])
            P.op("act", lambda E: E.copy(out=xn2T, in_=pT.rearrange("p (k t) -> p k t", k=8)), reads=[pTk], writes=["xn2T"])
            for hc in range(16):
                c_ = hc % 2
                load(wqs, d_wq.rearrange("(k p) n -> p k n", p=128)[:, :, hc * 128:(hc + 1) * 128], "wqs")
                P.op("act", lambda E: E.copy(out=wqb, in_=wqs), reads=["wqs"], writes=["wqb"])
                for kt in range(8):
                    P.op("pe", lambda E, kt=kt: E.matmul(pb[1][:, 0:128], lhsT=wqb[:, kt, :], rhs=xn2T[:, kt, :],
                                                          start=(kt == 0), stop=(kt == 7)),
                         reads=["wqb", "xn2T"], writes=["pb1"])
                P.op("act", lambda E: E.copy(out=q2c, in_=pb[1][:, 0:128]), reads=["pb1"], writes=["q2c"])
                P.op("pe", lambda E, c_=c_: E.matmul(pb[2][:, 0:128], lhsT=q2c, rhs=keysb[:, c_, :], start=True, stop=True),
                     reads=["q2c", "keysb"], writes=["pb2"])
                top16_of(pb[2][:, 0:128], 128, s_top[:, hc, :], i_top[:, hc, :], ["pb2"], ["s_top", "i_top"])
            P.op("dve", lambda E: E.tensor_copy(out=i_topf, in_=i_top), reads=["i_top"], writes=["i_topf"])
            for h in range(8):
                tt("dve", cand, s_top[:, 2 * h, :].unsqueeze(2).to_broadcast([128, 16, 16]),
                   s_top[:, 2 * h + 1, :].unsqueeze(1).to_broadcast([128, 16, 16]), ALU.add, ["s_top"], ["cand"])
                top16_of(cand.rearrange("p a b -> p (a b)"), 256, top16[:, h, :], pos[:, h, :], ["cand"], ["top16", "pos"])
            tt("dve", gsm, top16, top16[:, :, 0:1].to_broadcast([128, 8, 16]), ALU.subtract, ["top16"], ["gsm"])
            act(gsm, gsm, AF.Exp, ["gsm"], ["gsm"])
            P.op("dve", lambda E: E.tensor_reduce(out=gsum, in_=gsm, axis=AX.X, op=ALU.add), reads=["gsm"], writes=["gsum"])
            P.op("dve", lambda E: E.reciprocal(out=gsum, in_=gsum), reads=["gsum"], writes=["gsum"])
            tt("dve", gsm, gsm, gsum.unsqueeze(2).to_broadcast([128, 8, 16]), ALU.mult, ["gsum"], ["gsm"])
            tsc("dve", pa_u, pos, 4, ALU.logical_shift_right, ["pos"], ["pa_u"])
            tsc("dve", pb_u, pos, 15, ALU.bitwise_and, ["pos"], ["pb_u"])
            P.op("dve", lambda E: E.tensor_copy(out=pa_f, in_=pa_u), reads=["pa_u"], writes=["pa_f"])
            P.op("dve", lambda E: E.tensor_copy(out=pb_f, in_=pb_u), reads=["pb_u"], writes=["pb_f"])
            for h in range(8):
                for c_, pf in ((0, pa_f), (1, pb_f)):
                    tt("dve", eqt, pf[:, h, :].unsqueeze(2).to_broadcast([128, 16, 16]),
                       iot.unsqueeze(1).to_broadcast([128, 16, 16]), ALU.is_equal, ["pa_f", "pb_f", "iot"], ["eqt"])
                    tt("dve", eqt, eqt, i_topf[:, 2 * h + c_, :].unsqueeze(1).to_broadcast([128, 16, 16]), ALU.mult,
                       ["i_topf"], ["eqt"])
                    P.op("dve", lambda E, h=h, c_=c_: E.tensor_reduce(out=isel[:, c_, h, :], in_=eqt, axis=AX.X, op=ALU.add),
                         reads=["eqt"], writes=["isel"])
            stt(idxf, isel[:, 0, :, :].rearrange("p h j -> p (h j)"), 128.0, isel[:, 1, :, :].rearrange("p h j -> p (h j)"),
                ALU.mult, ALU.add, ["isel"], ["idxf"])
            P.op("dve", lambda E: E.tensor_copy(out=idxu, in_=idxf), reads=["idxf"], writes=["idxu"])
            for sl in range(128):
                gb = gbufs[sl % 2]
                gk = "gb%d" % (sl % 2)
                P.dma("pool", lambda E, sl=sl, gb=gb: E.indirect_dma_start(
                    out=gb, out_offset=None, in_=d_pu, in_offset=bass.IndirectOffsetOnAxis(ap=idxu[:, sl:sl + 1], axis=0)),
                    reads=["idxu"], writes=[gk])
                P.op("dve", lambda E, sl=sl, gb=gb: E.scalar_tensor_tensor(
                    out=xnb[:], in0=gb, scalar=1.0, in1=xn2, op0=ALU.mult, op1=ALU.mult, accum_out=pre[:, sl:sl + 1]),
                    reads=[gk, "xn2"], writes=["xnb", "pre"])
            tt("dve", pg1, pre, pre, ALU.mult, ["pre"], ["pg1"])
            tsc("dve", pg1, pg1, 0.044715, ALU.mult, ["pg1"], ["pg1"], s2=1.0, op1=ALU.add)
            tt("dve", pg1, pg1, pre, ALU.mult, ["pre"], ["pg1"])
            act(pg1, pg1, AF.Tanh, ["pg1"], ["pg1"], scale=0.7978845608028654)
            tsc("dve", pg1, pg1, 1.0, ALU.add, ["pg1"], ["pg1"], s2=0.5, op1=ALU.mult)
            tt("dve", pg1, pg1, pre, ALU.mult, ["pre"], ["pg1"])
            tt("dve", coef, pg1, gsm.rearrange("p h j -> p (h j)"), ALU.mult, ["pg1", "gsm"], ["coef"])
            for sl in range(128):
                gb = gbufs[sl % 2]
                gk = "gb%d" % (sl % 2)
                P.dma("pool", lambda E, sl=sl, gb=gb: E.indirect_dma_start(
                    out=gb, out_offset=None, in_=d_pv, in_offset=bass.IndirectOffsetOnAxis(ap=idxu[:, sl:sl + 1], axis=0)),
                    reads=["idxu"], writes=[gk])
                stt(x2_ap, gb, coef[:, sl:sl + 1], x2_ap, ALU.mult, ALU.add, [gk, "coef", x2k], [x2k])

        def q_part(x, xk, cs_ap):
            act(junk[:], x[:], AF.Square, [xk], ["junk", "ss0"], accum_out=ss[:, 0:1])
            rstd_from_ss(ss[:, 0:1], "ss0", D_MODEL)
            tsc("dve", xnb[:], x[:], ss[:, 0:1], ALU.mult, [xk, "ss0"], ["xnb"])
            for kt in range(8):
                P.op("pe", lambda E, kt=kt: E.transpose(out=pT[:, kt * 128:(kt + 1) * 128],
                                                         in_=xnb[:, kt * 128:(kt + 1) * 128], identity=ident[:]),
                     reads=["xnb", "ident"], writes=[pTk])
            P.op("act", lambda E: E.copy(out=xnT4[:, :, 0:128], in_=pT.rearrange("p (k t) -> p k t", k=8)),
                 reads=[pTk], writes=[("xnT4", 0)])
            for kt in range(8):
                P.op("pe", lambda E, kt=kt: E.matmul(pb[1][:, 0:384], lhsT=xnT4[:, kt, 0:128], rhs=w_in_b[:, kt, 512:896],
                                                      start=(kt == 0), stop=(kt == 7)),
                     reads=[("xnT4", 0), ("w_in_b", kt)], writes=["pb1"])
            act(junk[:, 0:384], pb[1][:, 0:384], AF.Square, ["pb1"], ["junk", "ss1"], accum_out=ss[:, 1:2])
            rstd_from_ss(ss[:, 1:2], "ss1", 384)
            stt(cqn[:], pb[1][:, 0:384], ss[:, 1:2], gql[:], ALU.mult, ALU.mult, ["pb1", "ss1", "gql"], ["cqn"])
            for c_ in range(3):
                P.op("pe", lambda E, c_=c_: E.transpose(out=pT[:, c_ * 128:(c_ + 1) * 128], in_=cqn[:, c_ * 128:(c_ + 1) * 128],
                                                         identity=ident[:]), reads=["cqn", "ident"], writes=[pTk])
            P.op("act", lambda E: E.copy(out=cqT[:], in_=pT[:, 0:384].rearrange("p (c t) -> p c t", c=3)),
                 reads=[pTk], writes=["cqT"])
            for c_ in range(3):
                P.op("pe", lambda E, c_=c_: E.matmul(pb[6][:], lhsT=cqT[:, c_, :], rhs=w_uq_b[:, c_, 0:512],
                                                      start=(c_ == 0), stop=(c_ == 2)),
                     reads=["cqT", "w_uq_b"], writes=["pb6"])
            for c_ in range(3):
                P.op("pe", lambda E, c_=c_: E.matmul(pb[7][:, 0:256], lhsT=cqT[:, c_, :], rhs=w_uq_b[:, c_, 512:768],
                                                      start=(c_ == 0), stop=(c_ == 2)),
                     reads=["cqT", "w_uq_b"], writes=["pb7"])
            act(junk[:, 0:512], pb[6][:], AF.Square, ["pb6"], ["junk"])
            P.op("dve", lambda E: E.tensor_reduce(out=ssk[:, 0:8], in_=junk[:, 0:512].rearrange("p (h d) -> p h d", h=8),
                                                  axis=AX.X, op=ALU.add), reads=["junk"], writes=["ssk"])
            rstd_from_ss(ssk[:, 0:8], "ssk", 64)
            tt("dve", tmpk[:], pb[6][:].rearrange("p (h d) -> p h d", h=8),
               ssk[:, 0:8].unsqueeze(2).to_broadcast([128, 8, 64]), ALU.mult, ["pb6", "ssk"], ["tmpk"])
            tt("dve", Qcat[:, :, 0:64], tmpk[:], gqn[:].unsqueeze(1).to_broadcast([128, 8, 64]), ALU.mult,
               ["tmpk", "gqn"], ["Qcat"])
            act(junk[:, 512:768], pb[7][:, 0:256], AF.Square, ["pb7"], ["junk2"])
            P.op("dve", lambda E: E.tensor_reduce(out=ssk[:, 8:16], in_=junk[:, 512:768].rearrange("p (h d) -> p h d", h=8),
                                                  axis=AX.X, op=ALU.add), reads=["junk2"], writes=["ssk2"])
            rstd_from_ss(ssk[:, 8:16], "ssk2", 32)
            tt("dve", qrn[:], pb[7][:, 0:256].rearrange("p (h d) -> p h d", h=8),
               ssk[:, 8:16].unsqueeze(2).to_broadcast([128, 8, 32]), ALU.mult, ["pb7", "ssk2"], ["qrn"])
            tt("dve", qrn[:], qrn[:], gqr[:].unsqueeze(1).to_broadcast([128, 8, 32]), ALU.mult, ["gqr"], ["qrn"])
            cosb = cs_ap[:, 0:16].unsqueeze(1).to_broadcast([128, 8, 16])
            sinb = cs_ap[:, 16:32].unsqueeze(1).to_broadcast([128, 8, 16])
            tt("dve", qra[:], qrn[:, :, 0:16], cosb, ALU.mult, ["qrn", "ropeO", "ropeS"], ["qra"])
            tt("dve", qrb[:], qrn[:, :, 16:32], sinb, ALU.mult, ["qrn", "ropeO", "ropeS"], ["qrb"])
            tt("dve", Qcat[:, :, 64:80], qra[:], qrb[:], ALU.subtract, ["qra", "qrb", "Qcat"], ["Qcat"])
            tt("dve", qra[:], qrn[:, :, 16:32], cosb, ALU.mult, ["qrn", "ropeO", "ropeS"], ["qra"])
            tt("dve", qrb[:], qrn[:, :, 0:16], sinb, ALU.mult, ["qrn", "ropeO", "ropeS"], ["qrb"])
            tt("dve", Qcat[:, :, 80:96], qra[:], qrb[:], ALU.add, ["qra", "qrb", "Qcat"], ["Qcat"])

        def own_tile(i):
            x = xt[i % 2]
            xk = "xt%d" % (i % 2)
            load(x[:], xo[i * 128:(i + 1) * 128, :], xk)
            q_part(x, xk, ropeO[:, i, :])
            for h in range(8):
                P.op("pe", lambda E, h=h: E.transpose(out=pT[0:96, h * 128:(h + 1) * 128], in_=Qcat[:, h, :], identity=ident[:]),
                     reads=["Qcat", "ident"], writes=[pTk])
            P.op("act", lambda E: E.copy(out=QT[:], in_=pT[0:96, :].rearrange("p (h t) -> p h t", h=8)),
                 reads=[pTk], writes=["QT"])
            for hg in range(2):
                P.op("pe", lambda E, hg=hg: E.matmul(pb[4 + hg][:], lhsT=zer[:, 0:128], rhs=zer[:], start=True, stop=False),
                     reads=["zer"], writes=["pb%d" % (4 + hg)])
            nkb = 2 * i + 2
            for kb in range(nkb):
                kbuf = KTb[kb % 2]
                kkey = "KTb%d" % (kb % 2)
                vbuf = Vb[kb % 2]
                vkey = "Vb%d" % (kb % 2)
                P.dma("sp", lambda E, kb=kb, kbuf=kbuf: E.dma_start(out=kbuf[:], in_=KT_d[:, :, kb * 128:(kb + 1) * 128]),
                      reads=[("KTd", kb)], writes=[kkey])
                P.dma("sp", lambda E, kb=kb, vbuf=vbuf: E.dma_start(out=vbuf[:], in_=V_d[kb * 128:(kb + 1) * 128, :]),
                      reads=[("Vd", kb)], writes=[vkey])
                for hg in range(2):
                    sp_ = pb[2 + hg]
                    spk = "pb%d" % (2 + hg)
                    for j in range(4):
                        h = hg * 4 + j
                        msk_i = kb - 2 * i
                        P.op("pe", lambda E, h=h, j=j, sp_=sp_, kbuf=kbuf, msk_i=msk_i: E.matmul(
                            sp_[:, j * 128:(j + 1) * 128], lhsT=kbuf[:, h, :], rhs=QT[:, h, :], start=True, stop=(msk_i < 0)),
                            reads=[kkey, "QT"], writes=[spk])
                        if msk_i >= 0:
                            P.op("pe", lambda E, j=j, sp_=sp_, msk_i=msk_i: E.matmul(
                                sp_[:, j * 128:(j + 1) * 128], lhsT=ident[:], rhs=msk[:, msk_i, :], start=False, stop=True),
                                reads=["ident", "msk"], writes=[spk])
                    pt_ = PT[hg]
                    ptk = "PT%d" % hg
                    act(pt_[:], sp_[:], AF.Exp, [spk], [ptk], scale=ATT_SCALE)
                    for j in range(4):
                        h = hg * 4 + j
                        P.op("pe", lambda E, h=h, j=j, hg=hg, pt_=pt_, vbuf=vbuf: E.matmul(
                            pb[4 + hg][:, j * 65:(j + 1) * 65], lhsT=pt_[:, j * 128:(j + 1) * 128],
                            rhs=vbuf[:, h * 65:(h + 1) * 65], start=False, stop=(kb == nkb - 1), skip_group_check=True),
                            reads=[ptk, vkey], writes=["pb%d" % (4 + hg)])
            for hg in range(2):
                ov = pb[4 + hg][:, 0:260].rearrange("p (h d) -> p h d", h=4)
                P.op("dve", lambda E, hg=hg, ov=ov: E.reciprocal(out=rec[:, hg * 4:(hg + 1) * 4].unsqueeze(2), in_=ov[:, :, 64:65]),
                     reads=["pb%d" % (4 + hg)], writes=["rec"])
                tt("dve", oatt[:, hg * 4:(hg + 1) * 4, :], ov[:, :, 0:64],
                   rec[:, hg * 4:(hg + 1) * 4].unsqueeze(2).to_broadcast([128, 4, 64]), ALU.mult,
                   ["pb%d" % (4 + hg), "rec"], ["oatt"])
            oflat = oatt[:].rearrange("p h d -> p (h d)")
            for c_ in range(4):
                P.op("pe", lambda E, c_=c_: E.transpose(out=pT[:, c_ * 128:(c_ + 1) * 128], in_=oflat[:, c_ * 128:(c_ + 1) * 128],
                                                         identity=ident[:]), reads=["oatt", "ident"], writes=[pTk])
            P.op("act", lambda E: E.copy(out=mixT[:, 4:8, :], in_=pT[:, 0:512].rearrange("p (c t) -> p c t", c=4)),
                 reads=[pTk], writes=["mixT_a"])
            glu_part(Yown[:, :, i * 128:(i + 1) * 128], [("Yown", ft, i // 2) for ft in range(4)], 128)
            out_part(x, xk)
            peer(x2, "x2", [junk[:], xt[(i + 1) % 2][:]])
            P.dma("sp", lambda E: E.dma_start(out=o_y_p[i * 128:(i + 1) * 128, :], in_=x2[:]), reads=["x2"],
                  writes=[("o_y_p", i)])
            out_keys.append(("o_y_p", i))

        def glu_part(yv_, ykeys, nt):
            g1v = g1[:].rearrange("p (f t) -> p f t", f=4)[:, :, 0:nt]
            g2v = g2[:].rearrange("p (f t) -> p f t", f=4)[:, :, 0:nt]
            tt("dve", g1v, yv_, yv_, ALU.mult, ykeys, ["g1"])
            tsc("dve", g1[:], g1[:], 0.044715, ALU.mult, ["g1"], ["g1"], s2=1.0, op1=ALU.add)
            tt("dve", g1v, g1v, yv_, ALU.mult, ykeys, ["g1"])
            act(g2[:], g1[:], AF.Tanh, ["g1"], ["g2"], scale=0.7978845608028654)
            tsc("dve", g2[:], g2[:], 1.0, ALU.add, ["g2"], ["g2"], s2=0.5, op1=ALU.mult)
            tt("dve", gy[:, :, 0:nt], g2v, yv_, ALU.mult, ykeys + ["g2"], ["gy"])
            for half in range(2):
                for ot in range(4):
                    o8 = half * 4 + ot
                    for k in range(4):
                        P.op("pe", lambda E, half=half, ot=ot, o8=o8, k=k: E.matmul(
                            pb[2 + half][:, ot * 128:(ot + 1) * 128], lhsT=w_glu_b[:, k, o8 * 128:(o8 + 1) * 128],
                            rhs=gy[:, k, :], start=(k == 0), stop=(k == 3)),
                            reads=["w_glu_b", "gy"], writes=["pb%d" % (2 + half)])
            act(sig[:], pb[3][:], AF.Sigmoid, ["pb3"], ["sig"])
            tt("dve", mixT[:, 0:4, :], pb[2][:].rearrange("p (c t) -> p c t", c=4), sig[:].rearrange("p (c t) -> p c t", c=4),
               ALU.mult, ["pb2", "sig"], ["mixT_g"])

        def out_part(x, xk):
            for half in range(2):
                for k in range(8):
                    P.op("pe", lambda E, half=half, k=k: E.matmul(
                        pb[6 + half][:], lhsT=mixT[:, k, :], rhs=w_out_b[:, k, half * 512:(half + 1) * 512],
                        start=(k == 0), stop=(k == 7)),
                        reads=["mixT_a", "mixT_g", "w_out_b"], writes=["pb%d" % (6 + half)])
                tt("dve", x2[:, half * 512:(half + 1) * 512], pb[6 + half][:], x[:, half * 512:(half + 1) * 512], ALU.add,
                   ["pb%d" % (6 + half), xk], ["x2"])

        for i in range(16):
            own_tile(i)

        P.barrier()
        Y16 = Yown[:].rearrange("p a b -> p (a b)")
        Y32 = Y16.bitcast(F32)
        YU = Y16.bitcast(U32)
        pidx = YU[:, 0:1024]
        latf = [Y32[:, 2048:2304], Y32[:, 2304:2560]]
        krf = [Y32[:, 2560:2592], Y32[:, 2592:2624]]
        lb = [Y16[:, 5248:5505], Y16[:, 5512:5769]]
        krb = [Y16[:, 5776:5808], Y16[:, 5808:5840]]
        lT = [Y16[:, 5840:6096].rearrange("p (c k) -> p c k", c=2), Y16[:, 6096:6352].rearrange("p (c k) -> p c k", c=2)]
        krT = [Y16[:, 6352:6480], Y16[:, 6480:6608]]
        qtT = Y16[:, 6608:7632].rearrange("p (c h t) -> p c h t", c=2, h=8)
        pts = [Y16[:, 7632:7664], Y16[:, 7664:7696]]
        sc1 = Y32[:, 3848:3880]
        olat = Y16[:, 7760:8016]
        olT = Y16[:, 8016:8080].rearrange("p (c n) -> p c n", c=2)
        maskn = czf[:, 330:362]
        load(maskn[0:4, :], d_maskn, "maskn")
        for bf in range(2):
            P.op("pool", lambda E, bf=bf: E.memset(lb[bf][:, 256:257], 1.0), writes=["lb%d" % bf])
        pti = xt[0][:].bitcast(I32)
        load(pti, d_ptab.to_broadcast([128, 1024]), "xt0")
        piota = czf[:, 364:365]
        load(piota, d_piota, "piota")
        tsc("dve", xt[1][:], pti, 128.0, ALU.mult, ["xt0"], ["xt1"])
        tsc("dve", pidx, xt[1][:], piota, ALU.add, ["xt1", "piota"], ["pidx"])
        wukT = [KTb[0][0:64, :, :].rearrange("p a b -> p (a b)"), KTb[1][0:64, :, :].rearrange("p a b -> p (a b)")]
        for hh in range(2):
            load(xt[hh][0:64, :], d_wukT[:, hh * 4:(hh + 1) * 4, :].rearrange("p a b -> p (a b)"), "xt%d" % hh)
            P.op("act", lambda E, hh=hh: E.copy(out=wukT[hh], in_=xt[hh][0:64, :]), reads=["xt%d" % hh], writes=["KTb%d" % hh])
        xsx = xt[0]
        load(xsx[:], xs[:, :], "xt0")
        q_part(xsx, "xt0", ropeS[:, :])
        Qg = Vb[0][:, 0:512].rearrange("p (h d) -> p h d", h=8)
        tt("dve", Qg, Qcat[:, :, 0:64], gkn[:].unsqueeze(1).to_broadcast([128, 8, 64]), ALU.mult, ["Qcat", "gkn"], ["Vb0"])
        for h in range(8):
            P.op("pe", lambda E, h=h: E.transpose(out=pT[0:64, h * 128:(h + 1) * 128], in_=Qg[:, h, :], identity=ident[:]),
                 reads=["Vb0", "ident"], writes=[pTk])
        P.op("act", lambda E: E.copy(out=QT[0:64, :, :], in_=pT[0:64, :].rearrange("p (h t) -> p h t", h=8)),
             reads=[pTk], writes=["QT"])
        for cc in range(2):
            for h in range(8):
                P.op("pe", lambda E, cc=cc, h=h: E.matmul(
                    pb[1 + cc][:, h * 64:(h + 1) * 64], lhsT=wukT[h // 4][:, (h % 4) * 256 + cc * 128:(h % 4) * 256 + (cc + 1) * 128],
                    rhs=QT[0:64, h, 0:64], start=True, stop=True),
                    reads=["KTb0", "KTb1", "QT"], writes=["pb%d" % (1 + cc)])
            P.op("act", lambda E, cc=cc: E.copy(out=qtT[:, cc, :, :], in_=pb[1 + cc][:].rearrange("p (h t) -> p h t", h=8)),
                 reads=["pb%d" % (1 + cc)], writes=["qtT"])
        for h in range(8):
            P.op("pe", lambda E, h=h: E.transpose(out=pT[0:32, h * 128:(h + 1) * 128], in_=Qcat[:, h, 64:96], identity=ident[:]),
                 reads=["Qcat", "ident"], writes=[pTk])
        qrT = [PT[0][0:32, :].rearrange("p (h t) -> p h t", h=4), PT[1][0:32, :].rearrange("p (h t) -> p h t", h=4)]
        for hh in range(2):
            P.op("act", lambda E, hh=hh: E.copy(out=PT[hh][0:32, :], in_=pT[0:32, hh * 512:(hh + 1) * 512]),
                 reads=[pTk], writes=["PT%d" % hh])

        cnt = [0]

        def page(b, j):
            n = 128 if j < NPAGE else 4
            bf = cnt[0] % 2
            cnt[0] += 1
            lk_, kk_, lbk, kbk, ltk, ktk, ptk = ("latf%d" % bf, "krf%d" % bf, "lb%d" % bf, "krb%d" % bf, "lT%d" % bf,
                                                 "krT%d" % bf, "pts%d" % bf)
            if j < NPAGE:
                col = b * NPAGE + j
                P.dma("pool", lambda E: E.indirect_dma_start(
                    out=latf[bf], out_offset=None, in_=d_clat,
                    in_offset=bass.IndirectOffsetOnAxis(ap=pidx[:, col:col + 1], axis=0)), reads=["pidx"], writes=[lk_])
                P.dma("pool", lambda E: E.indirect_dma_start(
                    out=krf[bf], out_offset=None, in_=d_ckr,
                    in_offset=bass.IndirectOffsetOnAxis(ap=pidx[:, col:col + 1], axis=0)), reads=["pidx"], writes=[kk_])
            else:
                P.dma("sp", lambda E: E.dma_start(out=latf[bf][0:4, :], in_=lat_s_scr[4 * b:4 * b + 4, :]),
                      reads=["lat_s_scr"], writes=[lk_])
                P.dma("sp", lambda E: E.dma_start(out=krf[bf][0:4, :], in_=kr_s_scr[4 * b:4 * b + 4, :]),
                      reads=["kr_s_scr"], writes=[kk_])
            P.op("act", lambda E: E.copy(out=lb[bf][0:n, 0:256], in_=latf[bf][0:n, :]), reads=[lk_], writes=[lbk])
            P.op("dve", lambda E: E.tensor_copy(out=krb[bf][0:n, :], in_=krf[bf][0:n, :]), reads=[kk_], writes=[kbk])
            for cc in range(2):
                P.op("pe", lambda E, cc=cc: E.transpose(out=pT[:, cc * 128:cc * 128 + n], in_=lb[bf][0:n, cc * 128:(cc + 1) * 128],
                                                         identity=ident[0:n, 0:n]), reads=[lbk, "ident"], writes=[pTk])
            P.op("pe", lambda E: E.transpose(out=pT[0:32, 256:256 + n], in_=krb[bf][0:n, :], identity=ident[0:n, 0:n]),
                 reads=[kbk, "ident"], writes=[pTk])
            P.op("act", lambda E: E.copy(out=lT[bf][:, :, 0:n], in_=pT[:, 0:256].rearrange("p (c k) -> p c k", c=2)[:, :, 0:n]),
                 reads=[pTk], writes=[ltk])
            P.op("act", lambda E: E.copy(out=krT[bf][0:32, 0:n], in_=pT[0:32, 256:256 + n]), reads=[pTk], writes=[ktk])
            for cc in range(2):
                P.op("pe", lambda E, cc=cc: E.matmul(pb[6][0:n, :], lhsT=lT[bf][:, cc, 0:n], rhs=w_uk_b[:, cc, :],
                                                      start=(cc == 0), stop=(cc == 1)), reads=[ltk, "w_uk_b"], writes=["pb6"])
            for cc in range(2):
                P.op("pe", lambda E, cc=cc: E.matmul(pb[7][0:n, 0:32], lhsT=lT[bf][:, cc, 0:n], rhs=qtT[:, cc, :, 4 * b:4 * b + 4],
                                                      start=(cc == 0), stop=(cc == 1)), reads=[ltk, "qtT"], writes=["pb7"])
            for hh in range(2):
                P.op("pe", lambda E, hh=hh: E.matmul(pb[7][0:n, 64 + 16 * hh:80 + 16 * hh], lhsT=krT[bf][0:32, 0:n],
                                                      rhs=qrT[hh][:, :, 4 * b:4 * b + 4], start=True, stop=True),
                     reads=[ktk, "PT%d" % hh], writes=["pb7"])
            act(junk[0:n, 0:512], pb[6][0:n, :], AF.Square, ["pb6"], ["junk"])
            P.op("dve", lambda E: E.tensor_reduce(out=ssk[0:n, 0:8], in_=junk[0:n, 0:512].rearrange("p (h d) -> p h d", h=8),
                                                  axis=AX.X, op=ALU.add), reads=["junk"], writes=["ssk"])
            rstd_from_ss(ssk[0:n, 0:8], "ssk", 64)
            s3 = sc1[0:n, :].rearrange("p (h q) -> p h q", h=8)
            tt("dve", s3, pb[7][0:n, 0:32].rearrange("p (h q) -> p h q", h=8),
               ssk[0:n, 0:8].unsqueeze(2).to_broadcast([n, 8, 4]), ALU.mult, ["pb7", "ssk"], ["sc1"])
            tt("dve", sc1[0:n, :], sc1[0:n, :], pb[7][0:n, 64:96], ALU.add, ["pb7"], ["sc1"])
            act(pts[bf][0:n, :], sc1[0:n, :], AF.Exp, ["sc1"], [ptk], scale=ATT_SCALE)
            if j == NPAGE:
                tt("dve", pts[bf][0:n, :], pts[bf][0:n, :], maskn[0:n, :], ALU.mult, ["maskn"], [ptk])
            P.op("pe", lambda E: E.matmul(pb[4][0:32, 0:257], lhsT=pts[bf][0:n, :], rhs=lb[bf][0:n, 0:257],
                                           start=(j == PAGE_LIST[0]), stop=(j == PAGE_LIST[-1])), reads=[ptk, lbk], writes=["pb4"])

        for b in range(SPC_RUN):
            for j in PAGE_LIST:
                page(b, j)
            P.op("dve", lambda E: E.reciprocal(out=rec[0:32, 0:1], in_=pb[4][0:32, 256:257]), reads=["pb4"], writes=["rec"])
            tsc("dve", olat[0:32, :], pb[4][0:32, 0:256], rec[0:32, 0:1], ALU.mult, ["pb4", "rec"], ["olat"])
            for cc in range(2):
                P.op("pe", lambda E, cc=cc: E.transpose(out=pT[:, cc * 32:(cc + 1) * 32], in_=olat[0:32, cc * 128:(cc + 1) * 128],
                                                         identity=ident[0:32, 0:32]), reads=["olat", "ident"], writes=[pTk])
            P.op("act", lambda E: E.copy(out=olT, in_=pT[:, 0:64].rearrange("p (c n) -> p c n", c=2)), reads=[pTk], writes=["olT"])
            for h in range(8):
                r0 = (h % 2) * 64
                c0 = (h // 2) * 64 + 4 * b
                for cc in range(2):
                    P.op("pe", lambda E, h=h, cc=cc, r0=r0, c0=c0: E.matmul(
                        pb[5][r0:r0 + 64, c0:c0 + 4], lhsT=w_uv_b[:, cc, h * 64:(h + 1) * 64], rhs=olT[:, cc, h * 4:(h + 1) * 4],
                        start=(cc == 0), stop=(cc == 1), skip_group_check=True), reads=["w_uv_b", "olT"], writes=["pb5"])
        P.op("act", lambda E: E.copy(out=mixT[:, 4:8, 0:64], in_=pb[5][:, 0:256].rearrange("p (c t) -> p c t", c=4)),
             reads=["pb5"], writes=["mixT_a"])
        glu_part(Ys, ["ytmp"], 64)
        out_part(xsx, "xt0")
        peer(x2, "x2", [junk[:], xt[1][:]])
        P.dma("sp", lambda E: E.dma_start(out=o_y_s, in_=x2[:]), reads=["x2"], writes=["o_y_s"])
        out_keys.append("o_y_s")

        P.finish(out_keys)
        P.run()
    return nc


def _rope_tables(pos):
    inv = (10000.0 ** (-np.arange(0, 32, 2, dtype=np.float32) / np.float32(32))).astype(np.float32)
    ang = pos.astype(np.float32)[:, None] * inv[None, :]
    return np.concatenate([np.cos(ang), np.sin(ang)], axis=1).astype(np.float32)


def _c(a):
    return np.ascontiguousarray(a, dtype=np.float32)


_DEBUG_SMALL = 0


def kernel(**inputs):
    f32 = np.float32
    x_prompt = np.asarray(inputs["x_prompt"], f32)
    x_sample = np.asarray(inputs["x_sample"], f32)

    dbg_small = bool(_DEBUG_SMALL)
    nc = build_program(1024 if dbg_small else 10240)

    rope_full = _rope_tables(np.arange(SEQ))
    rope_f = _c(rope_full.reshape(32, 128, 32).transpose(1, 0, 2))
    rs = _rope_tables(PAST + (np.arange(128) % 4))
    ident = np.eye(128, dtype=f32)

    def rows16(a):
        return _c(np.asarray(a, f32).reshape(16, 128).T)

    a_re = rows16(inputs["ssm_a_re"][0])
    a_im = rows16(inputs["ssm_a_im"][0])
    ldt = rows16(np.repeat(np.asarray(inputs["ssm_log_dt"][0], f32)[:, None], 64, axis=1))
    bb = _c(np.asarray(inputs["ssm_b"][0], f32).reshape(16, 128, 32).transpose(1, 0, 2))
    cc = _c(np.asarray(inputs["ssm_c"][0], f32).transpose(0, 2, 1, 3).reshape(16, 128, 32).transpose(1, 0, 2))
    dd = _c(np.asarray(inputs["ssm_d"][0], f32).reshape(4, 128).T)
    state = np.asarray(inputs["state_ssm"][0], f32)

    wuq = np.asarray(inputs["w_uq"][0], f32).reshape(384, 8, 96)
    wuq_p = _c(np.concatenate([wuq[:, :, :64].reshape(384, 512), wuq[:, :, 64:].reshape(384, 256)], axis=1))
    kk = np.arange(128)[:, None]
    qq = np.arange(128)[None, :]
    diag = np.where(kk <= qq, 0.0, -30000.0).astype(f32)
    shared = {
        "cache_lat": np.asarray(inputs["cache_kv_latent"][0], f32).reshape(-1, 256),
        "cache_kr": np.asarray(inputs["cache_k_rope"][0], f32).reshape(-1, 32),
        "piota": _c(np.arange(128, dtype=f32)[:, None]),
        "w_ukT": _c(np.asarray(inputs["w_uk"][0], f32).transpose(2, 1, 0)),
        "maskn": _c(np.tile((np.arange(4)[:, None] <= np.arange(4)[None, :]).astype(f32)[:, None, :], (1, 8, 1)).reshape(4, 32)),
        "norm_ffn": _c(inputs["norm_ffn"][0]),
        "peer_wq": _c(inputs["peer_wq"][0]),
        "peer_keysT": _c(np.asarray(inputs["peer_keys"][0], f32).transpose(2, 0, 1)),
        "peer_u": _c(inputs["peer_u"][0]),
        "peer_v": _c(inputs["peer_v"][0]),
        "iota16": _c(np.tile(np.arange(16, dtype=f32)[None, :], (128, 1))),
        "w_uk": _c(np.asarray(inputs["w_uk"][0], f32).reshape(256, 512)),
        "w_uv": _c(np.asarray(inputs["w_uv"][0], f32).reshape(256, 512)),
        "w_uq": wuq_p,
        "w_glu": _c(inputs["w_glu"][0]),
        "w_out": _c(inputs["w_out"][0]),
        "norm_q_lora": _c(inputs["norm_q_lora"][0]),
        "qk_gain_q_nope": _c(inputs["qk_gain_q_nope"][0]),
        "qk_gain_q_rope": _c(inputs["qk_gain_q_rope"][0]),
        "qk_gain_k_nope": _c(inputs["qk_gain_k_nope"][0]),
        "ident": ident,
        "rope_f": rope_f,
        "rope_s": rs,
        "norm_mix": _c(np.asarray(inputs["norm_mix"][0], f32).reshape(8, 128).T),
        "w_in": _c(inputs["w_in"][0]),
        "norm_kv_lora": _c(inputs["norm_kv_lora"][0]),
        "qk_gain_k_rope": _c(inputs["qk_gain_k_rope"][0]),
        "ssm_a_re": a_re, "ssm_a_im": a_im, "ssm_log_dt": ldt, "ssm_b": bb, "ssm_c": cc, "ssm_d": dd,
    }
    in_maps = []
    for c in range(NCORES):
        b = c // 2
        p = c % 2
        xs = np.zeros((128, D_MODEL), f32)
        xs[:STOK] = x_sample[c * SPC:(c + 1) * SPC].reshape(STOK, D_MODEL)
        st = state[c * SPC:(c + 1) * SPC].reshape(SPC, 16, 128, 2).transpose(2, 1, 0, 3)
        par = np.zeros((128, 2), f32)
        par[:, 0] = 1.0 - p
        par[:, 1] = p
        xb = x_prompt[b].reshape(16, 2, 128, D_MODEL)
        masks = np.zeros((128, 2, 128), f32)
        if p == 0:
            masks[:, 0, :] = diag
            masks[:, 1, :] = -30000.0
        else:
            masks[:, 1, :] = diag
        m = dict(shared)
        if dbg_small:
            ptc = np.asarray(inputs["page_table"], np.int32)[c * SPC:(c + 1) * SPC].reshape(-1)
            m["cache_lat"] = _c(np.asarray(inputs["cache_kv_latent"][0], f32)[ptc].reshape(-1, 256))
            m["cache_kr"] = _c(np.asarray(inputs["cache_k_rope"][0], f32)[ptc].reshape(-1, 32))
        m.update({
            "xo": _c(xb[:, p].reshape(2048, D_MODEL)),
            "rope_o": _c(rope_full.reshape(16, 2, 128, 32)[:, p].transpose(1, 0, 2)),
            "masks": masks,
            "xf": _c(x_prompt[b]),
            "xs": xs,
            "page_tab": (np.arange(1024, dtype=np.int32).reshape(1, 1024) if dbg_small else
                         np.ascontiguousarray(np.asarray(inputs["page_table"], np.int32)[c * SPC:(c + 1) * SPC].reshape(1, 1024))),
            "state_ssm": _c(st),
            "parity": par,
        })
        in_maps.append(m)

    res = run_bass_kernel_spmd(nc, in_maps, core_ids=list(range(NCORES)))
    R = res.results

    y_prompt = np.zeros((4, 16, 2, 128, D_MODEL), f32)
    for c in range(NCORES):
        y_prompt[c // 2, :, c % 2] = R[c]["o_y_p"].reshape(16, 128, D_MODEL)
    y_prompt = y_prompt.reshape(4, SEQ, D_MODEL)
    y_sample = np.concatenate([R[c]["o_y_s"][:STOK].reshape(SPC, 4, D_MODEL) for c in range(NCORES)]).astype(f32)
    lat_p = np.stack([R[2 * b]["o_lat_p"] for b in range(4)])[None]
    kr_p = np.stack([R[2 * b]["o_kr_p"] for b in range(4)])[None]
    ssm_p = np.stack([R[2 * b]["o_ssm_p"].reshape(32, 64, 2) for b in range(4)])[None]
    lat_s = np.concatenate([R[c]["o_lat_s"][:STOK].reshape(SPC, 4, 256) for c in range(NCORES)])[None]
    kr_s = np.concatenate([R[c]["o_kr_s"][:STOK].reshape(SPC, 4, 32) for c in range(NCORES)])[None]
    ssm_s = np.concatenate([R[c]["o_ssm_s"].reshape(SPC, 32, 64, 2) for c in range(NCORES)])[None]
    return (y_prompt, y_sample, lat_p.astype(f32), kr_p.astype(f32), ssm_p.astype(f32), lat_s.astype(f32),
            kr_s.astype(f32), ssm_s.astype(f32))
```

```python
from contextlib import ExitStack
import math
import numpy as np
import concourse.bass as bass
import concourse.mybir as mybir
from concourse.bass_utils import run_bass_kernel_spmd

F32 = mybir.dt.float32
BF16 = mybir.dt.bfloat16
I32 = mybir.dt.int32
U32 = mybir.dt.uint32
AF = mybir.ActivationFunctionType
ALU = mybir.AluOpType
AX = mybir.AxisListType

D_MODEL = 1024
SEQ = 4096
NCORES = 8
EPS = 1e-6
PAST = 8192
NPAGE = 64
SPC = 16
STOK = 64

STAGE = 1
SPC_RUN = 16
PAGE_LIST = list(range(65))


class Prog:
    ENGS = ("pe", "act", "dve", "pool", "sp")

    def __init__(self, nc, es):
        self.nc = nc
        self.es = es
        self.q = {e: [] for e in self.ENGS}
        self.count = {e: 0 for e in self.ENGS}
        self.sem = {e: nc.alloc_semaphore(name="c_" + e) for e in self.ENGS}
        self.seen = {e: {f: 0 for f in self.ENGS} for e in self.ENGS}
        self.last_w = {}
        self.readers = {}
        self.R = 12
        self.ring = {}
        self.ring_n = {}
        for qn in ("sp", "pool", "act"):
            self.ring[qn] = [nc.alloc_semaphore(name="d_%s%d" % (qn, i)) for i in range(self.R)]
            self.ring_n[qn] = 0
        self.dseen = {e: {} for e in self.ENGS}
        self.ntens = 0

    def sb(self, shape, dt, name=None):
        self.ntens += 1
        name = "s_" + (name or "t%d" % self.ntens)
        return self.es.enter_context(self.nc.sbuf_tensor(name, list(shape), dt))

    def ps(self, shape, dt, name=None):
        self.ntens += 1
        name = "ps_" + (name or "p%d" % self.ntens)
        return self.es.enter_context(self.nc.psum_tensor(name, list(shape), dt))

    def _deps(self, reads, writes):
        toks = []
        for k in list(reads) + list(writes):
            t = self.last_w.get(k)
            if t is not None:
                toks.append(t)
        for k in writes:
            toks.extend(self.readers.get(k, ()))
        return toks

    def _emit_waits(self, eng, toks):
        need = {}
        dneed = {}
        for t in toks:
            if t[0] == "e":
                _, f, c = t
                if f == eng and eng == "pe":
                    continue
                if f == eng and eng == "sp":
                    continue
                if self.seen[eng][f] < c:
                    need[f] = max(need.get(f, 0), c)
            else:
                _, qn, slot, val = t
                key = (qn, slot)
                if self.dseen[eng].get(key, 0) < val:
                    dneed[key] = max(dneed.get(key, 0), val)
        for f, c in need.items():
            self.seen[eng][f] = c
            sem = self.sem[f]
            self.q[eng].append(lambda E, sem=sem, c=c: E.wait_ge(sem, c))
        for (qn, slot), val in dneed.items():
            self.dseen[eng][(qn, slot)] = val
            sem = self.ring[qn][slot]
            self.q[eng].append(lambda E, sem=sem, val=val: E.wait_ge(sem, val))

    def _record(self, tok, reads, writes):
        for k in writes:
            self.last_w[k] = tok
            self.readers[k] = []
        for k in reads:
            if k in writes:
                continue
            self.readers.setdefault(k, []).append(tok)

    def op(self, eng, fn, reads=(), writes=()):
        self._emit_waits(eng, self._deps(reads, writes))
        self.count[eng] += 1
        c = self.count[eng]
        sem = self.sem[eng]
        self.q[eng].append(lambda E, fn=fn, sem=sem: fn(E).then_inc(sem, 1))
        self._record(("e", eng, c), reads, writes)

    def dma(self, qn, fn, reads=(), writes=()):
        toks = self._deps(reads, writes)
        n = self.ring_n[qn]
        self.ring_n[qn] = n + 1
        slot = n % self.R
        use = n // self.R
        if use > 0:
            toks.append(("d", qn, slot, 16 * use))
        self._emit_waits(qn, toks)
        sem = self.ring[qn][slot]
        self.q[qn].append(lambda E, fn=fn, sem=sem: fn(E).then_inc(sem, 16))
        self._record(("d", qn, slot, 16 * (use + 1)), reads, writes)

    def barrier(self):
        toks = [("e", f, self.count[f]) for f in self.ENGS if self.count[f] > 0]
        for qn in self.ring:
            n = self.ring_n[qn]
            for slot in range(min(n, self.R)):
                uses = (n - 1 - slot) // self.R + 1
                toks.append(("d", qn, slot, 16 * uses))
        for e in self.ENGS:
            self._emit_waits(e, [t for t in toks if not (t[0] == "e" and t[1] == e)])

    def finish(self, keys):
        toks = []
        for k in keys:
            t = self.last_w.get(k)
            if t is not None:
                toks.append(t)
        self._emit_waits("sp", toks)

    def run(self):
        nc = self.nc
        with nc.Block() as block:
            @block.tensor
            def _(E):
                for f in self.q["pe"]:
                    f(E)

            @block.scalar
            def _(E):
                for f in self.q["act"]:
                    f(E)

            @block.vector
            def _(E):
                for f in self.q["dve"]:
                    f(E)

            @block.gpsimd
            def _(E):
                for f in self.q["pool"]:
                    f(E)

            @block.sync
            def _(E):
                for f in self.q["sp"]:
                    f(E)


def build_program(n_phys=10240):
    nc = bass.Bass("TRN2", target_bir_lowering=False)
    es = ExitStack()
    with es:
        es.enter_context(nc.allow_low_precision("bf16 matmul operands, fp32 accumulation"))
        P = Prog(nc, es)

        def din(name, shape, dt=F32):
            return nc.dram_tensor(name, list(shape), dt, kind="ExternalInput").ap()

        def dout(name, shape, dt=F32):
            return nc.dram_tensor(name, list(shape), dt, kind="ExternalOutput").ap()

        xf = din("xf", [SEQ, D_MODEL])
        xs = din("xs", [128, D_MODEL])
        ident_d = din("ident", [128, 128])
        rope_f = din("rope_f", [128, 32, 32])
        rope_s = din("rope_s", [128, 32])
        norm_mix = din("norm_mix", [128, 8])
        w_in = din("w_in", [D_MODEL, 1184])
        norm_kv = din("norm_kv_lora", [256])
        g_kr = din("qk_gain_k_rope", [32])
        d_are = din("ssm_a_re", [128, 16])
        d_aim = din("ssm_a_im", [128, 16])
        d_ldt = din("ssm_log_dt", [128, 16])
        d_b = din("ssm_b", [128, 16, 32])
        d_c = din("ssm_c", [128, 16, 32])
        d_d = din("ssm_d", [128, 4])
        d_st = din("state_ssm", [128, 16, 16, 2])
        d_par = din("parity", [128, 2])
        xo = din("xo", [2048, D_MODEL])
        rope_o = din("rope_o", [128, 16, 32])
        d_masks = din("masks", [128, 2, 128])
        d_wuk = din("w_uk", [256, 512])
        d_wuv = din("w_uv", [256, 512])
        d_wuq = din("w_uq", [384, 768])
        d_wglu = din("w_glu", [512, 1024])
        d_wout = din("w_out", [1024, 1024])
        d_gql = din("norm_q_lora", [384])
        d_gqn = din("qk_gain_q_nope", [64])
        d_gqr = din("qk_gain_q_rope", [32])
        d_gkn = din("qk_gain_k_nope", [64])
        d_gffn = din("norm_ffn", [D_MODEL])
        d_wq = din("peer_wq", [D_MODEL, 2048])
        d_keysT = din("peer_keysT", [128, 2, 128])
        d_pu = din("peer_u", [16384, D_MODEL])
        d_pv = din("peer_v", [16384, D_MODEL])
        d_iota = din("iota16", [128, 16])
        d_clat = din("cache_lat", [n_phys * 128, 256])
        d_ckr = din("cache_kr", [n_phys * 128, 32])
        d_ptab = din("page_tab", [1, 1024], I32)
        d_piota = din("piota", [128, 1])
        d_wukT = din("w_ukT", [64, 8, 256])
        d_maskn = din("maskn", [4, 32])
        lat_s_scr = nc.dram_tensor("lat_s_scr", [128, 256], F32, kind="Internal").ap()
        kr_s_scr = nc.dram_tensor("kr_s_scr", [128, 32], F32, kind="Internal").ap()
        o_y_s = dout("o_y_s", [128, D_MODEL])
        KT_d = nc.dram_tensor("KT_scr", [96, 8, SEQ], BF16, kind="Internal").ap()
        V_d = nc.dram_tensor("V_scr", [SEQ, 520], BF16, kind="Internal").ap()

        o_lat_p = dout("o_lat_p", [SEQ, 256])
        o_kr_p = dout("o_kr_p", [SEQ, 32])
        o_lat_s = dout("o_lat_s", [128, 256])
        o_kr_s = dout("o_kr_s", [128, 32])
        o_y_p = dout("o_y_p", [2048, D_MODEL])
        o_ssm_p = dout("o_ssm_p", [16, 128, 2])
        o_ssm_s = dout("o_ssm_s", [16, 16, 128, 2])

        out_keys = []

        def tt(eng, out, a, b, op, r, w):
            P.op(eng, lambda E: E.tensor_tensor(out=out, in0=a, in1=b, op=op), reads=r, writes=w)

        def tsc(eng, out, a, s1, op0, r, w, s2=None, op1=None):
            if op1 is None:
                P.op(eng, lambda E: E.tensor_scalar(out=out, in0=a, scalar1=s1, scalar2=None, op0=op0),
                     reads=r, writes=w)
            else:
                P.op(eng, lambda E: E.tensor_scalar(out=out, in0=a, scalar1=s1, scalar2=s2, op0=op0, op1=op1),
                     reads=r, writes=w)

        def stt(out, a, sc, b, op0, op1, r, w):
            P.op("dve", lambda E: E.scalar_tensor_tensor(out=out, in0=a, scalar=sc, in1=b, op0=op0, op1=op1),
                 reads=r, writes=w)

        def act(out, a, func, r, w, **kw):
            P.op("act", lambda E: E.activation(out=out, in_=a, func=func, **kw), reads=r, writes=w)

        def load(dst, src, key):
            P.dma("sp", lambda E: E.dma_start(out=dst, in_=src), writes=[key])

        ident_f = P.sb([128, 128], F32, "ident_f")
        ident = P.sb([128, 128], BF16, "ident")
        load(ident_f[:], ident_d, "ident_f")
        P.op("dve", lambda E: E.tensor_copy(out=ident[:], in_=ident_f[:]), reads=["ident_f"], writes=["ident"])

        ropeF = P.sb([128, 32, 32], F32, "ropeF")
        ropeS = P.sb([128, 32], F32, "ropeS")
        load(ropeF[:], rope_f, "ropeF")
        load(ropeS[:], rope_s, "ropeS")
        gmix = P.sb([128, 8], F32, "gmix")
        load(gmix[:], norm_mix, "gmix")
        gkv = P.sb([128, 256], F32, "gkv")
        load(gkv[:], norm_kv.unsqueeze(0).to_broadcast([128, 256]), "gkv")
        gkr = P.sb([128, 32], F32, "gkr")
        load(gkr[:], g_kr.unsqueeze(0).to_broadcast([128, 32]), "gkr")
        par = P.sb([128, 2], F32, "par")
        load(par[:], d_par, "par")

        pb = [P.ps([128, 512], F32, "pb%d" % i) for i in range(8)]
        pT = pb[0][:].bitcast(BF16)
        pTk = "pb0"

        w_rest = P.sb([128, 8, 800], BF16, "w_rest")
        w_cq = P.sb([128, 8, 384], BF16, "w_cq")
        xt = [P.sb([128, D_MODEL], F32, "xt%d" % i) for i in range(2)]
        for kt in range(8):
            load(xt[0][:], w_in[kt * 128:(kt + 1) * 128, 0:1024], "xt0")
            load(xt[1][:, 0:160], w_in[kt * 128:(kt + 1) * 128, 1024:1184], "xt1")
            tsc("dve", w_rest[:, kt, 0:512], xt[0][:, 0:512], gmix[:, kt:kt + 1], ALU.mult, ["xt0", "gmix"], [("w_in_b", kt)])
            tsc("dve", w_cq[:, kt, :], xt[0][:, 512:896], gmix[:, kt:kt + 1], ALU.mult, ["xt0", "gmix"], [("w_cq", kt)])
            tsc("dve", w_rest[:, kt, 512:640], xt[0][:, 896:1024], gmix[:, kt:kt + 1], ALU.mult, ["xt0", "gmix", ("w_in_b", kt)],
                [("w_in_b", kt)])
            tsc("dve", w_rest[:, kt, 640:800], xt[1][:, 0:160], gmix[:, kt:kt + 1], ALU.mult, ["xt1", "gmix", ("w_in_b", kt)],
                [("w_in_b", kt)])

        NRT = 16
        TB = 512
        sc = P.sb([128, 24, 16], F32, "s5sc")
        SCK = "s5sc"
        for j, src in enumerate((d_are, d_aim, d_ldt)):
            load(sc[:, j, :], src, SCK)
        ARE, AIM, DT, AR, TH, RR, T0, T1, T2, T3, C0, S0, FRE, FIM, ABR, ABI, C511, S511 = range(18)

        def S(j):
            return sc[:, j, :]
        act(S(DT), S(DT), AF.Exp, [SCK], [SCK])
        tt("dve", S(AR), S(ARE), S(DT), ALU.mult, [SCK], [SCK])
        tt("dve", S(TH), S(AIM), S(DT), ALU.mult, [SCK], [SCK])
        act(S(RR), S(AR), AF.Exp, [SCK], [SCK])
        sci = P.sb([128, 16], I32, "s5sci")
        tsc("dve", S(T0), S(TH), 1.0 / (2.0 * math.pi), ALU.mult, [SCK], [SCK])
        P.op("dve", lambda E: E.tensor_copy(out=sci[:], in_=S(T0)), reads=[SCK], writes=["s5sci"])
        P.op("dve", lambda E: E.tensor_copy(out=S(T1), in_=sci[:]), reads=["s5sci"], writes=[SCK])
        tt("dve", S(T0), S(T0), S(T1), ALU.subtract, [SCK], [SCK])
        tsc("dve", S(T0), S(T0), 2.0 * math.pi / 4.0, ALU.mult, [SCK], [SCK])
        hp = P.sb([128, 1], F32, "halfpi")
        P.op("dve", lambda E: E.memset(hp[:], math.pi / 2.0), writes=["halfpi"])
        act(S(T1), S(T0), AF.Sin, [SCK], [SCK])
        act(S(T2), S(T0), AF.Sin, [SCK, "halfpi"], [SCK], bias=hp[:, 0:1])

        def dbl(cd, sd, cs_, ss_):
            tt("dve", S(T3), cs_, cs_, ALU.mult, [SCK], [SCK])
            tt("dve", cd, ss_, ss_, ALU.mult, [SCK], [SCK])
            tt("dve", cd, S(T3), cd, ALU.subtract, [SCK], [SCK])
            tt("dve", sd, cs_, ss_, ALU.mult, [SCK], [SCK])
            tsc("dve", sd, sd, 2.0, ALU.mult, [SCK], [SCK])
        dbl(S(C511), S(S511), S(T2), S(T1))
        dbl(S(C0), S(S0), S(C511), S(S511))
        tt("dve", S(ABR), S(RR), S(C0), ALU.mult, [SCK], [SCK])
        tt("dve", S(ABI), S(RR), S(S0), ALU.mult, [SCK], [SCK])
        tsc("dve", S(T0), S(ABR), -1.0, ALU.add, [SCK], [SCK])
        tt("dve", S(T1), S(ARE), S(ARE), ALU.mult, [SCK], [SCK])
        tt("dve", S(T2), S(AIM), S(AIM), ALU.mult, [SCK], [SCK])
        tt("dve", S(T1), S(T1), S(T2), ALU.add, [SCK], [SCK])
        P.op("dve", lambda E: E.reciprocal(out=S(T1), in_=S(T1)), reads=[SCK], writes=[SCK])
        tt("dve", S(T2), S(T0), S(ARE), ALU.mult, [SCK], [SCK])
        tt("dve", S(T3), S(ABI), S(AIM), ALU.mult, [SCK], [SCK])
        tt("dve", S(T2), S(T2), S(T3), ALU.add, [SCK], [SCK])
        tt("dve", S(FRE), S(T2), S(T1), ALU.mult, [SCK], [SCK])
        tt("dve", S(T2), S(ABI), S(ARE), ALU.mult, [SCK], [SCK])
        tt("dve", S(T3), S(T0), S(AIM), ALU.mult, [SCK], [SCK])
        tt("dve", S(T2), S(T2), S(T3), ALU.subtract, [SCK], [SCK])
        tt("dve", S(FIM), S(T2), S(T1), ALU.mult, [SCK], [SCK])

        rotc = P.sb([128, 10, 16], F32, "rotc")
        rots = P.sb([128, 10, 16], F32, "rots")
        RK = "rot"
        P.op("dve", lambda E: E.tensor_copy(out=rotc[:, 0, :], in_=S(C0)), reads=[SCK], writes=[RK])
        P.op("dve", lambda E: E.tensor_copy(out=rots[:, 0, :], in_=S(S0)), reads=[SCK], writes=[RK])
        for k in range(9):
            tt("dve", S(T3), rotc[:, k, :], rotc[:, k, :], ALU.mult, [RK, SCK], [SCK])
            tt("dve", S(T2), rots[:, k, :], rots[:, k, :], ALU.mult, [RK, SCK], [SCK])
            tt("dve", rotc[:, k + 1, :], S(T3), S(T2), ALU.subtract, [SCK, RK], [RK])
            tt("dve", S(T3), rotc[:, k, :], rots[:, k, :], ALU.mult, [RK, SCK], [SCK])
            tsc("dve", rots[:, k + 1, :], S(T3), 2.0, ALU.mult, [SCK, RK], [RK])
        tt("dve", S(T2), rotc[:, 9, :], S(C0), ALU.mult, [RK, SCK], [SCK])
        tt("dve", S(T3), rots[:, 9, :], S(S0), ALU.mult, [RK, SCK], [SCK])
        tt("dve", S(C511), S(T2), S(T3), ALU.add, [SCK], [SCK])
        tt("dve", S(T2), rots[:, 9, :], S(C0), ALU.mult, [RK, SCK], [SCK])
        tt("dve", S(T3), rotc[:, 9, :], S(S0), ALU.mult, [RK, SCK], [SCK])
        tt("dve", S(S511), S(T2), S(T3), ALU.subtract, [SCK], [SCK])

        cosT = P.sb([128, NRT, TB], BF16, "cosT")
        sinT = P.sb([128, NRT, TB], BF16, "sinT")
        Gre = P.sb([128, TB], F32, "Gre")
        Gim = P.sb([128, TB], F32, "Gim")
        q1 = P.sb([128, TB], F32, "q1")
        q2 = P.sb([128, TB], F32, "q2")
        for r0 in range(NRT):
            ch = r0 // 4
            P.op("pool", lambda E: E.memset(Gre[:, 0:1], 1.0), writes=["Gre"])
            P.op("pool", lambda E: E.memset(Gim[:, 0:1], 0.0), writes=["Gim"])
            for k in range(9):
                n = 1 << k
                ck = rotc[:, k, r0:r0 + 1]
                sk_ = rots[:, k, r0:r0 + 1]
                tsc("dve", q1[:, 0:n], Gre[:, 0:n], ck, ALU.mult, ["Gre", RK], ["q1"])
                tsc("dve", q2[:, 0:n], Gim[:, 0:n], ck, ALU.mult, ["Gim", RK], ["q2"])
                stt(Gre[:, n:2 * n], Gim[:, 0:n], sk_, q1[:, 0:n], ALU.mult, ALU.subtract, ["Gim", "q1", RK], ["Gre"])
                stt(Gim[:, n:2 * n], Gre[:, 0:n], sk_, q2[:, 0:n], ALU.mult, ALU.add, ["Gre", "q2", RK], ["Gim"])
                tsc("dve", Gre[:, n:2 * n], Gre[:, n:2 * n], -1.0, ALU.mult, ["Gre"], ["Gre"])
            P.op("act", lambda E, r0=r0: E.copy(out=cosT[:, r0, :], in_=Gre[:]), reads=["Gre"],
                 writes=[("cosT", ch)])
            P.op("act", lambda E, r0=r0: E.copy(out=sinT[:, r0, :], in_=Gim[:]), reads=["Gim"],
                 writes=[("sinT", ch)])

        bsb = P.sb([128, 16, 16, 2], F32, "bsb")
        csb = P.sb([128, 16, 16, 2], F32, "csb")
        load(bsb[:], d_b.rearrange("p t (h c) -> p t h c", c=2), "bsb")
        load(csb[:], d_c.rearrange("p t (h c) -> p t h c", c=2), "csb")
        bbr = P.sb([128, 16, 16], F32, "bbr")
        bbi = P.sb([128, 16, 16], F32, "bbi")
        tb1 = P.sb([128, 16, 16], F32, "tb1")
        fre_bc = S(FRE).unsqueeze(2).to_broadcast([128, 16, 16])
        fim_bc = S(FIM).unsqueeze(2).to_broadcast([128, 16, 16])
        tt("dve", bbr[:], bsb[:, :, :, 0], fre_bc, ALU.mult, ["bsb", SCK], ["bbr"])
        tt("dve", tb1[:], bsb[:, :, :, 1], fim_bc, ALU.mult, ["bsb", SCK], ["tb1"])
        tt("dve", bbr[:], bbr[:], tb1[:], ALU.subtract, ["tb1"], ["bbr"])
        tt("dve", bbi[:], bsb[:, :, :, 1], fre_bc, ALU.mult, ["bsb", SCK], ["bbi"])
        tt("dve", tb1[:], bsb[:, :, :, 0], fim_bc, ALU.mult, ["bsb", SCK], ["tb1"])
        tt("dve", bbi[:], bbi[:], tb1[:], ALU.add, ["tb1"], ["bbi"])

        BTr = P.sb([128, NRT, 128], BF16, "BTr")
        BTi = P.sb([128, NRT, 128], BF16, "BTi")
        CZr = P.sb([128, NRT, 128], BF16, "CZr")
        CZi = P.sb([128, NRT, 128], BF16, "CZi")
        zr = P.sb([128, 128], BF16, "zr")
        zi = P.sb([128, 128], BF16, "zi")
        P.op("pool", lambda E: E.memset(CZr[:], 0.0), writes=["CZr"])
        P.op("pool", lambda E: E.memset(CZi[:], 0.0), writes=["CZi"])
        for rt_ in range(NRT):
            c0 = 32 * (rt_ % 4)
            P.op("pool", lambda E: E.memset(zr[:], 0.0), writes=["zr"])
            P.op("pool", lambda E: E.memset(zi[:], 0.0), writes=["zi"])
            for gl in range(2):
                rows = slice(64 * gl, 64 * gl + 64)
                cols = slice(c0 + 16 * gl, c0 + 16 * gl + 16)
                P.op("dve", lambda E, rows=rows, cols=cols, rt_=rt_: E.tensor_copy(out=zr[rows, cols], in_=bbr[rows, rt_, :]),
                     reads=["bbr"], writes=["zr"])
                P.op("dve", lambda E, rows=rows, cols=cols, rt_=rt_: E.tensor_copy(out=zi[rows, cols], in_=bbi[rows, rt_, :]),
                     reads=["bbi"], writes=["zi"])
                P.op("dve", lambda E, rows=rows, cols=cols, rt_=rt_: E.tensor_copy(out=CZr[rows, rt_, cols], in_=csb[rows, rt_, :, 0]),
                     reads=["csb"], writes=["CZr"])
                P.op("dve", lambda E, rows=rows, cols=cols, rt_=rt_: E.tensor_scalar(
                    out=CZi[rows, rt_, cols], in0=csb[rows, rt_, :, 1], scalar1=-1.0, scalar2=None, op0=ALU.mult),
                    reads=["csb"], writes=["CZi"])
            P.op("pe", lambda E: E.transpose(out=pT[:, 0:128], in_=zr[:], identity=ident[:]),
                 reads=["zr", "ident"], writes=[pTk])
            P.op("pe", lambda E: E.transpose(out=pT[:, 128:256], in_=zi[:], identity=ident[:]),
                 reads=["zi", "ident"], writes=[pTk])
            P.op("act", lambda E, rt_=rt_: E.copy(out=BTr[:, rt_, :], in_=pT[:, 0:128]), reads=[pTk], writes=[("BTr", rt_)])
            P.op("act", lambda E, rt_=rt_: E.copy(out=BTi[:, rt_, :], in_=pT[:, 128:256]), reads=[pTk], writes=[("BTi", rt_)])
        dsk = P.sb([128, 4], F32, "dsk")
        load(dsk[:], d_d, "dsk")

        w_uk_b = P.sb([128, 2, 512], BF16, "w_uk_b")
        w_uv_b = P.sb([128, 2, 512], BF16, "w_uv_b")
        for c_ in range(2):
            load(xt[0][:, 0:512], d_wuk[c_ * 128:(c_ + 1) * 128, :], "xt0")
            P.op("act", lambda E, c_=c_: E.copy(out=w_uk_b[:, c_, :], in_=xt[0][:, 0:512]), reads=["xt0"], writes=["w_uk_b"])
            load(xt[1][:, 0:512], d_wuv[c_ * 128:(c_ + 1) * 128, :], "xt1")
            P.op("act", lambda E, c_=c_: E.copy(out=w_uv_b[:, c_, :], in_=xt[1][:, 0:512]), reads=["xt1"], writes=["w_uv_b"])
        gkn = P.sb([128, 64], F32, "gkn")
        load(gkn[:], d_gkn.unsqueeze(0).to_broadcast([128, 64]), "gkn")
        lnb = P.sb([128, 256], BF16, "lnb")
        ckvT = P.sb([128, 2, 128], BF16, "ckvT")
        ssk = P.sb([128, 16], F32, "ssk")
        tmpk = P.sb([128, 8, 64], F32, "tmpk")
        Kcat = P.sb([128, 8, 96], BF16, "Kcat")
        KTt = [P.sb([96, 8, 128], BF16, "KTt%d" % i_) for i_ in range(2)]
        Vt = [P.sb([128, 8, 65], BF16, "Vt%d" % i_) for i_ in range(2)]
        for i_ in range(2):
            P.op("pool", lambda E, i_=i_: E.memset(Vt[i_][:], 1.0), writes=["Vt%d" % i_])

        junk = P.sb([128, D_MODEL], F32, "junk")
        xnb = P.sb([128, D_MODEL], BF16, "xnb")
        xnT4 = P.sb([128, 8, 512], BF16, "xnT4")
        pkv = pb[1]
        ss = P.sb([128, 8], F32, "ss")
        latn = [P.sb([128, 256], F32, "latn%d" % i) for i in range(2)]
        krn = P.sb([128, 32], F32, "krn")
        kro = [P.sb([128, 32], F32, "kro%d" % i) for i in range(2)]
        rtt = P.sb([128, 4, 16], F32, "rt")
        uT = P.sb([128, 4, 512], BF16, "uT")

        def rstd_from_ss(ss_ap, key, d):
            tsc("dve", ss_ap, ss_ap, 1.0 / d, ALU.mult, [key], [key], s2=EPS, op1=ALU.add)
            act(ss_ap, ss_ap, AF.Sqrt, [key], [key])
            P.op("dve", lambda E: E.reciprocal(out=ss_ap, in_=ss_ap), reads=[key], writes=[key])

        def rope(dst, src, cs, keys_r, keys_w):
            cos = cs[:, 0:16]
            sin = cs[:, 16:32]
            x1 = src[:, 0:16]
            x2 = src[:, 16:32]
            tt("dve", rtt[:, 0, :], x1, cos, ALU.mult, keys_r, ["rt0"])
            tt("dve", rtt[:, 1, :], x2, sin, ALU.mult, keys_r, ["rt1"])
            tt("dve", rtt[:, 2, :], x2, cos, ALU.mult, keys_r, ["rt2"])
            tt("dve", rtt[:, 3, :], x1, sin, ALU.mult, keys_r, ["rt3"])
            tt("dve", dst[:, 0:16], rtt[:, 0, :], rtt[:, 1, :], ALU.subtract, ["rt0", "rt1"], keys_w)
            tt("dve", dst[:, 16:32], rtt[:, 2, :], rtt[:, 3, :], ALU.add, ["rt2", "rt3"] + list(keys_w), keys_w)

        def kv_tile(i, j, src_ap, cs_ap, lat_out, kr_out):
            x = xt[i % 2]
            xk = "xt%d" % (i % 2)
            load(x[:], src_ap, xk)
            act(junk[:], x[:], AF.Square, [xk], ["junk", "ss0"], accum_out=ss[:, 0:1])
            rstd_from_ss(ss[:, 0:1], "ss0", D_MODEL)
            tsc("dve", xnb[:], x[:], ss[:, 0:1], ALU.mult, [xk, "ss0"], ["xnb"])
            for kt in range(8):
                P.op("pe", lambda E, kt=kt: E.transpose(out=pT[:, kt * 128:(kt + 1) * 128],
                                                         in_=xnb[:, kt * 128:(kt + 1) * 128], identity=ident[:]),
                     reads=["xnb", "ident"], writes=[pTk])
            P.op("act", lambda E: E.copy(out=xnT4[:, :, j * 128:(j + 1) * 128],
                                         in_=pT.rearrange("p (k t) -> p k t", k=8)),
                 reads=[pTk], writes=[("xnT4", j)])
            for kt in range(8):
                P.op("pe", lambda E, kt=kt: E.matmul(pkv[:, 0:288], lhsT=xnT4[:, kt, j * 128:(j + 1) * 128],
                                                      rhs=w_rest[:, kt, 512:800], start=(kt == 0), stop=(kt == 7)),
                     reads=[("xnT4", j), ("w_in_b", kt)], writes=["pb1"])
            act(junk[:, 0:256], pkv[:, 0:256], AF.Square, ["pb1"], ["junk", "ss1"], accum_out=ss[:, 1:2])
            rstd_from_ss(ss[:, 1:2], "ss1", 256)
            ln = latn[i % 2]
            lk = "latn%d" % (i % 2)
            stt(ln[:], pkv[:, 0:256], ss[:, 1:2], gkv[:], ALU.mult, ALU.mult, ["pb1", "ss1", "gkv"], [lk])
            P.dma("sp", lambda E: E.dma_start(out=lat_out, in_=ln[:]), reads=[lk], writes=[("out", id(lat_out))])
            out_keys.append(("out", id(lat_out)))
            act(junk[:, 0:32], pkv[:, 256:288], AF.Square, ["pb1"], ["junk", "ss2"], accum_out=ss[:, 2:3])
            rstd_from_ss(ss[:, 2:3], "ss2", 32)
            stt(krn[:], pkv[:, 256:288], ss[:, 2:3], gkr[:], ALU.mult, ALU.mult, ["pb1", "ss2", "gkr"], ["krn"])
            ko = kro[i % 2]
            kk = "kro%d" % (i % 2)
            rope(ko, krn, cs_ap, ["krn", "ropeF", "ropeS"], [kk])
            P.dma("sp", lambda E: E.dma_start(out=kr_out, in_=ko[:]), reads=[kk], writes=[("out", id(kr_out))])
            out_keys.append(("out", id(kr_out)))
            if i >= 32:
                P.dma("sp", lambda E: E.dma_start(out=lat_s_scr, in_=ln[:]), reads=[lk], writes=["lat_s_scr"])
                P.dma("sp", lambda E: E.dma_start(out=kr_s_scr, in_=ko[:]), reads=[kk], writes=["kr_s_scr"])
                return
            P.op("act", lambda E: E.copy(out=lnb[:], in_=ln[:]), reads=[lk], writes=["lnb"])
            for c_ in range(2):
                P.op("pe", lambda E, c_=c_: E.transpose(out=pT[:, c_ * 128:(c_ + 1) * 128], in_=lnb[:, c_ * 128:(c_ + 1) * 128],
                                                         identity=ident[:]), reads=["lnb", "ident"], writes=[pTk])
            P.op("act", lambda E: E.copy(out=ckvT[:], in_=pT[:, 0:256].rearrange("p (c t) -> p c t", c=2)),
                 reads=[pTk], writes=["ckvT"])
            for c_ in range(2):
                P.op("pe", lambda E, c_=c_: E.matmul(pb[6][:], lhsT=ckvT[:, c_, :], rhs=w_uk_b[:, c_, :],
                                                      start=(c_ == 0), stop=(c_ == 1)),
                     reads=["ckvT", "w_uk_b"], writes=["pb6"])
            for c_ in range(2):
                P.op("pe", lambda E, c_=c_: E.matmul(pb[7][:], lhsT=ckvT[:, c_, :], rhs=w_uv_b[:, c_, :],
                                                      start=(c_ == 0), stop=(c_ == 1)),
                     reads=["ckvT", "w_uv_b"], writes=["pb7"])
            act(junk[:, 0:512], pb[6][:], AF.Square, ["pb6"], ["junk"])
            P.op("dve", lambda E: E.tensor_reduce(out=ssk[:, 0:8], in_=junk[:, 0:512].rearrange("p (h d) -> p h d", h=8),
                                                  axis=AX.X, op=ALU.add), reads=["junk"], writes=["ssk"])
            rstd_from_ss(ssk[:, 0:8], "ssk", 64)
            tt("dve", tmpk[:], pb[6][:].rearrange("p (h d) -> p h d", h=8),
               ssk[:, 0:8].unsqueeze(2).to_broadcast([128, 8, 64]), ALU.mult, ["pb6", "ssk"], ["tmpk"])
            tt("dve", Kcat[:, :, 0:64], tmpk[:], gkn[:].unsqueeze(1).to_broadcast([128, 8, 64]), ALU.mult,
               ["tmpk", "gkn"], ["Kcat"])
            P.op("dve", lambda E: E.tensor_copy(out=Kcat[:, :, 64:96], in_=ko[:].unsqueeze(1).to_broadcast([128, 8, 32])),
                 reads=[kk, "Kcat"], writes=["Kcat"])
            for h in range(8):
                P.op("pe", lambda E, h=h: E.transpose(out=pT[0:96, h * 128:(h + 1) * 128], in_=Kcat[:, h, :], identity=ident[:]),
                     reads=["Kcat", "ident"], writes=[pTk])
            kt_ = KTt[i % 2]
            ktk = "KTt%d" % (i % 2)
            P.op("act", lambda E: E.copy(out=kt_[:], in_=pT[0:96, :].rearrange("p (h t) -> p h t", h=8)),
                 reads=[pTk], writes=[ktk])
            P.dma("sp", lambda E: E.dma_start(out=KT_d[:, :, i * 128:(i + 1) * 128], in_=kt_[:]), reads=[ktk],
                  writes=[("KTd", i)])
            vt_ = Vt[i % 2]
            vtk = "Vt%d" % (i % 2)
            P.op("act", lambda E: E.copy(out=vt_[:, :, 0:64], in_=pb[7][:].rearrange("p (h d) -> p h d", h=8)),
                 reads=["pb7"], writes=[vtk])
            P.dma("sp", lambda E: E.dma_start(out=V_d[i * 128:(i + 1) * 128, :], in_=vt_[:].rearrange("p h d -> p (h d)")),
                  reads=[vtk], writes=[("Vd", i)])

        def u_proj(ntok):
            for ft in range(4):
                for kt in range(8):
                    P.op("pe", lambda E, ft=ft, kt=kt: E.matmul(
                        pb[2][:, 0:ntok], lhsT=w_rest[:, kt, ft * 128:(ft + 1) * 128], rhs=xnT4[:, kt, 0:ntok],
                        start=(kt == 0), stop=(kt == 7)),
                        reads=[("xnT4", jj) for jj in range(4)] + [("w_in_b", kt)], writes=["pb2"])
                P.op("act", lambda E, ft=ft: E.copy(out=uT[:, ft, 0:ntok], in_=pb[2][:, 0:ntok]),
                     reads=["pb2"], writes=[("uT", ft)])

        m1 = P.sb([128, TB], F32, "m1")
        m2 = P.sb([128, TB], F32, "m2")
        gre = P.sb([128, TB], F32, "gre")
        gim = P.sb([128, TB], F32, "gim")
        hre = P.sb([128, TB], BF16, "hre")
        him = P.sb([128, TB], BF16, "him")
        car = P.sb([128, NRT, 2], F32, "car")
        ctmp = P.sb([128, 2], F32, "ctmp")
        ytmp = P.sb([128, TB], F32, "ytmp")
        ytm2 = P.sb([128, 2, 128], F32, "ytm2")
        Yown = P.sb([128, 4, 2048], BF16, "Yown")
        ssmo = P.sb([128, NRT, 2], F32, "ssmo")
        P.op("pool", lambda E: E.memset(car[:], 0.0), writes=["car"])

        def s5_block(n, last):
            for ft in range(4):
                for r4 in range(4):
                    rt_ = ft * 4 + r4
                    ch = rt_ // 4
                    P.op("pe", lambda E, rt_=rt_, ft=ft: E.matmul(pb[3][:], lhsT=BTr[:, rt_, :], rhs=uT[:, ft, :],
                                                                 start=True, stop=True),
                         reads=[("BTr", rt_), ("uT", ft)], writes=["pb3"])
                    P.op("pe", lambda E, rt_=rt_, ft=ft: E.matmul(pb[4][:], lhsT=BTi[:, rt_, :], rhs=uT[:, ft, :],
                                                                 start=True, stop=True),
                         reads=[("BTi", rt_), ("uT", ft)], writes=["pb4"])
                    cT = cosT[:, rt_, :]
                    sT = sinT[:, rt_, :]
                    ck_, sk2 = ("cosT", ch), ("sinT", ch)
                    tt("dve", m1[:], pb[3][:], cT, ALU.mult, ["pb3", ck_], ["m1"])
                    tt("dve", m2[:], pb[4][:], sT, ALU.mult, ["pb4", sk2], ["m2"])
                    tt("pool", gre[:], m1[:], m2[:], ALU.add, ["m1", "m2"], ["gre"])
                    tt("dve", m1[:], pb[4][:], cT, ALU.mult, ["pb4", ck_], ["m1"])
                    tt("dve", m2[:], pb[3][:], sT, ALU.mult, ["pb3", sk2], ["m2"])
                    tt("pool", gim[:], m1[:], m2[:], ALU.subtract, ["m1", "m2"], ["gim"])
                    P.op("dve", lambda E, rt_=rt_: E.tensor_tensor_scan(
                        out=Gre[:], data0=sc[:, RR, rt_:rt_ + 1].to_broadcast([128, TB]), data1=gre[:], initial=car[:, rt_, 0:1],
                        op0=ALU.mult, op1=ALU.add), reads=[SCK, "gre", "car"], writes=["Gre"])
                    P.op("dve", lambda E, rt_=rt_: E.tensor_tensor_scan(
                        out=Gim[:], data0=sc[:, RR, rt_:rt_ + 1].to_broadcast([128, TB]), data1=gim[:], initial=car[:, rt_, 1:2],
                        op0=ALU.mult, op1=ALU.add), reads=[SCK, "gim", "car"], writes=["Gim"])
                    c9 = rotc[:, 9, rt_:rt_ + 1]
                    s9 = rots[:, 9, rt_:rt_ + 1]
                    if not last:
                        tsc("dve", ctmp[:, 0:1], Gim[:, TB - 1:TB], s9, ALU.mult, ["Gim", RK], ["ctmp"])
                        tsc("dve", ctmp[:, 1:2], Gre[:, TB - 1:TB], s9, ALU.mult, ["Gre", RK], ["ctmp"])
                        stt(car[:, rt_, 0:1], Gre[:, TB - 1:TB], c9, ctmp[:, 0:1], ALU.mult, ALU.subtract,
                            ["Gre", RK, "ctmp"], ["car"])
                        stt(car[:, rt_, 1:2], Gim[:, TB - 1:TB], c9, ctmp[:, 1:2], ALU.mult, ALU.add,
                            ["Gim", RK, "ctmp"], ["car"])
                    else:
                        c5 = sc[:, C511, rt_:rt_ + 1]
                        s5 = sc[:, S511, rt_:rt_ + 1]
                        tsc("dve", ctmp[:, 0:1], Gim[:, TB - 1:TB], s5, ALU.mult, ["Gim", SCK], ["ctmp"])
                        tsc("dve", ctmp[:, 1:2], Gre[:, TB - 1:TB], s5, ALU.mult, ["Gre", SCK], ["ctmp"])
                        stt(ssmo[:, rt_, 0:1], Gre[:, TB - 1:TB], c5, ctmp[:, 0:1], ALU.mult, ALU.subtract,
                            ["Gre", SCK, "ctmp"], ["ssmo"])
                        stt(ssmo[:, rt_, 1:2], Gim[:, TB - 1:TB], c5, ctmp[:, 1:2], ALU.mult, ALU.add,
                            ["Gim", SCK, "ctmp"], ["ssmo"])
                    tt("pool", q1[:], Gre[:], cT, ALU.mult, ["Gre", ck_], ["q1"])
                    tt("pool", q2[:], Gim[:], sT, ALU.mult, ["Gim", sk2], ["q2"])
                    tt("pool", hre[:], q1[:], q2[:], ALU.subtract, ["q1", "q2"], ["hre"])
                    tt("pool", q1[:], Gre[:], sT, ALU.mult, ["Gre", sk2], ["q1"])
                    tt("pool", q2[:], Gim[:], cT, ALU.mult, ["Gim", ck_], ["q2"])
                    tt("pool", him[:], q1[:], q2[:], ALU.add, ["q1", "q2"], ["him"])
                    P.op("pe", lambda E, rt_=rt_, r4=r4: E.matmul(pb[5][:], lhsT=CZr[:, rt_, :], rhs=hre[:],
                                                                 start=(r4 == 0), stop=False),
                         reads=["CZr", "hre"], writes=["pb5"])
                    P.op("pe", lambda E, rt_=rt_, r4=r4: E.matmul(pb[5][:], lhsT=CZi[:, rt_, :], rhs=him[:],
                                                                 start=False, stop=(r4 == 3)),
                         reads=["CZi", "him"], writes=["pb5"])
                stt(ytmp[:], uT[:, ft, :], dsk[:, ft:ft + 1], pb[5][:], ALU.mult, ALU.add,
                    [("uT", ft), "dsk", "pb5"], ["ytmp"])
                yv = ytmp[:].rearrange("p (a b t) -> p a b t", a=2, b=2)
                tsc("dve", ytm2[:], yv[:, :, 0, :], par[:, 0:1], ALU.mult, ["ytmp", "par"], ["ytm2"])
                stt(Yown[:, ft, n * 256:(n + 1) * 256].rearrange("p (a t) -> p a t", a=2), yv[:, :, 1, :],
                    par[:, 1:2], ytm2[:], ALU.mult, ALU.add, ["ytmp", "par", "ytm2"], [("Yown", ft, n)])

        for n in range(8):
            for j in range(4):
                i = n * 4 + j
                kv_tile(i, j, xf[i * 128:(i + 1) * 128, :], ropeF[:, i, :],
                        o_lat_p[i * 128:(i + 1) * 128, :], o_kr_p[i * 128:(i + 1) * 128, :])
            u_proj(512)
            s5_block(n, n == 7)
        P.dma("sp", lambda E: E.dma_start(out=o_ssm_p.rearrange("t p c -> p t c"), in_=ssmo[:]),
              reads=["ssmo"], writes=["o_ssm_p"])
        out_keys.append("o_ssm_p")

        kv_tile(32, 0, xs[:, :], ropeS[:, :], o_lat_s[:, :], o_kr_s[:, :])
        u_proj(128)
        bur = P.sb([128, NRT, 64], F32, "bur")
        bui = P.sb([128, NRT, 64], F32, "bui")
        for rt_ in range(NRT):
            ft = rt_ // 4
            P.op("pe", lambda E, rt_=rt_, ft=ft: E.matmul(pb[3][:, 0:128], lhsT=BTr[:, rt_, :], rhs=uT[:, ft, 0:128],
                                                         start=True, stop=True),
                 reads=[("BTr", rt_), ("uT", ft)], writes=["pb3"])
            P.op("pe", lambda E, rt_=rt_, ft=ft: E.matmul(pb[4][:, 0:128], lhsT=BTi[:, rt_, :], rhs=uT[:, ft, 0:128],
                                                         start=True, stop=True),
                 reads=[("BTi", rt_), ("uT", ft)], writes=["pb4"])
            P.op("act", lambda E, rt_=rt_: E.copy(out=bur[:, rt_, :], in_=pb[3][:, 0:64]), reads=["pb3"], writes=["bur"])
            P.op("act", lambda E, rt_=rt_: E.copy(out=bui[:, rt_, :], in_=pb[4][:, 0:64]), reads=["pb4"], writes=["bui"])
        st0 = P.sb([128, NRT, 16, 2], F32, "st0")
        load(st0[:], d_st, "st0")
        Hs = P.sb([128, NRT, 16, 4, 2], F32, "Hs")
        e1 = P.sb([128, NRT, 16], F32, "e1")
        e2 = P.sb([128, NRT, 16], F32, "e2")
        abr_bc = S(ABR).unsqueeze(2).to_broadcast([128, NRT, 16])
        abi_bc = S(ABI).unsqueeze(2).to_broadcast([128, NRT, 16])
        burv = bur[:].rearrange("p r (b t) -> p r b t", t=4)
        buiv = bui[:].rearrange("p r (b t) -> p r b t", t=4)
        for t in range(4):
            if t == 0:
                pr, pi_ = st0[:, :, :, 0], st0[:, :, :, 1]
                pk = ["st0"]
            else:
                pr, pi_ = Hs[:, :, :, t - 1, 0], Hs[:, :, :, t - 1, 1]
                pk = ["Hs"]
            tt("dve", e1[:], pr, abr_bc, ALU.mult, pk + [SCK], ["e1"])
            tt("dve", e2[:], pi_, abi_bc, ALU.mult, pk + [SCK], ["e2"])
            tt("dve", e1[:], e1[:], e2[:], ALU.subtract, ["e2"], ["e1"])
            tt("dve", Hs[:, :, :, t, 0], e1[:], burv[:, :, :, t], ALU.add, ["e1", "bur"], ["Hs"])
            tt("dve", e1[:], pi_, abr_bc, ALU.mult, pk + [SCK], ["e1"])
            tt("dve", e2[:], pr, abi_bc, ALU.mult, pk + [SCK], ["e2"])
            tt("dve", e1[:], e1[:], e2[:], ALU.add, ["e2"], ["e1"])
            tt("dve", Hs[:, :, :, t, 1], e1[:], buiv[:, :, :, t], ALU.add, ["e1", "bui"], ["Hs"])
        for rt_ in range(NRT):
            P.dma("sp", lambda E, rt_=rt_: E.dma_start(out=o_ssm_s[:, rt_, :, :].rearrange("b p c -> p b c"),
                                                     in_=Hs[:, rt_, :, 3, :]),
                  reads=["Hs"], writes=[("o_ssm_s", rt_)])
            out_keys.append(("o_ssm_s", rt_))

        hsr = Gre[:].bitcast(BF16)
        hsi = Gim[:].bitcast(BF16)
        P.op("act", lambda E: E.copy(out=hsr.rearrange("p (r b t) -> p r b t", r=16, b=16), in_=Hs[:, :, :, :, 0]),
             reads=["Hs"], writes=["Gre"])
        P.op("act", lambda E: E.copy(out=hsi.rearrange("p (r b t) -> p r b t", r=16, b=16), in_=Hs[:, :, :, :, 1]),
             reads=["Hs"], writes=["Gim"])
        Ys = ytmp[:, 0:256].rearrange("p (f t) -> p f t", f=4)
        for ft in range(4):
            for r4 in range(4):
                rt_ = ft * 4 + r4
                P.op("pe", lambda E, rt_=rt_, r4=r4: E.matmul(pb[5][:, 0:64], lhsT=CZr[:, rt_, :], rhs=hsr[:, rt_ * 64:(rt_ + 1) * 64],
                                                             start=(r4 == 0), stop=False), reads=["CZr", "Gre"], writes=["pb5"])
                P.op("pe", lambda E, rt_=rt_, r4=r4: E.matmul(pb[5][:, 0:64], lhsT=CZi[:, rt_, :], rhs=hsi[:, rt_ * 64:(rt_ + 1) * 64],
                                                             start=False, stop=(r4 == 3)), reads=["CZi", "Gim"], writes=["pb5"])
            stt(Ys[:, ft, :], uT[:, ft, 0:64], dsk[:, ft:ft + 1], pb[5][:, 0:64], ALU.mult, ALU.add,
                [("uT", ft), "dsk", "pb5"], ["ytmp"])

        P.barrier()
        w_out_b = cosT[:].rearrange("p (k a) t -> p k (a t)", k=8)
        w_glu_b = sinT[:, 0:8, :].rearrange("p (k a) t -> p k (a t)", k=4)
        w_uq_b = sinT[:, 8:13, :].rearrange("p a t -> p (a t)")[:, 0:2304].rearrange("p (c n) -> p c n", c=3)
        for k in range(8):
            load(xt[k % 2][:], d_wout[k * 128:(k + 1) * 128, :], "xt%d" % (k % 2))
            P.op("act", lambda E, k=k: E.copy(out=w_out_b[:, k, :], in_=xt[k % 2][:]), reads=["xt%d" % (k % 2)], writes=["w_out_b"])
        for k in range(4):
            load(xt[k % 2][:], d_wglu[k * 128:(k + 1) * 128, :], "xt%d" % (k % 2))
            P.op("act", lambda E, k=k: E.copy(out=w_glu_b[:, k, :], in_=xt[k % 2][:]), reads=["xt%d" % (k % 2)], writes=["w_glu_b"])
        for k in range(3):
            load(xt[k % 2][:, 0:768], d_wuq[k * 128:(k + 1) * 128, :], "xt%d" % (k % 2))
            P.op("act", lambda E, k=k: E.copy(out=w_uq_b[:, k, :], in_=xt[k % 2][:, 0:768]), reads=["xt%d" % (k % 2)], writes=["w_uq_b"])
        gql = P.sb([128, 384], F32, "gql")
        load(gql[:], d_gql.unsqueeze(0).to_broadcast([128, 384]), "gql")
        gqn = P.sb([128, 64], F32, "gqn")
        load(gqn[:], d_gqn.unsqueeze(0).to_broadcast([128, 64]), "gqn")
        gqr = P.sb([128, 32], F32, "gqr")
        load(gqr[:], d_gqr.unsqueeze(0).to_broadcast([128, 32]), "gqr")
        ropeO = P.sb([128, 16, 32], F32, "ropeO")
        load(ropeO[:], rope_o, "ropeO")
        mskf = P.sb([128, 2, 128], F32, "mskf")
        msk = P.sb([128, 2, 128], BF16, "msk")
        load(mskf[:], d_masks, "mskf")
        P.op("dve", lambda E: E.tensor_copy(out=msk[:], in_=mskf[:]), reads=["mskf"], writes=["msk"])
        zer = P.sb([128, 512], BF16, "zer")
        P.op("pool", lambda E: E.memset(zer[:], 0.0), writes=["zer"])
        cqn = P.sb([128, 384], BF16, "cqn")
        cqT = P.sb([128, 3, 128], BF16, "cqT")
        Qcat = P.sb([128, 8, 96], BF16, "Qcat")
        qrn = P.sb([128, 8, 32], F32, "qrn")
        qra = P.sb([128, 8, 16], F32, "qra")
        qrb = P.sb([128, 8, 16], F32, "qrb")
        QT = P.sb([96, 8, 128], BF16, "QT")
        KTb = [P.sb([96, 8, 128], BF16, "KTb%d" % i_) for i_ in range(2)]
        Vb = [P.sb([128, 520], BF16, "Vb%d" % i_) for i_ in range(2)]
        PT = [P.sb([128, 512], BF16, "PT%d" % i_) for i_ in range(2)]
        rec = P.sb([128, 8], F32, "rec")
        oatt = P.sb([128, 8, 64], BF16, "oatt")
        burb = bur[:].rearrange("p a b -> p (a b)").bitcast(BF16)
        mixT = burb[:, 0:1024].rearrange("p (k t) -> p k t", k=8)
        gy = burb[:, 1024:1536].rearrange("p (k t) -> p k t", k=4)
        g1 = m1
        g2 = m2
        sig = gre
        x2 = Hs[:].rearrange("p a b c d -> p (a b c d)")[:, 0:1024]
        ATT_SCALE = 1.0 / math.sqrt(96.0)

        gffn = bui[:].rearrange("p a b -> p (a b)")
        load(gffn, d_gffn.unsqueeze(0).to_broadcast([128, D_MODEL]), "gffn")
        xn2 = uT[:].rearrange("p a b -> p (a b)").bitcast(F32)
        xn2T = xnT4[:, :, 128:256]
        q2c = xnT4[:, 0, 256:384]
        wqs = BTr[:].rearrange("p a b -> p (a b)").bitcast(F32).rearrange("p (k n) -> p k n", k=8)
        wqb = CZr[:].rearrange("p a b -> p (a b)")[:, 0:1024].rearrange("p (k n) -> p k n", k=8)
        czf = CZi[:].rearrange("p a b -> p (a b)").bitcast(F32)
        czb = CZi[:].rearrange("p a b -> p (a b)")
        btu = BTi[:].rearrange("p a b -> p (a b)").bitcast(U32)
        keysf = czf[:, 0:256].rearrange("p (c k) -> p c k", c=2)
        keysb = czb[:, 1024:1280].rearrange("p (c k) -> p c k", c=2)
        load(keysf, d_keysT, "keysf")
        P.op("dve", lambda E: E.tensor_copy(out=keysb, in_=keysf), reads=["keysf"], writes=["keysb"])
        iot = czf[:, 300:316]
        load(iot, d_iota, "iot")
        gsum = czf[:, 320:328]
        s_top = Gre[:, 0:256].rearrange("p (a b) -> p a b", a=16)
        i_topf = Gre[:, 256:512].rearrange("p (a b) -> p a b", a=16)
        wrk = Gim[:, 0:256]
        cand = Gim[:, 256:512].rearrange("p (a b) -> p a b", a=16)
        top16 = q1[:, 0:128].rearrange("p (a b) -> p a b", a=8)
        pa_f = q1[:, 128:256].rearrange("p (a b) -> p a b", a=8)
        pb_f = q1[:, 256:384].rearrange("p (a b) -> p a b", a=8)
        gsm = q1[:, 384:512].rearrange("p (a b) -> p a b", a=8)
        eqt = q2[:, 0:256].rearrange("p (a b) -> p a b", a=16)
        isel = q2[:, 256:512].rearrange("p (c h j) -> p c h j", c=2, h=8)
        idxf = gim[:, 0:128]
        pre = gim[:, 128:256]
        pg1 = gim[:, 256:384]
        coef = gim[:, 384:512]
        i_top = btu[:, 0:256].rearrange("p (a b) -> p a b", a=16)
        pos = btu[:, 256:384].rearrange("p (a b) -> p a b", a=8)
        pa_u = btu[:, 384:512].rearrange("p (a b) -> p a b", a=8)
        pb_u = btu[:, 512:640].rearrange("p (a b) -> p a b", a=8)
        idxu = btu[:, 640:768]
        NEG = -1.0e30

        wr_flat = w_rest[:].rearrange("p a b -> p (a b)").bitcast(F32)
        xgb = [(Hs[:].rearrange("p a b c d -> p (a b c d)")[:, 1024:2048], "xgb0"),
               (ropeF[:].rearrange("p a b -> p (a b)"), "xgb1"),
               (wr_flat[:, 0:1024], "xgb2"), (wr_flat[:, 1024:2048], "xgb3"), (wr_flat[:, 2048:3072], "xgb4")]

        def top16_of(vals_ap, n, out_vals, out_idx, rkeys, wkeys):
            P.op("dve", lambda E: E.max(out=out_vals[:, 0:8], in_=vals_ap), reads=rkeys, writes=wkeys)
            P.op("dve", lambda E: E.max_index(out=out_idx[:, 0:8], in_max=out_vals[:, 0:8], in_values=vals_ap),
                 reads=rkeys + wkeys, writes=wkeys)
            P.op("dve", lambda E: E.match_replace(out=wrk[:, 0:n], in_to_replace=out_vals[:, 0:8], in_values=vals_ap,
                                                  imm_value=NEG), reads=rkeys + wkeys, writes=["wrk"])
            P.op("dve", lambda E: E.max(out=out_vals[:, 8:16], in_=wrk[:, 0:n]), reads=["wrk"] + wkeys, writes=wkeys)
            P.op("dve", lambda E: E.max_index(out=out_idx[:, 8:16], in_max=out_vals[:, 8:16], in_values=wrk[:, 0:n]),
                 reads=["wrk"] + wkeys, writes=wkeys)

        def peer(x2_ap, x2k, gbufs):
            act(xnb[:], x2_ap, AF.Square, [x2k], ["xnb", "ss0"], accum_out=ss[:, 0:1])
            rstd_from_ss(ss[:, 0:1], "ss0", D_MODEL)
            stt(xn2, x2_ap, ss[:, 0:1], gffn, ALU.mult, ALU.mult, [x2k, "ss0", "gffn"], ["xn2"])
            P.op("act", lambda E: E.copy(out=xnb[:], in_=xn2), reads=["xn2"], writes=["xnb"])
            for kt in range(8):
                P.op("pe", lambda E, kt=kt: E.transpose(out=pT[:, kt * 128:(kt + 1) * 128],
                                                         in_=xnb[:, kt * 128:(kt + 1) * 128], identity=ident[:]),
                     reads=["xnb", "ident"], writes=[pTk])
            P.op("act", lambda E: E.copy(out=xn2T, in_=pT.rearrange("p (k t) -> p k t", k=8)), reads=[pTk], writes=["xn2T"])
            for hc in range(16):
                c_ = hc % 2
                load(wqs, d_wq.rearrange("(k p) n -> p k n", p=128)[:, :, hc * 128:(hc + 1) * 128], "wqs")
                P.op("act", lambda E: E.copy(out=wqb, in_=wqs), reads=["wqs"], writes=["wqb"])
                for kt in range(8):
                    P.op("pe", lambda E, kt=kt: E.matmul(pb[1][:, 0:128], lhsT=wqb[:, kt, :], rhs=xn2T[:, kt, :],
                                                          start=(kt == 0), stop=(kt == 7)),
                         reads=["wqb", "xn2T"], writes=["pb1"])
                P.op("act", lambda E: E.copy(out=q2c, in_=pb[1][:, 0:128]), reads=["pb1"], writes=["q2c"])
                P.op("pe", lambda E, c_=c_: E.matmul(pb[2][:, 0:128], lhsT=q2c, rhs=keysb[:, c_, :], start=True, stop=True),
                     reads=["q2c", "keysb"], writes=["pb2"])
                top16_of(pb[2][:, 0:128], 128, s_top[:, hc, :], i_top[:, hc, :], ["pb2"], ["s_top", "i_top"])
            P.op("dve", lambda E: E.tensor_copy(out=i_topf, in_=i_top), reads=["i_top"], writes=["i_topf"])
            for h in range(8):
                tt("dve", cand, s_top[:, 2 * h, :].unsqueeze(2).to_broadcast([128, 16, 16]),
                   s_top[:, 2 * h + 1, :].unsqueeze(1).to_broadcast([128, 16, 16]), ALU.add, ["s_top"], ["cand"])
                top16_of(cand.rearrange("p a b -> p (a b)"), 256, top16[:, h, :], pos[:, h, :], ["cand"], ["top16", "pos"])
            tt("dve", gsm, top16, top16[:, :, 0:1].to_broadcast([128, 8, 16]), ALU.subtract, ["top16"], ["gsm"])
            act(gsm, gsm, AF.Exp, ["gsm"], ["gsm"])
            P.op("dve", lambda E: E.tensor_reduce(out=gsum, in_=gsm, axis=AX.X, op=ALU.add), reads=["gsm"], writes=["gsum"])
            P.op("dve", lambda E: E.reciprocal(out=gsum, in_=gsum), reads=["gsum"], writes=["gsum"])
            tt("dve", gsm, gsm, gsum.unsqueeze(2).to_broadcast([128, 8, 16]), ALU.mult, ["gsum"], ["gsm"])
            tsc("dve", pa_u, pos, 4, ALU.logical_shift_right, ["pos"], ["pa_u"])
            tsc("dve", pb_u, pos, 15, ALU.bitwise_and, ["pos"], ["pb_u"])
            P.op("dve", lambda E: E.tensor_copy(out=pa_f, in_=pa_u), reads=["pa_u"], writes=["pa_f"])
            P.op("dve", lambda E: E.tensor_copy(out=pb_f, in_=pb_u), reads=["pb_u"], writes=["pb_f"])
            for h in range(8):
                for c_, pf in ((0, pa_f), (1, pb_f)):
                    tt("dve", eqt, pf[:, h, :].unsqueeze(2).to_broadcast([128, 16, 16]),
                       iot.unsqueeze(1).to_broadcast([128, 16, 16]), ALU.is_equal, ["pa_f", "pb_f", "iot"], ["eqt"])
                    tt("dve", eqt, eqt, i_topf[:, 2 * h + c_, :].unsqueeze(1).to_broadcast([128, 16, 16]), ALU.mult,
                       ["i_topf"], ["eqt"])
                    P.op("dve", lambda E, h=h, c_=c_: E.tensor_reduce(out=isel[:, c_, h, :], in_=eqt, axis=AX.X, op=ALU.add),
                         reads=["eqt"], writes=["isel"])
            stt(idxf, isel[:, 0, :, :].rearrange("p h j -> p (h j)"), 128.0, isel[:, 1, :, :].rearrange("p h j -> p (h j)"),
                ALU.mult, ALU.add, ["isel"], ["idxf"])
            P.op("dve", lambda E: E.tensor_copy(out=idxu, in_=idxf), reads=["idxf"], writes=["idxu"])
            for sl in range(128):
                gb, gk = gbufs[sl % len(gbufs)]
                P.dma("pool", lambda E, sl=sl, gb=gb: E.indirect_dma_start(
                    out=gb, out_offset=None, in_=d_pu, in_offset=bass.IndirectOffsetOnAxis(ap=idxu[:, sl:sl + 1], axis=0)),
                    reads=["idxu"], writes=[gk])
                P.op("dve", lambda E, sl=sl, gb=gb: E.scalar_tensor_tensor(
                    out=xnb[:], in0=gb, scalar=1.0, in1=xn2, op0=ALU.mult, op1=ALU.mult, accum_out=pre[:, sl:sl + 1]),
                    reads=[gk, "xn2"], writes=["xnb", "pre"])
            tt("dve", pg1, pre, pre, ALU.mult, ["pre"], ["pg1"])
            tsc("dve", pg1, pg1, 0.044715, ALU.mult, ["pg1"], ["pg1"], s2=1.0, op1=ALU.add)
            tt("dve", pg1, pg1, pre, ALU.mult, ["pre"], ["pg1"])
            act(pg1, pg1, AF.Tanh, ["pg1"], ["pg1"], scale=0.7978845608028654)
            tsc("dve", pg1, pg1, 1.0, ALU.add, ["pg1"], ["pg1"], s2=0.5, op1=ALU.mult)
            tt("dve", pg1, pg1, pre, ALU.mult, ["pre"], ["pg1"])
            tt("dve", coef, pg1, gsm.rearrange("p h j -> p (h j)"), ALU.mult, ["pg1", "gsm"], ["coef"])
            for sl in range(128):
                gb, gk = gbufs[sl % len(gbufs)]
                P.dma("pool", lambda E, sl=sl, gb=gb: E.indirect_dma_start(
                    out=gb, out_offset=None, in_=d_pv, in_offset=bass.IndirectOffsetOnAxis(ap=idxu[:, sl:sl + 1], axis=0)),
                    reads=["idxu"], writes=[gk])
                stt(x2_ap, gb, coef[:, sl:sl + 1], x2_ap, ALU.mult, ALU.add, [gk, "coef", x2k], [x2k])

        def q_part(x, xk, cs_ap):
            act(junk[:], x[:], AF.Square, [xk], ["junk", "ss0"], accum_out=ss[:, 0:1])
            rstd_from_ss(ss[:, 0:1], "ss0", D_MODEL)
            tsc("dve", xnb[:], x[:], ss[:, 0:1], ALU.mult, [xk, "ss0"], ["xnb"])
            for kt in range(8):
                P.op("pe", lambda E, kt=kt: E.transpose(out=pT[:, kt * 128:(kt + 1) * 128],
                                                         in_=xnb[:, kt * 128:(kt + 1) * 128], identity=ident[:]),
                     reads=["xnb", "ident"], writes=[pTk])
            P.op("act", lambda E: E.copy(out=xnT4[:, :, 0:128], in_=pT.rearrange("p (k t) -> p k t", k=8)),
                 reads=[pTk], writes=[("xnT4", 0)])
            for kt in range(8):
                P.op("pe", lambda E, kt=kt: E.matmul(pb[1][:, 0:384], lhsT=xnT4[:, kt, 0:128], rhs=w_cq[:, kt, :],
                                                      start=(kt == 0), stop=(kt == 7)),
                     reads=[("xnT4", 0), ("w_cq", kt)], writes=["pb1"])
            act(junk[:, 0:384], pb[1][:, 0:384], AF.Square, ["pb1"], ["junk", "ss1"], accum_out=ss[:, 1:2])
            rstd_from_ss(ss[:, 1:2], "ss1", 384)
            stt(cqn[:], pb[1][:, 0:384], ss[:, 1:2], gql[:], ALU.mult, ALU.mult, ["pb1", "ss1", "gql"], ["cqn"])
            for c_ in range(3):
                P.op("pe", lambda E, c_=c_: E.transpose(out=pT[:, c_ * 128:(c_ + 1) * 128], in_=cqn[:, c_ * 128:(c_ + 1) * 128],
                                                         identity=ident[:]), reads=["cqn", "ident"], writes=[pTk])
            P.op("act", lambda E: E.copy(out=cqT[:], in_=pT[:, 0:384].rearrange("p (c t) -> p c t", c=3)),
                 reads=[pTk], writes=["cqT"])
            for c_ in range(3):
                P.op("pe", lambda E, c_=c_: E.matmul(pb[6][:], lhsT=cqT[:, c_, :], rhs=w_uq_b[:, c_, 0:512],
                                                      start=(c_ == 0), stop=(c_ == 2)),
                     reads=["cqT", "w_uq_b"], writes=["pb6"])
            for c_ in range(3):
                P.op("pe", lambda E, c_=c_: E.matmul(pb[7][:, 0:256], lhsT=cqT[:, c_, :], rhs=w_uq_b[:, c_, 512:768],
                                                      start=(c_ == 0), stop=(c_ == 2)),
                     reads=["cqT", "w_uq_b"], writes=["pb7"])
            act(junk[:, 0:512], pb[6][:], AF.Square, ["pb6"], ["junk"])
            P.op("dve", lambda E: E.tensor_reduce(out=ssk[:, 0:8], in_=junk[:, 0:512].rearrange("p (h d) -> p h d", h=8),
                                                  axis=AX.X, op=ALU.add), reads=["junk"], writes=["ssk"])
            rstd_from_ss(ssk[:, 0:8], "ssk", 64)
            tt("dve", tmpk[:], pb[6][:].rearrange("p (h d) -> p h d", h=8),
               ssk[:, 0:8].unsqueeze(2).to_broadcast([128, 8, 64]), ALU.mult, ["pb6", "ssk"], ["tmpk"])
            tt("dve", Qcat[:, :, 0:64], tmpk[:], gqn[:].unsqueeze(1).to_broadcast([128, 8, 64]), ALU.mult,
               ["tmpk", "gqn"], ["Qcat"])
            act(junk[:, 512:768], pb[7][:, 0:256], AF.Square, ["pb7"], ["junk2"])
            P.op("dve", lambda E: E.tensor_reduce(out=ssk[:, 8:16], in_=junk[:, 512:768].rearrange("p (h d) -> p h d", h=8),
                                                  axis=AX.X, op=ALU.add), reads=["junk2"], writes=["ssk2"])
            rstd_from_ss(ssk[:, 8:16], "ssk2", 32)
            tt("dve", qrn[:], pb[7][:, 0:256].rearrange("p (h d) -> p h d", h=8),
               ssk[:, 8:16].unsqueeze(2).to_broadcast([128, 8, 32]), ALU.mult, ["pb7", "ssk2"], ["qrn"])
            tt("dve", qrn[:], qrn[:], gqr[:].unsqueeze(1).to_broadcast([128, 8, 32]), ALU.mult, ["gqr"], ["qrn"])
            cosb = cs_ap[:, 0:16].unsqueeze(1).to_broadcast([128, 8, 16])
            sinb = cs_ap[:, 16:32].unsqueeze(1).to_broadcast([128, 8, 16])
            tt("dve", qra[:], qrn[:, :, 0:16], cosb, ALU.mult, ["qrn", "ropeO", "ropeS"], ["qra"])
            tt("dve", qrb[:], qrn[:, :, 16:32], sinb, ALU.mult, ["qrn", "ropeO", "ropeS"], ["qrb"])
            tt("dve", Qcat[:, :, 64:80], qra[:], qrb[:], ALU.subtract, ["qra", "qrb", "Qcat"], ["Qcat"])
            tt("dve", qra[:], qrn[:, :, 16:32], cosb, ALU.mult, ["qrn", "ropeO", "ropeS"], ["qra"])
            tt("dve", qrb[:], qrn[:, :, 0:16], sinb, ALU.mult, ["qrn", "ropeO", "ropeS"], ["qrb"])
            tt("dve", Qcat[:, :, 80:96], qra[:], qrb[:], ALU.add, ["qra", "qrb", "Qcat"], ["Qcat"])

        def own_tile(i):
            x = xt[i % 2]
            xk = "xt%d" % (i % 2)
            load(x[:], xo[i * 128:(i + 1) * 128, :], xk)
            q_part(x, xk, ropeO[:, i, :])
            for h in range(8):
                P.op("pe", lambda E, h=h: E.transpose(out=pT[0:96, h * 128:(h + 1) * 128], in_=Qcat[:, h, :], identity=ident[:]),
                     reads=["Qcat", "ident"], writes=[pTk])
            P.op("act", lambda E: E.copy(out=QT[:], in_=pT[0:96, :].rearrange("p (h t) -> p h t", h=8)),
                 reads=[pTk], writes=["QT"])
            for hg in range(2):
                P.op("pe", lambda E, hg=hg: E.matmul(pb[4 + hg][:], lhsT=zer[:, 0:128], rhs=zer[:], start=True, stop=False),
                     reads=["zer"], writes=["pb%d" % (4 + hg)])
            nkb = 2 * i + 2
            for kb in range(nkb):
                kbuf = KTb[kb % 2]
                kkey = "KTb%d" % (kb % 2)
                vbuf = Vb[kb % 2]
                vkey = "Vb%d" % (kb % 2)
                P.dma("sp", lambda E, kb=kb, kbuf=kbuf: E.dma_start(out=kbuf[:], in_=KT_d[:, :, kb * 128:(kb + 1) * 128]),
                      reads=[("KTd", kb)], writes=[kkey])
                P.dma("sp", lambda E, kb=kb, vbuf=vbuf: E.dma_start(out=vbuf[:], in_=V_d[kb * 128:(kb + 1) * 128, :]),
                      reads=[("Vd", kb)], writes=[vkey])
                for hg in range(2):
                    sp_ = pb[2 + hg]
                    spk = "pb%d" % (2 + hg)
                    for j in range(4):
                        h = hg * 4 + j
                        msk_i = kb - 2 * i
                        P.op("pe", lambda E, h=h, j=j, sp_=sp_, kbuf=kbuf, msk_i=msk_i: E.matmul(
                            sp_[:, j * 128:(j + 1) * 128], lhsT=kbuf[:, h, :], rhs=QT[:, h, :], start=True, stop=(msk_i < 0)),
                            reads=[kkey, "QT"], writes=[spk])
                        if msk_i >= 0:
                            P.op("pe", lambda E, j=j, sp_=sp_, msk_i=msk_i: E.matmul(
                                sp_[:, j * 128:(j + 1) * 128], lhsT=ident[:], rhs=msk[:, msk_i, :], start=False, stop=True),
                                reads=["ident", "msk"], writes=[spk])
                    pt_ = PT[hg]
                    ptk = "PT%d" % hg
                    act(pt_[:], sp_[:], AF.Exp, [spk], [ptk], scale=ATT_SCALE)
                    for j in range(4):
                        h = hg * 4 + j
                        P.op("pe", lambda E, h=h, j=j, hg=hg, pt_=pt_, vbuf=vbuf: E.matmul(
                            pb[4 + hg][:, j * 65:(j + 1) * 65], lhsT=pt_[:, j * 128:(j + 1) * 128],
                            rhs=vbuf[:, h * 65:(h + 1) * 65], start=False, stop=(kb == nkb - 1), skip_group_check=True),
                            reads=[ptk, vkey], writes=["pb%d" % (4 + hg)])
            for hg in range(2):
                ov = pb[4 + hg][:, 0:260].rearrange("p (h d) -> p h d", h=4)
                P.op("dve", lambda E, hg=hg, ov=ov: E.reciprocal(out=rec[:, hg * 4:(hg + 1) * 4].unsqueeze(2), in_=ov[:, :, 64:65]),
                     reads=["pb%d" % (4 + hg)], writes=["rec"])
                tt("dve", oatt[:, hg * 4:(hg + 1) * 4, :], ov[:, :, 0:64],
                   rec[:, hg * 4:(hg + 1) * 4].unsqueeze(2).to_broadcast([128, 4, 64]), ALU.mult,
                   ["pb%d" % (4 + hg), "rec"], ["oatt"])
            oflat = oatt[:].rearrange("p h d -> p (h d)")
            for c_ in range(4):
                P.op("pe", lambda E, c_=c_: E.transpose(out=pT[:, c_ * 128:(c_ + 1) * 128], in_=oflat[:, c_ * 128:(c_ + 1) * 128],
                                                         identity=ident[:]), reads=["oatt", "ident"], writes=[pTk])
            P.op("act", lambda E: E.copy(out=mixT[:, 4:8, :], in_=pT[:, 0:512].rearrange("p (c t) -> p c t", c=4)),
                 reads=[pTk], writes=["mixT_a"])
            glu_part(Yown[:, :, i * 128:(i + 1) * 128], [("Yown", ft, i // 2) for ft in range(4)], 128)
            out_part(x, xk)
            peer(x2, "x2", [(junk[:], "junk"), (xt[(i + 1) % 2][:], "xt%d" % ((i + 1) % 2))] + xgb)
            P.dma("sp", lambda E: E.dma_start(out=o_y_p[i * 128:(i + 1) * 128, :], in_=x2[:]), reads=["x2"],
                  writes=[("o_y_p", i)])
            out_keys.append(("o_y_p", i))

        def glu_part(yv_, ykeys, nt):
            g1v = g1[:].rearrange("p (f t) -> p f t", f=4)[:, :, 0:nt]
            g2v = g2[:].rearrange("p (f t) -> p f t", f=4)[:, :, 0:nt]
            tt("dve", g1v, yv_, yv_, ALU.mult, ykeys, ["g1"])
            tsc("dve", g1[:], g1[:], 0.044715, ALU.mult, ["g1"], ["g1"], s2=1.0, op1=ALU.add)
            tt("dve", g1v, g1v, yv_, ALU.mult, ykeys, ["g1"])
            act(g2[:], g1[:], AF.Tanh, ["g1"], ["g2"], scale=0.7978845608028654)
            tsc("dve", g2[:], g2[:], 1.0, ALU.add, ["g2"], ["g2"], s2=0.5, op1=ALU.mult)
            tt("dve", gy[:, :, 0:nt], g2v, yv_, ALU.mult, ykeys + ["g2"], ["gy"])
            for half in range(2):
                for ot in range(4):
                    o8 = half * 4 + ot
                    for k in range(4):
                        P.op("pe", lambda E, half=half, ot=ot, o8=o8, k=k: E.matmul(
                            pb[2 + half][:, ot * 128:(ot + 1) * 128], lhsT=w_glu_b[:, k, o8 * 128:(o8 + 1) * 128],
                            rhs=gy[:, k, :], start=(k == 0), stop=(k == 3)),
                            reads=["w_glu_b", "gy"], writes=["pb%d" % (2 + half)])
            act(sig[:], pb[3][:], AF.Sigmoid, ["pb3"], ["sig"])
            tt("dve", mixT[:, 0:4, :], pb[2][:].rearrange("p (c t) -> p c t", c=4), sig[:].rearrange("p (c t) -> p c t", c=4),
               ALU.mult, ["pb2", "sig"], ["mixT_g"])

        def out_part(x, xk):
            for half in range(2):
                for k in range(8):
                    P.op("pe", lambda E, half=half, k=k: E.matmul(
                        pb[6 + half][:], lhsT=mixT[:, k, :], rhs=w_out_b[:, k, half * 512:(half + 1) * 512],
                        start=(k == 0), stop=(k == 7)),
                        reads=["mixT_a", "mixT_g", "w_out_b"], writes=["pb%d" % (6 + half)])
                tt("dve", x2[:, half * 512:(half + 1) * 512], pb[6 + half][:], x[:, half * 512:(half + 1) * 512], ALU.add,
                   ["pb%d" % (6 + half), xk], ["x2"])

        for i in range(16):
            own_tile(i)

        P.barrier()
        Y16 = Yown[:].rearrange("p a b -> p (a b)")
        Y32 = Y16.bitcast(F32)
        YU = Y16.bitcast(U32)
        pidx = YU[:, 0:1024]
        latf = [Y32[:, 2048:2304], Y32[:, 2304:2560]]
        krf = [Y32[:, 2560:2592], Y32[:, 2592:2624]]
        lb = [Y16[:, 5248:5505], Y16[:, 5512:5769]]
        krb = [Y16[:, 5776:5808], Y16[:, 5808:5840]]
        lT = [Y16[:, 5840:6096].rearrange("p (c k) -> p c k", c=2), Y16[:, 6096:6352].rearrange("p (c k) -> p c k", c=2)]
        krT = [Y16[:, 6352:6480], Y16[:, 6480:6608]]
        qtT = Y16[:, 6608:7632].rearrange("p (c h t) -> p c h t", c=2, h=8)
        pts = [Y16[:, 7632:7664], Y16[:, 7664:7696]]
        sc1 = Y32[:, 3848:3880]
        olat = Y16[:, 7760:8016]
        olT = Y16[:, 8016:8080].rearrange("p (c n) -> p c n", c=2)
        maskn = czf[:, 330:362]
        load(maskn[0:4, :], d_maskn, "maskn")
        for bf in range(2):
            P.op("pool", lambda E, bf=bf: E.memset(lb[bf][:, 256:257], 1.0), writes=["lb%d" % bf])
        pti = xt[0][:].bitcast(I32)
        load(pti, d_ptab.to_broadcast([128, 1024]), "xt0")
        piota = czf[:, 364:365]
        load(piota, d_piota, "piota")
        tsc("dve", xt[1][:], pti, 128.0, ALU.mult, ["xt0"], ["xt1"])
        tsc("dve", pidx, xt[1][:], piota, ALU.add, ["xt1", "piota"], ["pidx"])
        wukT = [KTb[0][0:64, :, :].rearrange("p a b -> p (a b)"), KTb[1][0:64, :, :].rearrange("p a b -> p (a b)")]
        for hh in range(2):
            load(xt[hh][0:64, :], d_wukT[:, hh * 4:(hh + 1) * 4, :].rearrange("p a b -> p (a b)"), "xt%d" % hh)
            P.op("act", lambda E, hh=hh: E.copy(out=wukT[hh], in_=xt[hh][0:64, :]), reads=["xt%d" % hh], writes=["KTb%d" % hh])
        xsx = xt[0]
        load(xsx[:], xs[:, :], "xt0")
        q_part(xsx, "xt0", ropeS[:, :])
        Qg = Vb[0][:, 0:512].rearrange("p (h d) -> p h d", h=8)
        tt("dve", Qg, Qcat[:, :, 0:64], gkn[:].unsqueeze(1).to_broadcast([128, 8, 64]), ALU.mult, ["Qcat", "gkn"], ["Vb0"])
        for h in range(8):
            P.op("pe", lambda E, h=h: E.transpose(out=pT[0:64, h * 128:(h + 1) * 128], in_=Qg[:, h, :], identity=ident[:]),
                 reads=["Vb0", "ident"], writes=[pTk])
        P.op("act", lambda E: E.copy(out=QT[0:64, :, :], in_=pT[0:64, :].rearrange("p (h t) -> p h t", h=8)),
             reads=[pTk], writes=["QT"])
        for cc in range(2):
            for h in range(8):
                P.op("pe", lambda E, cc=cc, h=h: E.matmul(
                    pb[1 + cc][:, h * 64:(h + 1) * 64], lhsT=wukT[h // 4][:, (h % 4) * 256 + cc * 128:(h % 4) * 256 + (cc + 1) * 128],
                    rhs=QT[0:64, h, 0:64], start=True, stop=True),
                    reads=["KTb0", "KTb1", "QT"], writes=["pb%d" % (1 + cc)])
            P.op("act", lambda E, cc=cc: E.copy(out=qtT[:, cc, :, :], in_=pb[1 + cc][:].rearrange("p (h t) -> p h t", h=8)),
                 reads=["pb%d" % (1 + cc)], writes=["qtT"])
        for h in range(8):
            P.op("pe", lambda E, h=h: E.transpose(out=pT[0:32, h * 128:(h + 1) * 128], in_=Qcat[:, h, 64:96], identity=ident[:]),
                 reads=["Qcat", "ident"], writes=[pTk])
        qrT = [PT[0][0:32, :].rearrange("p (h t) -> p h t", h=4), PT[1][0:32, :].rearrange("p (h t) -> p h t", h=4)]
        for hh in range(2):
            P.op("act", lambda E, hh=hh: E.copy(out=PT[hh][0:32, :], in_=pT[0:32, hh * 512:(hh + 1) * 512]),
                 reads=[pTk], writes=["PT%d" % hh])

        cnt = [0]

        def page(b, j):
            n = 128 if j < NPAGE else 4
            bf = cnt[0] % 2
            cnt[0] += 1
            lk_, kk_, lbk, kbk, ltk, ktk, ptk = ("latf%d" % bf, "krf%d" % bf, "lb%d" % bf, "krb%d" % bf, "lT%d" % bf,
                                                 "krT%d" % bf, "pts%d" % bf)
            if j < NPAGE:
                col = b * NPAGE + j
                P.dma("pool", lambda E: E.indirect_dma_start(
                    out=latf[bf], out_offset=None, in_=d_clat,
                    in_offset=bass.IndirectOffsetOnAxis(ap=pidx[:, col:col + 1], axis=0)), reads=["pidx"], writes=[lk_])
                P.dma("pool", lambda E: E.indirect_dma_start(
                    out=krf[bf], out_offset=None, in_=d_ckr,
                    in_offset=bass.IndirectOffsetOnAxis(ap=pidx[:, col:col + 1], axis=0)), reads=["pidx"], writes=[kk_])
            else:
                P.dma("sp", lambda E: E.dma_start(out=latf[bf][0:4, :], in_=lat_s_scr[4 * b:4 * b + 4, :]),
                      reads=["lat_s_scr"], writes=[lk_])
                P.dma("sp", lambda E: E.dma_start(out=krf[bf][0:4, :], in_=kr_s_scr[4 * b:4 * b + 4, :]),
                      reads=["kr_s_scr"], writes=[kk_])
            P.op("act", lambda E: E.copy(out=lb[bf][0:n, 0:256], in_=latf[bf][0:n, :]), reads=[lk_], writes=[lbk])
            P.op("dve", lambda E: E.tensor_copy(out=krb[bf][0:n, :], in_=krf[bf][0:n, :]), reads=[kk_], writes=[kbk])
            for cc in range(2):
                P.op("pe", lambda E, cc=cc: E.transpose(out=pT[:, cc * 128:cc * 128 + n], in_=lb[bf][0:n, cc * 128:(cc + 1) * 128],
                                                         identity=ident[0:n, 0:n]), reads=[lbk, "ident"], writes=[pTk])
            P.op("pe", lambda E: E.transpose(out=pT[0:32, 256:256 + n], in_=krb[bf][0:n, :], identity=ident[0:n, 0:n]),
                 reads=[kbk, "ident"], writes=[pTk])
            P.op("act", lambda E: E.copy(out=lT[bf][:, :, 0:n], in_=pT[:, 0:256].rearrange("p (c k) -> p c k", c=2)[:, :, 0:n]),
                 reads=[pTk], writes=[ltk])
            P.op("act", lambda E: E.copy(out=krT[bf][0:32, 0:n], in_=pT[0:32, 256:256 + n]), reads=[pTk], writes=[ktk])
            for cc in range(2):
                P.op("pe", lambda E, cc=cc: E.matmul(pb[6][0:n, :], lhsT=lT[bf][:, cc, 0:n], rhs=w_uk_b[:, cc, :],
                                                      start=(cc == 0), stop=(cc == 1)), reads=[ltk, "w_uk_b"], writes=["pb6"])
            for cc in range(2):
                P.op("pe", lambda E, cc=cc: E.matmul(pb[7][0:n, 0:32], lhsT=lT[bf][:, cc, 0:n], rhs=qtT[:, cc, :, 4 * b:4 * b + 4],
                                                      start=(cc == 0), stop=(cc == 1)), reads=[ltk, "qtT"], writes=["pb7"])
            for hh in range(2):
                P.op("pe", lambda E, hh=hh: E.matmul(pb[7][0:n, 64 + 16 * hh:80 + 16 * hh], lhsT=krT[bf][0:32, 0:n],
                                                      rhs=qrT[hh][:, :, 4 * b:4 * b + 4], start=True, stop=True),
                     reads=[ktk, "PT%d" % hh], writes=["pb7"])
            act(junk[0:n, 0:512], pb[6][0:n, :], AF.Square, ["pb6"], ["junk"])
            P.op("dve", lambda E: E.tensor_reduce(out=ssk[0:n, 0:8], in_=junk[0:n, 0:512].rearrange("p (h d) -> p h d", h=8),
                                                  axis=AX.X, op=ALU.add), reads=["junk"], writes=["ssk"])
            rstd_from_ss(ssk[0:n, 0:8], "ssk", 64)
            s3 = sc1[0:n, :].rearrange("p (h q) -> p h q", h=8)
            tt("dve", s3, pb[7][0:n, 0:32].rearrange("p (h q) -> p h q", h=8),
               ssk[0:n, 0:8].unsqueeze(2).to_broadcast([n, 8, 4]), ALU.mult, ["pb7", "ssk"], ["sc1"])
            tt("dve", sc1[0:n, :], sc1[0:n, :], pb[7][0:n, 64:96], ALU.add, ["pb7"], ["sc1"])
            act(pts[bf][0:n, :], sc1[0:n, :], AF.Exp, ["sc1"], [ptk], scale=ATT_SCALE)
            if j == NPAGE:
                tt("dve", pts[bf][0:n, :], pts[bf][0:n, :], maskn[0:n, :], ALU.mult, ["maskn"], [ptk])
            P.op("pe", lambda E: E.matmul(pb[4][0:32, 0:257], lhsT=pts[bf][0:n, :], rhs=lb[bf][0:n, 0:257],
                                           start=(j == PAGE_LIST[0]), stop=(j == PAGE_LIST[-1])), reads=[ptk, lbk], writes=["pb4"])

        for b in range(SPC_RUN):
            for j in PAGE_LIST:
                page(b, j)
            P.op("dve", lambda E: E.reciprocal(out=rec[0:32, 0:1], in_=pb[4][0:32, 256:257]), reads=["pb4"], writes=["rec"])
            tsc("dve", olat[0:32, :], pb[4][0:32, 0:256], rec[0:32, 0:1], ALU.mult, ["pb4", "rec"], ["olat"])
            for cc in range(2):
                P.op("pe", lambda E, cc=cc: E.transpose(out=pT[:, cc * 32:(cc + 1) * 32], in_=olat[0:32, cc * 128:(cc + 1) * 128],
                                                         identity=ident[0:32, 0:32]), reads=["olat", "ident"], writes=[pTk])
            P.op("act", lambda E: E.copy(out=olT, in_=pT[:, 0:64].rearrange("p (c n) -> p c n", c=2)), reads=[pTk], writes=["olT"])
            for h in range(8):
                r0 = (h % 2) * 64
                c0 = (h // 2) * 64 + 4 * b
                for cc in range(2):
                    P.op("pe", lambda E, h=h, cc=cc, r0=r0, c0=c0: E.matmul(
                        pb[5][r0:r0 + 64, c0:c0 + 4], lhsT=w_uv_b[:, cc, h * 64:(h + 1) * 64], rhs=olT[:, cc, h * 4:(h + 1) * 4],
                        start=(cc == 0), stop=(cc == 1), skip_group_check=True), reads=["w_uv_b", "olT"], writes=["pb5"])
        P.op("act", lambda E: E.copy(out=mixT[:, 4:8, 0:64], in_=pb[5][:, 0:256].rearrange("p (c t) -> p c t", c=4)),
             reads=["pb5"], writes=["mixT_a"])
        glu_part(Ys, ["ytmp"], 64)
        out_part(xsx, "xt0")
        peer(x2, "x2", [(junk[:], "junk"), (xt[1][:], "xt1")] + xgb)
        P.dma("sp", lambda E: E.dma_start(out=o_y_s, in_=x2[:]), reads=["x2"], writes=["o_y_s"])
        out_keys.append("o_y_s")

        P.finish(out_keys)
        P.run()
    return nc


def _rope_tables(pos):
    inv = (10000.0 ** (-np.arange(0, 32, 2, dtype=np.float32) / np.float32(32))).astype(np.float32)
    ang = pos.astype(np.float32)[:, None] * inv[None, :]
    return np.concatenate([np.cos(ang), np.sin(ang)], axis=1).astype(np.float32)


def _c(a):
    return np.ascontiguousarray(a, dtype=np.float32)


_DEBUG_SMALL = 0


def kernel(**inputs):
    f32 = np.float32
    x_prompt = np.asarray(inputs["x_prompt"], f32)
    x_sample = np.asarray(inputs["x_sample"], f32)

    dbg_small = bool(_DEBUG_SMALL)
    nc = build_program(1024 if dbg_small else 10240)

    rope_full = _rope_tables(np.arange(SEQ))
    rope_f = _c(rope_full.reshape(32, 128, 32).transpose(1, 0, 2))
    rs = _rope_tables(PAST + (np.arange(128) % 4))
    ident = np.eye(128, dtype=f32)

    def rows16(a):
        return _c(np.asarray(a, f32).reshape(16, 128).T)

    a_re = rows16(inputs["ssm_a_re"][0])
    a_im = rows16(inputs["ssm_a_im"][0])
    ldt = rows16(np.repeat(np.asarray(inputs["ssm_log_dt"][0], f32)[:, None], 64, axis=1))
    bb = _c(np.asarray(inputs["ssm_b"][0], f32).reshape(16, 128, 32).transpose(1, 0, 2))
    cc = _c(np.asarray(inputs["ssm_c"][0], f32).transpose(0, 2, 1, 3).reshape(16, 128, 32).transpose(1, 0, 2))
    dd = _c(np.asarray(inputs["ssm_d"][0], f32).reshape(4, 128).T)
    state = np.asarray(inputs["state_ssm"][0], f32)

    wuq = np.asarray(inputs["w_uq"][0], f32).reshape(384, 8, 96)
    wuq_p = _c(np.concatenate([wuq[:, :, :64].reshape(384, 512), wuq[:, :, 64:].reshape(384, 256)], axis=1))
    kk = np.arange(128)[:, None]
    qq = np.arange(128)[None, :]
    diag = np.where(kk <= qq, 0.0, -30000.0).astype(f32)
    shared = {
        "cache_lat": np.asarray(inputs["cache_kv_latent"][0], f32).reshape(-1, 256),
        "cache_kr": np.asarray(inputs["cache_k_rope"][0], f32).reshape(-1, 32),
        "piota": _c(np.arange(128, dtype=f32)[:, None]),
        "w_ukT": _c(np.asarray(inputs["w_uk"][0], f32).transpose(2, 1, 0)),
        "maskn": _c(np.tile((np.arange(4)[:, None] <= np.arange(4)[None, :]).astype(f32)[:, None, :], (1, 8, 1)).reshape(4, 32)),
        "norm_ffn": _c(inputs["norm_ffn"][0]),
        "peer_wq": _c(inputs["peer_wq"][0]),
        "peer_keysT": _c(np.asarray(inputs["peer_keys"][0], f32).transpose(2, 0, 1)),
        "peer_u": _c(inputs["peer_u"][0]),
        "peer_v": _c(inputs["peer_v"][0]),
        "iota16": _c(np.tile(np.arange(16, dtype=f32)[None, :], (128, 1))),
        "w_uk": _c(np.asarray(inputs["w_uk"][0], f32).reshape(256, 512)),
        "w_uv": _c(np.asarray(inputs["w_uv"][0], f32).reshape(256, 512)),
        "w_uq": wuq_p,
        "w_glu": _c(inputs["w_glu"][0]),
        "w_out": _c(inputs["w_out"][0]),
        "norm_q_lora": _c(inputs["norm_q_lora"][0]),
        "qk_gain_q_nope": _c(inputs["qk_gain_q_nope"][0]),
        "qk_gain_q_rope": _c(inputs["qk_gain_q_rope"][0]),
        "qk_gain_k_nope": _c(inputs["qk_gain_k_nope"][0]),
        "ident": ident,
        "rope_f": rope_f,
        "rope_s": rs,
        "norm_mix": _c(np.asarray(inputs["norm_mix"][0], f32).reshape(8, 128).T),
        "w_in": _c(inputs["w_in"][0]),
        "norm_kv_lora": _c(inputs["norm_kv_lora"][0]),
        "qk_gain_k_rope": _c(inputs["qk_gain_k_rope"][0]),
        "ssm_a_re": a_re, "ssm_a_im": a_im, "ssm_log_dt": ldt, "ssm_b": bb, "ssm_c": cc, "ssm_d": dd,
    }
    in_maps = []
    for c in range(NCORES):
        b = c // 2
        p = c % 2
        xs = np.zeros((128, D_MODEL), f32)
        xs[:STOK] = x_sample[c * SPC:(c + 1) * SPC].reshape(STOK, D_MODEL)
        st = state[c * SPC:(c + 1) * SPC].reshape(SPC, 16, 128, 2).transpose(2, 1, 0, 3)
        par = np.zeros((128, 2), f32)
        par[:, 0] = 1.0 - p
        par[:, 1] = p
        xb = x_prompt[b].reshape(16, 2, 128, D_MODEL)
        masks = np.zeros((128, 2, 128), f32)
        if p == 0:
            masks[:, 0, :] = diag
            masks[:, 1, :] = -30000.0
        else:
            masks[:, 1, :] = diag
        m = dict(shared)
        if dbg_small:
            ptc = np.asarray(inputs["page_table"], np.int32)[c * SPC:(c + 1) * SPC].reshape(-1)
            m["cache_lat"] = _c(np.asarray(inputs["cache_kv_latent"][0], f32)[ptc].reshape(-1, 256))
            m["cache_kr"] = _c(np.asarray(inputs["cache_k_rope"][0], f32)[ptc].reshape(-1, 32))
        m.update({
            "xo": _c(xb[:, p].reshape(2048, D_MODEL)),
            "rope_o": _c(rope_full.reshape(16, 2, 128, 32)[:, p].transpose(1, 0, 2)),
            "masks": masks,
            "xf": _c(x_prompt[b]),
            "xs": xs,
            "page_tab": (np.arange(1024, dtype=np.int32).reshape(1, 1024) if dbg_small else
                         np.ascontiguousarray(np.asarray(inputs["page_table"], np.int32)[c * SPC:(c + 1) * SPC].reshape(1, 1024))),
            "state_ssm": _c(st),
            "parity": par,
        })
        in_maps.append(m)

    res = run_bass_kernel_spmd(nc, in_maps, core_ids=list(range(NCORES)))
    R = res.results

    y_prompt = np.zeros((4, 16, 2, 128, D_MODEL), f32)
    for c in range(NCORES):
        y_prompt[c // 2, :, c % 2] = R[c]["o_y_p"].reshape(16, 128, D_MODEL)
    y_prompt = y_prompt.reshape(4, SEQ, D_MODEL)
    y_sample = np.concatenate([R[c]["o_y_s"][:STOK].reshape(SPC, 4, D_MODEL) for c in range(NCORES)]).astype(f32)
    lat_p = np.stack([R[2 * b]["o_lat_p"] for b in range(4)])[None]
    kr_p = np.stack([R[2 * b]["o_kr_p"] for b in range(4)])[None]
    ssm_p = np.stack([R[2 * b]["o_ssm_p"].reshape(32, 64, 2) for b in range(4)])[None]
    lat_s = np.concatenate([R[c]["o_lat_s"][:STOK].reshape(SPC, 4, 256) for c in range(NCORES)])[None]
    kr_s = np.concatenate([R[c]["o_kr_s"][:STOK].reshape(SPC, 4, 32) for c in range(NCORES)])[None]
    ssm_s = np.concatenate([R[c]["o_ssm_s"].reshape(SPC, 32, 64, 2) for c in range(NCORES)])[None]
    return (y_prompt, y_sample, lat_p.astype(f32), kr_p.astype(f32), ssm_p.astype(f32), lat_s.astype(f32),
            kr_s.astype(f32), ssm_s.astype(f32))
```

```python
from contextlib import ExitStack
import math
import numpy as np
import concourse.bass as bass
import concourse.mybir as mybir
from concourse.bass_utils import run_bass_kernel_spmd

F32 = mybir.dt.float32
BF16 = mybir.dt.bfloat16
I32 = mybir.dt.int32
U32 = mybir.dt.uint32
AF = mybir.ActivationFunctionType
ALU = mybir.AluOpType
AX = mybir.AxisListType

D_MODEL = 1024
SEQ = 4096
NCORES = 8
EPS = 1e-6
PAST = 8192
NPAGE = 64
SPC = 16
STOK = 64

STAGE = 1
SPC_RUN = 16
PAGE_LIST = list(range(65))


class Prog:
    ENGS = ("pe", "act", "dve", "pool", "sp")

    def __init__(self, nc, es):
        self.nc = nc
        self.es = es
        self.q = {e: [] for e in self.ENGS}
        self.count = {e: 0 for e in self.ENGS}
        self.sem = {e: nc.alloc_semaphore(name="c_" + e) for e in self.ENGS}
        self.seen = {e: {f: 0 for f in self.ENGS} for e in self.ENGS}
        self.last_w = {}
        self.readers = {}
        self.R = 12
        self.ring = {}
        self.ring_n = {}
        for qn in ("sp", "pool", "act"):
            self.ring[qn] = [nc.alloc_semaphore(name="d_%s%d" % (qn, i)) for i in range(self.R)]
            self.ring_n[qn] = 0
        self.dseen = {e: {} for e in self.ENGS}
        self.ntens = 0

    def sb(self, shape, dt, name=None):
        self.ntens += 1
        name = "s_" + (name or "t%d" % self.ntens)
        return self.es.enter_context(self.nc.sbuf_tensor(name, list(shape), dt))

    def ps(self, shape, dt, name=None):
        self.ntens += 1
        name = "ps_" + (name or "p%d" % self.ntens)
        return self.es.enter_context(self.nc.psum_tensor(name, list(shape), dt))

    def _deps(self, reads, writes):
        toks = []
        for k in list(reads) + list(writes):
            t = self.last_w.get(k)
            if t is not None:
                toks.append(t)
        for k in writes:
            toks.extend(self.readers.get(k, ()))
        return toks

    def _emit_waits(self, eng, toks):
        need = {}
        dneed = {}
        for t in toks:
            if t[0] == "e":
                _, f, c = t
                if f == eng and eng == "pe":
                    continue
                if f == eng and eng == "sp":
                    continue
                if self.seen[eng][f] < c:
                    need[f] = max(need.get(f, 0), c)
            else:
                _, qn, slot, val = t
                key = (qn, slot)
                if self.dseen[eng].get(key, 0) < val:
                    dneed[key] = max(dneed.get(key, 0), val)
        for f, c in need.items():
            self.seen[eng][f] = c
            sem = self.sem[f]
            self.q[eng].append(lambda E, sem=sem, c=c: E.wait_ge(sem, c))
        for (qn, slot), val in dneed.items():
            self.dseen[eng][(qn, slot)] = val
            sem = self.ring[qn][slot]
            self.q[eng].append(lambda E, sem=sem, val=val: E.wait_ge(sem, val))

    def _record(self, tok, reads, writes):
        for k in writes:
            self.last_w[k] = tok
            self.readers[k] = []
        for k in reads:
            if k in writes:
                continue
            self.readers.setdefault(k, []).append(tok)

    def op(self, eng, fn, reads=(), writes=()):
        self._emit_waits(eng, self._deps(reads, writes))
        self.count[eng] += 1
        c = self.count[eng]
        sem = self.sem[eng]
        self.q[eng].append(lambda E, fn=fn, sem=sem: fn(E).then_inc(sem, 1))
        self._record(("e", eng, c), reads, writes)

    def dma(self, qn, fn, reads=(), writes=()):
        toks = self._deps(reads, writes)
        n = self.ring_n[qn]
        self.ring_n[qn] = n + 1
        slot = n % self.R
        use = n // self.R
        if use > 0:
            toks.append(("d", qn, slot, 16 * use))
        self._emit_waits(qn, toks)
        sem = self.ring[qn][slot]
        self.q[qn].append(lambda E, fn=fn, sem=sem: fn(E).then_inc(sem, 16))
        self._record(("d", qn, slot, 16 * (use + 1)), reads, writes)

    def barrier(self):
        toks = [("e", f, self.count[f]) for f in self.ENGS if self.count[f] > 0]
        for qn in self.ring:
            n = self.ring_n[qn]
            for slot in range(min(n, self.R)):
                uses = (n - 1 - slot) // self.R + 1
                toks.append(("d", qn, slot, 16 * uses))
        for e in self.ENGS:
            self._emit_waits(e, [t for t in toks if not (t[0] == "e" and t[1] == e)])

    def finish(self, keys):
        toks = []
        for k in keys:
            t = self.last_w.get(k)
            if t is not None:
                toks.append(t)
        self._emit_waits("sp", toks)

    def run(self):
        nc = self.nc
        with nc.Block() as block:
            @block.tensor
            def _(E):
                for f in self.q["pe"]:
                    f(E)

            @block.scalar
            def _(E):
                for f in self.q["act"]:
                    f(E)

            @block.vector
            def _(E):
                for f in self.q["dve"]:
                    f(E)

            @block.gpsimd
            def _(E):
                for f in self.q["pool"]:
                    f(E)

            @block.sync
            def _(E):
                for f in self.q["sp"]:
                    f(E)


def build_program(n_phys=10240):
    nc = bass.Bass("TRN2", target_bir_lowering=False)
    es = ExitStack()
    with es:
        es.enter_context(nc.allow_low_precision("bf16 matmul operands, fp32 accumulation"))
        P = Prog(nc, es)

        def din(name, shape, dt=F32):
            return nc.dram_tensor(name, list(shape), dt, kind="ExternalInput").ap()

        def dout(name, shape, dt=F32):
            return nc.dram_tensor(name, list(shape), dt, kind="ExternalOutput").ap()

        xf = din("xf", [SEQ, D_MODEL])
        xs = din("xs", [128, D_MODEL])
        ident_d = din("ident", [128, 128])
        rope_f = din("rope_f", [128, 32, 32])
        rope_s = din("rope_s", [128, 32])
        norm_mix = din("norm_mix", [128, 8])
        w_in = din("w_in", [D_MODEL, 1184])
        norm_kv = din("norm_kv_lora", [256])
        g_kr = din("qk_gain_k_rope", [32])
        d_are = din("ssm_a_re", [128, 16])
        d_aim = din("ssm_a_im", [128, 16])
        d_ldt = din("ssm_log_dt", [128, 16])
        d_b = din("ssm_b", [128, 16, 32])
        d_c = din("ssm_c", [128, 16, 32])
        d_d = din("ssm_d", [128, 4])
        d_st = din("state_ssm", [128, 16, 16, 2])
        d_par = din("parity", [128, 2])
        xo = din("xo", [2048, D_MODEL])
        rope_o = din("rope_o", [128, 16, 32])
        d_masks = din("masks", [128, 2, 128])
        d_wuk = din("w_uk", [256, 512])
        d_wuv = din("w_uv", [256, 512])
        d_wuq = din("w_uq", [384, 768])
        d_wglu = din("w_glu", [512, 1024])
        d_wout = din("w_out", [1024, 1024])
        d_gql = din("norm_q_lora", [384])
        d_gqn = din("qk_gain_q_nope", [64])
        d_gqr = din("qk_gain_q_rope", [32])
        d_gkn = din("qk_gain_k_nope", [64])
        d_gffn = din("norm_ffn", [D_MODEL])
        d_wq = din("peer_wq", [D_MODEL, 2048])
        d_keysT = din("peer_keysT", [128, 2, 128])
        d_pu = din("peer_u", [16384, D_MODEL])
        d_pv = din("peer_v", [16384, D_MODEL])
        d_iota = din("iota16", [128, 16])
        d_clat = din("cache_lat", [n_phys * 128, 256])
        d_ckr = din("cache_kr", [n_phys * 128, 32])
        d_ptab = din("page_tab", [1, 1024], I32)
        d_piota = din("piota", [128, 1])
        d_wukT = din("w_ukT", [64, 8, 256])
        d_maskn = din("maskn", [4, 32])
        lat_s_scr = nc.dram_tensor("lat_s_scr", [128, 256], F32, kind="Internal").ap()
        kr_s_scr = nc.dram_tensor("kr_s_scr", [128, 32], F32, kind="Internal").ap()
        o_y_s = dout("o_y_s", [128, D_MODEL])
        KT_d = nc.dram_tensor("KT_scr", [96, 8, SEQ], BF16, kind="Internal").ap()
        V_d = nc.dram_tensor("V_scr", [SEQ, 520], BF16, kind="Internal").ap()

        o_lat_p = dout("o_lat_p", [SEQ, 256])
        o_kr_p = dout("o_kr_p", [SEQ, 32])
        o_lat_s = dout("o_lat_s", [128, 256])
        o_kr_s = dout("o_kr_s", [128, 32])
        o_y_p = dout("o_y_p", [2048, D_MODEL])
        o_ssm_p = dout("o_ssm_p", [16, 128, 2])
        o_ssm_s = dout("o_ssm_s", [16, 16, 128, 2])

        out_keys = []

        def tt(eng, out, a, b, op, r, w):
            P.op(eng, lambda E: E.tensor_tensor(out=out, in0=a, in1=b, op=op), reads=r, writes=w)

        def tsc(eng, out, a, s1, op0, r, w, s2=None, op1=None):
            if op1 is None:
                P.op(eng, lambda E: E.tensor_scalar(out=out, in0=a, scalar1=s1, scalar2=None, op0=op0),
                     reads=r, writes=w)
            else:
                P.op(eng, lambda E: E.tensor_scalar(out=out, in0=a, scalar1=s1, scalar2=s2, op0=op0, op1=op1),
                     reads=r, writes=w)

        def stt(out, a, sc, b, op0, op1, r, w):
            P.op("dve", lambda E: E.scalar_tensor_tensor(out=out, in0=a, scalar=sc, in1=b, op0=op0, op1=op1),
                 reads=r, writes=w)

        def act(out, a, func, r, w, **kw):
            P.op("act", lambda E: E.activation(out=out, in_=a, func=func, **kw), reads=r, writes=w)

        def load(dst, src, key):
            P.dma("sp", lambda E: E.dma_start(out=dst, in_=src), writes=[key])

        ident_f = P.sb([128, 128], F32, "ident_f")
        ident = P.sb([128, 128], BF16, "ident")
        load(ident_f[:], ident_d, "ident_f")
        P.op("dve", lambda E: E.tensor_copy(out=ident[:], in_=ident_f[:]), reads=["ident_f"], writes=["ident"])

        ropeF = P.sb([128, 32, 32], F32, "ropeF")
        ropeS = P.sb([128, 32], F32, "ropeS")
        load(ropeF[:], rope_f, "ropeF")
        load(ropeS[:], rope_s, "ropeS")
        gmix = P.sb([128, 8], F32, "gmix")
        load(gmix[:], norm_mix, "gmix")
        gkv = P.sb([128, 256], F32, "gkv")
        load(gkv[:], norm_kv.unsqueeze(0).to_broadcast([128, 256]), "gkv")
        gkr = P.sb([128, 32], F32, "gkr")
        load(gkr[:], g_kr.unsqueeze(0).to_broadcast([128, 32]), "gkr")
        par = P.sb([128, 2], F32, "par")
        load(par[:], d_par, "par")

        pb = [P.ps([128, 512], F32, "pb%d" % i) for i in range(8)]
        pT = pb[0][:].bitcast(BF16)
        pTk = "pb0"

        w_rest = P.sb([128, 8, 800], BF16, "w_rest")
        w_cq = P.sb([128, 8, 384], BF16, "w_cq")
        xt = [P.sb([128, D_MODEL], F32, "xt%d" % i) for i in range(2)]
        for kt in range(8):
            load(xt[0][:], w_in[kt * 128:(kt + 1) * 128, 0:1024], "xt0")
            load(xt[1][:, 0:160], w_in[kt * 128:(kt + 1) * 128, 1024:1184], "xt1")
            tsc("dve", w_rest[:, kt, 0:512], xt[0][:, 0:512], gmix[:, kt:kt + 1], ALU.mult, ["xt0", "gmix"], [("w_in_b", kt)])
            tsc("dve", w_cq[:, kt, :], xt[0][:, 512:896], gmix[:, kt:kt + 1], ALU.mult, ["xt0", "gmix"], [("w_cq", kt)])
            tsc("dve", w_rest[:, kt, 512:640], xt[0][:, 896:1024], gmix[:, kt:kt + 1], ALU.mult, ["xt0", "gmix", ("w_in_b", kt)],
                [("w_in_b", kt)])
            tsc("dve", w_rest[:, kt, 640:800], xt[1][:, 0:160], gmix[:, kt:kt + 1], ALU.mult, ["xt1", "gmix", ("w_in_b", kt)],
                [("w_in_b", kt)])

        NRT = 16
        TB = 512
        sc = P.sb([128, 24, 16], F32, "s5sc")
        SCK = "s5sc"
        for j, src in enumerate((d_are, d_aim, d_ldt)):
            load(sc[:, j, :], src, SCK)
        ARE, AIM, DT, AR, TH, RR, T0, T1, T2, T3, C0, S0, FRE, FIM, ABR, ABI, C511, S511 = range(18)

        def S(j):
            return sc[:, j, :]
        act(S(DT), S(DT), AF.Exp, [SCK], [SCK])
        tt("dve", S(AR), S(ARE), S(DT), ALU.mult, [SCK], [SCK])
        tt("dve", S(TH), S(AIM), S(DT), ALU.mult, [SCK], [SCK])
        act(S(RR), S(AR), AF.Exp, [SCK], [SCK])
        sci = P.sb([128, 16], I32, "s5sci")
        tsc("dve", S(T0), S(TH), 1.0 / (2.0 * math.pi), ALU.mult, [SCK], [SCK])
        P.op("dve", lambda E: E.tensor_copy(out=sci[:], in_=S(T0)), reads=[SCK], writes=["s5sci"])
        P.op("dve", lambda E: E.tensor_copy(out=S(T1), in_=sci[:]), reads=["s5sci"], writes=[SCK])
        tt("dve", S(T0), S(T0), S(T1), ALU.subtract, [SCK], [SCK])
        tsc("dve", S(T0), S(T0), 2.0 * math.pi / 4.0, ALU.mult, [SCK], [SCK])
        hp = P.sb([128, 1], F32, "halfpi")
        P.op("dve", lambda E: E.memset(hp[:], math.pi / 2.0), writes=["halfpi"])
        act(S(T1), S(T0), AF.Sin, [SCK], [SCK])
        act(S(T2), S(T0), AF.Sin, [SCK, "halfpi"], [SCK], bias=hp[:, 0:1])

        def dbl(cd, sd, cs_, ss_):
            tt("dve", S(T3), cs_, cs_, ALU.mult, [SCK], [SCK])
            tt("dve", cd, ss_, ss_, ALU.mult, [SCK], [SCK])
            tt("dve", cd, S(T3), cd, ALU.subtract, [SCK], [SCK])
            tt("dve", sd, cs_, ss_, ALU.mult, [SCK], [SCK])
            tsc("dve", sd, sd, 2.0, ALU.mult, [SCK], [SCK])
        dbl(S(C511), S(S511), S(T2), S(T1))
        dbl(S(C0), S(S0), S(C511), S(S511))
        tt("dve", S(ABR), S(RR), S(C0), ALU.mult, [SCK], [SCK])
        tt("dve", S(ABI), S(RR), S(S0), ALU.mult, [SCK], [SCK])
        tsc("dve", S(T0), S(ABR), -1.0, ALU.add, [SCK], [SCK])
        tt("dve", S(T1), S(ARE), S(ARE), ALU.mult, [SCK], [SCK])
        tt("dve", S(T2), S(AIM), S(AIM), ALU.mult, [SCK], [SCK])
        tt("dve", S(T1), S(T1), S(T2), ALU.add, [SCK], [SCK])
        P.op("dve", lambda E: E.reciprocal(out=S(T1), in_=S(T1)), reads=[SCK], writes=[SCK])
        tt("dve", S(T2), S(T0), S(ARE), ALU.mult, [SCK], [SCK])
        tt("dve", S(T3), S(ABI), S(AIM), ALU.mult, [SCK], [SCK])
        tt("dve", S(T2), S(T2), S(T3), ALU.add, [SCK], [SCK])
        tt("dve", S(FRE), S(T2), S(T1), ALU.mult, [SCK], [SCK])
        tt("dve", S(T2), S(ABI), S(ARE), ALU.mult, [SCK], [SCK])
        tt("dve", S(T3), S(T0), S(AIM), ALU.mult, [SCK], [SCK])
        tt("dve", S(T2), S(T2), S(T3), ALU.subtract, [SCK], [SCK])
        tt("dve", S(FIM), S(T2), S(T1), ALU.mult, [SCK], [SCK])

        rotc = P.sb([128, 10, 16], F32, "rotc")
        rots = P.sb([128, 10, 16], F32, "rots")
        RK = "rot"
        P.op("dve", lambda E: E.tensor_copy(out=rotc[:, 0, :], in_=S(C0)), reads=[SCK], writes=[RK])
        P.op("dve", lambda E: E.tensor_copy(out=rots[:, 0, :], in_=S(S0)), reads=[SCK], writes=[RK])
        for k in range(9):
            tt("dve", S(T3), rotc[:, k, :], rotc[:, k, :], ALU.mult, [RK, SCK], [SCK])
            tt("dve", S(T2), rots[:, k, :], rots[:, k, :], ALU.mult, [RK, SCK], [SCK])
            tt("dve", rotc[:, k + 1, :], S(T3), S(T2), ALU.subtract, [SCK, RK], [RK])
            tt("dve", S(T3), rotc[:, k, :], rots[:, k, :], ALU.mult, [RK, SCK], [SCK])
            tsc("dve", rots[:, k + 1, :], S(T3), 2.0, ALU.mult, [SCK, RK], [RK])
        tt("dve", S(T2), rotc[:, 9, :], S(C0), ALU.mult, [RK, SCK], [SCK])
        tt("dve", S(T3), rots[:, 9, :], S(S0), ALU.mult, [RK, SCK], [SCK])
        tt("dve", S(C511), S(T2), S(T3), ALU.add, [SCK], [SCK])
        tt("dve", S(T2), rots[:, 9, :], S(C0), ALU.mult, [RK, SCK], [SCK])
        tt("dve", S(T3), rotc[:, 9, :], S(S0), ALU.mult, [RK, SCK], [SCK])
        tt("dve", S(S511), S(T2), S(T3), ALU.subtract, [SCK], [SCK])

        cosT = P.sb([128, NRT, TB], BF16, "cosT")
        sinT = P.sb([128, NRT, TB], BF16, "sinT")
        Gre = P.sb([128, TB], F32, "Gre")
        Gim = P.sb([128, TB], F32, "Gim")
        q1 = P.sb([128, TB], F32, "q1")
        q2 = P.sb([128, TB], F32, "q2")
        for r0 in range(NRT):
            ch = r0 // 4
            P.op("pool", lambda E: E.memset(Gre[:, 0:1], 1.0), writes=["Gre"])
            P.op("pool", lambda E: E.memset(Gim[:, 0:1], 0.0), writes=["Gim"])
            for k in range(9):
                n = 1 << k
                ck = rotc[:, k, r0:r0 + 1]
                sk_ = rots[:, k, r0:r0 + 1]
                tsc("dve", q1[:, 0:n], Gre[:, 0:n], ck, ALU.mult, ["Gre", RK], ["q1"])
                tsc("dve", q2[:, 0:n], Gim[:, 0:n], ck, ALU.mult, ["Gim", RK], ["q2"])
                stt(Gre[:, n:2 * n], Gim[:, 0:n], sk_, q1[:, 0:n], ALU.mult, ALU.subtract, ["Gim", "q1", RK], ["Gre"])
                stt(Gim[:, n:2 * n], Gre[:, 0:n], sk_, q2[:, 0:n], ALU.mult, ALU.add, ["Gre", "q2", RK], ["Gim"])
                tsc("dve", Gre[:, n:2 * n], Gre[:, n:2 * n], -1.0, ALU.mult, ["Gre"], ["Gre"])
            P.op("act", lambda E, r0=r0: E.copy(out=cosT[:, r0, :], in_=Gre[:]), reads=["Gre"],
                 writes=[("cosT", ch)])
            P.op("act", lambda E, r0=r0: E.copy(out=sinT[:, r0, :], in_=Gim[:]), reads=["Gim"],
                 writes=[("sinT", ch)])

        bsb = P.sb([128, 16, 16, 2], F32, "bsb")
        csb = P.sb([128, 16, 16, 2], F32, "csb")
        load(bsb[:], d_b.rearrange("p t (h c) -> p t h c", c=2), "bsb")
        load(csb[:], d_c.rearrange("p t (h c) -> p t h c", c=2), "csb")
        bbr = P.sb([128, 16, 16], F32, "bbr")
        bbi = P.sb([128, 16, 16], F32, "bbi")
        tb1 = P.sb([128, 16, 16], F32, "tb1")
        fre_bc = S(FRE).unsqueeze(2).to_broadcast([128, 16, 16])
        fim_bc = S(FIM).unsqueeze(2).to_broadcast([128, 16, 16])
        tt("dve", bbr[:], bsb[:, :, :, 0], fre_bc, ALU.mult, ["bsb", SCK], ["bbr"])
        tt("dve", tb1[:], bsb[:, :, :, 1], fim_bc, ALU.mult, ["bsb", SCK], ["tb1"])
        tt("dve", bbr[:], bbr[:], tb1[:], ALU.subtract, ["tb1"], ["bbr"])
        tt("dve", bbi[:], bsb[:, :, :, 1], fre_bc, ALU.mult, ["bsb", SCK], ["bbi"])
        tt("dve", tb1[:], bsb[:, :, :, 0], fim_bc, ALU.mult, ["bsb", SCK], ["tb1"])
        tt("dve", bbi[:], bbi[:], tb1[:], ALU.add, ["tb1"], ["bbi"])

        BTr = P.sb([128, NRT, 128], BF16, "BTr")
        BTi = P.sb([128, NRT, 128], BF16, "BTi")
        CZr = P.sb([128, NRT, 128], BF16, "CZr")
        CZi = P.sb([128, NRT, 128], BF16, "CZi")
        zr = P.sb([128, 128], BF16, "zr")
        zi = P.sb([128, 128], BF16, "zi")
        P.op("pool", lambda E: E.memset(CZr[:], 0.0), writes=["CZr"])
        P.op("pool", lambda E: E.memset(CZi[:], 0.0), writes=["CZi"])
        for rt_ in range(NRT):
            c0 = 32 * (rt_ % 4)
            P.op("pool", lambda E: E.memset(zr[:], 0.0), writes=["zr"])
            P.op("pool", lambda E: E.memset(zi[:], 0.0), writes=["zi"])
            for gl in range(2):
                rows = slice(64 * gl, 64 * gl + 64)
                cols = slice(c0 + 16 * gl, c0 + 16 * gl + 16)
                P.op("dve", lambda E, rows=rows, cols=cols, rt_=rt_: E.tensor_copy(out=zr[rows, cols], in_=bbr[rows, rt_, :]),
                     reads=["bbr"], writes=["zr"])
                P.op("dve", lambda E, rows=rows, cols=cols, rt_=rt_: E.tensor_copy(out=zi[rows, cols], in_=bbi[rows, rt_, :]),
                     reads=["bbi"], writes=["zi"])
                P.op("dve", lambda E, rows=rows, cols=cols, rt_=rt_: E.tensor_copy(out=CZr[rows, rt_, cols], in_=csb[rows, rt_, :, 0]),
                     reads=["csb"], writes=["CZr"])
                P.op("dve", lambda E, rows=rows, cols=cols, rt_=rt_: E.tensor_scalar(
                    out=CZi[rows, rt_, cols], in0=csb[rows, rt_, :, 1], scalar1=-1.0, scalar2=None, op0=ALU.mult),
                    reads=["csb"], writes=["CZi"])
            P.op("pe", lambda E: E.transpose(out=pT[:, 0:128], in_=zr[:], identity=ident[:]),
                 reads=["zr", "ident"], writes=[pTk])
            P.op("pe", lambda E: E.transpose(out=pT[:, 128:256], in_=zi[:], identity=ident[:]),
                 reads=["zi", "ident"], writes=[pTk])
            P.op("act", lambda E, rt_=rt_: E.copy(out=BTr[:, rt_, :], in_=pT[:, 0:128]), reads=[pTk], writes=[("BTr", rt_)])
            P.op("act", lambda E, rt_=rt_: E.copy(out=BTi[:, rt_, :], in_=pT[:, 128:256]), reads=[pTk], writes=[("BTi", rt_)])
        dsk = P.sb([128, 4], F32, "dsk")
        load(dsk[:], d_d, "dsk")

        w_uk_b = P.sb([128, 2, 512], BF16, "w_uk_b")
        w_uv_b = P.sb([128, 2, 512], BF16, "w_uv_b")
        for c_ in range(2):
            load(xt[0][:, 0:512], d_wuk[c_ * 128:(c_ + 1) * 128, :], "xt0")
            P.op("act", lambda E, c_=c_: E.copy(out=w_uk_b[:, c_, :], in_=xt[0][:, 0:512]), reads=["xt0"], writes=["w_uk_b"])
            load(xt[1][:, 0:512], d_wuv[c_ * 128:(c_ + 1) * 128, :], "xt1")
            P.op("act", lambda E, c_=c_: E.copy(out=w_uv_b[:, c_, :], in_=xt[1][:, 0:512]), reads=["xt1"], writes=["w_uv_b"])
        gkn = P.sb([128, 64], F32, "gkn")
        load(gkn[:], d_gkn.unsqueeze(0).to_broadcast([128, 64]), "gkn")
        lnb = P.sb([128, 256], BF16, "lnb")
        ckvT = P.sb([128, 2, 128], BF16, "ckvT")
        ssk = P.sb([128, 16], F32, "ssk")
        tmpk = P.sb([128, 8, 64], F32, "tmpk")
        Kcat = P.sb([128, 8, 96], BF16, "Kcat")
        KTt = [P.sb([96, 8, 128], BF16, "KTt%d" % i_) for i_ in range(2)]
        Vt = [P.sb([128, 8, 65], BF16, "Vt%d" % i_) for i_ in range(2)]
        for i_ in range(2):
            P.op("pool", lambda E, i_=i_: E.memset(Vt[i_][:], 1.0), writes=["Vt%d" % i_])

        junk = P.sb([128, D_MODEL], F32, "junk")
        xnb = P.sb([128, D_MODEL], BF16, "xnb")
        xnT4 = P.sb([128, 8, 512], BF16, "xnT4")
        pkv = pb[1]
        ss = P.sb([128, 8], F32, "ss")
        latn = [P.sb([128, 256], F32, "latn%d" % i) for i in range(2)]
        krn = P.sb([128, 32], F32, "krn")
        kro = [P.sb([128, 32], F32, "kro%d" % i) for i in range(2)]
        rtt = P.sb([128, 4, 16], F32, "rt")
        uT = P.sb([128, 4, 512], BF16, "uT")

        def rstd_from_ss(ss_ap, key, d):
            tsc("dve", ss_ap, ss_ap, 1.0 / d, ALU.mult, [key], [key], s2=EPS, op1=ALU.add)
            act(ss_ap, ss_ap, AF.Ln, [key], [key])
            act(ss_ap, ss_ap, AF.Exp, [key], [key], scale=-0.5)

        def rope(dst, src, cs, keys_r, keys_w):
            cos = cs[:, 0:16]
            sin = cs[:, 16:32]
            x1 = src[:, 0:16]
            x2 = src[:, 16:32]
            tt("dve", rtt[:, 0, :], x1, cos, ALU.mult, keys_r, ["rt0"])
            tt("dve", rtt[:, 1, :], x2, sin, ALU.mult, keys_r, ["rt1"])
            tt("dve", rtt[:, 2, :], x2, cos, ALU.mult, keys_r, ["rt2"])
            tt("dve", rtt[:, 3, :], x1, sin, ALU.mult, keys_r, ["rt3"])
            tt("dve", dst[:, 0:16], rtt[:, 0, :], rtt[:, 1, :], ALU.subtract, ["rt0", "rt1"], keys_w)
            tt("dve", dst[:, 16:32], rtt[:, 2, :], rtt[:, 3, :], ALU.add, ["rt2", "rt3"] + list(keys_w), keys_w)

        def kv_tile(i, j, src_ap, cs_ap, lat_out, kr_out):
            x = xt[i % 2]
            xk = "xt%d" % (i % 2)
            load(x[:], src_ap, xk)
            act(junk[:], x[:], AF.Square, [xk], ["junk", "ss0"], accum_out=ss[:, 0:1])
            rstd_from_ss(ss[:, 0:1], "ss0", D_MODEL)
            tsc("dve", xnb[:], x[:], ss[:, 0:1], ALU.mult, [xk, "ss0"], ["xnb"])
            for kt in range(8):
                P.op("pe", lambda E, kt=kt: E.transpose(out=pT[:, kt * 128:(kt + 1) * 128],
                                                         in_=xnb[:, kt * 128:(kt + 1) * 128], identity=ident[:]),
                     reads=["xnb", "ident"], writes=[pTk])
            P.op("act", lambda E: E.copy(out=xnT4[:, :, j * 128:(j + 1) * 128],
                                         in_=pT.rearrange("p (k t) -> p k t", k=8)),
                 reads=[pTk], writes=[("xnT4", j)])
            for kt in range(8):
                P.op("pe", lambda E, kt=kt: E.matmul(pkv[:, 0:288], lhsT=xnT4[:, kt, j * 128:(j + 1) * 128],
                                                      rhs=w_rest[:, kt, 512:800], start=(kt == 0), stop=(kt == 7)),
                     reads=[("xnT4", j), ("w_in_b", kt)], writes=["pb1"])
            act(junk[:, 0:256], pkv[:, 0:256], AF.Square, ["pb1"], ["junk", "ss1"], accum_out=ss[:, 1:2])
            rstd_from_ss(ss[:, 1:2], "ss1", 256)
            ln = latn[i % 2]
            lk = "latn%d" % (i % 2)
            stt(ln[:], pkv[:, 0:256], ss[:, 1:2], gkv[:], ALU.mult, ALU.mult, ["pb1", "ss1", "gkv"], [lk])
            P.dma("sp", lambda E: E.dma_start(out=lat_out, in_=ln[:]), reads=[lk], writes=[("out", id(lat_out))])
            out_keys.append(("out", id(lat_out)))
            act(junk[:, 0:32], pkv[:, 256:288], AF.Square, ["pb1"], ["junk", "ss2"], accum_out=ss[:, 2:3])
            rstd_from_ss(ss[:, 2:3], "ss2", 32)
            stt(krn[:], pkv[:, 256:288], ss[:, 2:3], gkr[:], ALU.mult, ALU.mult, ["pb1", "ss2", "gkr"], ["krn"])
            ko = kro[i % 2]
            kk = "kro%d" % (i % 2)
            rope(ko, krn, cs_ap, ["krn", "ropeF", "ropeS"], [kk])
            P.dma("sp", lambda E: E.dma_start(out=kr_out, in_=ko[:]), reads=[kk], writes=[("out", id(kr_out))])
            out_keys.append(("out", id(kr_out)))
            if i >= 32:
                P.dma("sp", lambda E: E.dma_start(out=lat_s_scr, in_=ln[:]), reads=[lk], writes=["lat_s_scr"])
                P.dma("sp", lambda E: E.dma_start(out=kr_s_scr, in_=ko[:]), reads=[kk], writes=["kr_s_scr"])
                return
            P.op("act", lambda E: E.copy(out=lnb[:], in_=ln[:]), reads=[lk], writes=["lnb"])
            for c_ in range(2):
                P.op("pe", lambda E, c_=c_: E.transpose(out=pT[:, c_ * 128:(c_ + 1) * 128], in_=lnb[:, c_ * 128:(c_ + 1) * 128],
                                                         identity=ident[:]), reads=["lnb", "ident"], writes=[pTk])
            P.op("act", lambda E: E.copy(out=ckvT[:], in_=pT[:, 0:256].rearrange("p (c t) -> p c t", c=2)),
                 reads=[pTk], writes=["ckvT"])
            for c_ in range(2):
                P.op("pe", lambda E, c_=c_: E.matmul(pb[6][:], lhsT=ckvT[:, c_, :], rhs=w_uk_b[:, c_, :],
                                                      start=(c_ == 0), stop=(c_ == 1)),
                     reads=["ckvT", "w_uk_b"], writes=["pb6"])
            for c_ in range(2):
                P.op("pe", lambda E, c_=c_: E.matmul(pb[7][:], lhsT=ckvT[:, c_, :], rhs=w_uv_b[:, c_, :],
                                                      start=(c_ == 0), stop=(c_ == 1)),
                     reads=["ckvT", "w_uv_b"], writes=["pb7"])
            act(junk[:, 0:512], pb[6][:], AF.Square, ["pb6"], ["junk"])
            P.op("dve", lambda E: E.tensor_reduce(out=ssk[:, 0:8], in_=junk[:, 0:512].rearrange("p (h d) -> p h d", h=8),
                                                  axis=AX.X, op=ALU.add), reads=["junk"], writes=["ssk"])
            rstd_from_ss(ssk[:, 0:8], "ssk", 64)
            tt("dve", tmpk[:], pb[6][:].rearrange("p (h d) -> p h d", h=8),
               ssk[:, 0:8].unsqueeze(2).to_broadcast([128, 8, 64]), ALU.mult, ["pb6", "ssk"], ["tmpk"])
            tt("dve", Kcat[:, :, 0:64], tmpk[:], gkn[:].unsqueeze(1).to_broadcast([128, 8, 64]), ALU.mult,
               ["tmpk", "gkn"], ["Kcat"])
            P.op("dve", lambda E: E.tensor_copy(out=Kcat[:, :, 64:96], in_=ko[:].unsqueeze(1).to_broadcast([128, 8, 32])),
                 reads=[kk, "Kcat"], writes=["Kcat"])
            for h in range(8):
                P.op("pe", lambda E, h=h: E.transpose(out=pT[0:96, h * 128:(h + 1) * 128], in_=Kcat[:, h, :], identity=ident[:]),
                     reads=["Kcat", "ident"], writes=[pTk])
            kt_ = KTt[i % 2]
            ktk = "KTt%d" % (i % 2)
            P.op("act", lambda E: E.copy(out=kt_[:], in_=pT[0:96, :].rearrange("p (h t) -> p h t", h=8)),
                 reads=[pTk], writes=[ktk])
            P.dma("sp", lambda E: E.dma_start(out=KT_d[:, :, i * 128:(i + 1) * 128], in_=kt_[:]), reads=[ktk],
                  writes=[("KTd", i)])
            vt_ = Vt[i % 2]
            vtk = "Vt%d" % (i % 2)
            P.op("act", lambda E: E.copy(out=vt_[:, :, 0:64], in_=pb[7][:].rearrange("p (h d) -> p h d", h=8)),
                 reads=["pb7"], writes=[vtk])
            P.dma("sp", lambda E: E.dma_start(out=V_d[i * 128:(i + 1) * 128, :], in_=vt_[:].rearrange("p h d -> p (h d)")),
                  reads=[vtk], writes=[("Vd", i)])

        def u_proj(ntok):
            for ft in range(4):
                for kt in range(8):
                    P.op("pe", lambda E, ft=ft, kt=kt: E.matmul(
                        pb[2][:, 0:ntok], lhsT=w_rest[:, kt, ft * 128:(ft + 1) * 128], rhs=xnT4[:, kt, 0:ntok],
                        start=(kt == 0), stop=(kt == 7)),
                        reads=[("xnT4", jj) for jj in range(4)] + [("w_in_b", kt)], writes=["pb2"])
                P.op("act", lambda E, ft=ft: E.copy(out=uT[:, ft, 0:ntok], in_=pb[2][:, 0:ntok]),
                     reads=["pb2"], writes=[("uT", ft)])

        m1 = P.sb([128, TB], F32, "m1")
        m2 = P.sb([128, TB], F32, "m2")
        gre = P.sb([128, TB], F32, "gre")
        gim = P.sb([128, TB], F32, "gim")
        hre = P.sb([128, TB], BF16, "hre")
        him = P.sb([128, TB], BF16, "him")
        car = P.sb([128, NRT, 2], F32, "car")
        ctmp = P.sb([128, 2], F32, "ctmp")
        ytmp = P.sb([128, TB], F32, "ytmp")
        ytm2 = P.sb([128, 2, 128], F32, "ytm2")
        Yown = P.sb([128, 4, 2048], BF16, "Yown")
        ssmo = P.sb([128, NRT, 2], F32, "ssmo")
        P.op("pool", lambda E: E.memset(car[:], 0.0), writes=["car"])

        def s5_block(n, last):
            for ft in range(4):
                for r4 in range(4):
                    rt_ = ft * 4 + r4
                    ch = rt_ // 4
                    P.op("pe", lambda E, rt_=rt_, ft=ft: E.matmul(pb[3][:], lhsT=BTr[:, rt_, :], rhs=uT[:, ft, :],
                                                                 start=True, stop=True),
                         reads=[("BTr", rt_), ("uT", ft)], writes=["pb3"])
                    P.op("pe", lambda E, rt_=rt_, ft=ft: E.matmul(pb[4][:], lhsT=BTi[:, rt_, :], rhs=uT[:, ft, :],
                                                                 start=True, stop=True),
                         reads=[("BTi", rt_), ("uT", ft)], writes=["pb4"])
                    cT = cosT[:, rt_, :]
                    sT = sinT[:, rt_, :]
                    ck_, sk2 = ("cosT", ch), ("sinT", ch)
                    tt("dve", m1[:], pb[3][:], cT, ALU.mult, ["pb3", ck_], ["m1"])
                    tt("dve", m2[:], pb[4][:], sT, ALU.mult, ["pb4", sk2], ["m2"])
                    tt("pool", gre[:], m1[:], m2[:], ALU.add, ["m1", "m2"], ["gre"])
                    tt("dve", m1[:], pb[4][:], cT, ALU.mult, ["pb4", ck_], ["m1"])
                    tt("dve", m2[:], pb[3][:], sT, ALU.mult, ["pb3", sk2], ["m2"])
                    tt("pool", gim[:], m1[:], m2[:], ALU.subtract, ["m1", "m2"], ["gim"])
                    P.op("dve", lambda E, rt_=rt_: E.tensor_tensor_scan(
                        out=Gre[:], data0=sc[:, RR, rt_:rt_ + 1].to_broadcast([128, TB]), data1=gre[:], initial=car[:, rt_, 0:1],
                        op0=ALU.mult, op1=ALU.add), reads=[SCK, "gre", "car"], writes=["Gre"])
                    P.op("dve", lambda E, rt_=rt_: E.tensor_tensor_scan(
                        out=Gim[:], data0=sc[:, RR, rt_:rt_ + 1].to_broadcast([128, TB]), data1=gim[:], initial=car[:, rt_, 1:2],
                        op0=ALU.mult, op1=ALU.add), reads=[SCK, "gim", "car"], writes=["Gim"])
                    c9 = rotc[:, 9, rt_:rt_ + 1]
                    s9 = rots[:, 9, rt_:rt_ + 1]
                    if not last:
                        tsc("dve", ctmp[:, 0:1], Gim[:, TB - 1:TB], s9, ALU.mult, ["Gim", RK], ["ctmp"])
                        tsc("dve", ctmp[:, 1:2], Gre[:, TB - 1:TB], s9, ALU.mult, ["Gre", RK], ["ctmp"])
                        stt(car[:, rt_, 0:1], Gre[:, TB - 1:TB], c9, ctmp[:, 0:1], ALU.mult, ALU.subtract,
                            ["Gre", RK, "ctmp"], ["car"])
                        stt(car[:, rt_, 1:2], Gim[:, TB - 1:TB], c9, ctmp[:, 1:2], ALU.mult, ALU.add,
                            ["Gim", RK, "ctmp"], ["car"])
                    else:
                        c5 = sc[:, C511, rt_:rt_ + 1]
                        s5 = sc[:, S511, rt_:rt_ + 1]
                        tsc("dve", ctmp[:, 0:1], Gim[:, TB - 1:TB], s5, ALU.mult, ["Gim", SCK], ["ctmp"])
                        tsc("dve", ctmp[:, 1:2], Gre[:, TB - 1:TB], s5, ALU.mult, ["Gre", SCK], ["ctmp"])
                        stt(ssmo[:, rt_, 0:1], Gre[:, TB - 1:TB], c5, ctmp[:, 0:1], ALU.mult, ALU.subtract,
                            ["Gre", SCK, "ctmp"], ["ssmo"])
                        stt(ssmo[:, rt_, 1:2], Gim[:, TB - 1:TB], c5, ctmp[:, 1:2], ALU.mult, ALU.add,
                            ["Gim", SCK, "ctmp"], ["ssmo"])
                    tt("pool", q1[:], Gre[:], cT, ALU.mult, ["Gre", ck_], ["q1"])
                    tt("pool", q2[:], Gim[:], sT, ALU.mult, ["Gim", sk2], ["q2"])
                    tt("pool", hre[:], q1[:], q2[:], ALU.subtract, ["q1", "q2"], ["hre"])
                    tt("pool", q1[:], Gre[:], sT, ALU.mult, ["Gre", sk2], ["q1"])
                    tt("pool", q2[:], Gim[:], cT, ALU.mult, ["Gim", ck_], ["q2"])
                    tt("pool", him[:], q1[:], q2[:], ALU.add, ["q1", "q2"], ["him"])
                    P.op("pe", lambda E, rt_=rt_, r4=r4: E.matmul(pb[5][:], lhsT=CZr[:, rt_, :], rhs=hre[:],
                                                                 start=(r4 == 0), stop=False),
                         reads=["CZr", "hre"], writes=["pb5"])
                    P.op("pe", lambda E, rt_=rt_, r4=r4: E.matmul(pb[5][:], lhsT=CZi[:, rt_, :], rhs=him[:],
                                                                 start=False, stop=(r4 == 3)),
                         reads=["CZi", "him"], writes=["pb5"])
                stt(ytmp[:], uT[:, ft, :], dsk[:, ft:ft + 1], pb[5][:], ALU.mult, ALU.add,
                    [("uT", ft), "dsk", "pb5"], ["ytmp"])
                yv = ytmp[:].rearrange("p (a b t) -> p a b t", a=2, b=2)
                tsc("dve", ytm2[:], yv[:, :, 0, :], par[:, 0:1], ALU.mult, ["ytmp", "par"], ["ytm2"])
                stt(Yown[:, ft, n * 256:(n + 1) * 256].rearrange("p (a t) -> p a t", a=2), yv[:, :, 1, :],
                    par[:, 1:2], ytm2[:], ALU.mult, ALU.add, ["ytmp", "par", "ytm2"], [("Yown", ft, n)])

        for n in range(8):
            for j in range(4):
                i = n * 4 + j
                kv_tile(i, j, xf[i * 128:(i + 1) * 128, :], ropeF[:, i, :],
                        o_lat_p[i * 128:(i + 1) * 128, :], o_kr_p[i * 128:(i + 1) * 128, :])
            u_proj(512)
            s5_block(n, n == 7)
        P.dma("sp", lambda E: E.dma_start(out=o_ssm_p.rearrange("t p c -> p t c"), in_=ssmo[:]),
              reads=["ssmo"], writes=["o_ssm_p"])
        out_keys.append("o_ssm_p")

        kv_tile(32, 0, xs[:, :], ropeS[:, :], o_lat_s[:, :], o_kr_s[:, :])
        u_proj(128)
        bur = P.sb([128, NRT, 64], F32, "bur")
        bui = P.sb([128, NRT, 64], F32, "bui")
        for rt_ in range(NRT):
            ft = rt_ // 4
            P.op("pe", lambda E, rt_=rt_, ft=ft: E.matmul(pb[3][:, 0:128], lhsT=BTr[:, rt_, :], rhs=uT[:, ft, 0:128],
                                                         start=True, stop=True),
                 reads=[("BTr", rt_), ("uT", ft)], writes=["pb3"])
            P.op("pe", lambda E, rt_=rt_, ft=ft: E.matmul(pb[4][:, 0:128], lhsT=BTi[:, rt_, :], rhs=uT[:, ft, 0:128],
                                                         start=True, stop=True),
                 reads=[("BTi", rt_), ("uT", ft)], writes=["pb4"])
            P.op("act", lambda E, rt_=rt_: E.copy(out=bur[:, rt_, :], in_=pb[3][:, 0:64]), reads=["pb3"], writes=["bur"])
            P.op("act", lambda E, rt_=rt_: E.copy(out=bui[:, rt_, :], in_=pb[4][:, 0:64]), reads=["pb4"], writes=["bui"])
        st0 = P.sb([128, NRT, 16, 2], F32, "st0")
        load(st0[:], d_st, "st0")
        Hs = P.sb([128, NRT, 16, 4, 2], F32, "Hs")
        e1 = P.sb([128, NRT, 16], F32, "e1")
        e2 = P.sb([128, NRT, 16], F32, "e2")
        abr_bc = S(ABR).unsqueeze(2).to_broadcast([128, NRT, 16])
        abi_bc = S(ABI).unsqueeze(2).to_broadcast([128, NRT, 16])
        burv = bur[:].rearrange("p r (b t) -> p r b t", t=4)
        buiv = bui[:].rearrange("p r (b t) -> p r b t", t=4)
        for t in range(4):
            if t == 0:
                pr, pi_ = st0[:, :, :, 0], st0[:, :, :, 1]
                pk = ["st0"]
            else:
                pr, pi_ = Hs[:, :, :, t - 1, 0], Hs[:, :, :, t - 1, 1]
                pk = ["Hs"]
            tt("dve", e1[:], pr, abr_bc, ALU.mult, pk + [SCK], ["e1"])
            tt("dve", e2[:], pi_, abi_bc, ALU.mult, pk + [SCK], ["e2"])
            tt("dve", e1[:], e1[:], e2[:], ALU.subtract, ["e2"], ["e1"])
            tt("dve", Hs[:, :, :, t, 0], e1[:], burv[:, :, :, t], ALU.add, ["e1", "bur"], ["Hs"])
            tt("dve", e1[:], pi_, abr_bc, ALU.mult, pk + [SCK], ["e1"])
            tt("dve", e2[:], pr, abi_bc, ALU.mult, pk + [SCK], ["e2"])
            tt("dve", e1[:], e1[:], e2[:], ALU.add, ["e2"], ["e1"])
            tt("dve", Hs[:, :, :, t, 1], e1[:], buiv[:, :, :, t], ALU.add, ["e1", "bui"], ["Hs"])
        for rt_ in range(NRT):
            P.dma("sp", lambda E, rt_=rt_: E.dma_start(out=o_ssm_s[:, rt_, :, :].rearrange("b p c -> p b c"),
                                                     in_=Hs[:, rt_, :, 3, :]),
                  reads=["Hs"], writes=[("o_ssm_s", rt_)])
            out_keys.append(("o_ssm_s", rt_))

        hsr = Gre[:].bitcast(BF16)
        hsi = Gim[:].bitcast(BF16)
        P.op("act", lambda E: E.copy(out=hsr.rearrange("p (r b t) -> p r b t", r=16, b=16), in_=Hs[:, :, :, :, 0]),
             reads=["Hs"], writes=["Gre"])
        P.op("act", lambda E: E.copy(out=hsi.rearrange("p (r b t) -> p r b t", r=16, b=16), in_=Hs[:, :, :, :, 1]),
             reads=["Hs"], writes=["Gim"])
        Ys = ytmp[:, 0:256].rearrange("p (f t) -> p f t", f=4)
        for ft in range(4):
            for r4 in range(4):
                rt_ = ft * 4 + r4
                P.op("pe", lambda E, rt_=rt_, r4=r4: E.matmul(pb[5][:, 0:64], lhsT=CZr[:, rt_, :], rhs=hsr[:, rt_ * 64:(rt_ + 1) * 64],
                                                             start=(r4 == 0), stop=False), reads=["CZr", "Gre"], writes=["pb5"])
                P.op("pe", lambda E, rt_=rt_, r4=r4: E.matmul(pb[5][:, 0:64], lhsT=CZi[:, rt_, :], rhs=hsi[:, rt_ * 64:(rt_ + 1) * 64],
                                                             start=False, stop=(r4 == 3)), reads=["CZi", "Gim"], writes=["pb5"])
            stt(Ys[:, ft, :], uT[:, ft, 0:64], dsk[:, ft:ft + 1], pb[5][:, 0:64], ALU.mult, ALU.add,
                [("uT", ft), "dsk", "pb5"], ["ytmp"])

        P.barrier()
        w_out_b = cosT[:].rearrange("p (k a) t -> p k (a t)", k=8)
        w_glu_b = sinT[:, 0:8, :].rearrange("p (k a) t -> p k (a t)", k=4)
        w_uq_b = sinT[:, 8:13, :].rearrange("p a t -> p (a t)")[:, 0:2304].rearrange("p (c n) -> p c n", c=3)
        for k in range(8):
            load(xt[k % 2][:], d_wout[k * 128:(k + 1) * 128, :], "xt%d" % (k % 2))
            P.op("act", lambda E, k=k: E.copy(out=w_out_b[:, k, :], in_=xt[k % 2][:]), reads=["xt%d" % (k % 2)], writes=["w_out_b"])
        for k in range(4):
            load(xt[k % 2][:], d_wglu[k * 128:(k + 1) * 128, :], "xt%d" % (k % 2))
            P.op("act", lambda E, k=k: E.copy(out=w_glu_b[:, k, :], in_=xt[k % 2][:]), reads=["xt%d" % (k % 2)], writes=["w_glu_b"])
        for k in range(3):
            load(xt[k % 2][:, 0:768], d_wuq[k * 128:(k + 1) * 128, :], "xt%d" % (k % 2))
            P.op("act", lambda E, k=k: E.copy(out=w_uq_b[:, k, :], in_=xt[k % 2][:, 0:768]), reads=["xt%d" % (k % 2)], writes=["w_uq_b"])
        gql = P.sb([128, 384], F32, "gql")
        load(gql[:], d_gql.unsqueeze(0).to_broadcast([128, 384]), "gql")
        gqn = P.sb([128, 64], F32, "gqn")
        load(gqn[:], d_gqn.unsqueeze(0).to_broadcast([128, 64]), "gqn")
        gqr = P.sb([128, 32], F32, "gqr")
        load(gqr[:], d_gqr.unsqueeze(0).to_broadcast([128, 32]), "gqr")
        ropeO = P.sb([128, 16, 32], F32, "ropeO")
        load(ropeO[:], rope_o, "ropeO")
        mskf = P.sb([128, 2, 128], F32, "mskf")
        msk = P.sb([128, 2, 128], BF16, "msk")
        load(mskf[:], d_masks, "mskf")
        P.op("dve", lambda E: E.tensor_copy(out=msk[:], in_=mskf[:]), reads=["mskf"], writes=["msk"])
        zer = P.sb([128, 512], BF16, "zer")
        P.op("pool", lambda E: E.memset(zer[:], 0.0), writes=["zer"])
        cqn = P.sb([128, 384], BF16, "cqn")
        cqT = P.sb([128, 3, 128], BF16, "cqT")
        Qcat = P.sb([128, 8, 96], BF16, "Qcat")
        qrn = P.sb([128, 8, 32], F32, "qrn")
        qra = P.sb([128, 8, 16], F32, "qra")
        qrb = P.sb([128, 8, 16], F32, "qrb")
        QT = P.sb([96, 8, 128], BF16, "QT")
        KTb = [P.sb([96, 8, 128], BF16, "KTb%d" % i_) for i_ in range(2)]
        Vb = [P.sb([128, 520], BF16, "Vb%d" % i_) for i_ in range(2)]
        PT = [P.sb([128, 512], BF16, "PT%d" % i_) for i_ in range(2)]
        rec = P.sb([128, 8], F32, "rec")
        oatt = P.sb([128, 8, 64], BF16, "oatt")
        burb = bur[:].rearrange("p a b -> p (a b)").bitcast(BF16)
        mixT = burb[:, 0:1024].rearrange("p (k t) -> p k t", k=8)
        gy = burb[:, 1024:1536].rearrange("p (k t) -> p k t", k=4)
        g1 = m1
        g2 = m2
        sig = gre
        x2 = Hs[:].rearrange("p a b c d -> p (a b c d)")[:, 0:1024]
        ATT_SCALE = 1.0 / math.sqrt(96.0)

        gffn = bui[:].rearrange("p a b -> p (a b)")
        load(gffn, d_gffn.unsqueeze(0).to_broadcast([128, D_MODEL]), "gffn")
        xn2 = uT[:].rearrange("p a b -> p (a b)").bitcast(F32)
        xn2T = xnT4[:, :, 128:256]
        q2c = xnT4[:, 0, 256:384]
        wqs = BTr[:].rearrange("p a b -> p (a b)").bitcast(F32).rearrange("p (k n) -> p k n", k=8)
        wqb = CZr[:].rearrange("p a b -> p (a b)")[:, 0:1024].rearrange("p (k n) -> p k n", k=8)
        czf = CZi[:].rearrange("p a b -> p (a b)").bitcast(F32)
        czb = CZi[:].rearrange("p a b -> p (a b)")
        btu = BTi[:].rearrange("p a b -> p (a b)").bitcast(U32)
        keysf = czf[:, 0:256].rearrange("p (c k) -> p c k", c=2)
        keysb = czb[:, 1024:1280].rearrange("p (c k) -> p c k", c=2)
        load(keysf, d_keysT, "keysf")
        P.op("dve", lambda E: E.tensor_copy(out=keysb, in_=keysf), reads=["keysf"], writes=["keysb"])
        iot = czf[:, 300:316]
        load(iot, d_iota, "iot")
        gsum = czf[:, 320:328]
        s_top = Gre[:, 0:256].rearrange("p (a b) -> p a b", a=16)
        i_topf = Gre[:, 256:512].rearrange("p (a b) -> p a b", a=16)
        wrk = Gim[:, 0:256]
        cand = Gim[:, 256:512].rearrange("p (a b) -> p a b", a=16)
        top16 = q1[:, 0:128].rearrange("p (a b) -> p a b", a=8)
        pa_f = q1[:, 128:256].rearrange("p (a b) -> p a b", a=8)
        pb_f = q1[:, 256:384].rearrange("p (a b) -> p a b", a=8)
        gsm = q1[:, 384:512].rearrange("p (a b) -> p a b", a=8)
        eqt = q2[:, 0:256].rearrange("p (a b) -> p a b", a=16)
        isel = q2[:, 256:512].rearrange("p (c h j) -> p c h j", c=2, h=8)
        idxf = gim[:, 0:128]
        pre = gim[:, 128:256]
        pg1 = gim[:, 256:384]
        coef = gim[:, 384:512]
        i_top = btu[:, 0:256].rearrange("p (a b) -> p a b", a=16)
        pos = btu[:, 256:384].rearrange("p (a b) -> p a b", a=8)
        pa_u = btu[:, 384:512].rearrange("p (a b) -> p a b", a=8)
        pb_u = btu[:, 512:640].rearrange("p (a b) -> p a b", a=8)
        idxu = btu[:, 640:768]
        NEG = -1.0e30

        wr_flat = w_rest[:].rearrange("p a b -> p (a b)").bitcast(F32)
        xgb = [(Hs[:].rearrange("p a b c d -> p (a b c d)")[:, 1024:2048], "xgb0"),
               (ropeF[:].rearrange("p a b -> p (a b)"), "xgb1"),
               (wr_flat[:, 0:1024], "xgb2"), (wr_flat[:, 1024:2048], "xgb3"), (wr_flat[:, 2048:3072], "xgb4")]

        def top16_of(vals_ap, n, out_vals, out_idx, rkeys, wkeys):
            P.op("dve", lambda E: E.max(out=out_vals[:, 0:8], in_=vals_ap), reads=rkeys, writes=wkeys)
            P.op("dve", lambda E: E.max_index(out=out_idx[:, 0:8], in_max=out_vals[:, 0:8], in_values=vals_ap),
                 reads=rkeys + wkeys, writes=wkeys)
            P.op("dve", lambda E: E.match_replace(out=wrk[:, 0:n], in_to_replace=out_vals[:, 0:8], in_values=vals_ap,
                                                  imm_value=NEG), reads=rkeys + wkeys, writes=["wrk"])
            P.op("dve", lambda E: E.max(out=out_vals[:, 8:16], in_=wrk[:, 0:n]), reads=["wrk"] + wkeys, writes=wkeys)
            P.op("dve", lambda E: E.max_index(out=out_idx[:, 8:16], in_max=out_vals[:, 8:16], in_values=wrk[:, 0:n]),
                 reads=["wrk"] + wkeys, writes=wkeys)

        def peer(x2_ap, x2k, gbufs):
            act(xnb[:], x2_ap, AF.Square, [x2k], ["xnb", "ss0"], accum_out=ss[:, 0:1])
            rstd_from_ss(ss[:, 0:1], "ss0", D_MODEL)
            stt(xn2, x2_ap, ss[:, 0:1], gffn, ALU.mult, ALU.mult, [x2k, "ss0", "gffn"], ["xn2"])
            P.op("act", lambda E: E.copy(out=xnb[:], in_=xn2), reads=["xn2"], writes=["xnb"])
            for kt in range(8):
                P.op("pe", lambda E, kt=kt: E.transpose(out=pT[:, kt * 128:(kt + 1) * 128],
                                                         in_=xnb[:, kt * 128:(kt + 1) * 128], identity=ident[:]),
                     reads=["xnb", "ident"], writes=[pTk])
            P.op("act", lambda E: E.copy(out=xn2T, in_=pT.rearrange("p (k t) -> p k t", k=8)), reads=[pTk], writes=["xn2T"])
            for hc in range(16):
                c_ = hc % 2
                load(wqs, d_wq.rearrange("(k p) n -> p k n", p=128)[:, :, hc * 128:(hc + 1) * 128], "wqs")
                P.op("act", lambda E: E.copy(out=wqb, in_=wqs), reads=["wqs"], writes=["wqb"])
                for kt in range(8):
                    P.op("pe", lambda E, kt=kt: E.matmul(pb[1][:, 0:128], lhsT=wqb[:, kt, :], rhs=xn2T[:, kt, :],
                                                          start=(kt == 0), stop=(kt == 7)),
                         reads=["wqb", "xn2T"], writes=["pb1"])
                P.op("act", lambda E: E.copy(out=q2c, in_=pb[1][:, 0:128]), reads=["pb1"], writes=["q2c"])
                P.op("pe", lambda E, c_=c_: E.matmul(pb[2][:, 0:128], lhsT=q2c, rhs=keysb[:, c_, :], start=True, stop=True),
                     reads=["q2c", "keysb"], writes=["pb2"])
                top16_of(pb[2][:, 0:128], 128, s_top[:, hc, :], i_top[:, hc, :], ["pb2"], ["s_top", "i_top"])
            P.op("dve", lambda E: E.tensor_copy(out=i_topf, in_=i_top), reads=["i_top"], writes=["i_topf"])
            for h in range(8):
                tt("dve", cand, s_top[:, 2 * h, :].unsqueeze(2).to_broadcast([128, 16, 16]),
                   s_top[:, 2 * h + 1, :].unsqueeze(1).to_broadcast([128, 16, 16]), ALU.add, ["s_top"], ["cand"])
                top16_of(cand.rearrange("p a b -> p (a b)"), 256, top16[:, h, :], pos[:, h, :], ["cand"], ["top16", "pos"])
            tt("dve", gsm, top16, top16[:, :, 0:1].to_broadcast([128, 8, 16]), ALU.subtract, ["top16"], ["gsm"])
            act(gsm, gsm, AF.Exp, ["gsm"], ["gsm"])
            P.op("dve", lambda E: E.tensor_reduce(out=gsum, in_=gsm, axis=AX.X, op=ALU.add), reads=["gsm"], writes=["gsum"])
            P.op("dve", lambda E: E.reciprocal(out=gsum, in_=gsum), reads=["gsum"], writes=["gsum"])
            tt("dve", gsm, gsm, gsum.unsqueeze(2).to_broadcast([128, 8, 16]), ALU.mult, ["gsum"], ["gsm"])
            tsc("dve", pa_u, pos, 4, ALU.logical_shift_right, ["pos"], ["pa_u"])
            tsc("dve", pb_u, pos, 15, ALU.bitwise_and, ["pos"], ["pb_u"])
            P.op("dve", lambda E: E.tensor_copy(out=pa_f, in_=pa_u), reads=["pa_u"], writes=["pa_f"])
            P.op("dve", lambda E: E.tensor_copy(out=pb_f, in_=pb_u), reads=["pb_u"], writes=["pb_f"])
            for h in range(8):
                for c_, pf in ((0, pa_f), (1, pb_f)):
                    tt("dve", eqt, pf[:, h, :].unsqueeze(2).to_broadcast([128, 16, 16]),
                       iot.unsqueeze(1).to_broadcast([128, 16, 16]), ALU.is_equal, ["pa_f", "pb_f", "iot"], ["eqt"])
                    tt("dve", eqt, eqt, i_topf[:, 2 * h + c_, :].unsqueeze(1).to_broadcast([128, 16, 16]), ALU.mult,
                       ["i_topf"], ["eqt"])
                    P.op("dve", lambda E, h=h, c_=c_: E.tensor_reduce(out=isel[:, c_, h, :], in_=eqt, axis=AX.X, op=ALU.add),
                         reads=["eqt"], writes=["isel"])
            stt(idxf, isel[:, 0, :, :].rearrange("p h j -> p (h j)"), 128.0, isel[:, 1, :, :].rearrange("p h j -> p (h j)"),
                ALU.mult, ALU.add, ["isel"], ["idxf"])
            P.op("dve", lambda E: E.tensor_copy(out=idxu, in_=idxf), reads=["idxf"], writes=["idxu"])
            for sl in range(128):
                gb, gk = gbufs[sl % len(gbufs)]
                P.dma("pool", lambda E, sl=sl, gb=gb: E.indirect_dma_start(
                    out=gb, out_offset=None, in_=d_pu, in_offset=bass.IndirectOffsetOnAxis(ap=idxu[:, sl:sl + 1], axis=0)),
                    reads=["idxu"], writes=[gk])
                P.op("dve", lambda E, sl=sl, gb=gb: E.scalar_tensor_tensor(
                    out=xnb[:], in0=gb, scalar=1.0, in1=xn2, op0=ALU.mult, op1=ALU.mult, accum_out=pre[:, sl:sl + 1]),
                    reads=[gk, "xn2"], writes=["xnb", "pre"])
            tt("dve", pg1, pre, pre, ALU.mult, ["pre"], ["pg1"])
            tsc("dve", pg1, pg1, 0.044715, ALU.mult, ["pg1"], ["pg1"], s2=1.0, op1=ALU.add)
            tt("dve", pg1, pg1, pre, ALU.mult, ["pre"], ["pg1"])
            act(pg1, pg1, AF.Tanh, ["pg1"], ["pg1"], scale=0.7978845608028654)
            tsc("dve", pg1, pg1, 1.0, ALU.add, ["pg1"], ["pg1"], s2=0.5, op1=ALU.mult)
            tt("dve", pg1, pg1, pre, ALU.mult, ["pre"], ["pg1"])
            tt("dve", coef, pg1, gsm.rearrange("p h j -> p (h j)"), ALU.mult, ["pg1", "gsm"], ["coef"])
            for sl in range(128):
                gb, gk = gbufs[sl % len(gbufs)]
                P.dma("pool", lambda E, sl=sl, gb=gb: E.indirect_dma_start(
                    out=gb, out_offset=None, in_=d_pv, in_offset=bass.IndirectOffsetOnAxis(ap=idxu[:, sl:sl + 1], axis=0)),
                    reads=["idxu"], writes=[gk])
                stt(x2_ap, gb, coef[:, sl:sl + 1], x2_ap, ALU.mult, ALU.add, [gk, "coef", x2k], [x2k])

        def q_part(x, xk, cs_ap):
            act(junk[:], x[:], AF.Square, [xk], ["junk", "ss0"], accum_out=ss[:, 0:1])
            rstd_from_ss(ss[:, 0:1], "ss0", D_MODEL)
            tsc("dve", xnb[:], x[:], ss[:, 0:1], ALU.mult, [xk, "ss0"], ["xnb"])
            for kt in range(8):
                P.op("pe", lambda E, kt=kt: E.transpose(out=pT[:, kt * 128:(kt + 1) * 128],
                                                         in_=xnb[:, kt * 128:(kt + 1) * 128], identity=ident[:]),
                     reads=["xnb", "ident"], writes=[pTk])
            P.op("act", lambda E: E.copy(out=xnT4[:, :, 0:128], in_=pT.rearrange("p (k t) -> p k t", k=8)),
                 reads=[pTk], writes=[("xnT4", 0)])
            for kt in range(8):
                P.op("pe", lambda E, kt=kt: E.matmul(pb[1][:, 0:384], lhsT=xnT4[:, kt, 0:128], rhs=w_cq[:, kt, :],
                                                      start=(kt == 0), stop=(kt == 7)),
                     reads=[("xnT4", 0), ("w_cq", kt)], writes=["pb1"])
            act(junk[:, 0:384], pb[1][:, 0:384], AF.Square, ["pb1"], ["junk", "ss1"], accum_out=ss[:, 1:2])
            rstd_from_ss(ss[:, 1:2], "ss1", 384)
            stt(cqn[:], pb[1][:, 0:384], ss[:, 1:2], gql[:], ALU.mult, ALU.mult, ["pb1", "ss1", "gql"], ["cqn"])
            for c_ in range(3):
                P.op("pe", lambda E, c_=c_: E.transpose(out=pT[:, c_ * 128:(c_ + 1) * 128], in_=cqn[:, c_ * 128:(c_ + 1) * 128],
                                                         identity=ident[:]), reads=["cqn", "ident"], writes=[pTk])
            P.op("act", lambda E: E.copy(out=cqT[:], in_=pT[:, 0:384].rearrange("p (c t) -> p c t", c=3)),
                 reads=[pTk], writes=["cqT"])
            for c_ in range(3):
                P.op("pe", lambda E, c_=c_: E.matmul(pb[6][:], lhsT=cqT[:, c_, :], rhs=w_uq_b[:, c_, 0:512],
                                                      start=(c_ == 0), stop=(c_ == 2)),
                     reads=["cqT", "w_uq_b"], writes=["pb6"])
            for c_ in range(3):
                P.op("pe", lambda E, c_=c_: E.matmul(pb[7][:, 0:256], lhsT=cqT[:, c_, :], rhs=w_uq_b[:, c_, 512:768],
                                                      start=(c_ == 0), stop=(c_ == 2)),
                     reads=["cqT", "w_uq_b"], writes=["pb7"])
            act(junk[:, 0:512], pb[6][:], AF.Square, ["pb6"], ["junk"])
            P.op("dve", lambda E: E.tensor_reduce(out=ssk[:, 0:8], in_=junk[:, 0:512].rearrange("p (h d) -> p h d", h=8),
                                                  axis=AX.X, op=ALU.add), reads=["junk"], writes=["ssk"])
            rstd_from_ss(ssk[:, 0:8], "ssk", 64)
            tt("dve", tmpk[:], pb[6][:].rearrange("p (h d) -> p h d", h=8),
               ssk[:, 0:8].unsqueeze(2).to_broadcast([128, 8, 64]), ALU.mult, ["pb6", "ssk"], ["tmpk"])
            tt("dve", Qcat[:, :, 0:64], tmpk[:], gqn[:].unsqueeze(1).to_broadcast([128, 8, 64]), ALU.mult,
               ["tmpk", "gqn"], ["Qcat"])
            act(junk[:, 512:768], pb[7][:, 0:256], AF.Square, ["pb7"], ["junk2"])
            P.op("dve", lambda E: E.tensor_reduce(out=ssk[:, 8:16], in_=junk[:, 512:768].rearrange("p (h d) -> p h d", h=8),
                                                  axis=AX.X, op=ALU.add), reads=["junk2"], writes=["ssk2"])
            rstd_from_ss(ssk[:, 8:16], "ssk2", 32)
            tt("dve", qrn[:], pb[7][:, 0:256].rearrange("p (h d) -> p h d", h=8),
               ssk[:, 8:16].unsqueeze(2).to_broadcast([128, 8, 32]), ALU.mult, ["pb7", "ssk2"], ["qrn"])
            tt("dve", qrn[:], qrn[:], gqr[:].unsqueeze(1).to_broadcast([128, 8, 32]), ALU.mult, ["gqr"], ["qrn"])
            cosb = cs_ap[:, 0:16].unsqueeze(1).to_broadcast([128, 8, 16])
            sinb = cs_ap[:, 16:32].unsqueeze(1).to_broadcast([128, 8, 16])
            tt("dve", qra[:], qrn[:, :, 0:16], cosb, ALU.mult, ["qrn", "ropeO", "ropeS"], ["qra"])
            tt("dve", qrb[:], qrn[:, :, 16:32], sinb, ALU.mult, ["qrn", "ropeO", "ropeS"], ["qrb"])
            tt("dve", Qcat[:, :, 64:80], qra[:], qrb[:], ALU.subtract, ["qra", "qrb", "Qcat"], ["Qcat"])
            tt("dve", qra[:], qrn[:, :, 16:32], cosb, ALU.mult, ["qrn", "ropeO", "ropeS"], ["qra"])
            tt("dve", qrb[:], qrn[:, :, 0:16], sinb, ALU.mult, ["qrn", "ropeO", "ropeS"], ["qrb"])
            tt("dve", Qcat[:, :, 80:96], qra[:], qrb[:], ALU.add, ["qra", "qrb", "Qcat"], ["Qcat"])

        def own_tile(i):
            x = xt[i % 2]
            xk = "xt%d" % (i % 2)
            load(x[:], xo[i * 128:(i + 1) * 128, :], xk)
            q_part(x, xk, ropeO[:, i, :])
            for h in range(8):
                P.op("pe", lambda E, h=h: E.transpose(out=pT[0:96, h * 128:(h + 1) * 128], in_=Qcat[:, h, :], identity=ident[:]),
                     reads=["Qcat", "ident"], writes=[pTk])
            P.op("act", lambda E: E.copy(out=QT[:], in_=pT[0:96, :].rearrange("p (h t) -> p h t", h=8)),
                 reads=[pTk], writes=["QT"])
            for hg in range(2):
                P.op("pe", lambda E, hg=hg: E.matmul(pb[4 + hg][:], lhsT=zer[:, 0:128], rhs=zer[:], start=True, stop=False),
                     reads=["zer"], writes=["pb%d" % (4 + hg)])
            nkb = 2 * i + 2
            for kb in range(nkb):
                kbuf = KTb[kb % 2]
                kkey = "KTb%d" % (kb % 2)
                vbuf = Vb[kb % 2]
                vkey = "Vb%d" % (kb % 2)
                P.dma("sp", lambda E, kb=kb, kbuf=kbuf: E.dma_start(out=kbuf[:], in_=KT_d[:, :, kb * 128:(kb + 1) * 128]),
                      reads=[("KTd", kb)], writes=[kkey])
                P.dma("sp", lambda E, kb=kb, vbuf=vbuf: E.dma_start(out=vbuf[:], in_=V_d[kb * 128:(kb + 1) * 128, :]),
                      reads=[("Vd", kb)], writes=[vkey])
                for hg in range(2):
                    sp_ = pb[2 + hg]
                    spk = "pb%d" % (2 + hg)
                    for j in range(4):
                        h = hg * 4 + j
                        msk_i = kb - 2 * i
                        P.op("pe", lambda E, h=h, j=j, sp_=sp_, kbuf=kbuf, msk_i=msk_i: E.matmul(
                            sp_[:, j * 128:(j + 1) * 128], lhsT=kbuf[:, h, :], rhs=QT[:, h, :], start=True, stop=(msk_i < 0)),
                            reads=[kkey, "QT"], writes=[spk])
                        if msk_i >= 0:
                            P.op("pe", lambda E, j=j, sp_=sp_, msk_i=msk_i: E.matmul(
                                sp_[:, j * 128:(j + 1) * 128], lhsT=ident[:], rhs=msk[:, msk_i, :], start=False, stop=True),
                                reads=["ident", "msk"], writes=[spk])
                    pt_ = PT[hg]
                    ptk = "PT%d" % hg
                    act(pt_[:], sp_[:], AF.Exp, [spk], [ptk], scale=ATT_SCALE)
                    for j in range(4):
                        h = hg * 4 + j
                        P.op("pe", lambda E, h=h, j=j, hg=hg, pt_=pt_, vbuf=vbuf: E.matmul(
                            pb[4 + hg][:, j * 65:(j + 1) * 65], lhsT=pt_[:, j * 128:(j + 1) * 128],
                            rhs=vbuf[:, h * 65:(h + 1) * 65], start=False, stop=(kb == nkb - 1), skip_group_check=True),
                            reads=[ptk, vkey], writes=["pb%d" % (4 + hg)])
            for hg in range(2):
                ov = pb[4 + hg][:, 0:260].rearrange("p (h d) -> p h d", h=4)
                P.op("dve", lambda E, hg=hg, ov=ov: E.reciprocal(out=rec[:, hg * 4:(hg + 1) * 4].unsqueeze(2), in_=ov[:, :, 64:65]),
                     reads=["pb%d" % (4 + hg)], writes=["rec"])
                tt("dve", oatt[:, hg * 4:(hg + 1) * 4, :], ov[:, :, 0:64],
                   rec[:, hg * 4:(hg + 1) * 4].unsqueeze(2).to_broadcast([128, 4, 64]), ALU.mult,
                   ["pb%d" % (4 + hg), "rec"], ["oatt"])
            oflat = oatt[:].rearrange("p h d -> p (h d)")
            for c_ in range(4):
                P.op("pe", lambda E, c_=c_: E.transpose(out=pT[:, c_ * 128:(c_ + 1) * 128], in_=oflat[:, c_ * 128:(c_ + 1) * 128],
                                                         identity=ident[:]), reads=["oatt", "ident"], writes=[pTk])
            P.op("act", lambda E: E.copy(out=mixT[:, 4:8, :], in_=pT[:, 0:512].rearrange("p (c t) -> p c t", c=4)),
                 reads=[pTk], writes=["mixT_a"])
            glu_part(Yown[:, :, i * 128:(i + 1) * 128], [("Yown", ft, i // 2) for ft in range(4)], 128)
            out_part(x, xk)
            peer(x2, "x2", [(junk[:], "junk"), (xt[(i + 1) % 2][:], "xt%d" % ((i + 1) % 2))] + xgb)
            P.dma("sp", lambda E: E.dma_start(out=o_y_p[i * 128:(i + 1) * 128, :], in_=x2[:]), reads=["x2"],
                  writes=[("o_y_p", i)])
            out_keys.append(("o_y_p", i))

        def glu_part(yv_, ykeys, nt):
            g1v = g1[:].rearrange("p (f t) -> p f t", f=4)[:, :, 0:nt]
            g2v = g2[:].rearrange("p (f t) -> p f t", f=4)[:, :, 0:nt]
            tt("dve", g1v, yv_, yv_, ALU.mult, ykeys, ["g1"])
            tsc("dve", g1[:], g1[:], 0.044715, ALU.mult, ["g1"], ["g1"], s2=1.0, op1=ALU.add)
            tt("dve", g1v, g1v, yv_, ALU.mult, ykeys, ["g1"])
            act(g2[:], g1[:], AF.Tanh, ["g1"], ["g2"], scale=0.7978845608028654)
            tsc("dve", g2[:], g2[:], 1.0, ALU.add, ["g2"], ["g2"], s2=0.5, op1=ALU.mult)
            tt("dve", gy[:, :, 0:nt], g2v, yv_, ALU.mult, ykeys + ["g2"], ["gy"])
            for half in range(2):
                for ot in range(4):
                    o8 = half * 4 + ot
                    for k in range(4):
                        P.op("pe", lambda E, half=half, ot=ot, o8=o8, k=k: E.matmul(
                            pb[2 + half][:, ot * 128:(ot + 1) * 128], lhsT=w_glu_b[:, k, o8 * 128:(o8 + 1) * 128],
                            rhs=gy[:, k, :], start=(k == 0), stop=(k == 3)),
                            reads=["w_glu_b", "gy"], writes=["pb%d" % (2 + half)])
            act(sig[:], pb[3][:], AF.Sigmoid, ["pb3"], ["sig"])
            tt("dve", mixT[:, 0:4, :], pb[2][:].rearrange("p (c t) -> p c t", c=4), sig[:].rearrange("p (c t) -> p c t", c=4),
               ALU.mult, ["pb2", "sig"], ["mixT_g"])

        def out_part(x, xk):
            for half in range(2):
                for k in range(8):
                    P.op("pe", lambda E, half=half, k=k: E.matmul(
                        pb[6 + half][:], lhsT=mixT[:, k, :], rhs=w_out_b[:, k, half * 512:(half + 1) * 512],
                        start=(k == 0), stop=(k == 7)),
                        reads=["mixT_a", "mixT_g", "w_out_b"], writes=["pb%d" % (6 + half)])
                tt("dve", x2[:, half * 512:(half + 1) * 512], pb[6 + half][:], x[:, half * 512:(half + 1) * 512], ALU.add,
                   ["pb%d" % (6 + half), xk], ["x2"])

        for i in range(16):
            own_tile(i)

        P.barrier()
        Y16 = Yown[:].rearrange("p a b -> p (a b)")
        Y32 = Y16.bitcast(F32)
        YU = Y16.bitcast(U32)
        pidx = YU[:, 0:1024]
        latf = [Y32[:, 2048:2304], Y32[:, 2304:2560]]
        krf = [Y32[:, 2560:2592], Y32[:, 2592:2624]]
        lb = [Y16[:, 5248:5505], Y16[:, 5512:5769]]
        krb = [Y16[:, 5776:5808], Y16[:, 5808:5840]]
        lT = [Y16[:, 5840:6096].rearrange("p (c k) -> p c k", c=2), Y16[:, 6096:6352].rearrange("p (c k) -> p c k", c=2)]
        krT = [Y16[:, 6352:6480], Y16[:, 6480:6608]]
        qtT = Y16[:, 6608:7632].rearrange("p (c h t) -> p c h t", c=2, h=8)
        pts = [Y16[:, 7632:7664], Y16[:, 7664:7696]]
        sc1 = Y32[:, 3848:3880]
        olat = Y16[:, 7760:8016]
        olT = Y16[:, 8016:8080].rearrange("p (c n) -> p c n", c=2)
        maskn = czf[:, 330:362]
        load(maskn[0:4, :], d_maskn, "maskn")
        for bf in range(2):
            P.op("pool", lambda E, bf=bf: E.memset(lb[bf][:, 256:257], 1.0), writes=["lb%d" % bf])
        pti = xt[0][:].bitcast(I32)
        load(pti, d_ptab.to_broadcast([128, 1024]), "xt0")
        piota = czf[:, 364:365]
        load(piota, d_piota, "piota")
        tsc("dve", xt[1][:], pti, 128.0, ALU.mult, ["xt0"], ["xt1"])
        tsc("dve", pidx, xt[1][:], piota, ALU.add, ["xt1", "piota"], ["pidx"])
        wukT = [KTb[0][0:64, :, :].rearrange("p a b -> p (a b)"), KTb[1][0:64, :, :].rearrange("p a b -> p (a b)")]
        for hh in range(2):
            load(xt[hh][0:64, :], d_wukT[:, hh * 4:(hh + 1) * 4, :].rearrange("p a b -> p (a b)"), "xt%d" % hh)
            P.op("act", lambda E, hh=hh: E.copy(out=wukT[hh], in_=xt[hh][0:64, :]), reads=["xt%d" % hh], writes=["KTb%d" % hh])
        xsx = xt[0]
        load(xsx[:], xs[:, :], "xt0")
        q_part(xsx, "xt0", ropeS[:, :])
        Qg = Vb[0][:, 0:512].rearrange("p (h d) -> p h d", h=8)
        tt("dve", Qg, Qcat[:, :, 0:64], gkn[:].unsqueeze(1).to_broadcast([128, 8, 64]), ALU.mult, ["Qcat", "gkn"], ["Vb0"])
        for h in range(8):
            P.op("pe", lambda E, h=h: E.transpose(out=pT[0:64, h * 128:(h + 1) * 128], in_=Qg[:, h, :], identity=ident[:]),
                 reads=["Vb0", "ident"], writes=[pTk])
        P.op("act", lambda E: E.copy(out=QT[0:64, :, :], in_=pT[0:64, :].rearrange("p (h t) -> p h t", h=8)),
             reads=[pTk], writes=["QT"])
        for cc in range(2):
            for h in range(8):
                P.op("pe", lambda E, cc=cc, h=h: E.matmul(
                    pb[1 + cc][:, h * 64:(h + 1) * 64], lhsT=wukT[h // 4][:, (h % 4) * 256 + cc * 128:(h % 4) * 256 + (cc + 1) * 128],
                    rhs=QT[0:64, h, 0:64], start=True, stop=True),
                    reads=["KTb0", "KTb1", "QT"], writes=["pb%d" % (1 + cc)])
            P.op("act", lambda E, cc=cc: E.copy(out=qtT[:, cc, :, :], in_=pb[1 + cc][:].rearrange("p (h t) -> p h t", h=8)),
                 reads=["pb%d" % (1 + cc)], writes=["qtT"])
        for h in range(8):
            P.op("pe", lambda E, h=h: E.transpose(out=pT[0:32, h * 128:(h + 1) * 128], in_=Qcat[:, h, 64:96], identity=ident[:]),
                 reads=["Qcat", "ident"], writes=[pTk])
        qrT = [PT[0][0:32, :].rearrange("p (h t) -> p h t", h=4), PT[1][0:32, :].rearrange("p (h t) -> p h t", h=4)]
        for hh in range(2):
            P.op("act", lambda E, hh=hh: E.copy(out=PT[hh][0:32, :], in_=pT[0:32, hh * 512:(hh + 1) * 512]),
                 reads=[pTk], writes=["PT%d" % hh])

        cnt = [0]

        pTs = [pb[0][:].bitcast(BF16), pb[3][:].bitcast(BF16)]
        pTks = ["pb0", "pb3"]
        p6s = [pb[6], pb[1]]
        p6k = ["pb6", "pb1"]
        p7s = [pb[7], pb[2]]
        p7k = ["pb7", "pb2"]
        sc1s = [Y32[:, 3848:3880], Y32[:, 4040:4072]]

        def page(b, j):
            n = 128 if j < NPAGE else 4
            bf = cnt[0] % 2
            cnt[0] += 1
            lk_, kk_, lbk, kbk, ltk, ktk, ptk = ("latf%d" % bf, "krf%d" % bf, "lb%d" % bf, "krb%d" % bf, "lT%d" % bf,
                                                 "krT%d" % bf, "pts%d" % bf)
            pTx, pTxk, p6, k6, p7, k7 = pTs[bf], pTks[bf], p6s[bf], p6k[bf], p7s[bf], p7k[bf]
            jk, skk, s1k = "junkP%d" % bf, "sskP%d" % bf, "sc1P%d" % bf
            jv = junk[0:n, bf * 512:(bf + 1) * 512]
            sv = ssk[0:n, bf * 8:(bf + 1) * 8]
            sc1 = sc1s[bf]
            if j < NPAGE:
                col = b * NPAGE + j
                P.dma("pool", lambda E: E.indirect_dma_start(
                    out=latf[bf], out_offset=None, in_=d_clat,
                    in_offset=bass.IndirectOffsetOnAxis(ap=pidx[:, col:col + 1], axis=0)), reads=["pidx"], writes=[lk_])
                P.dma("pool", lambda E: E.indirect_dma_start(
                    out=krf[bf], out_offset=None, in_=d_ckr,
                    in_offset=bass.IndirectOffsetOnAxis(ap=pidx[:, col:col + 1], axis=0)), reads=["pidx"], writes=[kk_])
            else:
                P.dma("sp", lambda E: E.dma_start(out=latf[bf][0:4, :], in_=lat_s_scr[4 * b:4 * b + 4, :]),
                      reads=["lat_s_scr"], writes=[lk_])
                P.dma("sp", lambda E: E.dma_start(out=krf[bf][0:4, :], in_=kr_s_scr[4 * b:4 * b + 4, :]),
                      reads=["kr_s_scr"], writes=[kk_])
            P.op("act", lambda E: E.copy(out=lb[bf][0:n, 0:256], in_=latf[bf][0:n, :]), reads=[lk_], writes=[lbk])
            P.op("dve", lambda E: E.tensor_copy(out=krb[bf][0:n, :], in_=krf[bf][0:n, :]), reads=[kk_], writes=[kbk])
            for cc in range(2):
                P.op("pe", lambda E, cc=cc: E.transpose(out=pTx[:, cc * 128:cc * 128 + n], in_=lb[bf][0:n, cc * 128:(cc + 1) * 128],
                                                         identity=ident[0:n, 0:n]), reads=[lbk, "ident"], writes=[pTxk])
            P.op("pe", lambda E: E.transpose(out=pTx[0:32, 256:256 + n], in_=krb[bf][0:n, :], identity=ident[0:n, 0:n]),
                 reads=[kbk, "ident"], writes=[pTxk])
            P.op("act", lambda E: E.copy(out=lT[bf][:, :, 0:n], in_=pTx[:, 0:256].rearrange("p (c k) -> p c k", c=2)[:, :, 0:n]),
                 reads=[pTxk], writes=[ltk])
            P.op("act", lambda E: E.copy(out=krT[bf][0:32, 0:n], in_=pTx[0:32, 256:256 + n]), reads=[pTxk], writes=[ktk])
            for cc in range(2):
                P.op("pe", lambda E, cc=cc: E.matmul(p6[0:n, :], lhsT=lT[bf][:, cc, 0:n], rhs=w_uk_b[:, cc, :],
                                                      start=(cc == 0), stop=(cc == 1)), reads=[ltk, "w_uk_b"], writes=[k6])
            for cc in range(2):
                P.op("pe", lambda E, cc=cc: E.matmul(p7[0:n, 0:32], lhsT=lT[bf][:, cc, 0:n], rhs=qtT[:, cc, :, 4 * b:4 * b + 4],
                                                      start=(cc == 0), stop=(cc == 1)), reads=[ltk, "qtT"], writes=[k7])
            for hh in range(2):
                P.op("pe", lambda E, hh=hh: E.matmul(p7[0:n, 64 + 16 * hh:80 + 16 * hh], lhsT=krT[bf][0:32, 0:n],
                                                      rhs=qrT[hh][:, :, 4 * b:4 * b + 4], start=True, stop=True),
                     reads=[ktk, "PT%d" % hh], writes=[k7])
            act(jv, p6[0:n, :], AF.Square, [k6], [jk])
            P.op("dve", lambda E: E.tensor_reduce(out=sv, in_=jv.rearrange("p (h d) -> p h d", h=8),
                                                  axis=AX.X, op=ALU.add), reads=[jk], writes=[skk])
            rstd_from_ss(sv, skk, 64)
            s3 = sc1[0:n, :].rearrange("p (h q) -> p h q", h=8)
            tt("dve", s3, p7[0:n, 0:32].rearrange("p (h q) -> p h q", h=8),
               sv.unsqueeze(2).to_broadcast([n, 8, 4]), ALU.mult, [k7, skk], [s1k])
            tt("dve", sc1[0:n, :], sc1[0:n, :], p7[0:n, 64:96], ALU.add, [k7], [s1k])
            act(pts[bf][0:n, :], sc1[0:n, :], AF.Exp, [s1k], [ptk], scale=ATT_SCALE)
            if j == NPAGE:
                tt("dve", pts[bf][0:n, :], pts[bf][0:n, :], maskn[0:n, :], ALU.mult, ["maskn"], [ptk])
            P.op("pe", lambda E: E.matmul(pb[4][0:32, 0:257], lhsT=pts[bf][0:n, :], rhs=lb[bf][0:n, 0:257],
                                           start=(j == PAGE_LIST[0]), stop=(j == PAGE_LIST[-1])), reads=[ptk, lbk], writes=["pb4"])

        for b in range(SPC_RUN):
            for j in PAGE_LIST:
                page(b, j)
            P.op("dve", lambda E: E.reciprocal(out=rec[0:32, 0:1], in_=pb[4][0:32, 256:257]), reads=["pb4"], writes=["rec"])
            tsc("dve", olat[0:32, :], pb[4][0:32, 0:256], rec[0:32, 0:1], ALU.mult, ["pb4", "rec"], ["olat"])
            for cc in range(2):
                P.op("pe", lambda E, cc=cc: E.transpose(out=pT[:, cc * 32:(cc + 1) * 32], in_=olat[0:32, cc * 128:(cc + 1) * 128],
                                                         identity=ident[0:32, 0:32]), reads=["olat", "ident"], writes=[pTk])
            P.op("act", lambda E: E.copy(out=olT, in_=pT[:, 0:64].rearrange("p (c n) -> p c n", c=2)), reads=[pTk], writes=["olT"])
            for h in range(8):
                r0 = (h % 2) * 64
                c0 = (h // 2) * 64 + 4 * b
                for cc in range(2):
                    P.op("pe", lambda E, h=h, cc=cc, r0=r0, c0=c0: E.matmul(
                        pb[5][r0:r0 + 64, c0:c0 + 4], lhsT=w_uv_b[:, cc, h * 64:(h + 1) * 64], rhs=olT[:, cc, h * 4:(h + 1) * 4],
                        start=(cc == 0), stop=(cc == 1), skip_group_check=True), reads=["w_uv_b", "olT"], writes=["pb5"])
        P.op("act", lambda E: E.copy(out=mixT[:, 4:8, 0:64], in_=pb[5][:, 0:256].rearrange("p (c t) -> p c t", c=4)),
             reads=["pb5"], writes=["mixT_a"])
        glu_part(Ys, ["ytmp"], 64)
        out_part(xsx, "xt0")
        peer(x2, "x2", [(junk[:], "junk"), (xt[1][:], "xt1")] + xgb)
        P.dma("sp", lambda E: E.dma_start(out=o_y_s, in_=x2[:]), reads=["x2"], writes=["o_y_s"])
        out_keys.append("o_y_s")

        P.finish(out_keys)
        P.run()
    return nc


def _rope_tables(pos):
    inv = (10000.0 ** (-np.arange(0, 32, 2, dtype=np.float32) / np.float32(32))).astype(np.float32)
    ang = pos.astype(np.float32)[:, None] * inv[None, :]
    return np.concatenate([np.cos(ang), np.sin(ang)], axis=1).astype(np.float32)


def _c(a):
    return np.ascontiguousarray(a, dtype=np.float32)


_DEBUG_SMALL = 0


def kernel(**inputs):
    f32 = np.float32
    x_prompt = np.asarray(inputs["x_prompt"], f32)
    x_sample = np.asarray(inputs["x_sample"], f32)

    dbg_small = bool(_DEBUG_SMALL)
    nc = build_program(1024 if dbg_small else 10240)

    rope_full = _rope_tables(np.arange(SEQ))
    rope_f = _c(rope_full.reshape(32, 128, 32).transpose(1, 0, 2))
    rs = _rope_tables(PAST + (np.arange(128) % 4))
    ident = np.eye(128, dtype=f32)

    def rows16(a):
        return _c(np.asarray(a, f32).reshape(16, 128).T)

    a_re = rows16(inputs["ssm_a_re"][0])
    a_im = rows16(inputs["ssm_a_im"][0])
    ldt = rows16(np.repeat(np.asarray(inputs["ssm_log_dt"][0], f32)[:, None], 64, axis=1))
    bb = _c(np.asarray(inputs["ssm_b"][0], f32).reshape(16, 128, 32).transpose(1, 0, 2))
    cc = _c(np.asarray(inputs["ssm_c"][0], f32).transpose(0, 2, 1, 3).reshape(16, 128, 32).transpose(1, 0, 2))
    dd = _c(np.asarray(inputs["ssm_d"][0], f32).reshape(4, 128).T)
    state = np.asarray(inputs["state_ssm"][0], f32)

    wuq = np.asarray(inputs["w_uq"][0], f32).reshape(384, 8, 96)
    wuq_p = _c(np.concatenate([wuq[:, :, :64].reshape(384, 512), wuq[:, :, 64:].reshape(384, 256)], axis=1))
    kk = np.arange(128)[:, None]
    qq = np.arange(128)[None, :]
    diag = np.where(kk <= qq, 0.0, -30000.0).astype(f32)
    shared = {
        "cache_lat": np.asarray(inputs["cache_kv_latent"][0], f32).reshape(-1, 256),
        "cache_kr": np.asarray(inputs["cache_k_rope"][0], f32).reshape(-1, 32),
        "piota": _c(np.arange(128, dtype=f32)[:, None]),
        "w_ukT": _c(np.asarray(inputs["w_uk"][0], f32).transpose(2, 1, 0)),
        "maskn": _c(np.tile((np.arange(4)[:, None] <= np.arange(4)[None, :]).astype(f32)[:, None, :], (1, 8, 1)).reshape(4, 32)),
        "norm_ffn": _c(inputs["norm_ffn"][0]),
        "peer_wq": _c(inputs["peer_wq"][0]),
        "peer_keysT": _c(np.asarray(inputs["peer_keys"][0], f32).transpose(2, 0, 1)),
        "peer_u": _c(inputs["peer_u"][0]),
        "peer_v": _c(inputs["peer_v"][0]),
        "iota16": _c(np.tile(np.arange(16, dtype=f32)[None, :], (128, 1))),
        "w_uk": _c(np.asarray(inputs["w_uk"][0], f32).reshape(256, 512)),
        "w_uv": _c(np.asarray(inputs["w_uv"][0], f32).reshape(256, 512)),
        "w_uq": wuq_p,
        "w_glu": _c(inputs["w_glu"][0]),
        "w_out": _c(inputs["w_out"][0]),
        "norm_q_lora": _c(inputs["norm_q_lora"][0]),
        "qk_gain_q_nope": _c(inputs["qk_gain_q_nope"][0]),
        "qk_gain_q_rope": _c(inputs["qk_gain_q_rope"][0]),
        "qk_gain_k_nope": _c(inputs["qk_gain_k_nope"][0]),
        "ident": ident,
        "rope_f": rope_f,
        "rope_s": rs,
        "norm_mix": _c(np.asarray(inputs["norm_mix"][0], f32).reshape(8, 128).T),
        "w_in": _c(inputs["w_in"][0]),
        "norm_kv_lora": _c(inputs["norm_kv_lora"][0]),
        "qk_gain_k_rope": _c(inputs["qk_gain_k_rope"][0]),
        "ssm_a_re": a_re, "ssm_a_im": a_im, "ssm_log_dt": ldt, "ssm_b": bb, "ssm_c": cc, "ssm_d": dd,
    }
    in_maps = []
    for c in range(NCORES):
        b = c // 2
        p = c % 2
        xs = np.zeros((128, D_MODEL), f32)
        xs[:STOK] = x_sample[c * SPC:(c + 1) * SPC].reshape(STOK, D_MODEL)
        st = state[c * SPC:(c + 1) * SPC].reshape(SPC, 16, 128, 2).transpose(2, 1, 0, 3)
        par = np.zeros((128, 2), f32)
        par[:, 0] = 1.0 - p
        par[:, 1] = p
        xb = x_prompt[b].reshape(16, 2, 128, D_MODEL)
        masks = np.zeros((128, 2, 128), f32)
        if p == 0:
            masks[:, 0, :] = diag
            masks[:, 1, :] = -30000.0
        else:
            masks[:, 1, :] = diag
        m = dict(shared)
        if dbg_small:
            ptc = np.asarray(inputs["page_table"], np.int32)[c * SPC:(c + 1) * SPC].reshape(-1)
            m["cache_lat"] = _c(np.asarray(inputs["cache_kv_latent"][0], f32)[ptc].reshape(-1, 256))
            m["cache_kr"] = _c(np.asarray(inputs["cache_k_rope"][0], f32)[ptc].reshape(-1, 32))
        m.update({
            "xo": _c(xb[:, p].reshape(2048, D_MODEL)),
            "rope_o": _c(rope_full.reshape(16, 2, 128, 32)[:, p].transpose(1, 0, 2)),
            "masks": masks,
            "xf": _c(x_prompt[b]),
            "xs": xs,
            "page_tab": (np.arange(1024, dtype=np.int32).reshape(1, 1024) if dbg_small else
                         np.ascontiguousarray(np.asarray(inputs["page_table"], np.int32)[c * SPC:(c + 1) * SPC].reshape(1, 1024))),
            "state_ssm": _c(st),
            "parity": par,
        })
        in_maps.append(m)

    res = run_bass_kernel_spmd(nc, in_maps, core_ids=list(range(NCORES)))
    R = res.results

    y_prompt = np.zeros((4, 16, 2, 128, D_MODEL), f32)
    for c in range(NCORES):
        y_prompt[c // 2, :, c % 2] = R[c]["o_y_p"].reshape(16, 128, D_MODEL)
    y_prompt = y_prompt.reshape(4, SEQ, D_MODEL)
    y_sample = np.concatenate([R[c]["o_y_s"][:STOK].reshape(SPC, 4, D_MODEL) for c in range(NCORES)]).astype(f32)
    lat_p = np.stack([R[2 * b]["o_lat_p"] for b in range(4)])[None]
    kr_p = np.stack([R[2 * b]["o_kr_p"] for b in range(4)])[None]
    ssm_p = np.stack([R[2 * b]["o_ssm_p"].reshape(32, 64, 2) for b in range(4)])[None]
    lat_s = np.concatenate([R[c]["o_lat_s"][:STOK].reshape(SPC, 4, 256) for c in range(NCORES)])[None]
    kr_s = np.concatenate([R[c]["o_kr_s"][:STOK].reshape(SPC, 4, 32) for c in range(NCORES)])[None]
    ssm_s = np.concatenate([R[c]["o_ssm_s"].reshape(SPC, 32, 64, 2) for c in range(NCORES)])[None]
    return (y_prompt, y_sample, lat_p.astype(f32), kr_p.astype(f32), ssm_p.astype(f32), lat_s.astype(f32),
            kr_s.astype(f32), ssm_s.astype(f32))
```

```python
from contextlib import ExitStack
import math
import numpy as np
import concourse.bass as bass
import concourse.mybir as mybir
from concourse.bass_utils import run_bass_kernel_spmd

F32 = mybir.dt.float32
BF16 = mybir.dt.bfloat16
I32 = mybir.dt.int32
U32 = mybir.dt.uint32
AF = mybir.ActivationFunctionType
ALU = mybir.AluOpType
AX = mybir.AxisListType

D_MODEL = 1024
SEQ = 4096
NCORES = 8
EPS = 1e-6
PAST = 8192
NPAGE = 64
SPC = 16
STOK = 64

STAGE = 1
SPC_RUN = 16
PAGE_LIST = list(range(65))


class Prog:
    ENGS = ("pe", "act", "dve", "pool", "sp")

    def __init__(self, nc, es):
        self.nc = nc
        self.es = es
        self.q = {e: [] for e in self.ENGS}
        self.count = {e: 0 for e in self.ENGS}
        self.sem = {e: nc.alloc_semaphore(name="c_" + e) for e in self.ENGS}
        self.seen = {e: {f: 0 for f in self.ENGS} for e in self.ENGS}
        self.last_w = {}
        self.readers = {}
        self.R = 16
        self.ring = {}
        self.ring_n = {}
        for qn in ("sp", "pool", "act"):
            self.ring[qn] = [nc.alloc_semaphore(name="d_%s%d" % (qn, i)) for i in range(self.R)]
            self.ring_n[qn] = 0
        self.dseen = {e: {} for e in self.ENGS}
        self.ntens = 0

    def sb(self, shape, dt, name=None):
        self.ntens += 1
        name = "s_" + (name or "t%d" % self.ntens)
        return self.es.enter_context(self.nc.sbuf_tensor(name, list(shape), dt))

    def ps(self, shape, dt, name=None):
        self.ntens += 1
        name = "ps_" + (name or "p%d" % self.ntens)
        return self.es.enter_context(self.nc.psum_tensor(name, list(shape), dt))

    def _deps(self, reads, writes):
        toks = []
        for k in list(reads) + list(writes):
            t = self.last_w.get(k)
            if t is not None:
                toks.append(t)
        for k in writes:
            toks.extend(self.readers.get(k, ()))
        return toks

    def _emit_waits(self, eng, toks):
        need = {}
        dneed = {}
        for t in toks:
            if t[0] == "e":
                _, f, c = t
                if f == eng and eng == "pe":
                    continue
                if f == eng and eng == "sp":
                    continue
                if self.seen[eng][f] < c:
                    need[f] = max(need.get(f, 0), c)
            else:
                _, qn, slot, val = t
                key = (qn, slot)
                if self.dseen[eng].get(key, 0) < val:
                    dneed[key] = max(dneed.get(key, 0), val)
        for f, c in need.items():
            self.seen[eng][f] = c
            sem = self.sem[f]
            self.q[eng].append(lambda E, sem=sem, c=c: E.wait_ge(sem, c))
        for (qn, slot), val in dneed.items():
            self.dseen[eng][(qn, slot)] = val
            sem = self.ring[qn][slot]
            self.q[eng].append(lambda E, sem=sem, val=val: E.wait_ge(sem, val))

    def _record(self, tok, reads, writes):
        for k in writes:
            self.last_w[k] = tok
            self.readers[k] = []
        for k in reads:
            if k in writes:
                continue
            self.readers.setdefault(k, []).append(tok)

    def op(self, eng, fn, reads=(), writes=()):
        self._emit_waits(eng, self._deps(reads, writes))
        self.count[eng] += 1
        c = self.count[eng]
        sem = self.sem[eng]
        self.q[eng].append(lambda E, fn=fn, sem=sem: fn(E).then_inc(sem, 1))
        self._record(("e", eng, c), reads, writes)

    def dma(self, qn, fn, reads=(), writes=()):
        toks = self._deps(reads, writes)
        n = self.ring_n[qn]
        self.ring_n[qn] = n + 1
        slot = n % self.R
        use = n // self.R
        if use > 0:
            toks.append(("d", qn, slot, 16 * use))
        self._emit_waits(qn, toks)
        sem = self.ring[qn][slot]
        self.q[qn].append(lambda E, fn=fn, sem=sem: fn(E).then_inc(sem, 16))
        self._record(("d", qn, slot, 16 * (use + 1)), reads, writes)

    def barrier(self):
        toks = [("e", f, self.count[f]) for f in self.ENGS if self.count[f] > 0]
        for qn in self.ring:
            n = self.ring_n[qn]
            for slot in range(min(n, self.R)):
                uses = (n - 1 - slot) // self.R + 1
                toks.append(("d", qn, slot, 16 * uses))
        for e in self.ENGS:
            self._emit_waits(e, [t for t in toks if not (t[0] == "e" and t[1] == e)])

    def finish(self, keys):
        toks = []
        for k in keys:
            t = self.last_w.get(k)
            if t is not None:
                toks.append(t)
        self._emit_waits("sp", toks)

    def run(self):
        nc = self.nc
        with nc.Block() as block:
            @block.tensor
            def _(E):
                for f in self.q["pe"]:
                    f(E)

            @block.scalar
            def _(E):
                for f in self.q["act"]:
                    f(E)

            @block.vector
            def _(E):
                for f in self.q["dve"]:
                    f(E)

            @block.gpsimd
            def _(E):
                for f in self.q["pool"]:
                    f(E)

            @block.sync
            def _(E):
                for f in self.q["sp"]:
                    f(E)


def build_program(n_phys=10240):
    nc = bass.Bass("TRN2", target_bir_lowering=False)
    es = ExitStack()
    with es:
        es.enter_context(nc.allow_low_precision("bf16 matmul operands, fp32 accumulation"))
        P = Prog(nc, es)

        def din(name, shape, dt=F32):
            return nc.dram_tensor(name, list(shape), dt, kind="ExternalInput").ap()

        def dout(name, shape, dt=F32):
            return nc.dram_tensor(name, list(shape), dt, kind="ExternalOutput").ap()

        xf = din("xf", [SEQ, D_MODEL])
        xs = din("xs", [128, D_MODEL])
        ident_d = din("ident", [128, 128])
        rope_f = din("rope_f", [128, 32, 32])
        rope_s = din("rope_s", [128, 32])
        norm_mix = din("norm_mix", [128, 8])
        w_in = din("w_in", [D_MODEL, 1184])
        norm_kv = din("norm_kv_lora", [256])
        g_kr = din("qk_gain_k_rope", [32])
        d_are = din("ssm_a_re", [128, 16])
        d_aim = din("ssm_a_im", [128, 16])
        d_ldt = din("ssm_log_dt", [128, 16])
        d_b = din("ssm_b", [128, 16, 32])
        d_c = din("ssm_c", [128, 16, 32])
        d_d = din("ssm_d", [128, 4])
        d_st = din("state_ssm", [128, 16, 16, 2])
        d_par = din("parity", [128, 2])
        xo = din("xo", [2048, D_MODEL])
        rope_o = din("rope_o", [128, 16, 32])
        d_masks = din("masks", [128, 2, 128])
        d_wuk = din("w_uk", [256, 512])
        d_wuv = din("w_uv", [256, 512])
        d_wuq = din("w_uq", [384, 768])
        d_wglu = din("w_glu", [512, 1024])
        d_wout = din("w_out", [1024, 1024])
        d_gql = din("norm_q_lora", [384])
        d_gqn = din("qk_gain_q_nope", [64])
        d_gqr = din("qk_gain_q_rope", [32])
        d_gkn = din("qk_gain_k_nope", [64])
        d_gffn = din("norm_ffn", [D_MODEL])
        d_wq = din("peer_wq", [D_MODEL, 2048])
        d_keysT = din("peer_keysT", [128, 2, 128])
        d_pu = din("peer_u", [16384, D_MODEL])
        d_pv = din("peer_v", [16384, D_MODEL])
        d_iota = din("iota16", [128, 16])
        d_clat = din("cache_lat", [n_phys * 128, 256])
        d_ckr = din("cache_kr", [n_phys * 128, 32])
        d_ptab = din("page_tab", [1, 1024], I32)
        d_piota = din("piota", [128, 1])
        d_wukT = din("w_ukT", [64, 8, 256])
        d_maskn = din("maskn", [4, 32])
        lat_s_scr = nc.dram_tensor("lat_s_scr", [128, 256], F32, kind="Internal").ap()
        kr_s_scr = nc.dram_tensor("kr_s_scr", [128, 32], F32, kind="Internal").ap()
        o_y_s = dout("o_y_s", [128, D_MODEL])
        pu_b = nc.dram_tensor("pu_b", [16384, D_MODEL], BF16, kind="Internal").ap()
        pv_b = nc.dram_tensor("pv_b", [16384, D_MODEL], BF16, kind="Internal").ap()
        KT_d = nc.dram_tensor("KT_scr", [96, 8, SEQ], BF16, kind="Internal").ap()
        V_d = nc.dram_tensor("V_scr", [SEQ, 520], BF16, kind="Internal").ap()

        o_lat_p = dout("o_lat_p", [SEQ, 256])
        o_kr_p = dout("o_kr_p", [SEQ, 32])
        o_lat_s = dout("o_lat_s", [128, 256])
        o_kr_s = dout("o_kr_s", [128, 32])
        o_y_p = dout("o_y_p", [2048, D_MODEL])
        o_ssm_p = dout("o_ssm_p", [16, 128, 2])
        o_ssm_s = dout("o_ssm_s", [16, 16, 128, 2])

        out_keys = []

        def tt(eng, out, a, b, op, r, w):
            P.op(eng, lambda E: E.tensor_tensor(out=out, in0=a, in1=b, op=op), reads=r, writes=w)

        def tsc(eng, out, a, s1, op0, r, w, s2=None, op1=None):
            if op1 is None:
                P.op(eng, lambda E: E.tensor_scalar(out=out, in0=a, scalar1=s1, scalar2=None, op0=op0),
                     reads=r, writes=w)
            else:
                P.op(eng, lambda E: E.tensor_scalar(out=out, in0=a, scalar1=s1, scalar2=s2, op0=op0, op1=op1),
                     reads=r, writes=w)

        def stt(out, a, sc, b, op0, op1, r, w):
            P.op("dve", lambda E: E.scalar_tensor_tensor(out=out, in0=a, scalar=sc, in1=b, op0=op0, op1=op1),
                 reads=r, writes=w)

        def act(out, a, func, r, w, **kw):
            P.op("act", lambda E: E.activation(out=out, in_=a, func=func, **kw), reads=r, writes=w)

        def load(dst, src, key):
            P.dma("sp", lambda E: E.dma_start(out=dst, in_=src), writes=[key])

        ident_f = P.sb([128, 128], F32, "ident_f")
        ident = P.sb([128, 128], BF16, "ident")
        load(ident_f[:], ident_d, "ident_f")
        P.op("dve", lambda E: E.tensor_copy(out=ident[:], in_=ident_f[:]), reads=["ident_f"], writes=["ident"])

        ropeF = P.sb([128, 32, 32], F32, "ropeF")
        ropeS = P.sb([128, 32], F32, "ropeS")
        load(ropeF[:], rope_f, "ropeF")
        load(ropeS[:], rope_s, "ropeS")
        gmix = P.sb([128, 8], F32, "gmix")
        load(gmix[:], norm_mix, "gmix")
        gkv = P.sb([128, 256], F32, "gkv")
        load(gkv[:], norm_kv.unsqueeze(0).to_broadcast([128, 256]), "gkv")
        gkr = P.sb([128, 32], F32, "gkr")
        load(gkr[:], g_kr.unsqueeze(0).to_broadcast([128, 32]), "gkr")
        par = P.sb([128, 2], F32, "par")
        load(par[:], d_par, "par")

        pb = [P.ps([128, 512], F32, "pb%d" % i) for i in range(8)]
        pT = pb[0][:].bitcast(BF16)
        pTk = "pb0"

        w_rest = P.sb([128, 8, 800], BF16, "w_rest")
        w_cq = P.sb([128, 8, 384], BF16, "w_cq")
        xt = [P.sb([128, D_MODEL], F32, "xt%d" % i) for i in range(2)]
        for kt in range(8):
            load(xt[0][:], w_in[kt * 128:(kt + 1) * 128, 0:1024], "xt0")
            load(xt[1][:, 0:160], w_in[kt * 128:(kt + 1) * 128, 1024:1184], "xt1")
            tsc("dve", w_rest[:, kt, 0:512], xt[0][:, 0:512], gmix[:, kt:kt + 1], ALU.mult, ["xt0", "gmix"], [("w_in_b", kt)])
            tsc("dve", w_cq[:, kt, :], xt[0][:, 512:896], gmix[:, kt:kt + 1], ALU.mult, ["xt0", "gmix"], [("w_cq", kt)])
            tsc("dve", w_rest[:, kt, 512:640], xt[0][:, 896:1024], gmix[:, kt:kt + 1], ALU.mult, ["xt0", "gmix", ("w_in_b", kt)],
                [("w_in_b", kt)])
            tsc("dve", w_rest[:, kt, 640:800], xt[1][:, 0:160], gmix[:, kt:kt + 1], ALU.mult, ["xt1", "gmix", ("w_in_b", kt)],
                [("w_in_b", kt)])

        NRT = 16
        TB = 512
        sc = P.sb([128, 24, 16], F32, "s5sc")
        SCK = "s5sc"
        for j, src in enumerate((d_are, d_aim, d_ldt)):
            load(sc[:, j, :], src, SCK)
        ARE, AIM, DT, AR, TH, RR, T0, T1, T2, T3, C0, S0, FRE, FIM, ABR, ABI, C511, S511 = range(18)

        def S(j):
            return sc[:, j, :]
        act(S(DT), S(DT), AF.Exp, [SCK], [SCK])
        tt("dve", S(AR), S(ARE), S(DT), ALU.mult, [SCK], [SCK])
        tt("dve", S(TH), S(AIM), S(DT), ALU.mult, [SCK], [SCK])
        act(S(RR), S(AR), AF.Exp, [SCK], [SCK])
        sci = P.sb([128, 16], I32, "s5sci")
        tsc("dve", S(T0), S(TH), 1.0 / (2.0 * math.pi), ALU.mult, [SCK], [SCK])
        P.op("dve", lambda E: E.tensor_copy(out=sci[:], in_=S(T0)), reads=[SCK], writes=["s5sci"])
        P.op("dve", lambda E: E.tensor_copy(out=S(T1), in_=sci[:]), reads=["s5sci"], writes=[SCK])
        tt("dve", S(T0), S(T0), S(T1), ALU.subtract, [SCK], [SCK])
        tsc("dve", S(T0), S(T0), 2.0 * math.pi / 4.0, ALU.mult, [SCK], [SCK])
        hp = P.sb([128, 1], F32, "halfpi")
        P.op("dve", lambda E: E.memset(hp[:], math.pi / 2.0), writes=["halfpi"])
        act(S(T1), S(T0), AF.Sin, [SCK], [SCK])
        act(S(T2), S(T0), AF.Sin, [SCK, "halfpi"], [SCK], bias=hp[:, 0:1])

        def dbl(cd, sd, cs_, ss_):
            tt("dve", S(T3), cs_, cs_, ALU.mult, [SCK], [SCK])
            tt("dve", cd, ss_, ss_, ALU.mult, [SCK], [SCK])
            tt("dve", cd, S(T3), cd, ALU.subtract, [SCK], [SCK])
            tt("dve", sd, cs_, ss_, ALU.mult, [SCK], [SCK])
            tsc("dve", sd, sd, 2.0, ALU.mult, [SCK], [SCK])
        dbl(S(C511), S(S511), S(T2), S(T1))
        dbl(S(C0), S(S0), S(C511), S(S511))
        tt("dve", S(ABR), S(RR), S(C0), ALU.mult, [SCK], [SCK])
        tt("dve", S(ABI), S(RR), S(S0), ALU.mult, [SCK], [SCK])
        tsc("dve", S(T0), S(ABR), -1.0, ALU.add, [SCK], [SCK])
        tt("dve", S(T1), S(ARE), S(ARE), ALU.mult, [SCK], [SCK])
        tt("dve", S(T2), S(AIM), S(AIM), ALU.mult, [SCK], [SCK])
        tt("dve", S(T1), S(T1), S(T2), ALU.add, [SCK], [SCK])
        P.op("dve", lambda E: E.reciprocal(out=S(T1), in_=S(T1)), reads=[SCK], writes=[SCK])
        tt("dve", S(T2), S(T0), S(ARE), ALU.mult, [SCK], [SCK])
        tt("dve", S(T3), S(ABI), S(AIM), ALU.mult, [SCK], [SCK])
        tt("dve", S(T2), S(T2), S(T3), ALU.add, [SCK], [SCK])
        tt("dve", S(FRE), S(T2), S(T1), ALU.mult, [SCK], [SCK])
        tt("dve", S(T2), S(ABI), S(ARE), ALU.mult, [SCK], [SCK])
        tt("dve", S(T3), S(T0), S(AIM), ALU.mult, [SCK], [SCK])
        tt("dve", S(T2), S(T2), S(T3), ALU.subtract, [SCK], [SCK])
        tt("dve", S(FIM), S(T2), S(T1), ALU.mult, [SCK], [SCK])

        rotc = P.sb([128, 10, 16], F32, "rotc")
        rots = P.sb([128, 10, 16], F32, "rots")
        RK = "rot"
        P.op("dve", lambda E: E.tensor_copy(out=rotc[:, 0, :], in_=S(C0)), reads=[SCK], writes=[RK])
        P.op("dve", lambda E: E.tensor_copy(out=rots[:, 0, :], in_=S(S0)), reads=[SCK], writes=[RK])
        for k in range(9):
            tt("dve", S(T3), rotc[:, k, :], rotc[:, k, :], ALU.mult, [RK, SCK], [SCK])
            tt("dve", S(T2), rots[:, k, :], rots[:, k, :], ALU.mult, [RK, SCK], [SCK])
            tt("dve", rotc[:, k + 1, :], S(T3), S(T2), ALU.subtract, [SCK, RK], [RK])
            tt("dve", S(T3), rotc[:, k, :], rots[:, k, :], ALU.mult, [RK, SCK], [SCK])
            tsc("dve", rots[:, k + 1, :], S(T3), 2.0, ALU.mult, [SCK, RK], [RK])
        tt("dve", S(T2), rotc[:, 9, :], S(C0), ALU.mult, [RK, SCK], [SCK])
        tt("dve", S(T3), rots[:, 9, :], S(S0), ALU.mult, [RK, SCK], [SCK])
        tt("dve", S(C511), S(T2), S(T3), ALU.add, [SCK], [SCK])
        tt("dve", S(T2), rots[:, 9, :], S(C0), ALU.mult, [RK, SCK], [SCK])
        tt("dve", S(T3), rotc[:, 9, :], S(S0), ALU.mult, [RK, SCK], [SCK])
        tt("dve", S(S511), S(T2), S(T3), ALU.subtract, [SCK], [SCK])

        cosT = P.sb([128, NRT, TB], BF16, "cosT")
        sinT = P.sb([128, NRT, TB], BF16, "sinT")
        Gre = P.sb([128, TB], F32, "Gre")
        Gim = P.sb([128, TB], F32, "Gim")
        q1 = P.sb([128, TB], F32, "q1")
        q2 = P.sb([128, TB], F32, "q2")
        for r0 in range(NRT):
            ch = r0 // 4
            P.op("pool", lambda E: E.memset(Gre[:, 0:1], 1.0), writes=["Gre"])
            P.op("pool", lambda E: E.memset(Gim[:, 0:1], 0.0), writes=["Gim"])
            for k in range(9):
                n = 1 << k
                ck = rotc[:, k, r0:r0 + 1]
                sk_ = rots[:, k, r0:r0 + 1]
                tsc("dve", q1[:, 0:n], Gre[:, 0:n], ck, ALU.mult, ["Gre", RK], ["q1"])
                tsc("dve", q2[:, 0:n], Gim[:, 0:n], ck, ALU.mult, ["Gim", RK], ["q2"])
                stt(Gre[:, n:2 * n], Gim[:, 0:n], sk_, q1[:, 0:n], ALU.mult, ALU.subtract, ["Gim", "q1", RK], ["Gre"])
                stt(Gim[:, n:2 * n], Gre[:, 0:n], sk_, q2[:, 0:n], ALU.mult, ALU.add, ["Gre", "q2", RK], ["Gim"])
                tsc("dve", Gre[:, n:2 * n], Gre[:, n:2 * n], -1.0, ALU.mult, ["Gre"], ["Gre"])
            P.op("act", lambda E, r0=r0: E.copy(out=cosT[:, r0, :], in_=Gre[:]), reads=["Gre"],
                 writes=[("cosT", ch)])
            P.op("act", lambda E, r0=r0: E.copy(out=sinT[:, r0, :], in_=Gim[:]), reads=["Gim"],
                 writes=[("sinT", ch)])

        bsb = P.sb([128, 16, 16, 2], F32, "bsb")
        csb = P.sb([128, 16, 16, 2], F32, "csb")
        load(bsb[:], d_b.rearrange("p t (h c) -> p t h c", c=2), "bsb")
        load(csb[:], d_c.rearrange("p t (h c) -> p t h c", c=2), "csb")
        bbr = P.sb([128, 16, 16], F32, "bbr")
        bbi = P.sb([128, 16, 16], F32, "bbi")
        tb1 = P.sb([128, 16, 16], F32, "tb1")
        fre_bc = S(FRE).unsqueeze(2).to_broadcast([128, 16, 16])
        fim_bc = S(FIM).unsqueeze(2).to_broadcast([128, 16, 16])
        tt("dve", bbr[:], bsb[:, :, :, 0], fre_bc, ALU.mult, ["bsb", SCK], ["bbr"])
        tt("dve", tb1[:], bsb[:, :, :, 1], fim_bc, ALU.mult, ["bsb", SCK], ["tb1"])
        tt("dve", bbr[:], bbr[:], tb1[:], ALU.subtract, ["tb1"], ["bbr"])
        tt("dve", bbi[:], bsb[:, :, :, 1], fre_bc, ALU.mult, ["bsb", SCK], ["bbi"])
        tt("dve", tb1[:], bsb[:, :, :, 0], fim_bc, ALU.mult, ["bsb", SCK], ["tb1"])
        tt("dve", bbi[:], bbi[:], tb1[:], ALU.add, ["tb1"], ["bbi"])

        BTr = P.sb([128, NRT, 128], BF16, "BTr")
        BTi = P.sb([128, NRT, 128], BF16, "BTi")
        CZr = P.sb([128, NRT, 128], BF16, "CZr")
        CZi = P.sb([128, NRT, 128], BF16, "CZi")
        zr = P.sb([128, 128], BF16, "zr")
        zi = P.sb([128, 128], BF16, "zi")
        P.op("pool", lambda E: E.memset(CZr[:], 0.0), writes=["CZr"])
        P.op("pool", lambda E: E.memset(CZi[:], 0.0), writes=["CZi"])
        for rt_ in range(NRT):
            c0 = 32 * (rt_ % 4)
            P.op("pool", lambda E: E.memset(zr[:], 0.0), writes=["zr"])
            P.op("pool", lambda E: E.memset(zi[:], 0.0), writes=["zi"])
            for gl in range(2):
                rows = slice(64 * gl, 64 * gl + 64)
                cols = slice(c0 + 16 * gl, c0 + 16 * gl + 16)
                P.op("dve", lambda E, rows=rows, cols=cols, rt_=rt_: E.tensor_copy(out=zr[rows, cols], in_=bbr[rows, rt_, :]),
                     reads=["bbr"], writes=["zr"])
                P.op("dve", lambda E, rows=rows, cols=cols, rt_=rt_: E.tensor_copy(out=zi[rows, cols], in_=bbi[rows, rt_, :]),
                     reads=["bbi"], writes=["zi"])
                P.op("dve", lambda E, rows=rows, cols=cols, rt_=rt_: E.tensor_copy(out=CZr[rows, rt_, cols], in_=csb[rows, rt_, :, 0]),
                     reads=["csb"], writes=["CZr"])
                P.op("dve", lambda E, rows=rows, cols=cols, rt_=rt_: E.tensor_scalar(
                    out=CZi[rows, rt_, cols], in0=csb[rows, rt_, :, 1], scalar1=-1.0, scalar2=None, op0=ALU.mult),
                    reads=["csb"], writes=["CZi"])
            P.op("pe", lambda E: E.transpose(out=pT[:, 0:128], in_=zr[:], identity=ident[:]),
                 reads=["zr", "ident"], writes=[pTk])
            P.op("pe", lambda E: E.transpose(out=pT[:, 128:256], in_=zi[:], identity=ident[:]),
                 reads=["zi", "ident"], writes=[pTk])
            P.op("act", lambda E, rt_=rt_: E.copy(out=BTr[:, rt_, :], in_=pT[:, 0:128]), reads=[pTk], writes=[("BTr", rt_)])
            P.op("act", lambda E, rt_=rt_: E.copy(out=BTi[:, rt_, :], in_=pT[:, 128:256]), reads=[pTk], writes=[("BTi", rt_)])
        dsk = P.sb([128, 4], F32, "dsk")
        load(dsk[:], d_d, "dsk")

        w_uk_b = P.sb([128, 2, 512], BF16, "w_uk_b")
        w_uv_b = P.sb([128, 2, 512], BF16, "w_uv_b")
        for c_ in range(2):
            load(xt[0][:, 0:512], d_wuk[c_ * 128:(c_ + 1) * 128, :], "xt0")
            P.op("act", lambda E, c_=c_: E.copy(out=w_uk_b[:, c_, :], in_=xt[0][:, 0:512]), reads=["xt0"], writes=["w_uk_b"])
            load(xt[1][:, 0:512], d_wuv[c_ * 128:(c_ + 1) * 128, :], "xt1")
            P.op("act", lambda E, c_=c_: E.copy(out=w_uv_b[:, c_, :], in_=xt[1][:, 0:512]), reads=["xt1"], writes=["w_uv_b"])
        gkn = P.sb([128, 64], F32, "gkn")
        load(gkn[:], d_gkn.unsqueeze(0).to_broadcast([128, 64]), "gkn")
        lnb = P.sb([128, 256], BF16, "lnb")
        ckvT = P.sb([128, 2, 128], BF16, "ckvT")
        ssk = P.sb([128, 16], F32, "ssk")
        tmpk = P.sb([128, 8, 64], F32, "tmpk")
        Kcat = P.sb([128, 8, 96], BF16, "Kcat")
        KTt = [P.sb([96, 8, 128], BF16, "KTt%d" % i_) for i_ in range(2)]
        Vt = [P.sb([128, 8, 65], BF16, "Vt%d" % i_) for i_ in range(2)]
        for i_ in range(2):
            P.op("pool", lambda E, i_=i_: E.memset(Vt[i_][:], 1.0), writes=["Vt%d" % i_])

        junk = P.sb([128, D_MODEL], F32, "junk")
        xnb = P.sb([128, D_MODEL], BF16, "xnb")
        xnT4 = P.sb([128, 8, 512], BF16, "xnT4")
        pkv = pb[1]
        ss = P.sb([128, 8], F32, "ss")
        latn = [P.sb([128, 256], F32, "latn%d" % i) for i in range(2)]
        krn = P.sb([128, 32], F32, "krn")
        kro = [P.sb([128, 32], F32, "kro%d" % i) for i in range(2)]
        rtt = P.sb([128, 4, 16], F32, "rt")
        uT = P.sb([128, 4, 512], BF16, "uT")

        def rstd_from_ss(ss_ap, key, d):
            tsc("dve", ss_ap, ss_ap, 1.0 / d, ALU.mult, [key], [key], s2=EPS, op1=ALU.add)
            act(ss_ap, ss_ap, AF.Ln, [key], [key])
            act(ss_ap, ss_ap, AF.Exp, [key], [key], scale=-0.5)

        def rope(dst, src, cs, keys_r, keys_w):
            cos = cs[:, 0:16]
            sin = cs[:, 16:32]
            x1 = src[:, 0:16]
            x2 = src[:, 16:32]
            tt("dve", rtt[:, 0, :], x1, cos, ALU.mult, keys_r, ["rt0"])
            tt("dve", rtt[:, 1, :], x2, sin, ALU.mult, keys_r, ["rt1"])
            tt("dve", rtt[:, 2, :], x2, cos, ALU.mult, keys_r, ["rt2"])
            tt("dve", rtt[:, 3, :], x1, sin, ALU.mult, keys_r, ["rt3"])
            tt("dve", dst[:, 0:16], rtt[:, 0, :], rtt[:, 1, :], ALU.subtract, ["rt0", "rt1"], keys_w)
            tt("dve", dst[:, 16:32], rtt[:, 2, :], rtt[:, 3, :], ALU.add, ["rt2", "rt3"] + list(keys_w), keys_w)

        def kv_tile(i, j, src_ap, cs_ap, lat_out, kr_out):
            x = xt[i % 2]
            xk = "xt%d" % (i % 2)
            load(x[:], src_ap, xk)
            act(junk[:], x[:], AF.Square, [xk], ["junk", "ss0"], accum_out=ss[:, 0:1])
            rstd_from_ss(ss[:, 0:1], "ss0", D_MODEL)
            tsc("dve", xnb[:], x[:], ss[:, 0:1], ALU.mult, [xk, "ss0"], ["xnb"])
            for kt in range(8):
                P.op("pe", lambda E, kt=kt: E.transpose(out=pT[:, kt * 128:(kt + 1) * 128],
                                                         in_=xnb[:, kt * 128:(kt + 1) * 128], identity=ident[:]),
                     reads=["xnb", "ident"], writes=[pTk])
            P.op("act", lambda E: E.copy(out=xnT4[:, :, j * 128:(j + 1) * 128],
                                         in_=pT.rearrange("p (k t) -> p k t", k=8)),
                 reads=[pTk], writes=[("xnT4", j)])
            for kt in range(8):
                P.op("pe", lambda E, kt=kt: E.matmul(pkv[:, 0:288], lhsT=xnT4[:, kt, j * 128:(j + 1) * 128],
                                                      rhs=w_rest[:, kt, 512:800], start=(kt == 0), stop=(kt == 7)),
                     reads=[("xnT4", j), ("w_in_b", kt)], writes=["pb1"])
            act(junk[:, 0:256], pkv[:, 0:256], AF.Square, ["pb1"], ["junk", "ss1"], accum_out=ss[:, 1:2])
            rstd_from_ss(ss[:, 1:2], "ss1", 256)
            ln = latn[i % 2]
            lk = "latn%d" % (i % 2)
            stt(ln[:], pkv[:, 0:256], ss[:, 1:2], gkv[:], ALU.mult, ALU.mult, ["pb1", "ss1", "gkv"], [lk])
            P.dma("sp", lambda E: E.dma_start(out=lat_out, in_=ln[:]), reads=[lk], writes=[("out", id(lat_out))])
            out_keys.append(("out", id(lat_out)))
            act(junk[:, 0:32], pkv[:, 256:288], AF.Square, ["pb1"], ["junk", "ss2"], accum_out=ss[:, 2:3])
            rstd_from_ss(ss[:, 2:3], "ss2", 32)
            stt(krn[:], pkv[:, 256:288], ss[:, 2:3], gkr[:], ALU.mult, ALU.mult, ["pb1", "ss2", "gkr"], ["krn"])
            ko = kro[i % 2]
            kk = "kro%d" % (i % 2)
            rope(ko, krn, cs_ap, ["krn", "ropeF", "ropeS"], [kk])
            P.dma("sp", lambda E: E.dma_start(out=kr_out, in_=ko[:]), reads=[kk], writes=[("out", id(kr_out))])
            out_keys.append(("out", id(kr_out)))
            if i >= 32:
                P.dma("sp", lambda E: E.dma_start(out=lat_s_scr, in_=ln[:]), reads=[lk], writes=["lat_s_scr"])
                P.dma("sp", lambda E: E.dma_start(out=kr_s_scr, in_=ko[:]), reads=[kk], writes=["kr_s_scr"])
                return
            P.op("act", lambda E: E.copy(out=lnb[:], in_=ln[:]), reads=[lk], writes=["lnb"])
            for c_ in range(2):
                P.op("pe", lambda E, c_=c_: E.transpose(out=pT[:, c_ * 128:(c_ + 1) * 128], in_=lnb[:, c_ * 128:(c_ + 1) * 128],
                                                         identity=ident[:]), reads=["lnb", "ident"], writes=[pTk])
            P.op("act", lambda E: E.copy(out=ckvT[:], in_=pT[:, 0:256].rearrange("p (c t) -> p c t", c=2)),
                 reads=[pTk], writes=["ckvT"])
            for c_ in range(2):
                P.op("pe", lambda E, c_=c_: E.matmul(pb[6][:], lhsT=ckvT[:, c_, :], rhs=w_uk_b[:, c_, :],
                                                      start=(c_ == 0), stop=(c_ == 1)),
                     reads=["ckvT", "w_uk_b"], writes=["pb6"])
            for c_ in range(2):
                P.op("pe", lambda E, c_=c_: E.matmul(pb[7][:], lhsT=ckvT[:, c_, :], rhs=w_uv_b[:, c_, :],
                                                      start=(c_ == 0), stop=(c_ == 1)),
                     reads=["ckvT", "w_uv_b"], writes=["pb7"])
            act(junk[:, 0:512], pb[6][:], AF.Square, ["pb6"], ["junk"])
            P.op("dve", lambda E: E.tensor_reduce(out=ssk[:, 0:8], in_=junk[:, 0:512].rearrange("p (h d) -> p h d", h=8),
                                                  axis=AX.X, op=ALU.add), reads=["junk"], writes=["ssk"])
            rstd_from_ss(ssk[:, 0:8], "ssk", 64)
            tt("dve", tmpk[:], pb[6][:].rearrange("p (h d) -> p h d", h=8),
               ssk[:, 0:8].unsqueeze(2).to_broadcast([128, 8, 64]), ALU.mult, ["pb6", "ssk"], ["tmpk"])
            tt("dve", Kcat[:, :, 0:64], tmpk[:], gkn[:].unsqueeze(1).to_broadcast([128, 8, 64]), ALU.mult,
               ["tmpk", "gkn"], ["Kcat"])
            P.op("dve", lambda E: E.tensor_copy(out=Kcat[:, :, 64:96], in_=ko[:].unsqueeze(1).to_broadcast([128, 8, 32])),
                 reads=[kk, "Kcat"], writes=["Kcat"])
            for h in range(8):
                P.op("pe", lambda E, h=h: E.transpose(out=pT[0:96, h * 128:(h + 1) * 128], in_=Kcat[:, h, :], identity=ident[:]),
                     reads=["Kcat", "ident"], writes=[pTk])
            kt_ = KTt[i % 2]
            ktk = "KTt%d" % (i % 2)
            P.op("act", lambda E: E.copy(out=kt_[:], in_=pT[0:96, :].rearrange("p (h t) -> p h t", h=8)),
                 reads=[pTk], writes=[ktk])
            P.dma("sp", lambda E: E.dma_start(out=KT_d[:, :, i * 128:(i + 1) * 128], in_=kt_[:]), reads=[ktk],
                  writes=[("KTd", i)])
            vt_ = Vt[i % 2]
            vtk = "Vt%d" % (i % 2)
            P.op("act", lambda E: E.copy(out=vt_[:, :, 0:64], in_=pb[7][:].rearrange("p (h d) -> p h d", h=8)),
                 reads=["pb7"], writes=[vtk])
            P.dma("sp", lambda E: E.dma_start(out=V_d[i * 128:(i + 1) * 128, :], in_=vt_[:].rearrange("p h d -> p (h d)")),
                  reads=[vtk], writes=[("Vd", i)])

        def u_proj(ntok):
            for ft in range(4):
                for kt in range(8):
                    P.op("pe", lambda E, ft=ft, kt=kt: E.matmul(
                        pb[2][:, 0:ntok], lhsT=w_rest[:, kt, ft * 128:(ft + 1) * 128], rhs=xnT4[:, kt, 0:ntok],
                        start=(kt == 0), stop=(kt == 7)),
                        reads=[("xnT4", jj) for jj in range(4)] + [("w_in_b", kt)], writes=["pb2"])
                P.op("act", lambda E, ft=ft: E.copy(out=uT[:, ft, 0:ntok], in_=pb[2][:, 0:ntok]),
                     reads=["pb2"], writes=[("uT", ft)])

        m1 = P.sb([128, TB], F32, "m1")
        m2 = P.sb([128, TB], F32, "m2")
        gre = P.sb([128, TB], F32, "gre")
        gim = P.sb([128, TB], F32, "gim")
        hre = P.sb([128, TB], BF16, "hre")
        him = P.sb([128, TB], BF16, "him")
        car = P.sb([128, NRT, 2], F32, "car")
        ctmp = P.sb([128, 2], F32, "ctmp")
        ytmp = P.sb([128, TB], F32, "ytmp")
        ytm2 = P.sb([128, 2, 128], F32, "ytm2")
        Yown = P.sb([128, 4, 2048], BF16, "Yown")
        ssmo = P.sb([128, NRT, 2], F32, "ssmo")
        P.op("pool", lambda E: E.memset(car[:], 0.0), writes=["car"])

        def s5_block(n, last, between=()):
            for ft in range(4):
                for r4 in range(4):
                    rt_ = ft * 4 + r4
                    ch = rt_ // 4
                    P.op("pe", lambda E, rt_=rt_, ft=ft: E.matmul(pb[3][:], lhsT=BTr[:, rt_, :], rhs=uT[:, ft, :],
                                                                 start=True, stop=True),
                         reads=[("BTr", rt_), ("uT", ft)], writes=["pb3"])
                    P.op("pe", lambda E, rt_=rt_, ft=ft: E.matmul(pb[4][:], lhsT=BTi[:, rt_, :], rhs=uT[:, ft, :],
                                                                 start=True, stop=True),
                         reads=[("BTi", rt_), ("uT", ft)], writes=["pb4"])
                    cT = cosT[:, rt_, :]
                    sT = sinT[:, rt_, :]
                    ck_, sk2 = ("cosT", ch), ("sinT", ch)
                    tt("dve", m1[:], pb[3][:], cT, ALU.mult, ["pb3", ck_], ["m1"])
                    tt("dve", m2[:], pb[4][:], sT, ALU.mult, ["pb4", sk2], ["m2"])
                    tt("pool", gre[:], m1[:], m2[:], ALU.add, ["m1", "m2"], ["gre"])
                    tt("dve", m1[:], pb[4][:], cT, ALU.mult, ["pb4", ck_], ["m1"])
                    tt("dve", m2[:], pb[3][:], sT, ALU.mult, ["pb3", sk2], ["m2"])
                    tt("pool", gim[:], m1[:], m2[:], ALU.subtract, ["m1", "m2"], ["gim"])
                    P.op("dve", lambda E, rt_=rt_: E.tensor_tensor_scan(
                        out=Gre[:], data0=sc[:, RR, rt_:rt_ + 1].to_broadcast([128, TB]), data1=gre[:], initial=car[:, rt_, 0:1],
                        op0=ALU.mult, op1=ALU.add), reads=[SCK, "gre", "car"], writes=["Gre"])
                    P.op("dve", lambda E, rt_=rt_: E.tensor_tensor_scan(
                        out=Gim[:], data0=sc[:, RR, rt_:rt_ + 1].to_broadcast([128, TB]), data1=gim[:], initial=car[:, rt_, 1:2],
                        op0=ALU.mult, op1=ALU.add), reads=[SCK, "gim", "car"], writes=["Gim"])
                    c9 = rotc[:, 9, rt_:rt_ + 1]
                    s9 = rots[:, 9, rt_:rt_ + 1]
                    if not last:
                        tsc("dve", ctmp[:, 0:1], Gim[:, TB - 1:TB], s9, ALU.mult, ["Gim", RK], ["ctmp"])
                        tsc("dve", ctmp[:, 1:2], Gre[:, TB - 1:TB], s9, ALU.mult, ["Gre", RK], ["ctmp"])
                        stt(car[:, rt_, 0:1], Gre[:, TB - 1:TB], c9, ctmp[:, 0:1], ALU.mult, ALU.subtract,
                            ["Gre", RK, "ctmp"], ["car"])
                        stt(car[:, rt_, 1:2], Gim[:, TB - 1:TB], c9, ctmp[:, 1:2], ALU.mult, ALU.add,
                            ["Gim", RK, "ctmp"], ["car"])
                    else:
                        c5 = sc[:, C511, rt_:rt_ + 1]
                        s5 = sc[:, S511, rt_:rt_ + 1]
                        tsc("dve", ctmp[:, 0:1], Gim[:, TB - 1:TB], s5, ALU.mult, ["Gim", SCK], ["ctmp"])
                        tsc("dve", ctmp[:, 1:2], Gre[:, TB - 1:TB], s5, ALU.mult, ["Gre", SCK], ["ctmp"])
                        stt(ssmo[:, rt_, 0:1], Gre[:, TB - 1:TB], c5, ctmp[:, 0:1], ALU.mult, ALU.subtract,
                            ["Gre", SCK, "ctmp"], ["ssmo"])
                        stt(ssmo[:, rt_, 1:2], Gim[:, TB - 1:TB], c5, ctmp[:, 1:2], ALU.mult, ALU.add,
                            ["Gim", SCK, "ctmp"], ["ssmo"])
                    tt("pool", q1[:], Gre[:], cT, ALU.mult, ["Gre", ck_], ["q1"])
                    tt("pool", q2[:], Gim[:], sT, ALU.mult, ["Gim", sk2], ["q2"])
                    tt("pool", hre[:], q1[:], q2[:], ALU.subtract, ["q1", "q2"], ["hre"])
                    tt("pool", q1[:], Gre[:], sT, ALU.mult, ["Gre", sk2], ["q1"])
                    tt("pool", q2[:], Gim[:], cT, ALU.mult, ["Gim", ck_], ["q2"])
                    tt("pool", him[:], q1[:], q2[:], ALU.add, ["q1", "q2"], ["him"])
                    P.op("pe", lambda E, rt_=rt_, r4=r4: E.matmul(pb[5][:], lhsT=CZr[:, rt_, :], rhs=hre[:],
                                                                 start=(r4 == 0), stop=False),
                         reads=["CZr", "hre"], writes=["pb5"])
                    P.op("pe", lambda E, rt_=rt_, r4=r4: E.matmul(pb[5][:], lhsT=CZi[:, rt_, :], rhs=him[:],
                                                                 start=False, stop=(r4 == 3)),
                         reads=["CZi", "him"], writes=["pb5"])
                stt(ytmp[:], uT[:, ft, :], dsk[:, ft:ft + 1], pb[5][:], ALU.mult, ALU.add,
                    [("uT", ft), "dsk", "pb5"], ["ytmp"])
                yv = ytmp[:].rearrange("p (a b t) -> p a b t", a=2, b=2)
                tsc("dve", ytm2[:], yv[:, :, 0, :], par[:, 0:1], ALU.mult, ["ytmp", "par"], ["ytm2"])
                stt(Yown[:, ft, n * 256:(n + 1) * 256].rearrange("p (a t) -> p a t", a=2), yv[:, :, 1, :],
                    par[:, 1:2], ytm2[:], ALU.mult, ALU.add, ["ytmp", "par", "ytm2"], [("Yown", ft, n)])
                if ft < len(between):
                    between[ft]()

        def kv_full(i):
            kv_tile(i, i % 4, xf[i * 128:(i + 1) * 128, :], ropeF[:, i, :],
                    o_lat_p[i * 128:(i + 1) * 128, :], o_kr_p[i * 128:(i + 1) * 128, :])

        for j in range(4):
            kv_full(j)
        for n in range(8):
            u_proj(512)
            nxt = [(lambda i=(n + 1) * 4 + j: kv_full(i)) for j in range(4)] if n < 7 else []
            s5_block(n, n == 7, nxt)
        P.dma("sp", lambda E: E.dma_start(out=o_ssm_p.rearrange("t p c -> p t c"), in_=ssmo[:]),
              reads=["ssmo"], writes=["o_ssm_p"])
        out_keys.append("o_ssm_p")

        kv_tile(32, 0, xs[:, :], ropeS[:, :], o_lat_s[:, :], o_kr_s[:, :])
        u_proj(128)
        bur = P.sb([128, NRT, 64], F32, "bur")
        bui = P.sb([128, NRT, 64], F32, "bui")
        for rt_ in range(NRT):
            ft = rt_ // 4
            P.op("pe", lambda E, rt_=rt_, ft=ft: E.matmul(pb[3][:, 0:128], lhsT=BTr[:, rt_, :], rhs=uT[:, ft, 0:128],
                                                         start=True, stop=True),
                 reads=[("BTr", rt_), ("uT", ft)], writes=["pb3"])
            P.op("pe", lambda E, rt_=rt_, ft=ft: E.matmul(pb[4][:, 0:128], lhsT=BTi[:, rt_, :], rhs=uT[:, ft, 0:128],
                                                         start=True, stop=True),
                 reads=[("BTi", rt_), ("uT", ft)], writes=["pb4"])
            P.op("act", lambda E, rt_=rt_: E.copy(out=bur[:, rt_, :], in_=pb[3][:, 0:64]), reads=["pb3"], writes=["bur"])
            P.op("act", lambda E, rt_=rt_: E.copy(out=bui[:, rt_, :], in_=pb[4][:, 0:64]), reads=["pb4"], writes=["bui"])
        st0 = P.sb([128, NRT, 16, 2], F32, "st0")
        load(st0[:], d_st, "st0")
        Hs = P.sb([128, NRT, 16, 4, 2], F32, "Hs")
        e1 = P.sb([128, NRT, 16], F32, "e1")
        e2 = P.sb([128, NRT, 16], F32, "e2")
        abr_bc = S(ABR).unsqueeze(2).to_broadcast([128, NRT, 16])
        abi_bc = S(ABI).unsqueeze(2).to_broadcast([128, NRT, 16])
        burv = bur[:].rearrange("p r (b t) -> p r b t", t=4)
        buiv = bui[:].rearrange("p r (b t) -> p r b t", t=4)
        for t in range(4):
            if t == 0:
                pr, pi_ = st0[:, :, :, 0], st0[:, :, :, 1]
                pk = ["st0"]
            else:
                pr, pi_ = Hs[:, :, :, t - 1, 0], Hs[:, :, :, t - 1, 1]
                pk = ["Hs"]
            tt("dve", e1[:], pr, abr_bc, ALU.mult, pk + [SCK], ["e1"])
            tt("dve", e2[:], pi_, abi_bc, ALU.mult, pk + [SCK], ["e2"])
            tt("dve", e1[:], e1[:], e2[:], ALU.subtract, ["e2"], ["e1"])
            tt("dve", Hs[:, :, :, t, 0], e1[:], burv[:, :, :, t], ALU.add, ["e1", "bur"], ["Hs"])
            tt("dve", e1[:], pi_, abr_bc, ALU.mult, pk + [SCK], ["e1"])
            tt("dve", e2[:], pr, abi_bc, ALU.mult, pk + [SCK], ["e2"])
            tt("dve", e1[:], e1[:], e2[:], ALU.add, ["e2"], ["e1"])
            tt("dve", Hs[:, :, :, t, 1], e1[:], buiv[:, :, :, t], ALU.add, ["e1", "bui"], ["Hs"])
        for rt_ in range(NRT):
            P.dma("sp", lambda E, rt_=rt_: E.dma_start(out=o_ssm_s[:, rt_, :, :].rearrange("b p c -> p b c"),
                                                     in_=Hs[:, rt_, :, 3, :]),
                  reads=["Hs"], writes=[("o_ssm_s", rt_)])
            out_keys.append(("o_ssm_s", rt_))

        hsr = Gre[:].bitcast(BF16)
        hsi = Gim[:].bitcast(BF16)
        P.op("act", lambda E: E.copy(out=hsr.rearrange("p (r b t) -> p r b t", r=16, b=16), in_=Hs[:, :, :, :, 0]),
             reads=["Hs"], writes=["Gre"])
        P.op("act", lambda E: E.copy(out=hsi.rearrange("p (r b t) -> p r b t", r=16, b=16), in_=Hs[:, :, :, :, 1]),
             reads=["Hs"], writes=["Gim"])
        Ys = ytmp[:, 0:256].rearrange("p (f t) -> p f t", f=4)
        for ft in range(4):
            for r4 in range(4):
                rt_ = ft * 4 + r4
                P.op("pe", lambda E, rt_=rt_, r4=r4: E.matmul(pb[5][:, 0:64], lhsT=CZr[:, rt_, :], rhs=hsr[:, rt_ * 64:(rt_ + 1) * 64],
                                                             start=(r4 == 0), stop=False), reads=["CZr", "Gre"], writes=["pb5"])
                P.op("pe", lambda E, rt_=rt_, r4=r4: E.matmul(pb[5][:, 0:64], lhsT=CZi[:, rt_, :], rhs=hsi[:, rt_ * 64:(rt_ + 1) * 64],
                                                             start=False, stop=(r4 == 3)), reads=["CZi", "Gim"], writes=["pb5"])
            stt(Ys[:, ft, :], uT[:, ft, 0:64], dsk[:, ft:ft + 1], pb[5][:, 0:64], ALU.mult, ALU.add,
                [("uT", ft), "dsk", "pb5"], ["ytmp"])

        P.barrier()
        w_out_b = cosT[:].rearrange("p (k a) t -> p k (a t)", k=8)
        w_glu_b = sinT[:, 0:8, :].rearrange("p (k a) t -> p k (a t)", k=4)
        w_uq_b = sinT[:, 8:13, :].rearrange("p a t -> p (a t)")[:, 0:2304].rearrange("p (c n) -> p c n", c=3)
        for k in range(8):
            load(xt[k % 2][:], d_wout[k * 128:(k + 1) * 128, :], "xt%d" % (k % 2))
            P.op("act", lambda E, k=k: E.copy(out=w_out_b[:, k, :], in_=xt[k % 2][:]), reads=["xt%d" % (k % 2)], writes=["w_out_b"])
        for k in range(4):
            load(xt[k % 2][:], d_wglu[k * 128:(k + 1) * 128, :], "xt%d" % (k % 2))
            P.op("act", lambda E, k=k: E.copy(out=w_glu_b[:, k, :], in_=xt[k % 2][:]), reads=["xt%d" % (k % 2)], writes=["w_glu_b"])
        for k in range(3):
            load(xt[k % 2][:, 0:768], d_wuq[k * 128:(k + 1) * 128, :], "xt%d" % (k % 2))
            P.op("act", lambda E, k=k: E.copy(out=w_uq_b[:, k, :], in_=xt[k % 2][:, 0:768]), reads=["xt%d" % (k % 2)], writes=["w_uq_b"])
        gql = P.sb([128, 384], F32, "gql")
        load(gql[:], d_gql.unsqueeze(0).to_broadcast([128, 384]), "gql")
        gqn = P.sb([128, 64], F32, "gqn")
        load(gqn[:], d_gqn.unsqueeze(0).to_broadcast([128, 64]), "gqn")
        gqr = P.sb([128, 32], F32, "gqr")
        load(gqr[:], d_gqr.unsqueeze(0).to_broadcast([128, 32]), "gqr")
        ropeO = P.sb([128, 16, 32], F32, "ropeO")
        load(ropeO[:], rope_o, "ropeO")
        mskf = P.sb([128, 2, 128], F32, "mskf")
        msk = P.sb([128, 2, 128], BF16, "msk")
        load(mskf[:], d_masks, "mskf")
        P.op("dve", lambda E: E.tensor_copy(out=msk[:], in_=mskf[:]), reads=["mskf"], writes=["msk"])
        zer = P.sb([128, 512], BF16, "zer")
        P.op("pool", lambda E: E.memset(zer[:], 0.0), writes=["zer"])
        cqn = P.sb([128, 384], BF16, "cqn")
        cqT = P.sb([128, 3, 128], BF16, "cqT")
        Qcat = P.sb([128, 8, 96], BF16, "Qcat")
        qrn = P.sb([128, 8, 32], F32, "qrn")
        qra = P.sb([128, 8, 16], F32, "qra")
        qrb = P.sb([128, 8, 16], F32, "qrb")
        QT = P.sb([96, 8, 128], BF16, "QT")
        KTb = [P.sb([96, 8, 128], BF16, "KTb%d" % i_) for i_ in range(2)]
        Vb = [P.sb([128, 520], BF16, "Vb%d" % i_) for i_ in range(2)]
        PT = [P.sb([128, 512], BF16, "PT%d" % i_) for i_ in range(2)]
        rec = P.sb([128, 8], F32, "rec")
        oatt = P.sb([128, 8, 64], BF16, "oatt")
        burb = bur[:].rearrange("p a b -> p (a b)").bitcast(BF16)
        mixT = burb[:, 0:1024].rearrange("p (k t) -> p k t", k=8)
        gy = burb[:, 1024:1536].rearrange("p (k t) -> p k t", k=4)
        g1 = m1
        g2 = m2
        sig = gre
        x2 = Hs[:].rearrange("p a b c d -> p (a b c d)")[:, 0:1024]
        ATT_SCALE = 1.0 / math.sqrt(96.0)

        gffn = bui[:].rearrange("p a b -> p (a b)")
        load(gffn, d_gffn.unsqueeze(0).to_broadcast([128, D_MODEL]), "gffn")
        xn2 = uT[:].rearrange("p a b -> p (a b)").bitcast(F32)
        xn2T = xnT4[:, :, 128:256]
        q2c = xnT4[:, 0, 256:384]
        wqs = BTr[:].rearrange("p a b -> p (a b)").bitcast(F32).rearrange("p (k n) -> p k n", k=8)
        wqb = CZr[:].rearrange("p a b -> p (a b)")[:, 0:1024].rearrange("p (k n) -> p k n", k=8)
        czf = CZi[:].rearrange("p a b -> p (a b)").bitcast(F32)
        czb = CZi[:].rearrange("p a b -> p (a b)")
        btu = BTi[:].rearrange("p a b -> p (a b)").bitcast(U32)
        keysf = czf[:, 0:256].rearrange("p (c k) -> p c k", c=2)
        keysb = czb[:, 1024:1280].rearrange("p (c k) -> p c k", c=2)
        load(keysf, d_keysT, "keysf")
        P.op("dve", lambda E: E.tensor_copy(out=keysb, in_=keysf), reads=["keysf"], writes=["keysb"])
        iot = czf[:, 300:316]
        load(iot, d_iota, "iot")
        gsum = czf[:, 320:328]
        s_top = Gre[:, 0:256].rearrange("p (a b) -> p a b", a=16)
        i_topf = Gre[:, 256:512].rearrange("p (a b) -> p a b", a=16)
        wrk = Gim[:, 0:256]
        cand = Gim[:, 256:512].rearrange("p (a b) -> p a b", a=16)
        top16 = q1[:, 0:128].rearrange("p (a b) -> p a b", a=8)
        pa_f = q1[:, 128:256].rearrange("p (a b) -> p a b", a=8)
        pb_f = q1[:, 256:384].rearrange("p (a b) -> p a b", a=8)
        gsm = q1[:, 384:512].rearrange("p (a b) -> p a b", a=8)
        eqt = q2[:, 0:256].rearrange("p (a b) -> p a b", a=16)
        isel = q2[:, 256:512].rearrange("p (c h j) -> p c h j", c=2, h=8)
        idxf = gim[:, 0:128]
        pre = gim[:, 128:256]
        pg1 = gim[:, 256:384]
        coef = gim[:, 384:512]
        i_top = btu[:, 0:256].rearrange("p (a b) -> p a b", a=16)
        pos = btu[:, 256:384].rearrange("p (a b) -> p a b", a=8)
        pa_u = btu[:, 384:512].rearrange("p (a b) -> p a b", a=8)
        pb_u = btu[:, 512:640].rearrange("p (a b) -> p a b", a=8)
        idxu = btu[:, 640:768]
        NEG = -1.0e30

        wr_flat = w_rest[:].rearrange("p a b -> p (a b)").bitcast(F32)
        xgb = [(Hs[:].rearrange("p a b c d -> p (a b c d)")[:, 1024:2048], "xgb0"),
               (ropeF[:].rearrange("p a b -> p (a b)"), "xgb1"),
               (wr_flat[:, 0:1024], "xgb2"), (wr_flat[:, 1024:2048], "xgb3"), (wr_flat[:, 2048:3072], "xgb4")]

        def halves(t, key):
            return [(t[:], key)]

        vsc = [xnb[:], CZr[:].rearrange("p a b -> p (a b)")[:, 1024:2048]]

        def top16_of(vals_ap, n, out_vals, out_idx, rkeys, wkeys):
            P.op("dve", lambda E: E.max(out=out_vals[:, 0:8], in_=vals_ap), reads=rkeys, writes=wkeys)
            P.op("dve", lambda E: E.max_index(out=out_idx[:, 0:8], in_max=out_vals[:, 0:8], in_values=vals_ap),
                 reads=rkeys + wkeys, writes=wkeys)
            P.op("dve", lambda E: E.match_replace(out=wrk[:, 0:n], in_to_replace=out_vals[:, 0:8], in_values=vals_ap,
                                                  imm_value=NEG), reads=rkeys + wkeys, writes=["wrk"])
            P.op("dve", lambda E: E.max(out=out_vals[:, 8:16], in_=wrk[:, 0:n]), reads=["wrk"] + wkeys, writes=wkeys)
            P.op("dve", lambda E: E.max_index(out=out_idx[:, 8:16], in_max=out_vals[:, 8:16], in_values=wrk[:, 0:n]),
                 reads=["wrk"] + wkeys, writes=wkeys)

        def peer(x2_ap, x2k, gbufs):
            act(xnb[:], x2_ap, AF.Square, [x2k], ["xnb", "ss0"], accum_out=ss[:, 0:1])
            rstd_from_ss(ss[:, 0:1], "ss0", D_MODEL)
            stt(xn2, x2_ap, ss[:, 0:1], gffn, ALU.mult, ALU.mult, [x2k, "ss0", "gffn"], ["xn2"])
            P.op("act", lambda E: E.copy(out=xnb[:], in_=xn2), reads=["xn2"], writes=["xnb"])
            for kt in range(8):
                P.op("pe", lambda E, kt=kt: E.transpose(out=pT[:, kt * 128:(kt + 1) * 128],
                                                         in_=xnb[:, kt * 128:(kt + 1) * 128], identity=ident[:]),
                     reads=["xnb", "ident"], writes=[pTk])
            P.op("act", lambda E: E.copy(out=xn2T, in_=pT.rearrange("p (k t) -> p k t", k=8)), reads=[pTk], writes=["xn2T"])
            for hc in range(16):
                c_ = hc % 2
                load(wqs, d_wq.rearrange("(k p) n -> p k n", p=128)[:, :, hc * 128:(hc + 1) * 128], "wqs")
                P.op("act", lambda E: E.copy(out=wqb, in_=wqs), reads=["wqs"], writes=["wqb"])
                for kt in range(8):
                    P.op("pe", lambda E, kt=kt: E.matmul(pb[1][:, 0:128], lhsT=wqb[:, kt, :], rhs=xn2T[:, kt, :],
                                                          start=(kt == 0), stop=(kt == 7)),
                         reads=["wqb", "xn2T"], writes=["pb1"])
                P.op("act", lambda E: E.copy(out=q2c, in_=pb[1][:, 0:128]), reads=["pb1"], writes=["q2c"])
                P.op("pe", lambda E, c_=c_: E.matmul(pb[2][:, 0:128], lhsT=q2c, rhs=keysb[:, c_, :], start=True, stop=True),
                     reads=["q2c", "keysb"], writes=["pb2"])
                top16_of(pb[2][:, 0:128], 128, s_top[:, hc, :], i_top[:, hc, :], ["pb2"], ["s_top", "i_top"])
            P.op("dve", lambda E: E.tensor_copy(out=i_topf, in_=i_top), reads=["i_top"], writes=["i_topf"])
            for h in range(8):
                tt("dve", cand, s_top[:, 2 * h, :].unsqueeze(2).to_broadcast([128, 16, 16]),
                   s_top[:, 2 * h + 1, :].unsqueeze(1).to_broadcast([128, 16, 16]), ALU.add, ["s_top"], ["cand"])
                top16_of(cand.rearrange("p a b -> p (a b)"), 256, top16[:, h, :], pos[:, h, :], ["cand"], ["top16", "pos"])
            tt("dve", gsm, top16, top16[:, :, 0:1].to_broadcast([128, 8, 16]), ALU.subtract, ["top16"], ["gsm"])
            act(gsm, gsm, AF.Exp, ["gsm"], ["gsm"])
            P.op("dve", lambda E: E.tensor_reduce(out=gsum, in_=gsm, axis=AX.X, op=ALU.add), reads=["gsm"], writes=["gsum"])
            P.op("dve", lambda E: E.reciprocal(out=gsum, in_=gsum), reads=["gsum"], writes=["gsum"])
            tt("dve", gsm, gsm, gsum.unsqueeze(2).to_broadcast([128, 8, 16]), ALU.mult, ["gsum"], ["gsm"])
            tsc("dve", pa_u, pos, 4, ALU.logical_shift_right, ["pos"], ["pa_u"])
            tsc("dve", pb_u, pos, 15, ALU.bitwise_and, ["pos"], ["pb_u"])
            P.op("dve", lambda E: E.tensor_copy(out=pa_f, in_=pa_u), reads=["pa_u"], writes=["pa_f"])
            P.op("dve", lambda E: E.tensor_copy(out=pb_f, in_=pb_u), reads=["pb_u"], writes=["pb_f"])
            for h in range(8):
                for c_, pf in ((0, pa_f), (1, pb_f)):
                    tt("dve", eqt, pf[:, h, :].unsqueeze(2).to_broadcast([128, 16, 16]),
                       iot.unsqueeze(1).to_broadcast([128, 16, 16]), ALU.is_equal, ["pa_f", "pb_f", "iot"], ["eqt"])
                    tt("dve", eqt, eqt, i_topf[:, 2 * h + c_, :].unsqueeze(1).to_broadcast([128, 16, 16]), ALU.mult,
                       ["i_topf"], ["eqt"])
                    P.op("dve", lambda E, h=h, c_=c_: E.tensor_reduce(out=isel[:, c_, h, :], in_=eqt, axis=AX.X, op=ALU.add),
                         reads=["eqt"], writes=["isel"])
            stt(idxf, isel[:, 0, :, :].rearrange("p h j -> p (h j)"), 128.0, isel[:, 1, :, :].rearrange("p h j -> p (h j)"),
                ALU.mult, ALU.add, ["isel"], ["idxf"])
            P.op("dve", lambda E: E.tensor_copy(out=idxu, in_=idxf), reads=["idxf"], writes=["idxu"])
            for sl in range(128):
                gb, gk = gbufs[sl % len(gbufs)]
                P.dma("pool", lambda E, sl=sl, gb=gb: E.indirect_dma_start(
                    out=gb, out_offset=None, in_=d_pu, in_offset=bass.IndirectOffsetOnAxis(ap=idxu[:, sl:sl + 1], axis=0)),
                    reads=["idxu"], writes=[gk])
                P.op("dve", lambda E, sl=sl, gb=gb: E.scalar_tensor_tensor(
                    out=xnb[:], in0=gb, scalar=1.0, in1=xn2, op0=ALU.mult, op1=ALU.mult, accum_out=pre[:, sl:sl + 1]),
                    reads=[gk, "xn2"], writes=["xnb", "pre"])
            tt("dve", pg1, pre, pre, ALU.mult, ["pre"], ["pg1"])
            tsc("dve", pg1, pg1, 0.044715, ALU.mult, ["pg1"], ["pg1"], s2=1.0, op1=ALU.add)
            tt("dve", pg1, pg1, pre, ALU.mult, ["pre"], ["pg1"])
            act(pg1, pg1, AF.Tanh, ["pg1"], ["pg1"], scale=0.7978845608028654)
            tsc("dve", pg1, pg1, 1.0, ALU.add, ["pg1"], ["pg1"], s2=0.5, op1=ALU.mult)
            tt("dve", pg1, pg1, pre, ALU.mult, ["pre"], ["pg1"])
            tt("dve", coef, pg1, gsm.rearrange("p h j -> p (h j)"), ALU.mult, ["pg1", "gsm"], ["coef"])
            for sl in range(128):
                gb, gk = gbufs[sl % len(gbufs)]
                P.dma("pool", lambda E, sl=sl, gb=gb: E.indirect_dma_start(
                    out=gb, out_offset=None, in_=d_pv, in_offset=bass.IndirectOffsetOnAxis(ap=idxu[:, sl:sl + 1], axis=0)),
                    reads=["idxu"], writes=[gk])
                vs = vsc[sl % 2]
                vk = ("xnb", "vsc1")[sl % 2]
                P.op("act", lambda E, sl=sl, gb=gb, vs=vs: E.activation(out=vs, in_=gb, func=AF.Copy, scale=coef[:, sl:sl + 1]),
                     reads=[gk, "coef"], writes=[vk])
                for half in range(2):
                    P.op("pe", lambda E, sl=sl, vs=vs, half=half: E.matmul(
                        pb[6 + half][:], lhsT=ident[:], rhs=vs[:, half * 512:(half + 1) * 512],
                        start=(sl == 0), stop=(sl == 127)), reads=[vk, "ident"], writes=["pb%d" % (6 + half)])
            for half in range(2):
                tt("dve", x2_ap[:, half * 512:(half + 1) * 512], x2_ap[:, half * 512:(half + 1) * 512], pb[6 + half][:], ALU.add,
                   ["pb%d" % (6 + half), x2k], [x2k])

        def q_part(x, xk, cs_ap):
            act(junk[:], x[:], AF.Square, [xk], ["junk", "ss0"], accum_out=ss[:, 0:1])
            rstd_from_ss(ss[:, 0:1], "ss0", D_MODEL)
            tsc("dve", xnb[:], x[:], ss[:, 0:1], ALU.mult, [xk, "ss0"], ["xnb"])
            for kt in range(8):
                P.op("pe", lambda E, kt=kt: E.transpose(out=pT[:, kt * 128:(kt + 1) * 128],
                                                         in_=xnb[:, kt * 128:(kt + 1) * 128], identity=ident[:]),
                     reads=["xnb", "ident"], writes=[pTk])
            P.op("act", lambda E: E.copy(out=xnT4[:, :, 0:128], in_=pT.rearrange("p (k t) -> p k t", k=8)),
                 reads=[pTk], writes=[("xnT4", 0)])
            for kt in range(8):
                P.op("pe", lambda E, kt=kt: E.matmul(pb[1][:, 0:384], lhsT=xnT4[:, kt, 0:128], rhs=w_cq[:, kt, :],
                                                      start=(kt == 0), stop=(kt == 7)),
                     reads=[("xnT4", 0), ("w_cq", kt)], writes=["pb1"])
            act(junk[:, 0:384], pb[1][:, 0:384], AF.Square, ["pb1"], ["junk", "ss1"], accum_out=ss[:, 1:2])
            rstd_from_ss(ss[:, 1:2], "ss1", 384)
            stt(cqn[:], pb[1][:, 0:384], ss[:, 1:2], gql[:], ALU.mult, ALU.mult, ["pb1", "ss1", "gql"], ["cqn"])
            for c_ in range(3):
                P.op("pe", lambda E, c_=c_: E.transpose(out=pT[:, c_ * 128:(c_ + 1) * 128], in_=cqn[:, c_ * 128:(c_ + 1) * 128],
                                                         identity=ident[:]), reads=["cqn", "ident"], writes=[pTk])
            P.op("act", lambda E: E.copy(out=cqT[:], in_=pT[:, 0:384].rearrange("p (c t) -> p c t", c=3)),
                 reads=[pTk], writes=["cqT"])
            for c_ in range(3):
                P.op("pe", lambda E, c_=c_: E.matmul(pb[6][:], lhsT=cqT[:, c_, :], rhs=w_uq_b[:, c_, 0:512],
                                                      start=(c_ == 0), stop=(c_ == 2)),
                     reads=["cqT", "w_uq_b"], writes=["pb6"])
            for c_ in range(3):
                P.op("pe", lambda E, c_=c_: E.matmul(pb[7][:, 0:256], lhsT=cqT[:, c_, :], rhs=w_uq_b[:, c_, 512:768],
                                                      start=(c_ == 0), stop=(c_ == 2)),
                     reads=["cqT", "w_uq_b"], writes=["pb7"])
            act(junk[:, 0:512], pb[6][:], AF.Square, ["pb6"], ["junk"])
            P.op("dve", lambda E: E.tensor_reduce(out=ssk[:, 0:8], in_=junk[:, 0:512].rearrange("p (h d) -> p h d", h=8),
                                                  axis=AX.X, op=ALU.add), reads=["junk"], writes=["ssk"])
            rstd_from_ss(ssk[:, 0:8], "ssk", 64)
            tt("dve", tmpk[:], pb[6][:].rearrange("p (h d) -> p h d", h=8),
               ssk[:, 0:8].unsqueeze(2).to_broadcast([128, 8, 64]), ALU.mult, ["pb6", "ssk"], ["tmpk"])
            tt("dve", Qcat[:, :, 0:64], tmpk[:], gqn[:].unsqueeze(1).to_broadcast([128, 8, 64]), ALU.mult,
               ["tmpk", "gqn"], ["Qcat"])
            act(junk[:, 512:768], pb[7][:, 0:256], AF.Square, ["pb7"], ["junk2"])
            P.op("dve", lambda E: E.tensor_reduce(out=ssk[:, 8:16], in_=junk[:, 512:768].rearrange("p (h d) -> p h d", h=8),
                                                  axis=AX.X, op=ALU.add), reads=["junk2"], writes=["ssk2"])
            rstd_from_ss(ssk[:, 8:16], "ssk2", 32)
            tt("dve", qrn[:], pb[7][:, 0:256].rearrange("p (h d) -> p h d", h=8),
               ssk[:, 8:16].unsqueeze(2).to_broadcast([128, 8, 32]), ALU.mult, ["pb7", "ssk2"], ["qrn"])
            tt("dve", qrn[:], qrn[:], gqr[:].unsqueeze(1).to_broadcast([128, 8, 32]), ALU.mult, ["gqr"], ["qrn"])
            cosb = cs_ap[:, 0:16].unsqueeze(1).to_broadcast([128, 8, 16])
            sinb = cs_ap[:, 16:32].unsqueeze(1).to_broadcast([128, 8, 16])
            tt("dve", qra[:], qrn[:, :, 0:16], cosb, ALU.mult, ["qrn", "ropeO", "ropeS"], ["qra"])
            tt("dve", qrb[:], qrn[:, :, 16:32], sinb, ALU.mult, ["qrn", "ropeO", "ropeS"], ["qrb"])
            tt("dve", Qcat[:, :, 64:80], qra[:], qrb[:], ALU.subtract, ["qra", "qrb", "Qcat"], ["Qcat"])
            tt("dve", qra[:], qrn[:, :, 16:32], cosb, ALU.mult, ["qrn", "ropeO", "ropeS"], ["qra"])
            tt("dve", qrb[:], qrn[:, :, 0:16], sinb, ALU.mult, ["qrn", "ropeO", "ropeS"], ["qrb"])
            tt("dve", Qcat[:, :, 80:96], qra[:], qrb[:], ALU.add, ["qra", "qrb", "Qcat"], ["Qcat"])

        def own_tile(i):
            x = xt[i % 2]
            xk = "xt%d" % (i % 2)
            load(x[:], xo[i * 128:(i + 1) * 128, :], xk)
            q_part(x, xk, ropeO[:, i, :])
            for h in range(8):
                P.op("pe", lambda E, h=h: E.transpose(out=pT[0:96, h * 128:(h + 1) * 128], in_=Qcat[:, h, :], identity=ident[:]),
                     reads=["Qcat", "ident"], writes=[pTk])
            P.op("act", lambda E: E.copy(out=QT[:], in_=pT[0:96, :].rearrange("p (h t) -> p h t", h=8)),
                 reads=[pTk], writes=["QT"])
            for hg in range(2):
                P.op("pe", lambda E, hg=hg: E.matmul(pb[4 + hg][:], lhsT=zer[:, 0:128], rhs=zer[:], start=True, stop=False),
                     reads=["zer"], writes=["pb%d" % (4 + hg)])
            nkb = 2 * i + 2
            for kb in range(nkb):
                kbuf = KTb[kb % 2]
                kkey = "KTb%d" % (kb % 2)
                vbuf = Vb[kb % 2]
                vkey = "Vb%d" % (kb % 2)
                P.dma("sp", lambda E, kb=kb, kbuf=kbuf: E.dma_start(out=kbuf[:], in_=KT_d[:, :, kb * 128:(kb + 1) * 128]),
                      reads=[("KTd", kb)], writes=[kkey])
                P.dma("sp", lambda E, kb=kb, vbuf=vbuf: E.dma_start(out=vbuf[:], in_=V_d[kb * 128:(kb + 1) * 128, :]),
                      reads=[("Vd", kb)], writes=[vkey])
                for hg in range(2):
                    sp_ = pb[2 + hg]
                    spk = "pb%d" % (2 + hg)
                    for j in range(4):
                        h = hg * 4 + j
                        msk_i = kb - 2 * i
                        P.op("pe", lambda E, h=h, j=j, sp_=sp_, kbuf=kbuf, msk_i=msk_i: E.matmul(
                            sp_[:, j * 128:(j + 1) * 128], lhsT=kbuf[:, h, :], rhs=QT[:, h, :], start=True, stop=(msk_i < 0)),
                            reads=[kkey, "QT"], writes=[spk])
                        if msk_i >= 0:
                            P.op("pe", lambda E, j=j, sp_=sp_, msk_i=msk_i: E.matmul(
                                sp_[:, j * 128:(j + 1) * 128], lhsT=ident[:], rhs=msk[:, msk_i, :], start=False, stop=True),
                                reads=["ident", "msk"], writes=[spk])
                    pt_ = PT[hg]
                    ptk = "PT%d" % hg
                    act(pt_[:], sp_[:], AF.Exp, [spk], [ptk], scale=ATT_SCALE)
                    for j in range(4):
                        h = hg * 4 + j
                        P.op("pe", lambda E, h=h, j=j, hg=hg, pt_=pt_, vbuf=vbuf: E.matmul(
                            pb[4 + hg][:, j * 65:(j + 1) * 65], lhsT=pt_[:, j * 128:(j + 1) * 128],
                            rhs=vbuf[:, h * 65:(h + 1) * 65], start=False, stop=(kb == nkb - 1), skip_group_check=True),
                            reads=[ptk, vkey], writes=["pb%d" % (4 + hg)])
            for hg in range(2):
                ov = pb[4 + hg][:, 0:260].rearrange("p (h d) -> p h d", h=4)
                P.op("dve", lambda E, hg=hg, ov=ov: E.reciprocal(out=rec[:, hg * 4:(hg + 1) * 4].unsqueeze(2), in_=ov[:, :, 64:65]),
                     reads=["pb%d" % (4 + hg)], writes=["rec"])
                tt("dve", oatt[:, hg * 4:(hg + 1) * 4, :], ov[:, :, 0:64],
                   rec[:, hg * 4:(hg + 1) * 4].unsqueeze(2).to_broadcast([128, 4, 64]), ALU.mult,
                   ["pb%d" % (4 + hg), "rec"], ["oatt"])
            oflat = oatt[:].rearrange("p h d -> p (h d)")
            for c_ in range(4):
                P.op("pe", lambda E, c_=c_: E.transpose(out=pT[:, c_ * 128:(c_ + 1) * 128], in_=oflat[:, c_ * 128:(c_ + 1) * 128],
                                                         identity=ident[:]), reads=["oatt", "ident"], writes=[pTk])
            P.op("act", lambda E: E.copy(out=mixT[:, 4:8, :], in_=pT[:, 0:512].rearrange("p (c t) -> p c t", c=4)),
                 reads=[pTk], writes=["mixT_a"])
            glu_part(Yown[:, :, i * 128:(i + 1) * 128], [("Yown", ft, i // 2) for ft in range(4)], 128)
            out_part(x, xk)
            peer(x2, "x2", halves(junk, "junk") + halves(xt[(i + 1) % 2], "xt%d" % ((i + 1) % 2)) + xgb)
            P.dma("sp", lambda E: E.dma_start(out=o_y_p[i * 128:(i + 1) * 128, :], in_=x2[:]), reads=["x2"],
                  writes=[("o_y_p", i)])
            out_keys.append(("o_y_p", i))

        def glu_part(yv_, ykeys, nt):
            g1v = g1[:].rearrange("p (f t) -> p f t", f=4)[:, :, 0:nt]
            g2v = g2[:].rearrange("p (f t) -> p f t", f=4)[:, :, 0:nt]
            tt("dve", g1v, yv_, yv_, ALU.mult, ykeys, ["g1"])
            tsc("dve", g1[:], g1[:], 0.044715, ALU.mult, ["g1"], ["g1"], s2=1.0, op1=ALU.add)
            tt("dve", g1v, g1v, yv_, ALU.mult, ykeys, ["g1"])
            act(g2[:], g1[:], AF.Tanh, ["g1"], ["g2"], scale=0.7978845608028654)
            tsc("dve", g2[:], g2[:], 1.0, ALU.add, ["g2"], ["g2"], s2=0.5, op1=ALU.mult)
            tt("dve", gy[:, :, 0:nt], g2v, yv_, ALU.mult, ykeys + ["g2"], ["gy"])
            for half in range(2):
                for ot in range(4):
                    o8 = half * 4 + ot
                    for k in range(4):
                        P.op("pe", lambda E, half=half, ot=ot, o8=o8, k=k: E.matmul(
                            pb[2 + half][:, ot * 128:(ot + 1) * 128], lhsT=w_glu_b[:, k, o8 * 128:(o8 + 1) * 128],
                            rhs=gy[:, k, :], start=(k == 0), stop=(k == 3)),
                            reads=["w_glu_b", "gy"], writes=["pb%d" % (2 + half)])
            act(sig[:], pb[3][:], AF.Sigmoid, ["pb3"], ["sig"])
            tt("dve", mixT[:, 0:4, :], pb[2][:].rearrange("p (c t) -> p c t", c=4), sig[:].rearrange("p (c t) -> p c t", c=4),
               ALU.mult, ["pb2", "sig"], ["mixT_g"])

        def out_part(x, xk):
            for half in range(2):
                for k in range(8):
                    P.op("pe", lambda E, half=half, k=k: E.matmul(
                        pb[6 + half][:], lhsT=mixT[:, k, :], rhs=w_out_b[:, k, half * 512:(half + 1) * 512],
                        start=(k == 0), stop=(k == 7)),
                        reads=["mixT_a", "mixT_g", "w_out_b"], writes=["pb%d" % (6 + half)])
                tt("dve", x2[:, half * 512:(half + 1) * 512], pb[6 + half][:], x[:, half * 512:(half + 1) * 512], ALU.add,
                   ["pb%d" % (6 + half), xk], ["x2"])

        for i in range(16):
            own_tile(i)

        P.barrier()
        Y16 = Yown[:].rearrange("p a b -> p (a b)")
        Y32 = Y16.bitcast(F32)
        YU = Y16.bitcast(U32)
        pidx = YU[:, 0:1024]
        latf = [Y32[:, 2048:2304], Y32[:, 2304:2560]]
        krf = [Y32[:, 2560:2592], Y32[:, 2592:2624]]
        lb = [Y16[:, 5248:5505], Y16[:, 5512:5769]]
        krb = [Y16[:, 5776:5808], Y16[:, 5808:5840]]
        lT = [Y16[:, 5840:6096].rearrange("p (c k) -> p c k", c=2), Y16[:, 6096:6352].rearrange("p (c k) -> p c k", c=2)]
        krT = [Y16[:, 6352:6480], Y16[:, 6480:6608]]
        qtT = Y16[:, 6608:7632].rearrange("p (c h t) -> p c h t", c=2, h=8)
        pts = [Y16[:, 7632:7664], Y16[:, 7664:7696]]
        sc1 = Y32[:, 3848:3880]
        olat = Y16[:, 7760:8016]
        olT = Y16[:, 8016:8080].rearrange("p (c n) -> p c n", c=2)
        maskn = czf[:, 330:362]
        load(maskn[0:4, :], d_maskn, "maskn")
        for bf in range(2):
            P.op("pool", lambda E, bf=bf: E.memset(lb[bf][:, 256:257], 1.0), writes=["lb%d" % bf])
        pti = xt[0][:].bitcast(I32)
        load(pti, d_ptab.to_broadcast([128, 1024]), "xt0")
        piota = czf[:, 364:365]
        load(piota, d_piota, "piota")
        tsc("dve", xt[1][:], pti, 128.0, ALU.mult, ["xt0"], ["xt1"])
        tsc("dve", pidx, xt[1][:], piota, ALU.add, ["xt1", "piota"], ["pidx"])
        wukT = [KTb[0][0:64, :, :].rearrange("p a b -> p (a b)"), KTb[1][0:64, :, :].rearrange("p a b -> p (a b)")]
        for hh in range(2):
            load(xt[hh][0:64, :], d_wukT[:, hh * 4:(hh + 1) * 4, :].rearrange("p a b -> p (a b)"), "xt%d" % hh)
            P.op("act", lambda E, hh=hh: E.copy(out=wukT[hh], in_=xt[hh][0:64, :]), reads=["xt%d" % hh], writes=["KTb%d" % hh])
        xsx = xt[0]
        load(xsx[:], xs[:, :], "xt0")
        q_part(xsx, "xt0", ropeS[:, :])
        Qg = Vb[0][:, 0:512].rearrange("p (h d) -> p h d", h=8)
        tt("dve", Qg, Qcat[:, :, 0:64], gkn[:].unsqueeze(1).to_broadcast([128, 8, 64]), ALU.mult, ["Qcat", "gkn"], ["Vb0"])
        for h in range(8):
            P.op("pe", lambda E, h=h: E.transpose(out=pT[0:64, h * 128:(h + 1) * 128], in_=Qg[:, h, :], identity=ident[:]),
                 reads=["Vb0", "ident"], writes=[pTk])
        P.op("act", lambda E: E.copy(out=QT[0:64, :, :], in_=pT[0:64, :].rearrange("p (h t) -> p h t", h=8)),
             reads=[pTk], writes=["QT"])
        for cc in range(2):
            for h in range(8):
                P.op("pe", lambda E, cc=cc, h=h: E.matmul(
                    pb[1 + cc][:, h * 64:(h + 1) * 64], lhsT=wukT[h // 4][:, (h % 4) * 256 + cc * 128:(h % 4) * 256 + (cc + 1) * 128],
                    rhs=QT[0:64, h, 0:64], start=True, stop=True),
                    reads=["KTb0", "KTb1", "QT"], writes=["pb%d" % (1 + cc)])
            P.op("act", lambda E, cc=cc: E.copy(out=qtT[:, cc, :, :], in_=pb[1 + cc][:].rearrange("p (h t) -> p h t", h=8)),
                 reads=["pb%d" % (1 + cc)], writes=["qtT"])
        for h in range(8):
            P.op("pe", lambda E, h=h: E.transpose(out=pT[0:32, h * 128:(h + 1) * 128], in_=Qcat[:, h, 64:96], identity=ident[:]),
                 reads=["Qcat", "ident"], writes=[pTk])
        qrT = [PT[0][0:32, :].rearrange("p (h t) -> p h t", h=4), PT[1][0:32, :].rearrange("p (h t) -> p h t", h=4)]
        for hh in range(2):
            P.op("act", lambda E, hh=hh: E.copy(out=PT[hh][0:32, :], in_=pT[0:32, hh * 512:(hh + 1) * 512]),
                 reads=[pTk], writes=["PT%d" % hh])

        cnt = [0]

        pTs = [pb[0][:].bitcast(BF16), pb[3][:].bitcast(BF16)]
        pTks = ["pb0", "pb3"]
        p6s = [pb[6], pb[1]]
        p6k = ["pb6", "pb1"]
        p7s = [pb[7], pb[2]]
        p7k = ["pb7", "pb2"]
        sc1s = [Y32[:, 3848:3880], Y32[:, 4040:4072]]

        def page(b, j):
            n = 128 if j < NPAGE else 4
            bf = cnt[0] % 2
            cnt[0] += 1
            lk_, kk_, lbk, kbk, ltk, ktk, ptk = ("latf%d" % bf, "krf%d" % bf, "lb%d" % bf, "krb%d" % bf, "lT%d" % bf,
                                                 "krT%d" % bf, "pts%d" % bf)
            pTx, pTxk, p6, k6, p7, k7 = pTs[bf], pTks[bf], p6s[bf], p6k[bf], p7s[bf], p7k[bf]
            jk, skk, s1k = "junkP%d" % bf, "sskP%d" % bf, "sc1P%d" % bf
            jv = junk[0:n, bf * 512:(bf + 1) * 512]
            sv = ssk[0:n, bf * 8:(bf + 1) * 8]
            sc1 = sc1s[bf]
            if j < NPAGE:
                col = b * NPAGE + j
                P.dma("pool", lambda E: E.indirect_dma_start(
                    out=latf[bf], out_offset=None, in_=d_clat,
                    in_offset=bass.IndirectOffsetOnAxis(ap=pidx[:, col:col + 1], axis=0)), reads=["pidx"], writes=[lk_])
                P.dma("pool", lambda E: E.indirect_dma_start(
                    out=krf[bf], out_offset=None, in_=d_ckr,
                    in_offset=bass.IndirectOffsetOnAxis(ap=pidx[:, col:col + 1], axis=0)), reads=["pidx"], writes=[kk_])
            else:
                P.dma("sp", lambda E: E.dma_start(out=latf[bf][0:4, :], in_=lat_s_scr[4 * b:4 * b + 4, :]),
                      reads=["lat_s_scr"], writes=[lk_])
                P.dma("sp", lambda E: E.dma_start(out=krf[bf][0:4, :], in_=kr_s_scr[4 * b:4 * b + 4, :]),
                      reads=["kr_s_scr"], writes=[kk_])
            P.op("act", lambda E: E.copy(out=lb[bf][0:n, 0:256], in_=latf[bf][0:n, :]), reads=[lk_], writes=[lbk])
            P.op("dve", lambda E: E.tensor_copy(out=krb[bf][0:n, :], in_=krf[bf][0:n, :]), reads=[kk_], writes=[kbk])
            for cc in range(2):
                P.op("pe", lambda E, cc=cc: E.transpose(out=pTx[:, cc * 128:cc * 128 + n], in_=lb[bf][0:n, cc * 128:(cc + 1) * 128],
                                                         identity=ident[0:n, 0:n]), reads=[lbk, "ident"], writes=[pTxk])
            P.op("pe", lambda E: E.transpose(out=pTx[0:32, 256:256 + n], in_=krb[bf][0:n, :], identity=ident[0:n, 0:n]),
                 reads=[kbk, "ident"], writes=[pTxk])
            P.op("act", lambda E: E.copy(out=lT[bf][:, :, 0:n], in_=pTx[:, 0:256].rearrange("p (c k) -> p c k", c=2)[:, :, 0:n]),
                 reads=[pTxk], writes=[ltk])
            P.op("act", lambda E: E.copy(out=krT[bf][0:32, 0:n], in_=pTx[0:32, 256:256 + n]), reads=[pTxk], writes=[ktk])
            for cc in range(2):
                P.op("pe", lambda E, cc=cc: E.matmul(p6[0:n, :], lhsT=lT[bf][:, cc, 0:n], rhs=w_uk_b[:, cc, :],
                                                      start=(cc == 0), stop=(cc == 1)), reads=[ltk, "w_uk_b"], writes=[k6])
            for cc in range(2):
                P.op("pe", lambda E, cc=cc: E.matmul(p7[0:n, 0:32], lhsT=lT[bf][:, cc, 0:n], rhs=qtT[:, cc, :, 4 * b:4 * b + 4],
                                                      start=(cc == 0), stop=(cc == 1)), reads=[ltk, "qtT"], writes=[k7])
            for hh in range(2):
                P.op("pe", lambda E, hh=hh: E.matmul(p7[0:n, 64 + 16 * hh:80 + 16 * hh], lhsT=krT[bf][0:32, 0:n],
                                                      rhs=qrT[hh][:, :, 4 * b:4 * b + 4], start=True, stop=True),
                     reads=[ktk, "PT%d" % hh], writes=[k7])
            act(jv, p6[0:n, :], AF.Square, [k6], [jk])
            P.op("dve", lambda E: E.tensor_reduce(out=sv, in_=jv.rearrange("p (h d) -> p h d", h=8),
                                                  axis=AX.X, op=ALU.add), reads=[jk], writes=[skk])
            rstd_from_ss(sv, skk, 64)
            s3 = sc1[0:n, :].rearrange("p (h q) -> p h q", h=8)
            tt("dve", s3, p7[0:n, 0:32].rearrange("p (h q) -> p h q", h=8),
               sv.unsqueeze(2).to_broadcast([n, 8, 4]), ALU.mult, [k7, skk], [s1k])
            tt("dve", sc1[0:n, :], sc1[0:n, :], p7[0:n, 64:96], ALU.add, [k7], [s1k])
            act(pts[bf][0:n, :], sc1[0:n, :], AF.Exp, [s1k], [ptk], scale=ATT_SCALE)
            if j == NPAGE:
                tt("dve", pts[bf][0:n, :], pts[bf][0:n, :], maskn[0:n, :], ALU.mult, ["maskn"], [ptk])
            P.op("pe", lambda E: E.matmul(pb[4][0:32, 0:257], lhsT=pts[bf][0:n, :], rhs=lb[bf][0:n, 0:257],
                                           start=(j == PAGE_LIST[0]), stop=(j == PAGE_LIST[-1])), reads=[ptk, lbk], writes=["pb4"])

        for b in range(SPC_RUN):
            for j in PAGE_LIST:
                page(b, j)
            P.op("dve", lambda E: E.reciprocal(out=rec[0:32, 0:1], in_=pb[4][0:32, 256:257]), reads=["pb4"], writes=["rec"])
            tsc("dve", olat[0:32, :], pb[4][0:32, 0:256], rec[0:32, 0:1], ALU.mult, ["pb4", "rec"], ["olat"])
            for cc in range(2):
                P.op("pe", lambda E, cc=cc: E.transpose(out=pT[:, cc * 32:(cc + 1) * 32], in_=olat[0:32, cc * 128:(cc + 1) * 128],
                                                         identity=ident[0:32, 0:32]), reads=["olat", "ident"], writes=[pTk])
            P.op("act", lambda E: E.copy(out=olT, in_=pT[:, 0:64].rearrange("p (c n) -> p c n", c=2)), reads=[pTk], writes=["olT"])
            for h in range(8):
                r0 = (h % 2) * 64
                c0 = (h // 2) * 64 + 4 * b
                for cc in range(2):
                    P.op("pe", lambda E, h=h, cc=cc, r0=r0, c0=c0: E.matmul(
                        pb[5][r0:r0 + 64, c0:c0 + 4], lhsT=w_uv_b[:, cc, h * 64:(h + 1) * 64], rhs=olT[:, cc, h * 4:(h + 1) * 4],
                        start=(cc == 0), stop=(cc == 1), skip_group_check=True), reads=["w_uv_b", "olT"], writes=["pb5"])
        P.op("act", lambda E: E.copy(out=mixT[:, 4:8, 0:64], in_=pb[5][:, 0:256].rearrange("p (c t) -> p c t", c=4)),
             reads=["pb5"], writes=["mixT_a"])
        glu_part(Ys, ["ytmp"], 64)
        out_part(xsx, "xt0")
        peer(x2, "x2", halves(junk, "junk") + halves(xt[1], "xt1") + xgb)
        P.dma("sp", lambda E: E.dma_start(out=o_y_s, in_=x2[:]), reads=["x2"], writes=["o_y_s"])
        out_keys.append("o_y_s")

        P.finish(out_keys)
        P.run()
    return nc


def _rope_tables(pos):
    inv = (10000.0 ** (-np.arange(0, 32, 2, dtype=np.float32) / np.float32(32))).astype(np.float32)
    ang = pos.astype(np.float32)[:, None] * inv[None, :]
    return np.concatenate([np.cos(ang), np.sin(ang)], axis=1).astype(np.float32)


def _c(a):
    return np.ascontiguousarray(a, dtype=np.float32)


_DEBUG_SMALL = 0


def kernel(**inputs):
    f32 = np.float32
    x_prompt = np.asarray(inputs["x_prompt"], f32)
    x_sample = np.asarray(inputs["x_sample"], f32)

    dbg_small = bool(_DEBUG_SMALL)
    nc = build_program(1024 if dbg_small else 10240)

    rope_full = _rope_tables(np.arange(SEQ))
    rope_f = _c(rope_full.reshape(32, 128, 32).transpose(1, 0, 2))
    rs = _rope_tables(PAST + (np.arange(128) % 4))
    ident = np.eye(128, dtype=f32)

    def rows16(a):
        return _c(np.asarray(a, f32).reshape(16, 128).T)

    a_re = rows16(inputs["ssm_a_re"][0])
    a_im = rows16(inputs["ssm_a_im"][0])
    ldt = rows16(np.repeat(np.asarray(inputs["ssm_log_dt"][0], f32)[:, None], 64, axis=1))
    bb = _c(np.asarray(inputs["ssm_b"][0], f32).reshape(16, 128, 32).transpose(1, 0, 2))
    cc = _c(np.asarray(inputs["ssm_c"][0], f32).transpose(0, 2, 1, 3).reshape(16, 128, 32).transpose(1, 0, 2))
    dd = _c(np.asarray(inputs["ssm_d"][0], f32).reshape(4, 128).T)
    state = np.asarray(inputs["state_ssm"][0], f32)

    wuq = np.asarray(inputs["w_uq"][0], f32).reshape(384, 8, 96)
    wuq_p = _c(np.concatenate([wuq[:, :, :64].reshape(384, 512), wuq[:, :, 64:].reshape(384, 256)], axis=1))
    kk = np.arange(128)[:, None]
    qq = np.arange(128)[None, :]
    diag = np.where(kk <= qq, 0.0, -30000.0).astype(f32)
    shared = {
        "cache_lat": np.asarray(inputs["cache_kv_latent"][0], f32).reshape(-1, 256),
        "cache_kr": np.asarray(inputs["cache_k_rope"][0], f32).reshape(-1, 32),
        "piota": _c(np.arange(128, dtype=f32)[:, None]),
        "w_ukT": _c(np.asarray(inputs["w_uk"][0], f32).transpose(2, 1, 0)),
        "maskn": _c(np.tile((np.arange(4)[:, None] <= np.arange(4)[None, :]).astype(f32)[:, None, :], (1, 8, 1)).reshape(4, 32)),
        "norm_ffn": _c(inputs["norm_ffn"][0]),
        "peer_wq": _c(inputs["peer_wq"][0]),
        "peer_keysT": _c(np.asarray(inputs["peer_keys"][0], f32).transpose(2, 0, 1)),
        "peer_u": _c(inputs["peer_u"][0]),
        "peer_v": _c(inputs["peer_v"][0]),
        "iota16": _c(np.tile(np.arange(16, dtype=f32)[None, :], (128, 1))),
        "w_uk": _c(np.asarray(inputs["w_uk"][0], f32).reshape(256, 512)),
        "w_uv": _c(np.asarray(inputs["w_uv"][0], f32).reshape(256, 512)),
        "w_uq": wuq_p,
        "w_glu": _c(inputs["w_glu"][0]),
        "w_out": _c(inputs["w_out"][0]),
        "norm_q_lora": _c(inputs["norm_q_lora"][0]),
        "qk_gain_q_nope": _c(inputs["qk_gain_q_nope"][0]),
        "qk_gain_q_rope": _c(inputs["qk_gain_q_rope"][0]),
        "qk_gain_k_nope": _c(inputs["qk_gain_k_nope"][0]),
        "ident": ident,
        "rope_f": rope_f,
        "rope_s": rs,
        "norm_mix": _c(np.asarray(inputs["norm_mix"][0], f32).reshape(8, 128).T),
        "w_in": _c(inputs["w_in"][0]),
        "norm_kv_lora": _c(inputs["norm_kv_lora"][0]),
        "qk_gain_k_rope": _c(inputs["qk_gain_k_rope"][0]),
        "ssm_a_re": a_re, "ssm_a_im": a_im, "ssm_log_dt": ldt, "ssm_b": bb, "ssm_c": cc, "ssm_d": dd,
    }
    in_maps = []
    for c in range(NCORES):
        b = c // 2
        p = c % 2
        xs = np.zeros((128, D_MODEL), f32)
        xs[:STOK] = x_sample[c * SPC:(c + 1) * SPC].reshape(STOK, D_MODEL)
        st = state[c * SPC:(c + 1) * SPC].reshape(SPC, 16, 128, 2).transpose(2, 1, 0, 3)
        par = np.zeros((128, 2), f32)
        par[:, 0] = 1.0 - p
        par[:, 1] = p
        xb = x_prompt[b].reshape(16, 2, 128, D_MODEL)
        masks = np.zeros((128, 2, 128), f32)
        if p == 0:
            masks[:, 0, :] = diag
            masks[:, 1, :] = -30000.0
        else:
            masks[:, 1, :] = diag
        m = dict(shared)
        if dbg_small:
            ptc = np.asarray(inputs["page_table"], np.int32)[c * SPC:(c + 1) * SPC].reshape(-1)
            m["cache_lat"] = _c(np.asarray(inputs["cache_kv_latent"][0], f32)[ptc].reshape(-1, 256))
            m["cache_kr"] = _c(np.asarray(inputs["cache_k_rope"][0], f32)[ptc].reshape(-1, 32))
        m.update({
            "xo": _c(xb[:, p].reshape(2048, D_MODEL)),
            "rope_o": _c(rope_full.reshape(16, 2, 128, 32)[:, p].transpose(1, 0, 2)),
            "masks": masks,
            "xf": _c(x_prompt[b]),
            "xs": xs,
            "page_tab": (np.arange(1024, dtype=np.int32).reshape(1, 1024) if dbg_small else
                         np.ascontiguousarray(np.asarray(inputs["page_table"], np.int32)[c * SPC:(c + 1) * SPC].reshape(1, 1024))),
            "state_ssm": _c(st),
            "parity": par,
        })
        in_maps.append(m)

    res = run_bass_kernel_spmd(nc, in_maps, core_ids=list(range(NCORES)))
    R = res.results

    y_prompt = np.zeros((4, 16, 2, 128, D_MODEL), f32)
    for c in range(NCORES):
        y_prompt[c // 2, :, c % 2] = R[c]["o_y_p"].reshape(16, 128, D_MODEL)
    y_prompt = y_prompt.reshape(4, SEQ, D_MODEL)
    y_sample = np.concatenate([R[c]["o_y_s"][:STOK].reshape(SPC, 4, D_MODEL) for c in range(NCORES)]).astype(f32)
    lat_p = np.stack([R[2 * b]["o_lat_p"] for b in range(4)])[None]
    kr_p = np.stack([R[2 * b]["o_kr_p"] for b in range(4)])[None]
    ssm_p = np.stack([R[2 * b]["o_ssm_p"].reshape(32, 64, 2) for b in range(4)])[None]
    lat_s = np.concatenate([R[c]["o_lat_s"][:STOK].reshape(SPC, 4, 256) for c in range(NCORES)])[None]
    kr_s = np.concatenate([R[c]["o_kr_s"][:STOK].reshape(SPC, 4, 32) for c in range(NCORES)])[None]
    ssm_s = np.concatenate([R[c]["o_ssm_s"].reshape(SPC, 32, 64, 2) for c in range(NCORES)])[None]
    return (y_prompt, y_sample, lat_p.astype(f32), kr_p.astype(f32), ssm_p.astype(f32), lat_s.astype(f32),
            kr_s.astype(f32), ssm_s.astype(f32))
```
